# Optimizing a Trainium2 kernel written in Bass

```python
import math
import jax, jax.numpy as jnp
from jax import lax
import numpy as np

D_MODEL = 1024
BATCH = 8
SEQ = 2048
DEPTH = 1

NSA_HEADS = 8
NSA_GROUPS = 2
NSA_HPG = NSA_HEADS // NSA_GROUPS
HEAD_DIM = 64
NSA_DIM = NSA_HEADS * HEAD_DIM
NSA_KV_DIM = NSA_GROUPS * HEAD_DIM
CMP_BLOCK = 32
CMP_STRIDE = 16
CMP_HIDDEN = 256
SEL_BLOCK = 64
SEL_TOPN = 16
WINDOW = 512
Q_BLOCK = 128
SEL_Q_BLOCK = 32
FORCE_BONUS = 1.0e4
ATTN_SCALE = HEAD_DIM ** -0.5
RWKV_HEADS = 8
RWKV_HEAD_DIM = 64
RWKV_DIM = RWKV_HEADS * RWKV_HEAD_DIM
LORA_W = 64
LORA_A = 64
LORA_G = 128
RWKV_GN_EPS = 64e-5
N_BUCKETS = 32
MAX_DISTANCE = 128
PEER_HEADS = 8
N_KEYS = 128
N_EXPERTS = N_KEYS * N_KEYS
PEER_KEY_DIM = 128
PEER_HALF = PEER_KEY_DIM // 2
PEER_TOPK = 16
PEER_TOKEN_BLOCK = 128
ALPHA = (2.0 * DEPTH) ** 0.25
BETA = (8.0 * DEPTH) ** -0.25
LN_EPS = 1e-5
NEG = -1e30
RWKV_COLS = 3 * RWKV_DIM + LORA_W + LORA_A + LORA_G
IN_SIZES = (NSA_DIM,) + (NSA_KV_DIM,) * 6 + (3 * NSA_HEADS, RWKV_COLS, 2 * D_MODEL)
D_IN = int(sum(IN_SIZES))
IN_SPLITS = tuple(int(c) for c in np.cumsum(IN_SIZES)[:-1])
RWKV_SPLITS = tuple(int(c) for c in np.cumsum((RWKV_DIM, RWKV_DIM, RWKV_DIM, LORA_W, LORA_A, LORA_G))[:-1])

kernel_name = "nsa_rwkv7_peer_hybrid_deepnorm"


def _layer_norm(x, g, b):
    xf = x.astype(jnp.float32)
    mu = xf.mean(-1, keepdims=True)
    var = jnp.square(xf - mu).mean(-1, keepdims=True)
    return ((xf - mu) * lax.rsqrt(var + LN_EPS) * g + b).astype(x.dtype)


def _t5_bucket(dist):
    max_exact = N_BUCKETS // 2
    d = jnp.maximum(dist, 0)
    large = max_exact + (jnp.log(jnp.maximum(d, 1).astype(jnp.float32) / max_exact)
                         / math.log(MAX_DISTANCE / max_exact) * (N_BUCKETS - max_exact)).astype(jnp.int32)
    large = jnp.minimum(large, N_BUCKETS - 1)
    return jnp.where(d < max_exact, d, large)


def _masked_softmax(logits, mask):
    p = jax.nn.softmax(jnp.where(mask, logits.astype(jnp.float32), NEG), axis=-1)
    return jnp.where(mask, p, 0.0)


def _compress(kv, pos, w1, b1, w2, b2):
    B, S, G, D = kv.shape
    c = kv.reshape(B, S // CMP_STRIDE, CMP_STRIDE, G, D)
    blocks = jnp.concatenate([c[:, :-1], c[:, 1:]], axis=2) + pos[None, None, :, None, :]
    nc = blocks.shape[1]
    flat = blocks.transpose(0, 1, 3, 2, 4).reshape(B, nc, G, CMP_BLOCK * D)
    return jax.nn.gelu(flat @ w1 + b1) @ w2 + b2


def _nsa_compressed(qg, kc, vc, rel_bias):
    S = qg.shape[1]
    nc = kc.shape[1]
    t = jnp.arange(S)
    blk_end = jnp.arange(nc) * CMP_STRIDE + CMP_BLOCK - 1
    dist = t[:, None] - blk_end[None, :]
    mask = dist >= 0
    bias = rel_bias[_t5_bucket(dist)].reshape(S, nc, NSA_GROUPS, NSA_HPG).transpose(2, 3, 0, 1)
    logits = jnp.einsum('bsghd,bcgd->bghsc', qg, kc).astype(jnp.float32) * ATTN_SCALE + bias
    p = _masked_softmax(logits, mask)
    out = jnp.einsum('bghsc,bcgd->bsghd', p.astype(vc.dtype), vc)
    return out, p


def _selection_indices(p_cmp, S):
    nc = p_cmp.shape[-1]
    nsb = S // SEL_BLOCK
    c0 = jnp.arange(nc) * CMP_STRIDE
    s0 = jnp.arange(nsb) * SEL_BLOCK
    overlap = jnp.clip(jnp.minimum(c0[:, None] + CMP_BLOCK, s0[None, :] + SEL_BLOCK)
                       - jnp.maximum(c0[:, None], s0[None, :]), 0, None).astype(jnp.float32) / CMP_BLOCK
    imp = jnp.einsum('bghsc,cj->bgsj', p_cmp, overlap)
    t = jnp.arange(S)
    cur = t // SEL_BLOCK
    j = jnp.arange(nsb)
    forced = (j[None, :] == 0) | (j[None, :] == cur[:, None]) | (j[None, :] == cur[:, None] - 1)
    future = j[None, :] > cur[:, None]
    score = jnp.where(future, NEG, imp + jnp.where(forced, FORCE_BONUS, 0.0))
    n_sel = min(SEL_TOPN, nsb)
    _, idx = lax.top_k(score, n_sel)
    ok = idx <= cur[None, None, :, None]
    return idx, ok


def _nsa_selected(qg, k, v, idx, ok, rel_bias):
    B, S, G, HPG, D = qg.shape
    nsb = S // SEL_BLOCK
    n_sel = idx.shape[-1]
    kb = k.reshape(B, nsb, SEL_BLOCK, G, D).transpose(0, 3, 1, 2, 4)
    vb = v.reshape(B, nsb, SEL_BLOCK, G, D).transpose(0, 3, 1, 2, 4)
    nq = S // SEL_Q_BLOCK
    q_c = qg.reshape(B, nq, SEL_Q_BLOCK, G, HPG, D).transpose(1, 0, 2, 3, 4, 5)
    idx_c = idx.reshape(B, G, nq, SEL_Q_BLOCK, n_sel).transpose(2, 0, 1, 3, 4)
    ok_c = ok.reshape(B, G, nq, SEL_Q_BLOCK, n_sel).transpose(2, 0, 1, 3, 4)
    t_c = jnp.arange(S).reshape(nq, SEL_Q_BLOCK)
    bi = jnp.arange(B)[:, None, None, None]
    gi = jnp.arange(G)[None, :, None, None]
    gi5 = jnp.arange(G)[None, :, None, None, None]
    tbl = rel_bias.reshape(N_BUCKETS, G, HPG)

    def block(args):
        qb, ib, okb, tb = args
        kg = kb[bi, gi, ib]
        vg = vb[bi, gi, ib]
        pos = ib[..., None] * SEL_BLOCK + jnp.arange(SEL_BLOCK)
        dist = tb[None, None, :, None, None] - pos
        mask = (okb[..., None] & (dist >= 0)).reshape(B, G, 1, SEL_Q_BLOCK, n_sel * SEL_BLOCK)
        bias = tbl[_t5_bucket(dist), gi5].transpose(0, 1, 5, 2, 3, 4)
        logits = jnp.einsum('bqghd,bgqnsd->bghqns', qb, kg).astype(jnp.float32) * ATTN_SCALE + bias
        p = _masked_softmax(logits.reshape(B, G, HPG, SEL_Q_BLOCK, n_sel * SEL_BLOCK), mask)
        return jnp.einsum('bghqn,bgqnd->bqghd', p.astype(vg.dtype),
                          vg.reshape(B, G, SEL_Q_BLOCK, n_sel * SEL_BLOCK, D))

    out = lax.map(block, (q_c, idx_c, ok_c, t_c))
    return out.transpose(1, 0, 2, 3, 4, 5).reshape(B, S, G, HPG, D)


def _nsa_window(qg, k, v, rel_bias):
    B, S, G, HPG, D = qg.shape
    nq = S // Q_BLOCK
    span = WINDOW + Q_BLOCK
    kp = jnp.pad(k, ((0, 0), (WINDOW, 0), (0, 0), (0, 0)))
    vp = jnp.pad(v, ((0, 0), (WINDOW, 0), (0, 0), (0, 0)))
    q_c = qg.reshape(B, nq, Q_BLOCK, G, HPG, D).transpose(1, 0, 2, 3, 4, 5)
    tq = jnp.arange(Q_BLOCK)
    sk = jnp.arange(span) - WINDOW
    dist = tq[:, None] - sk[None, :]
    band = (dist >= 0) & (dist < WINDOW)
    bias = rel_bias[_t5_bucket(dist)].reshape(Q_BLOCK, span, G, HPG).transpose(2, 3, 0, 1)

    def block(args):
        i, qb = args
        start = i * Q_BLOCK
        kb = lax.dynamic_slice_in_dim(kp, start, span, axis=1)
        vb = lax.dynamic_slice_in_dim(vp, start, span, axis=1)
        mask = band & ((start + sk) >= 0)[None, :]
        logits = jnp.einsum('bqghd,bkgd->bghqk', qb, kb).astype(jnp.float32) * ATTN_SCALE + bias
        p = _masked_softmax(logits, mask)
        return jnp.einsum('bghqk,bkgd->bqghd', p.astype(vb.dtype), vb)

    out = lax.map(block, (jnp.arange(nq), q_c))
    return out.transpose(1, 0, 2, 3, 4, 5).reshape(B, S, G, HPG, D)


def _nsa(q, k_cmp, v_cmp, k_slc, v_slc, k_win, v_win, gate_logit, rel_bias,
         cmp_pos, cmp_w1, cmp_b1, cmp_w2, cmp_b2):
    B, S, _ = q.shape
    qg = q.reshape(B, S, NSA_GROUPS, NSA_HPG, HEAD_DIM)
    kvs = lambda z: z.reshape(B, S, NSA_GROUPS, HEAD_DIM)
    kc = _compress(kvs(k_cmp), cmp_pos[0], cmp_w1[0], cmp_b1[0], cmp_w2[0], cmp_b2[0])
    vc = _compress(kvs(v_cmp), cmp_pos[1], cmp_w1[1], cmp_b1[1], cmp_w2[1], cmp_b2[1])
    o_cmp, p_cmp = _nsa_compressed(qg, kc, vc, rel_bias)
    idx, ok = _selection_indices(p_cmp, S)
    o_slc = _nsa_selected(qg, kvs(k_slc), kvs(v_slc), idx, ok, rel_bias)
    o_win = _nsa_window(qg, kvs(k_win), kvs(v_win), rel_bias)
    g = jax.nn.sigmoid(gate_logit).reshape(B, S, NSA_GROUPS, NSA_HPG, 3)
    o = g[..., 0:1] * o_cmp + g[..., 1:2] * o_slc + g[..., 2:3] * o_win
    return o.reshape(B, S, NSA_DIM)


def _rwkv7(r, k, v, w_lo, a_lo, g_lo, w0, w2, a0, a2, g2, k_k, k_a, r_k, lnx_g, lnx_b):
    B, S, _ = r.shape
    H, N = RWKV_HEADS, RWKV_HEAD_DIM
    w = -jax.nn.softplus(-(w0 + jnp.tanh(w_lo) @ w2)) - 0.5
    decay = jnp.exp(-jnp.exp(w.astype(jnp.float32)))
    a = jax.nn.sigmoid(a0 + a_lo @ a2)
    g = jax.nn.sigmoid(g_lo) @ g2
    kk = (k * k_k).astype(jnp.float32).reshape(B, S, H, N)
    kk = kk / jnp.maximum(jnp.sqrt(jnp.sum(kk * kk, -1, keepdims=True)), 1e-12)
    k = k * (1.0 + (a - 1.0) * k_a)
    heads = lambda z: z.astype(jnp.float32).reshape(B, S, H, N)
    rh, kh, vh, ah, wh = heads(r), heads(k), heads(v), heads(a), heads(decay)

    def step(state, inp):
        rt, wt, kt, vt, kkt, at = inp
        sa = jnp.einsum('bhvk,bhk->bhv', state, -kkt)
        state = (state * wt[:, :, None, :] + sa[..., None] * (kkt * at)[:, :, None, :]
                 + vt[..., None] * kt[:, :, None, :])
        return state, jnp.einsum('bhvk,bhk->bhv', state, rt)

    seq_first = lambda z: z.transpose(1, 0, 2, 3)
    init = jnp.zeros((B, H, N, N), jnp.float32)
    _, y = lax.scan(step, init, (seq_first(rh), seq_first(wh), seq_first(kh),
                                 seq_first(vh), seq_first(kk), seq_first(ah)))
    y = y.transpose(1, 0, 2, 3)
    mu = y.mean(-1, keepdims=True)
    var = jnp.square(y - mu).mean(-1, keepdims=True)
    y = ((y - mu) * lax.rsqrt(var + RWKV_GN_EPS)).reshape(B, S, RWKV_DIM) * lnx_g + lnx_b
    bonus = (jnp.sum(rh * kh * r_k, -1, keepdims=True) * vh).reshape(B, S, RWKV_DIM)
    return ((y + bonus) * g).astype(r.dtype)


def _peer(x, w_query, sub_keys, u_table, v_table):
    B, S, D = x.shape
    q = (x @ w_query).reshape(B, S, PEER_HEADS, 2, PEER_HALF)
    sc = jnp.einsum('bshpd,hpnd->bshpn', q, sub_keys).astype(jnp.float32)
    s1, i1 = lax.top_k(sc[..., 0, :], PEER_TOPK)
    s2, i2 = lax.top_k(sc[..., 1, :], PEER_TOPK)
    cand = (s1[..., :, None] + s2[..., None, :]).reshape(B, S, PEER_HEADS, PEER_TOPK * PEER_TOPK)
    cand_idx = (i1[..., :, None] * N_KEYS + i2[..., None, :]).reshape(B, S, PEER_HEADS, PEER_TOPK * PEER_TOPK)
    top_s, pos = lax.top_k(cand, PEER_TOPK)
    idx = jnp.take_along_axis(cand_idx, pos, axis=-1)
    gate = jax.nn.softmax(top_s, axis=-1)
    T = B * S
    nb = T // PEER_TOKEN_BLOCK
    xb = x.reshape(nb, PEER_TOKEN_BLOCK, D)
    ib = idx.reshape(nb, PEER_TOKEN_BLOCK, PEER_HEADS * PEER_TOPK)
    gb = gate.reshape(nb, PEER_TOKEN_BLOCK, PEER_HEADS * PEER_TOPK)

    def block(args):
        xt, it, gt = args
        h = jax.nn.gelu(jnp.einsum('ted,td->te', u_table[it], xt).astype(jnp.float32))
        return jnp.einsum('te,ted->td', (gt * h).astype(v_table.dtype), v_table[it])

    out = lax.map(block, (xb, ib, gb))
    return out.reshape(B, S, D).astype(x.dtype)


def _hybrid_layer(x, rel_bias, w_in, token_mu, cmp_pos, cmp_w1, cmp_b1, cmp_w2, cmp_b2,
                  rwkv_w0, rwkv_w2, rwkv_a0, rwkv_a2, rwkv_g2, rwkv_k_k, rwkv_k_a, rwkv_r_k,
                  rwkv_lnx_g, rwkv_lnx_b, w_o_nsa, w_o_rwkv, w_out, ln_mix_g, ln_mix_b,
                  peer_w_query, peer_sub_keys, peer_u, peer_v, ln_ffn_g, ln_ffn_b):
    z = x @ w_in
    q, k_cmp, v_cmp, k_slc, v_slc, k_win, v_win, nsa_gate, rw, merge_gate = jnp.split(z, IN_SPLITS, axis=-1)
    y_nsa = _nsa(q, k_cmp, v_cmp, k_slc, v_slc, k_win, v_win, nsa_gate, rel_bias,
                 cmp_pos, cmp_w1, cmp_b1, cmp_w2, cmp_b2)
    prev = jnp.pad(rw, ((0, 0), (1, 0), (0, 0)))[:, :-1]
    rw = rw + token_mu * (prev - rw)
    r, k, v, w_lo, a_lo, g_lo = jnp.split(rw, RWKV_SPLITS, axis=-1)
    y_rwkv = _rwkv7(r, k, v, w_lo, a_lo, g_lo, rwkv_w0, rwkv_w2, rwkv_a0, rwkv_a2, rwkv_g2,
                    rwkv_k_k, rwkv_k_a, rwkv_r_k, rwkv_lnx_g, rwkv_lnx_b)
    g_nsa, g_rwkv = jnp.split(jax.nn.sigmoid(merge_gate), 2, axis=-1)
    mixed = (g_nsa * (y_nsa @ w_o_nsa) + g_rwkv * (y_rwkv @ w_o_rwkv)) @ w_out
    x = _layer_norm(ALPHA * x + mixed, ln_mix_g, ln_mix_b)
    x = _layer_norm(ALPHA * x + _peer(x, peer_w_query, peer_sub_keys, peer_u, peer_v), ln_ffn_g, ln_ffn_b)
    return x


def setup_inputs(seed: int = 0) -> dict:
    key = jax.random.key(seed)
    ks = iter(jax.random.split(key, 48))
    L, D = DEPTH, D_MODEL

    def nrm(shape, scale):
        return jax.random.normal(next(ks), shape, jnp.float32) * scale

    def unif(shape, lo, hi):
        return jax.random.uniform(next(ks), shape, jnp.float32, lo, hi)

    return {
        "x": nrm((BATCH, SEQ, D), 1.0),
        "ln_in_g": 1.0 + nrm((D,), 0.02),
        "ln_in_b": nrm((D,), 0.02),
        "rel_bias": nrm((N_BUCKETS, NSA_HEADS), 0.1),
        "w_in": nrm((L, D, D_IN), D ** -0.5),
        "token_mu": unif((L, RWKV_COLS), 0.0, 1.0),
        "cmp_pos": nrm((L, 2, CMP_BLOCK, HEAD_DIM), 0.1),
        "cmp_w1": nrm((L, 2, CMP_BLOCK * HEAD_DIM, CMP_HIDDEN), (CMP_BLOCK * HEAD_DIM) ** -0.5),
        "cmp_b1": nrm((L, 2, CMP_HIDDEN), 0.02),
        "cmp_w2": nrm((L, 2, CMP_HIDDEN, HEAD_DIM), CMP_HIDDEN ** -0.5),
        "cmp_b2": nrm((L, 2, HEAD_DIM), 0.02),
        "rwkv_w0": unif((L, RWKV_DIM), -6.5, -1.5),
        "rwkv_w2": nrm((L, LORA_W, RWKV_DIM), 0.5 * LORA_W ** -0.5),
        "rwkv_a0": nrm((L, RWKV_DIM), 0.1),
        "rwkv_a2": nrm((L, LORA_A, RWKV_DIM), LORA_A ** -0.5),
        "rwkv_g2": nrm((L, LORA_G, RWKV_DIM), LORA_G ** -0.5),
        "rwkv_k_k": 1.0 + nrm((L, RWKV_DIM), 0.05),
        "rwkv_k_a": 1.0 + nrm((L, RWKV_DIM), 0.05),
        "rwkv_r_k": nrm((L, RWKV_HEADS, RWKV_HEAD_DIM), 0.1),
        "rwkv_lnx_g": 1.0 + nrm((L, RWKV_DIM), 0.02),
        "rwkv_lnx_b": nrm((L, RWKV_DIM), 0.02),
        "w_o_nsa": nrm((L, NSA_DIM, D), NSA_DIM ** -0.5),
        "w_o_rwkv": nrm((L, RWKV_DIM, D), RWKV_DIM ** -0.5),
        "w_out": nrm((L, D, D), BETA * D ** -0.5),
        "ln_mix_g": 1.0 + nrm((L, D), 0.02),
        "ln_mix_b": nrm((L, D), 0.02),
        "peer_w_query": nrm((L, D, PEER_HEADS * PEER_KEY_DIM), D ** -0.5),
        "peer_sub_keys": nrm((L, PEER_HEADS, 2, N_KEYS, PEER_HALF), PEER_HALF ** -0.5),
        "peer_u": nrm((L, N_EXPERTS, D), D ** -0.5),
        "peer_v": nrm((L, N_EXPERTS, D), BETA * PEER_HEADS ** -0.5),
        "ln_ffn_g": 1.0 + nrm((L, D), 0.02),
        "ln_ffn_b": nrm((L, D), 0.02),
    }


def reference(x, ln_in_g, ln_in_b, rel_bias, w_in, token_mu, cmp_pos, cmp_w1, cmp_b1, cmp_w2, cmp_b2,
              rwkv_w0, rwkv_w2, rwkv_a0, rwkv_a2, rwkv_g2, rwkv_k_k, rwkv_k_a, rwkv_r_k,
              rwkv_lnx_g, rwkv_lnx_b, w_o_nsa, w_o_rwkv, w_out, ln_mix_g, ln_mix_b,
              peer_w_query, peer_sub_keys, peer_u, peer_v, ln_ffn_g, ln_ffn_b):
    h = _layer_norm(x, ln_in_g, ln_in_b)
    for l in range(DEPTH):
        h = _hybrid_layer(h, rel_bias, w_in[l], token_mu[l], cmp_pos[l], cmp_w1[l], cmp_b1[l],
                          cmp_w2[l], cmp_b2[l], rwkv_w0[l], rwkv_w2[l], rwkv_a0[l], rwkv_a2[l],
                          rwkv_g2[l], rwkv_k_k[l], rwkv_k_a[l], rwkv_r_k[l], rwkv_lnx_g[l],
                          rwkv_lnx_b[l], w_o_nsa[l], w_o_rwkv[l], w_out[l], ln_mix_g[l], ln_mix_b[l],
                          peer_w_query[l], peer_sub_keys[l], peer_u[l], peer_v[l],
                          ln_ffn_g[l], ln_ffn_b[l])
    return h
```

```python
import math
import numpy as np
import ml_dtypes
from contextlib import ExitStack
import concourse.bass as bass
import concourse.mybir as mybir
from concourse.alu_op_type import AluOpType as ALU
from concourse.mybir import ActivationFunctionType as AF
from concourse.bass_utils import run_bass_kernel_spmd

F32 = mybir.dt.float32
BF16 = mybir.dt.bfloat16
AX = mybir.AxisListType

S_ = 2048
D_ = 1024
NT = 16
D_IN = 5144
BIGNEG = -240000.0
LN_EPS = 1e-5
ALPHA = 2.0 ** 0.25
ATT_SCALE = 0.125


class Tok:
    __slots__ = ("w", "r", "name", "excl")

    def __init__(self, name="", excl=False):
        self.w = None
        self.r = {}
        self.name = name
        self.excl = excl


class _RecEng:
    def __init__(self):
        self.call = None

    def __getattr__(self, name):
        def f(*a, **k):
            self.call = (name, a, k)
            return self
        return f


class Sched:
    NDMA = 40

    def __init__(self, nc, es):
        self.nc = nc
        self.E = {"pe": nc.tensor, "act": nc.scalar, "dve": nc.vector,
                  "pool": nc.gpsimd, "sp": nc.sync}
        self.sem = {k: es.enter_context(nc.semaphore("s_" + k)) for k in self.E}
        self.cnt = {k: 0 for k in self.E}
        self.dsem = [es.enter_context(nc.semaphore("d%d" % i)) for i in range(self.NDMA)]
        self.dval = [0] * self.NDMA
        self.dnext = 0
        self.seen = {k: {} for k in self.E}
        self.ninst = 0
        self.es = es
        self.swsem = []
        self.nobar = set()
        self.rec = None

    def _semof(self, key):
        if isinstance(key, tuple):
            if key[0] == "w":
                return self.swsem[key[1]]
            return self.dsem[key[1]]
        return self.sem[key]

    def _wait(self, eng, key, val):
        if eng == "pe" and key == "pe":
            return
        if self.seen[eng].get(key, 0) >= val:
            return
        self.E[eng].wait_ge(self._semof(key), val)
        self.seen[eng][key] = val
        self.ninst += 1

    def _deps(self, eng, reads, writes):
        deps = {}
        for t in reads:
            if t.w is not None:
                k, c = t.w
                deps[k] = max(deps.get(k, 0), c)
        for t in writes:
            if t.w is not None:
                k, c = t.w
                deps[k] = max(deps.get(k, 0), c)
            for k, c in t.r.items():
                deps[k] = max(deps.get(k, 0), c)
        for k, c in deps.items():
            self._wait(eng, k, c)

    def replay(self, ops):
        for o in ops:
            if o[0] == "op":
                _, eng, (name, a, k), reads, writes = o
                self.op(eng, lambda e: getattr(e, name)(*a, **k), reads, writes)
            else:
                _, out, in_, reads, writes, q, kw = o
                self.dma(out, in_, reads, writes, q, **kw)

    def op(self, eng, fn, reads=(), writes=()):
        if self.rec is not None:
            pr = _RecEng()
            fn(pr)
            self.rec.append(("op", eng, pr.call, list(reads), list(writes)))
            return None
        ex = [t for t in reads if t.excl]
        if ex:
            reads = [t for t in reads if not t.excl]
            writes = list(writes) + ex
        self._deps(eng, reads, writes)
        inst = fn(self.E[eng])
        self.cnt[eng] += 1
        c = self.cnt[eng]
        inst.then_inc(self.sem[eng], 1)
        for t in reads:
            t.r[eng] = c
        for t in writes:
            t.w = (eng, c)
            t.r = {}
        self.ninst += 1
        return inst

    def dma(self, out, in_, reads=(), writes=(), q="sp", nobar=False, **kw):
        if self.rec is not None:
            self.rec.append(("dma", out, in_, list(reads), list(writes), q, kw))
            return None
        if q == "pool":
            if nobar:
                self.nobar.add(len(self.swsem))
            sem = self.es.enter_context(self.nc.semaphore("w%d" % len(self.swsem)))
            self.swsem.append(sem)
            key = ("w", len(self.swsem) - 1)
            self._deps(q, reads, writes)
            inst = self.E[q].dma_start(out=out, in_=in_, **kw)
            inst.then_inc(sem, 16)
            for t in reads:
                t.r[key] = 16
            for t in writes:
                t.w = (key, 16)
                t.r = {}
            self.ninst += 1
            return inst
        i = self.dnext
        self.dnext = (self.dnext + 1) % self.NDMA
        key = ("d", i)
        if self.dval[i] > 0:
            self._wait(q, key, self.dval[i])
        self._deps(q, reads, writes)
        inst = self.E[q].dma_start(out=out, in_=in_, **kw)
        self.dval[i] += 16
        inst.then_inc(self.dsem[i], 16)
        v = self.dval[i]
        for t in reads:
            t.r[key] = v
        for t in writes:
            t.w = (key, v)
            t.r = {}
        self.ninst += 1
        return inst

    def barrier(self):
        for eng in self.E:
            for key in self.E:
                if self.cnt[key] > 0 and not (eng == key == "pe"):
                    self._wait(eng, key, self.cnt[key])
            for i in range(self.NDMA):
                if self.dval[i] > 0:
                    self._wait(eng, ("d", i), self.dval[i])
            for i in range(len(self.swsem)):
                if i not in self.nobar:
                    self._wait(eng, ("w", i), 16)

    def finish(self, toks):
        for t in toks:
            if t.w is not None:
                self._wait("sp", t.w[0], t.w[1])


def _t5_bucket_np(dist):
    d = np.maximum(dist, 0)
    large = 16 + (np.log(np.maximum(d, 1).astype(np.float32) / 16) / math.log(128 / 16) * 16).astype(np.int32)
    large = np.minimum(large, 31)
    return np.where(d < 16, d, large)


FOFF = 2048
FLEN = 4352


def host_consts():
    c = {}
    c["ident_bf"] = np.eye(128, dtype=np.float32).astype(ml_dtypes.bfloat16)
    c["ident_f"] = np.eye(128, dtype=np.float32)
    c["J128"] = np.ascontiguousarray(np.eye(128, dtype=np.float32)[::-1])
    j127 = np.zeros((128, 128), np.float32)
    j127[:127, :127] = np.eye(127, dtype=np.float32)[::-1]
    c["J127"] = j127
    x = np.arange(FLEN)
    dist = x - FOFF
    oh = np.zeros((33, FLEN), np.float32)
    b = _t5_bucket_np(dist)
    valid = dist >= 0
    oh[b[valid], x[valid]] = 1.0
    oh[32, ~valid] = 1.0
    c["OHD"] = oh
    c0 = np.arange(127) * 16
    s0 = np.arange(32) * 64
    ov = np.clip(np.minimum(c0[:, None] + 32, s0[None, :] + 64) - np.maximum(c0[:, None], s0[None, :]), 0, None).astype(np.float32) / 32
    ovp = np.zeros((128, 32), np.float32)
    ovp[:127] = ov
    c["overlap"] = ovp
    t = np.arange(S_)
    cur = t // 64
    j = np.arange(32)
    forced = (j[None, :] == 0) | (j[None, :] == cur[:, None]) | (j[None, :] == cur[:, None] - 1)
    future = j[None, :] > cur[:, None]
    cs = np.where(future, -1e30, np.where(forced, 1.0e4, 0.0)).astype(np.float32)
    c["Csel"] = np.ascontiguousarray(cs.reshape(NT, 128, 32).transpose(1, 0, 2))
    ex = np.zeros((32, NT, 128), np.float32)
    for kt in range(NT):
        for s in range(128):
            ex[2 * kt + s // 64, kt, s] = 1.0
    c["Expand"] = ex.astype(ml_dtypes.bfloat16)
    s_ = np.arange(128)[:, None]
    t_ = np.arange(128)[None, :]
    c["TriLT"] = np.where(t_ < s_, 0.0, BIGNEG).astype(np.float32).astype(ml_dtypes.bfloat16)
    i_ = np.arange(128)[:, None]
    su = (i_ < t_).astype(np.float32)
    ui = (i_ <= t_).astype(np.float32)
    c["MaskAB"] = np.concatenate([su, ui], axis=1)
    c["MaskSL"] = (i_ > t_).astype(np.float32)
    return c


CONST_DT = {"ident_bf": BF16, "ident_f": F32, "J128": F32, "J127": F32, "OHD": F32, "overlap": F32,
            "Csel": F32, "Expand": BF16, "TriLT": BF16, "MaskAB": F32, "MaskSL": F32}

IN_SPECS = [
    ("x", [S_, D_]), ("ln_in_g", [D_]), ("ln_in_b", [D_]), ("rel_bias", [32, 8]), ("w_in", [D_, D_IN]),
    ("token_mu", [1792]), ("cmp_pos", [2, 32, 64]), ("cmp_w1", [2, 2048, 256]), ("cmp_b1", [2, 256]),
    ("cmp_w2", [2, 256, 64]), ("cmp_b2", [2, 64]), ("rwkv_w0", [512]), ("rwkv_w2", [64, 512]),
    ("rwkv_a0", [512]), ("rwkv_a2", [64, 512]), ("rwkv_g2", [128, 512]), ("rwkv_k_k", [512]),
    ("rwkv_k_a", [512]), ("rwkv_r_k", [512]), ("rwkv_lnx_g", [512]), ("rwkv_lnx_b", [512]),
    ("w_o_nsa", [512, D_]), ("w_o_rwkv", [512, D_]), ("w_out", [D_, D_]), ("ln_mix_g", [D_]), ("ln_mix_b", [D_]),
    ("peer_w_query", [D_, D_]), ("peer_sub_keys", [8, 2, 128, 64]), ("peer_uT", [D_, 16384]), ("peer_v", [16384, D_]),
    ("ln_ffn_g", [D_]), ("ln_ffn_b", [D_]),
]


def build(stage=99, debug=()):
    nc = bass.Bass("TRN2", target_bir_lowering=False)
    I = {}
    for name, shp in IN_SPECS:
        I[name] = nc.dram_tensor(name, shp, F32, kind="ExternalInput").ap()
    hc = host_consts()
    C = {}
    for name, arr in hc.items():
        C[name] = nc.dram_tensor("c_" + name, list(arr.shape), CONST_DT[name], kind="ExternalInput").ap()
    out_d = nc.dram_tensor("out", [S_, D_], F32, kind="ExternalOutput").ap()
    DBG = {}

    def dbg_out(name, shape, dt=F32):
        DBG[name] = nc.dram_tensor("dbg_" + name, shape, dt, kind="ExternalOutput").ap()
        return DBG[name]

    h_d = nc.dram_tensor("h_scr", [S_, D_], F32, kind="Internal").ap()
    F_d = nc.dram_tensor("F_scr", [8, FLEN], F32, kind="Internal").ap()

    with ExitStack() as es, nc.allow_non_contiguous_dma(reason="small param loads"):
        S = Sched(nc, es)

        def sb(name, shape, dt, stack=es):
            return stack.enter_context(nc.sbuf_tensor(name, shape, dt))

        PSALL = es.enter_context(nc.psum_tensor("psall", [128, 8, 512], F32))
        PS = [PSALL[:, i, :] for i in range(8)]
        PT = [Tok("ps%d" % i, excl=True) for i in range(8)]
        psrr = {}

        def psbank(lo=0, hi=8):
            i = psrr.get((lo, hi), lo)
            psrr[(lo, hi)] = lo + (i + 1 - lo) % (hi - lo)
            return PS[i], PT[i]

        evrr = [0]

        def evac_eng():
            evrr[0] ^= 1
            return "act" if evrr[0] else "dve"

        def copy(eng, out, in_, reads, writes):
            if eng == "act":
                return S.op("act", lambda e: e.copy(out=out, in_=in_), reads, writes)
            return S.op(eng, lambda e: e.tensor_copy(out=out, in_=in_), reads, writes)

        def mm(out, lhsT, rhs, start, stop, reads, writes):
            return S.op("pe", lambda e: e.matmul(out, lhsT=lhsT, rhs=rhs, start=start, stop=stop), reads, writes)

        final_toks = []
        dbg_tok = Tok("dbg")
        final_toks.append(dbg_tok)

        ident_bf = sb("ident_bf", [128, 128], BF16)
        ident_f = sb("ident_f", [128, 128], F32)
        t_const = Tok("const")
        S.dma(ident_bf[:], C["ident_bf"][:, :], writes=[t_const])
        S.dma(ident_f[:], C["ident_f"][:, :], writes=[t_const])

        uTb_d = nc.dram_tensor("uTb_scr", [128, 128, 8, 128], BF16, kind="Internal").ap()
        vb_d = nc.dram_tensor("vb_scr", [16384, D_], BF16, kind="Internal").ap()
        t_uTb = [Tok("uTb%d" % i) for i in range(16)]
        t_vb = [Tok("vb%d" % i) for i in range(16)]
        wq_d = nc.dram_tensor("wq_scr", [128, 8, D_], BF16, kind="Internal").ap()
        t_wqd = Tok("wqd")
        gb_bc = sb("gb_bc", [128, 2, D_], F32)
        t_gb = Tok("gb")

        def load_gb(g_ap, b_ap):
            S.dma(gb_bc[:, 0, :], g_ap.unsqueeze(0).to_broadcast([128, D_]), writes=[t_gb])
            S.dma(gb_bc[:, 1, :], b_ap.unsqueeze(0).to_broadcast([128, D_]), writes=[t_gb])

        lnst = sb("lnst", [128, 2, 2, 6], F32)
        lnmv = sb("lnmv", [128, 2, 4], F32)
        t_ln = [Tok("ln0"), Tok("ln1")]

        def ln_tile(par, src, dst, t_src, t_dst, xn_buf, t_xn, alpha_src=None):
            st = lnst[:, par]
            mv = lnmv[:, par]
            tl = t_ln[par]
            S.op("dve", lambda e: e.bn_stats(out=st[:, 0, :], in_=src[:, 0:512]), [t_src], [tl])
            S.op("dve", lambda e: e.bn_stats(out=st[:, 1, :], in_=src[:, 512:1024]), [t_src], [tl])
            S.op("dve", lambda e: e.bn_aggr(out=mv[:, 0:2], in_=st.rearrange("p a b -> p (a b)")), [tl], [tl])
            S.op("dve", lambda e: e.tensor_scalar(out=mv[:, 2:3], in0=mv[:, 1:2], scalar1=LN_EPS, scalar2=None, op0=ALU.add), [tl], [tl])
            S.op("act", lambda e: e.activation(out=mv[:, 2:3], in_=mv[:, 2:3], func=AF.Sqrt), [tl], [tl])
            S.op("dve", lambda e: e.reciprocal(out=mv[:, 2:3], in_=mv[:, 2:3]), [tl], [tl])
            S.op("dve", lambda e: e.tensor_scalar(out=mv[:, 3:4], in0=mv[:, 0:1], scalar1=mv[:, 2:3], scalar2=-1.0,
                                                  op0=ALU.mult, op1=ALU.mult), [tl], [tl])
            S.op("act", lambda e: e.activation(out=xn_buf, in_=src, func=AF.Identity, bias=mv[:, 3:4], scale=mv[:, 2:3]),
                 [t_src, tl], [t_xn])
            S.op("dve", lambda e: e.tensor_tensor(out=xn_buf, in0=xn_buf, in1=gb_bc[:, 0, :], op=ALU.mult), [t_xn, t_gb], [t_xn])
            S.op("pool", lambda e: e.tensor_tensor(out=dst, in0=xn_buf, in1=gb_bc[:, 1, :], op=ALU.add), [t_xn, t_gb], [t_dst])

        onT_d = nc.dram_tensor("onT_scr", [512, S_], BF16, kind="Internal").ap()
        yrw_d = nc.dram_tensor("yrw_scr", [512, S_], BF16, kind="Internal").ap()
        t_onT = [Tok("onT%d" % i) for i in range(NT)]
        t_yrw = [Tok("yrw%d" % i) for i in range(8)]
        rw_d = nc.dram_tensor("rw_scr", [1792, S_], F32, kind="Internal").ap()
        mg_d = nc.dram_tensor("mg_scr", [S_, 2048], F32, kind="Internal").ap()
        t_rwd = [Tok("rwd%d" % i) for i in range(28)]
        t_mgd = [Tok("mgd%d" % i) for i in range(NT)]
        wv = I["w_in"].rearrange("(k p) c -> p k c", p=128)
        with ExitStack() as sn:
            o_nsa = sb("o_nsa", [128, NT, 512], F32, sn)
            t_on = [Tok("on%d" % i) for i in range(NT)]
            qT = sb("qT", [64, 8, S_], BF16, sn)
            t_q = [[Tok() for _ in range(4)] for _ in range(8)]
            kT = sb("kT", [64, 4, S_], BF16, sn)
            t_k = [[Tok() for _ in range(4)] for _ in range(4)]
            Vt = sb("Vt", [128, NT, 4, 65], BF16, sn)
            t_V = [Tok() for _ in range(NT)]
            gate = sb("gate", [128, NT, 24], F32, sn)
            kcT = sb("kcT", [64, 2, 128], BF16, sn)
            t_kc = [Tok(), Tok()]
            VC = sb("VC", [128, 2, 98], F32, sn)
            t_VC = [Tok(), Tok()]
            S.op("pool", lambda e: e.memset(Vt[:, :, :, 64:65], 1.0), [], t_V)
            S.op("pool", lambda e: e.memset(VC[:], 0.0), [], t_VC)
            S.op("pool", lambda e: e.memset(VC[:, :, 64:65], 1.0), [], t_VC)
            S.dma(VC[:, 0, 65:97], C["overlap"][:, :], writes=[t_VC[0]])
            S.dma(VC[:, 1, 65:97], C["overlap"][:, :], writes=[t_VC[1]])

            with ExitStack() as s1:
                hT = sb("hT", [128, 8, S_], BF16, s1)
                t_hT = [Tok("hT%d" % i) for i in range(4)]
                t_hd = [Tok("hd%d" % i) for i in range(NT)]
                load_gb(I["ln_in_g"], I["ln_in_b"])
                xv = I["x"].rearrange("(n p) d -> n p d", p=128)
                hv = h_d.rearrange("(n p) d -> n p d", p=128)
                with ExitStack() as sa:
                    xbuf = [sb("xa%d" % i, [128, D_], F32, sa) for i in range(2)]
                    xnb = [sb("xn%d" % i, [128, D_], F32, sa) for i in range(2)]
                    hfb = [sb("hf%d" % i, [128, D_], F32, sa) for i in range(2)]
                    hbb = [sb("hb%d" % i, [128, D_], BF16, sa) for i in range(2)]
                    t_x = [Tok(), Tok()]
                    t_xn = [Tok(), Tok()]
                    t_hf = [Tok(), Tok()]
                    t_hb = [Tok(), Tok()]
                    for i in range(NT):
                        p = i % 2
                        S.dma(xbuf[p][:], xv[i], writes=[t_x[p]])
                        ln_tile(p, xbuf[p][:], hfb[p][:], t_x[p], t_hf[p], xnb[p][:], t_xn[p])
                        S.dma(hv[i], hfb[p][:], reads=[t_hf[p]], writes=[t_hd[i]])
                        S.op("act", lambda e: e.copy(out=hbb[p][:], in_=hfb[p][:]), [t_hf[p]], [t_hb[p]])
                        ps, pt = psbank()
                        psb = ps.bitcast(BF16)
                        for dk in range(8):
                            S.op("pe", lambda e: e.transpose(out=psb[:, dk * 128:(dk + 1) * 128], in_=hbb[p][:, dk * 128:(dk + 1) * 128],
                                                             identity=ident_bf[:]), [t_hb[p], t_const], [pt])
                        copy("dve", hT[:, :, i * 128:(i + 1) * 128], psb.rearrange("p (k t) -> p k t", k=8), [pt], [t_hT[i // 4]])
                S.barrier()
                Wn = sb("Wn", [128, 8, 1304], BF16, s1)
                t_Wn = [Tok(), Tok(), Tok()]
                for ci, c0 in enumerate(range(0, 1304, 512)):
                    c1 = min(1304, c0 + 512)
                    S.dma(Wn[:, :, c0:c1], wv[:, :, c0:c1], writes=[t_Wn[ci]], q="pool")
                s1c = ExitStack()
                kvA = sb("kvA", [64, 4, S_], BF16, s1c)
                kvB = sb("kvB", [64, 4, S_], BF16, s1c)
                t_kv = [[Tok() for _ in range(4)] for _ in range(4)]
                pos_sb = sb("pos_sb", [32, 2, 64], F32, s1c)
                posT = sb("posT", [64, 2, 32], F32, s1c)
                t_pos = Tok()
                S.dma(pos_sb[:], I["cmp_pos"].rearrange("k j d -> j k d"), writes=[t_pos])
                for kv in range(2):
                    ps, pt = psbank()
                    S.op("pe", lambda e: e.transpose(out=ps[0:64, 0:32], in_=pos_sb[:, kv, :], identity=ident_f[0:32, 0:32]),
                         [t_pos, t_const], [pt])
                    copy("dve", posT[:, kv, :], ps[0:64, 0:32], [pt], [t_pos])

                def proj_cm(col0, M, tc, evac):
                    ps, pt = psbank()
                    for dk in range(8):
                        mm(ps[0:M, :], Wn[:, dk, col0:col0 + M], hT[:, dk, tc * 512:(tc + 1) * 512], dk == 0, dk == 7,
                           t_Wn + [t_hT[tc]], [pt])
                    evac(ps[0:M, :], pt)

                for tc in range(4):
                    tsl = slice(tc * 512, (tc + 1) * 512)
                    for h in range(8):
                        proj_cm(h * 64, 64, tc, lambda ps, pt: copy(evac_eng(), qT[:, h, tsl], ps, [pt], [t_q[h][tc]]))
                    for b, base in ((0, 768), (1, 1024)):
                        for g in range(2):
                            proj_cm(base + g * 64, 64, tc,
                                    lambda ps, pt: copy(evac_eng(), kT[:, b * 2 + g, tsl], ps, [pt], [t_k[b * 2 + g][tc]]))
                    for kv, base in ((0, 512), (1, 640)):
                        for g in range(2):
                            idx = kv * 2 + g

                            def ev(ps, pt):
                                for dst, j0 in ((kvA, 0), (kvB, 16)):
                                    S.op("dve", lambda e: e.tensor_tensor(
                                        out=dst[:, idx, tsl].rearrange("p (b j) -> p b j", j=16),
                                        in0=ps.rearrange("p (b j) -> p b j", j=16),
                                        in1=posT[:, kv, j0:j0 + 16].unsqueeze(1).to_broadcast([64, 32, 16]), op=ALU.add),
                                        [pt, t_pos], [t_kv[idx][tc]])
                            proj_cm(base + g * 64, 64, tc, ev)
                for i in range(NT):
                    ps, pt = psbank()
                    for ri, (c0, n) in enumerate(((896, 128), (1152, 128), (1280, 24))):
                        for dk in range(8):
                            mm(ps[:, ri * 128:ri * 128 + n], hT[:, dk, i * 128:(i + 1) * 128], Wn[:, dk, c0:c0 + n], dk == 0, dk == 7,
                               t_Wn + [t_hT[i // 4]], [pt])
                    copy("dve", Vt[:, i, :, 0:64], ps[:, 0:256].rearrange("p (a d) -> p a d", d=64), [pt], [t_V[i]])
                    S.op("act", lambda e: e.activation(out=gate[:, i, :], in_=ps[:, 256:280], func=AF.Sigmoid), [pt], [t_V[i]])

                w2b = sb("w2b", [128, 2, 2, 64], BF16, s1c)
                b1T = sb("b1T", [128, 2, 2], F32, s1c)
                b2k = sb("b2k", [64, 1], F32, s1c)
                b2v = sb("b2v", [128, 64], F32, s1c)
                t_cw = Tok()
                for kv in range(2):
                    S.dma(w2b[:, kv], I["cmp_w2"][kv].rearrange("(c p) d -> p c d", p=128), writes=[t_cw], q="pool")
                    S.dma(b1T[:, kv, :], I["cmp_b1"][kv].rearrange("(c p) -> p c", p=128), writes=[t_cw])
                S.dma(b2k[:], I["cmp_b2"][0].unsqueeze(1), writes=[t_cw])
                S.dma(b2v[:], I["cmp_b2"][1].unsqueeze(0).to_broadcast([128, 64]), writes=[t_cw])
                w1s = sb("w1s", [64, 32, 256], BF16, s1c)
                w1b = [w1s, w1s]
                t_w1s = Tok()
                t_w1 = [t_w1s, t_w1s]
                hid = [sb("hid%d" % i, [128, 2, 128], BF16, s1c) for i in range(2)]
                t_hid = [Tok(), Tok()]
                for kv in range(2):
                    S.dma(w1s[:], I["cmp_w1"][kv].rearrange("(j d) h -> d j h", d=64), writes=[t_w1s], q="pool")
                    for g in range(2):
                        idx = kv * 2 + g
                        hb_ = hid[idx % 2]
                        th = t_hid[idx % 2]
                        for hcx in range(2):
                            ps, pt = psbank()
                            for j in range(32):
                                src = kvA if j < 16 else kvB
                                off = j if j < 16 else j
                                rhs = src[:, idx, off:off + 16 * 126 + 1:16]
                                mm(ps[:, 0:127], w1b[kv][:, j, hcx * 128:(hcx + 1) * 128], rhs, j == 0, j == 31,
                                   [t_w1[kv]] + t_kv[idx], [pt])
                            S.op("act", lambda e: e.activation(out=hb_[:, hcx, 0:127], in_=ps[:, 0:127], func=AF.Gelu_apprx_tanh,
                                                               bias=b1T[:, kv, hcx:hcx + 1], scale=1.0), [pt, t_cw], [th])
                        ps, pt = psbank()
                        if kv == 0:
                            for hcx in range(2):
                                mm(ps[0:64, 0:127], w2b[:, 0, hcx, :], hb_[:, hcx, 0:127], hcx == 0, hcx == 1, [t_cw, th], [pt])
                            S.op("act", lambda e: e.activation(out=kcT[:, g, 0:127], in_=ps[0:64, 0:127], func=AF.Identity,
                                                               bias=b2k[:, 0:1], scale=1.0), [pt, t_cw], [t_kc[g]])
                        else:
                            for hcx in range(2):
                                mm(ps[0:127, 0:64], hb_[:, hcx, 0:127], w2b[:, 1, hcx, :], hcx == 0, hcx == 1, [t_cw, th], [pt])
                            S.op("dve", lambda e: e.tensor_tensor(out=VC[0:127, g, 0:64], in0=ps[0:127, 0:64], in1=b2v[0:127, :],
                                                                  op=ALU.add), [pt, t_cw], [t_VC[g]])
                s1c.close()
                S.barrier()
                stg = [sb("stg%d" % i, [128, S_], F32, s1) for i in range(2)]
                t_stg = [Tok(), Tok()]
                t_Wr = Tok()
                for half in range(2):
                    for q2 in range(2):
                        S.dma(Wn[:, :, q2 * 448:(q2 + 1) * 448], wv[:, :, 1304 + half * 896 + q2 * 448:1304 + half * 896 + (q2 + 1) * 448],
                              reads=[], writes=t_Wn + [t_Wr], q="pool")
                    for cc in range(14):
                        c = half * 14 + cc
                        p = c % 2
                        for tc in range(4):
                            ps, pt = psbank()
                            for dk in range(8):
                                mm(ps[0:64, :], Wn[:, dk, cc * 64:(cc + 1) * 64], hT[:, dk, tc * 512:(tc + 1) * 512], dk == 0, dk == 7,
                                   [t_Wr, t_hT[tc]], [pt])
                            copy(evac_eng(), stg[p][0:64, tc * 512:(tc + 1) * 512], ps[0:64, :], [pt], [t_stg[p]])
                        S.dma(rw_d[c * 64:(c + 1) * 64, :], stg[p][0:64, :], reads=[t_stg[p]], writes=[t_rwd[c]])
                mgv = mg_d.rearrange("(n p) c -> n p c", p=128)
                for half in range(2):
                    for q4 in range(2):
                        S.dma(Wn[:, :, q4 * 512:(q4 + 1) * 512], wv[:, :, 3096 + half * 1024 + q4 * 512:3096 + half * 1024 + (q4 + 1) * 512],
                              reads=[], writes=t_Wn + [t_Wr], q="pool")
                    for i in range(NT):
                        p = i % 2
                        for q4 in range(2):
                            ps, pt = psbank()
                            for dk in range(8):
                                mm(ps[:, :], hT[:, dk, i * 128:(i + 1) * 128], Wn[:, dk, q4 * 512:(q4 + 1) * 512], dk == 0, dk == 7,
                                   [t_Wr, t_hT[i // 4]], [pt])
                            S.op("act", lambda e: e.activation(out=stg[p][:, q4 * 512:(q4 + 1) * 512], in_=ps[:, :], func=AF.Sigmoid),
                                 [pt], [t_stg[p]])
                        S.dma(mgv[i][:, half * 1024:(half + 1) * 1024], stg[p][:, 0:1024], reads=[t_stg[p]], writes=[t_mgd[i]])
            S.barrier()
            if "kv2" in debug:
                o = dbg_out("kT", [64, 4, S_], BF16)
                S.dma(o[:, :, :], kT[:], reads=[t for l in t_k for t in l], writes=[dbg_tok])
                o = dbg_out("Vt", [128, NT, 4, 65], BF16)
                S.dma(o[:, :, :, :], Vt[:], reads=t_V, writes=[dbg_tok])
                o = dbg_out("gate", [128, NT, 24])
                S.dma(o[:, :, :], gate[:], reads=t_V, writes=[dbg_tok])
            if "kc" in debug:
                o = dbg_out("kcT", [64, 2, 128], BF16)
                S.dma(o[:, :, :], kcT[:], reads=t_kc, writes=[dbg_tok])
                o = dbg_out("VC", [128, 2, 98])
                S.dma(o[:, :, :], VC[:], reads=t_VC, writes=[dbg_tok])
                o = dbg_out("qT", [64, 8, S_], BF16)
                S.dma(o[:, :, :], qT[:], reads=[t for l in t_q for t in l], writes=[dbg_tok])

            Theta = sb("Theta", [128, 8, 3, 128], BF16, sn)
            t_Th = [Tok() for _ in range(8)]
            TriLT = sb("TriLT", [128, 128], BF16, sn)
            Expand = sb("Expand", [32, NT, 128], BF16, sn)
            J128 = sb("J128", [128, 128], F32, sn)
            J127 = sb("J127", [128, 128], F32, sn)
            Csel = sb("Csel", [128, NT, 32], F32, sn)
            t_c2 = Tok("const2")
            S.dma(TriLT[:], C["TriLT"][:, :], writes=[t_c2])
            S.dma(Expand[:], C["Expand"][:, :, :], writes=[t_c2])
            S.dma(J128[:], C["J128"][:, :], writes=[t_c2])
            S.dma(J127[:], C["J127"][:, :], writes=[t_c2])
            S.dma(Csel[:], C["Csel"][:, :, :], writes=[t_c2])
            with ExitStack() as s2:
                OHD = sb("OHD", [33, FLEN], F32, s2)
                relbx = sb("relbx", [33, 8], F32, s2)
                F_sb = sb("F_sb", [8, FLEN], F32, s2)
                Hk = sb("Hk", [128, 8, 384], F32, s2)
                t_f = Tok()
                t_Fsb = Tok()
                t_Fd = Tok()
                t_Hk = Tok()
                S.dma(OHD[:], C["OHD"][:, :], writes=[t_f])
                S.op("pool", lambda e: e.memset(relbx[:], BIGNEG / 8.0), [], [t_f])
                S.dma(relbx[0:32, :], I["rel_bias"][:, :], writes=[t_f])
                for c0 in range(0, FLEN, 512):
                    n = min(512, FLEN - c0)
                    ps, pt = psbank()
                    mm(ps[0:8, 0:n], relbx[:, :], OHD[:, c0:c0 + n], True, True, [t_f], [pt])
                    S.op("act", lambda e: e.mul(F_sb[:, c0:c0 + n], ps[0:8, 0:n], 8.0), [pt], [t_Fsb])
                S.dma(F_d[:, :], F_sb[:], reads=[t_Fsb], writes=[t_Fd])
                hk_src = bass.AP(tensor=F_d.tensor, offset=1921, ap=[[1, 128], [FLEN, 8], [1, 384]])
                S.dma(Hk[:], hk_src, reads=[t_Fd], writes=[t_Hk])
                for h in range(8):
                    ps, pt = psbank()
                    mm(ps[:, 0:384], J128[:], Hk[:, h, :], True, True, [t_c2, t_Hk], [pt])
                    copy(evac_eng(), Theta[:, h, :, :], ps[:, 0:384].rearrange("p (a t) -> p a t", t=128), [pt], [t_Th[h]])
            S.barrier()
            if "theta" in debug:
                o = dbg_out("Theta", [128, 8, 3, 128], BF16)
                S.dma(o[:, :, :, :], Theta[:], reads=t_Th, writes=[dbg_tok])

            for k_ in range(8):
                for hf in range(2):
                    S.dma(uTb_d[hf * 64:(hf + 1) * 64, :, k_, :].rearrange("c p e -> p c e"),
                          I["peer_uT"][k_ * 128:(k_ + 1) * 128, hf * 8192:(hf + 1) * 8192].rearrange("p (c e) -> p c e", e=128),
                          writes=[t_uTb[k_ * 2 + hf]], q="pool", nobar=True)
            for k_ in range(16):
                S.dma(vb_d[k_ * 1024:(k_ + 1) * 1024, :], I["peer_v"][k_ * 1024:(k_ + 1) * 1024, :], writes=[t_vb[k_]], q="pool", nobar=True)
            S.dma(wq_d[:, :, :], I["peer_w_query"].rearrange("(k p) d -> p k d", p=128), writes=[t_wqd], q="pool", nobar=True)
            with ExitStack() as s3:
                imp = sb("imp", [128, NT, 2, 32], F32, s3)
                t_imp = [[Tok() for _ in range(2)] for _ in range(NT)]
                MnegT = sb("MnegT", [32, 2, S_], BF16, s3)
                t_Mn = [[Tok() for _ in range(NT)] for _ in range(2)]
                sm = [sb("sm%d" % i, [128, 4], F32, s3) for i in range(4)]
                t_sm = [Tok() for _ in range(4)]
                PTs = [sb("PTs%d" % i, [128, S_], BF16, s3) for i in range(2)]
                t_PTs = [Tok(), Tok()]
                PTw = [sb("PTw%d" % i, [128, 640], BF16, s3) for i in range(2)]
                t_PTw = [Tok(), Tok()]
                s3a = ExitStack()
                H16_0 = sb("H16_0", [128, S_], F32, s3a)
                H16 = [H16_0, H16_0]
                t_H16_0 = Tok()
                t_H16 = [t_H16_0, t_H16_0]
                Phi = [sb("Phi%d" % i, [128, S_], BF16, s3a) for i in range(2)]
                t_Phi = [Tok(), Tok()]
                PTc_0 = sb("PTc_0", [128, S_], F32, s3a)
                PTc = [PTc_0, PTc_0]
                t_PTc_0 = Tok()
                t_PTc = [t_PTc_0, t_PTc_0]
                smrr = [0]

                def evac_attn(ps, pt, i, h, br, first):
                    k = smrr[0]
                    smrr[0] = (k + 1) % 4
                    s_, ts = sm[k], t_sm[k]
                    S.op("dve", lambda e: e.tensor_scalar(out=s_[:, 0:1], in0=ps[:, 64:65], scalar1=1e-30, scalar2=None, op0=ALU.max),
                         [pt], [ts])
                    S.op("dve", lambda e: e.reciprocal(out=s_[:, 1:2], in_=s_[:, 0:1]), [ts], [ts])
                    S.op("dve", lambda e: e.tensor_tensor(out=s_[:, 2:3], in0=s_[:, 1:2], in1=gate[:, i, h * 3 + br:h * 3 + br + 1],
                                                          op=ALU.mult), [ts, t_V[i]], [ts])
                    dst = o_nsa[:, i, h * 64:(h + 1) * 64]
                    if first:
                        S.op("dve", lambda e: e.tensor_scalar(out=dst, in0=ps[:, 0:64], scalar1=s_[:, 2:3], scalar2=None, op0=ALU.mult),
                             [pt, ts], [t_on[i]])
                    else:
                        S.op("dve", lambda e: e.scalar_tensor_tensor(out=dst, in0=ps[:, 0:64], scalar=s_[:, 2:3], in1=dst,
                                                                     op0=ALU.mult, op1=ALU.add), [pt, ts], [t_on[i]])
                    return s_, ts

                for h in range(8):
                    g = h // 4
                    p = h % 2
                    src = bass.AP(tensor=F_d.tensor, offset=h * FLEN + 1, ap=[[16, 127], [1, S_]])
                    S.dma(H16[p][0:127, :], src, reads=[t_Fd], writes=[t_H16[p]])
                    for tc in range(4):
                        ps, pt = psbank()
                        mm(ps[0:127, :], J127[0:127, 0:127], H16[p][0:127, tc * 512:(tc + 1) * 512], True, True, [t_c2, t_H16[p]], [pt])
                        copy(evac_eng(), Phi[p][0:127, tc * 512:(tc + 1) * 512], ps[0:127, :], [pt], [t_Phi[p]])
                    for tc in range(4):
                        ps, pt = psbank()
                        mm(ps[0:127, :], kcT[:, g, 0:127], qT[:, h, tc * 512:(tc + 1) * 512], True, False, [t_kc[g], t_q[h][tc]], [pt])
                        mm(ps[0:127, :], ident_bf[0:127, 0:127], Phi[p][0:127, tc * 512:(tc + 1) * 512], False, True, [t_const, t_Phi[p]], [pt])
                        S.op("act", lambda e: e.activation(out=PTc[p][0:127, tc * 512:(tc + 1) * 512], in_=ps[0:127, :], func=AF.Exp,
                                                           scale=ATT_SCALE), [pt], [t_PTc[p]])
                    for i in range(NT):
                        ps, pt = psbank()
                        mm(ps[:, 0:98], PTc[p][0:127, i * 128:(i + 1) * 128], VC[0:127, g, :], True, True, [t_PTc[p], t_VC[g]], [pt])
                        s_, ts = evac_attn(ps, pt, i, h, 0, True)
                        dsti = imp[:, i, g, :]
                        if h % 4 == 0:
                            S.op("dve", lambda e: e.tensor_scalar(out=dsti, in0=ps[:, 65:97], scalar1=s_[:, 1:2], scalar2=None, op0=ALU.mult),
                                 [pt, ts], [t_imp[i][g]])
                        else:
                            S.op("dve", lambda e: e.scalar_tensor_tensor(out=dsti, in0=ps[:, 65:97], scalar=s_[:, 1:2], in1=dsti,
                                                                         op0=ALU.mult, op1=ALU.add), [pt, ts], [t_imp[i][g]])
                if "cmp" in debug:
                    o = dbg_out("o_cmp", [128, NT, 512])
                    S.dma(o[:, :, :], o_nsa[:], reads=t_on, writes=[dbg_tok])
                    o = dbg_out("imp", [128, NT, 2, 32])
                    S.dma(o[:, :, :, :], imp[:], reads=[t for l in t_imp for t in l], writes=[dbg_tok])
                scb = [sb("scb%d" % i, [128, 32], F32, s3a) for i in range(2)]
                cmb = [sb("cmb%d" % i, [128, 32, 32], F32, s3a) for i in range(2)]
                rkb = [sb("rkb%d" % i, [128, 32], F32, s3a) for i in range(2)]
                t_sel = [Tok(), Tok()]
                for i in range(NT):
                    for g in range(2):
                        p = (i * 2 + g) % 2
                        ts = t_sel[p]
                        S.op("dve", lambda e: e.tensor_tensor(out=scb[p][:], in0=imp[:, i, g, :], in1=Csel[:, i, :], op=ALU.add),
                             [t_imp[i][g], t_c2], [ts])
                        S.op("dve", lambda e: e.tensor_tensor(out=cmb[p][:], in0=scb[p][:].unsqueeze(1).to_broadcast([128, 32, 32]),
                                                              in1=scb[p][:].unsqueeze(2).to_broadcast([128, 32, 32]), op=ALU.is_gt), [ts], [ts])
                        S.op("dve", lambda e: e.tensor_reduce(out=rkb[p][:], in_=cmb[p][:], axis=AX.X, op=ALU.add), [ts], [ts])
                        S.op("dve", lambda e: e.tensor_scalar(out=rkb[p][:], in0=rkb[p][:], scalar1=16.0, scalar2=BIGNEG, op0=ALU.is_ge,
                                                              op1=ALU.mult), [ts], [ts])
                        ps, pt = psbank()
                        S.op("pe", lambda e: e.transpose(out=ps[0:32, 0:128], in_=rkb[p][:], identity=ident_f[:]), [ts, t_const], [pt])
                        copy("act", MnegT[:, g, i * 128:(i + 1) * 128], ps[0:32, 0:128], [pt], [t_Mn[g][i]])
                if "cmp" in debug:
                    o = dbg_out("MnegT", [32, 2, S_], BF16)
                    S.dma(o[:, :, :], MnegT[:], reads=[t for l in t_Mn for t in l], writes=[dbg_tok])
                if stage <= 2:
                    s3a.close()
                    S.finish(final_toks)
                    return nc, hc, DBG
                s3a.close()
                S.barrier()
                cnt = 0
                for h in range(8):
                    g = h // 4
                    for i in range(NT):
                        p = cnt % 2
                        cnt += 1
                        qsl = slice(i * 128, (i + 1) * 128)
                        nk = i + 1
                        kts = list(range(max(0, i - 4), i + 1))
                        for kt in range(nk):
                            b = kt // 4
                            out = PS[b][:, (kt % 4) * 128:(kt % 4 + 1) * 128]
                            pt = PT[b]
                            dl = i - kt
                            mm(out, ident_bf[:], Theta[:, h, min(dl, 2), :], True, False, [t_const, t_Th[h]], [pt])
                            mm(out, kT[:, g, kt * 128:(kt + 1) * 128], qT[:, h, qsl], False, False, [t_k[g][kt // 4], t_q[h][i // 4]], [pt])
                            mm(out, Expand[:, kt, :], MnegT[:, g, qsl], False, True, [t_c2, t_Mn[g][i]], [pt])
                        for n_, kt in enumerate(kts):
                            b = 4 + n_ // 4
                            out = PS[b][:, (n_ % 4) * 128:(n_ % 4 + 1) * 128]
                            pt = PT[b]
                            dl = i - kt
                            mm(out, ident_bf[:], Theta[:, h, min(dl, 2), :], True, False, [t_const, t_Th[h]], [pt])
                            if dl == 4:
                                mm(out, ident_bf[:], TriLT[:], False, False, [t_const, t_c2], [pt])
                            mm(out, kT[:, 2 + g, kt * 128:(kt + 1) * 128], qT[:, h, qsl], False, True, [t_k[2 + g][kt // 4], t_q[h][i // 4]], [pt])
                        for b in range((nk + 3) // 4):
                            n = min(4, nk - b * 4) * 128
                            S.op("act", lambda e: e.activation(out=PTs[p][:, b * 512:b * 512 + n], in_=PS[b][:, 0:n], func=AF.Exp,
                                                               scale=ATT_SCALE), [PT[b]], [t_PTs[p]])
                        for b in range((len(kts) + 3) // 4):
                            n = min(4, len(kts) - b * 4) * 128
                            S.op("act", lambda e: e.activation(out=PTw[p][:, b * 512:b * 512 + n], in_=PS[4 + b][:, 0:n], func=AF.Exp,
                                                               scale=ATT_SCALE), [PT[4 + b]], [t_PTw[p]])
                        ps, pt = psbank(6, 8)
                        for kt in range(nk):
                            mm(ps[:, 0:65], PTs[p][:, kt * 128:(kt + 1) * 128], Vt[:, kt, g, :], kt == 0, kt == nk - 1,
                               [t_PTs[p], t_V[kt]], [pt])
                        evac_attn(ps, pt, i, h, 1, False)
                        ps, pt = psbank(6, 8)
                        for n_, kt in enumerate(kts):
                            mm(ps[:, 0:65], PTw[p][:, n_ * 128:(n_ + 1) * 128], Vt[:, kt, 2 + g, :], n_ == 0, n_ == len(kts) - 1,
                               [t_PTw[p], t_V[kt]], [pt])
                        evac_attn(ps, pt, i, h, 2, False)
                onv = onT_d.rearrange("(cc p) t -> p cc t", p=128)
                stT = [sb("stT%d" % i, [128, 4, 128], BF16, s3) for i in range(2)]
                t_stT = [Tok(), Tok()]
                for i in range(NT):
                    p = i % 2
                    ps, pt = psbank(6, 8)
                    for cc in range(4):
                        S.op("pe", lambda e: e.transpose(out=ps[:, cc * 128:(cc + 1) * 128], in_=o_nsa[:, i, cc * 128:(cc + 1) * 128],
                                                         identity=ident_f[:]), [t_on[i], t_const], [pt])
                    copy(evac_eng(), stT[p][:], ps.rearrange("p (c t) -> p c t", c=4), [pt], [t_stT[p]])
                    S.dma(onv[:, :, i * 128:(i + 1) * 128], stT[p][:], reads=[t_stT[p]], writes=[t_onT[i]])
            S.barrier()
            if "theta2" in debug:
                o = dbg_out("Theta2", [128, 8, 3, 128], BF16)
                S.dma(o[:, :, :, :], Theta[:], reads=t_Th, writes=[dbg_tok])
        S.barrier()
        if "nsa" in debug:
            o = dbg_out("o_nsa", [128, NT, 512])
            S.dma(o[:, :, :], o_nsa[:], reads=t_on, writes=[dbg_tok])
        if stage <= 3:
            S.finish(final_toks)
            return nc, hc, DBG

        T1 = S_ + 1
        with ExitStack() as sr:
            MaskAB = sb("MaskAB", [128, 256], F32, sr)
            MaskSL = sb("MaskSL", [128, 128], F32, sr)
            ones64 = sb("ones64", [64, 64], F32, sr)
            onesm = sb("onesm", [64, 64], F32, sr)
            muT = sb("muT", [64, 28], F32, sr)
            mu128 = sb("mu128", [128, 1], F32, sr)
            par = sb("par", [64, 7, 8], F32, sr)
            parn = sb("parn", [64, 8], F32, sr)
            w2f = sb("w2f", [64, 512], F32, sr)
            a2f = sb("a2f", [64, 512], F32, sr)
            g2f = sb("g2f", [128, 512], F32, sr)
            t_rc = Tok("rconst")
            S.dma(MaskAB[:], C["MaskAB"][:, :], writes=[t_rc])
            S.dma(MaskSL[:], C["MaskSL"][:, :], writes=[t_rc])
            S.dma(muT[:], I["token_mu"].rearrange("(c p) -> p c", p=64), writes=[t_rc])
            S.dma(mu128[:], I["token_mu"][1664:1792].unsqueeze(1), writes=[t_rc])
            for k_, nm in enumerate(["rwkv_w0", "rwkv_a0", "rwkv_k_k", "rwkv_k_a", "rwkv_r_k", "rwkv_lnx_g", "rwkv_lnx_b"]):
                S.dma(par[:, k_, :], I[nm].rearrange("(h p) -> p h", p=64), writes=[t_rc])
            S.dma(w2f[:], I["rwkv_w2"][:, :], writes=[t_rc])
            S.dma(a2f[:], I["rwkv_a2"][:, :], writes=[t_rc])
            S.dma(g2f[:], I["rwkv_g2"][:, :], writes=[t_rc])
            S.op("pool", lambda e: e.memset(ones64[:], 1.0), [], [t_rc])
            S.op("pool", lambda e: e.memset(onesm[:], 1.0 / 64.0), [], [t_rc])
            S.op("dve", lambda e: e.tensor_scalar(out=parn[:], in0=par[:, 0, :], scalar1=-1.0, scalar2=None, op0=ALU.mult), [t_rc], [t_rc])

            raw = sb("raw", [128, T1], F32, sr)
            Db = sb("Db", [128, S_], F32, sr)
            t_raw = Tok("raw")
            t_D = Tok("D")
            S.op("pool", lambda e: e.memset(raw[:, 0:1], 0.0), [], [t_raw])

            def shift(P, mu_ap, out_ap, t_out):
                S.op("dve", lambda e: e.tensor_tensor(out=Db[0:P, :], in0=raw[0:P, 0:S_], in1=raw[0:P, 1:T1], op=ALU.subtract),
                     [t_raw], [t_D])
                S.op("dve", lambda e: e.scalar_tensor_tensor(out=out_ap, in0=Db[0:P, :], scalar=mu_ap, in1=raw[0:P, 1:T1],
                                                             op0=ALU.mult, op1=ALU.add), [t_D, t_raw, t_rc], [t_out])

            TW = sb("TW", [64, S_], F32, sr)
            AL = sb("AL", [64, S_], F32, sr)
            GL = sb("GL", [128, S_], F32, sr)
            t_lora = Tok("lora")
            S.dma(raw[0:64, 1:T1], rw_d[1536:1600, :], reads=[t_rwd[24]], writes=[t_raw])
            shift(64, muT[:, 24:25], TW[:], t_lora)
            S.op("act", lambda e: e.activation(out=TW[:], in_=TW[:], func=AF.Tanh), [t_lora], [t_lora])
            S.dma(raw[0:64, 1:T1], rw_d[1600:1664, :], reads=[t_rwd[25]], writes=[t_raw])
            shift(64, muT[:, 25:26], AL[:], t_lora)
            S.dma(raw[:, 1:T1], rw_d[1664:1792, :], reads=[t_rwd[26], t_rwd[27]], writes=[t_raw])
            shift(128, mu128[:, 0:1], GL[:], t_lora)
            S.op("act", lambda e: e.activation(out=GL[:], in_=GL[:], func=AF.Sigmoid), [t_lora], [t_lora])

            Rb = sb("Rb", [64, S_], F32, sr)
            Kb = sb("Kb", [64, S_], F32, sr)
            Vb = sb("Vb", [64, S_], F32, sr)
            Ab = sb("Ab", [64, S_], F32, sr)
            Eb = sb("Eb", [64, S_], F32, sr)
            Lb = sb("Lb", [64, S_], F32, sr)
            AR = sb("AR", [64, NT, 256], F32, sr)
            BT = sb("BT", [64, S_], F32, sr)
            KT = sb("KT", [64, S_], F32, sr)
            tok = sb("tok", [128, NT, 192], F32, sr)
            WCb = sb("WCb", [64, NT], F32, sr)
            Rk = sb("Rk", [64, 64], F32, sr)
            ybf = sb("ybf", [64, S_], BF16, sr)
            ST = [sb("ST%d" % i, [64, 64], F32, sr) for i in range(2)]
            NSET = 4
            AB1 = [sb("AB1_%d" % i, [128, 256], F32, sr) for i in range(NSET)]
            AB2 = [sb("AB2_%d" % i, [128, 256], F32, sr) for i in range(NSET)]
            Pp = [[sb("Pp%d_%d" % (i, j), [128, 128], BF16, sr) for j in range(2)] for i in range(NSET)]
            PTp = [[sb("PTp%d_%d" % (i, j), [128, 128], BF16, sr) for j in range(2)] for i in range(NSET)]
            Np = [[sb("Np%d_%d" % (i, j), [128, 128], BF16, sr) for j in range(2)] for i in range(NSET)]
            XTs = sb("XTs", [128, 64], BF16, sr)
            UTs = sb("UTs", [128, 64], F32, sr)
            t_R, t_K, t_V, t_A, t_E, t_L, t_AR, t_BT, t_KT, t_WC, t_Rk, t_ybf, t_XT, t_UT = [Tok() for _ in range(14)]
            t_tok = [Tok() for _ in range(NT)]
            t_ST = [Tok(), Tok()]
            t_set = [[Tok() for _ in range(6)] for _ in range(NSET)]
            v3 = lambda ap: ap.rearrange("p (n t) -> p n t", t=128)
            onesb = ones64[:, 0:1].to_broadcast([64, S_])

            for h in range(8):
                hs = slice(h * 64, (h + 1) * 64)
                for (row0, ci, dst, td) in ((h * 64, h, Rb, t_R), (512 + h * 64, 8 + h, Kb, t_K), (1024 + h * 64, 16 + h, Vb, t_V)):
                    S.dma(raw[0:64, 1:T1], rw_d[row0:row0 + 64, :], reads=[t_rwd[ci]], writes=[t_raw])
                    shift(64, muT[:, ci:ci + 1], dst[:], td)
                for tc in range(4):
                    ps, pt = psbank()
                    mm(ps[0:64, :], w2f[:, hs], TW[:, tc * 512:(tc + 1) * 512], True, True, [t_rc, t_lora], [pt])
                    S.op("act", lambda e: e.activation(out=Db[0:64, tc * 512:(tc + 1) * 512], in_=ps[0:64, :], func=AF.Exp,
                                                       bias=parn[:, h:h + 1], scale=-1.0), [pt, t_rc], [t_D])
                S.op("act", lambda e: e.activation(out=Db[0:64, :], in_=Db[0:64, :], func=AF.Ln, bias=1.0, scale=1.0), [t_D], [t_D])
                S.op("act", lambda e: e.activation(out=Db[0:64, :], in_=Db[0:64, :], func=AF.Exp, bias=-0.5, scale=-1.0), [t_D], [t_D])
                S.op("dve", lambda e: e.tensor_tensor_scan(out=raw[0:64, 1:T1], data0=onesb, data1=Db[0:64, :], initial=0.0,
                                                           op0=ALU.mult, op1=ALU.subtract), [t_D, t_rc], [t_raw])
                S.op("dve", lambda e: e.tensor_tensor(out=v3(Lb[:]), in0=v3(raw[0:64, 1:T1]),
                                                      in1=raw[0:64, 0:S_:128].unsqueeze(2).to_broadcast([64, NT, 128]), op=ALU.subtract),
                     [t_raw], [t_L])
                for tc in range(4):
                    ps, pt = psbank()
                    mm(ps[0:64, :], a2f[:, hs], AL[:, tc * 512:(tc + 1) * 512], True, True, [t_rc, t_lora], [pt])
                    S.op("act", lambda e: e.activation(out=Ab[:, tc * 512:(tc + 1) * 512], in_=ps[0:64, :], func=AF.Sigmoid,
                                                       bias=par[:, 1, h:h + 1], scale=1.0), [pt, t_rc], [t_A])
                S.op("dve", lambda e: e.tensor_scalar(out=Eb[:], in0=Kb[:], scalar1=par[:, 2, h:h + 1], scalar2=None, op0=ALU.mult),
                     [t_K, t_rc], [t_E])
                S.op("dve", lambda e: e.tensor_tensor(out=BT[:], in0=Eb[:], in1=Eb[:], op=ALU.mult), [t_E], [t_BT])
                for tc in range(4):
                    ps, pt = psbank()
                    mm(ps[0:64, :], ones64[:], BT[:, tc * 512:(tc + 1) * 512], True, True, [t_rc, t_BT], [pt])
                    S.op("act", lambda e: e.activation(out=KT[:, tc * 512:(tc + 1) * 512], in_=ps[0:64, :], func=AF.Sqrt), [pt], [t_KT])
                S.op("dve", lambda e: e.tensor_scalar(out=KT[:], in0=KT[:], scalar1=1e-12, scalar2=None, op0=ALU.max), [t_KT], [t_KT])
                S.op("dve", lambda e: e.reciprocal(out=KT[:], in_=KT[:]), [t_KT], [t_KT])
                S.op("dve", lambda e: e.tensor_tensor(out=Eb[:], in0=Eb[:], in1=KT[:], op=ALU.mult), [t_E, t_KT], [t_E])
                S.op("dve", lambda e: e.tensor_scalar(out=BT[:], in0=Ab[:], scalar1=-1.0, scalar2=par[:, 3, h:h + 1], op0=ALU.add, op1=ALU.mult),
                     [t_A, t_rc], [t_BT])
                S.op("dve", lambda e: e.scalar_tensor_tensor(out=Kb[:], in0=BT[:], scalar=1.0, in1=Kb[:], op0=ALU.add, op1=ALU.mult),
                     [t_BT, t_K], [t_K])
                S.op("dve", lambda e: e.tensor_tensor(out=Db[0:64, :], in0=Db[0:64, :], in1=Lb[:], op=ALU.add), [t_D, t_L], [t_D])
                S.op("act", lambda e: e.activation(out=Db[0:64, :], in_=Db[0:64, :], func=AF.Exp), [t_D], [t_D])
                S.op("dve", lambda e: e.scalar_tensor_tensor(out=AR[:, :, 0:128], in0=v3(Eb[:]), scalar=-1.0, in1=v3(Db[0:64, :]),
                                                             op0=ALU.mult, op1=ALU.mult), [t_E, t_D], [t_AR])
                S.op("act", lambda e: e.activation(out=raw[0:64, 1:T1], in_=Lb[:], func=AF.Exp), [t_L], [t_raw])
                S.op("dve", lambda e: e.tensor_tensor(out=AR[:, :, 128:256], in0=v3(Rb[:]), in1=v3(raw[0:64, 1:T1]), op=ALU.mult),
                     [t_R, t_raw], [t_AR])
                S.op("dve", lambda e: e.tensor_copy(out=WCb[:], in_=raw[0:64, 128:T1:128]), [t_raw], [t_WC])
                S.op("act", lambda e: e.activation(out=Db[0:64, :], in_=Lb[:], func=AF.Exp, scale=-1.0), [t_L, t_AR], [t_D])
                S.op("dve", lambda e: e.tensor_tensor(out=BT[:], in0=Eb[:], in1=Ab[:], op=ALU.mult), [t_E, t_A], [t_BT])
                S.op("dve", lambda e: e.tensor_tensor(out=BT[:], in0=BT[:], in1=Db[0:64, :], op=ALU.mult), [t_BT, t_D], [t_BT])
                S.op("dve", lambda e: e.tensor_tensor(out=KT[:], in0=Kb[:], in1=Db[0:64, :], op=ALU.mult), [t_K, t_D], [t_KT])
                wcb = WCb[:].unsqueeze(2).to_broadcast([64, NT, 128])
                S.op("dve", lambda e: e.tensor_tensor(out=v3(raw[0:64, 1:T1]), in0=v3(BT[:]), in1=wcb, op=ALU.mult), [t_BT, t_WC], [t_raw])
                S.op("dve", lambda e: e.tensor_tensor(out=v3(Db[0:64, :]), in0=v3(KT[:]), in1=wcb, op=ALU.mult), [t_KT, t_WC], [t_D])
                for n in range(NT):
                    ps, pt = psbank()
                    tsl = slice(n * 128, (n + 1) * 128)
                    S.op("pe", lambda e: e.transpose(out=ps[:, 0:64], in_=Vb[:, tsl], identity=ident_f[0:64, 0:64]), [t_V, t_const], [pt])
                    S.op("pe", lambda e: e.transpose(out=ps[:, 64:128], in_=raw[0:64, 1 + n * 128:1 + (n + 1) * 128],
                                                     identity=ident_f[0:64, 0:64]), [t_raw, t_const], [pt])
                    S.op("pe", lambda e: e.transpose(out=ps[:, 128:192], in_=Db[0:64, tsl], identity=ident_f[0:64, 0:64]), [t_D, t_const], [pt])
                    copy(evac_eng(), tok[:, n, :], ps[:, 0:192], [pt], [t_tok[n]])
                S.op("dve", lambda e: e.tensor_copy(out=Rk[:], in_=par[:, 4, h:h + 1].to_broadcast([64, 64])), [t_rc], [t_Rk])
                S.op("pool", lambda e: e.memset(ST[0][:], 0.0), [], [t_ST[0]])

                def precompute(n, s):
                    tsl = slice(n * 128, (n + 1) * 128)
                    tk = t_set[s]
                    ps, pt = psbank()
                    mm(ps[:, 0:256], BT[:, tsl], AR[:, n, :], True, True, [t_BT, t_AR], [pt])
                    S.op("dve", lambda e: e.tensor_tensor(out=AB1[s][:], in0=ps[:, 0:256], in1=MaskAB[:], op=ALU.mult), [pt, t_rc], [tk[0]])
                    yield
                    ps, pt = psbank()
                    mm(ps[:, 0:256], KT[:, tsl], AR[:, n, :], True, True, [t_KT, t_AR], [pt])
                    S.op("dve", lambda e: e.tensor_tensor(out=AB2[s][:], in0=ps[:, 0:256], in1=MaskAB[:], op=ALU.mult), [pt, t_rc], [tk[1]])
                    yield
                    ps, pt = psbank()
                    mm(ps[:, 0:128], AR[:, n, 0:128], BT[:, tsl], True, True, [t_BT, t_AR], [pt])
                    S.op("dve", lambda e: e.tensor_tensor(out=PTp[s][0][:], in0=ps[:, 0:128], in1=MaskSL[:], op=ALU.mult), [pt, t_rc], [tk[3]])
                    yield
                    S.op("pool", lambda e: e.tensor_copy(out=Pp[s][0][:], in_=AB1[s][:, 0:128]), [tk[0]], [tk[2]])
                    S.op("pool", lambda e: e.tensor_tensor(out=Np[s][0][:], in0=AB1[s][:, 0:128], in1=ident_f[:], op=ALU.add), [tk[0], t_const], [tk[4]])
                    yield
                    cur = 0
                    for j in range(1, 7):
                        nxt = 1 - cur
                        if j < 6:
                            ps, pt = psbank()
                            mm(ps[:, 0:128], PTp[s][cur][:], Pp[s][cur][:], True, True, [tk[2], tk[3]], [pt])
                            ps2, pt2 = psbank()
                            mm(ps2[:, 0:128], Pp[s][cur][:], PTp[s][cur][:], True, True, [tk[2], tk[3]], [pt2])
                            copy("act", Pp[s][nxt][:], ps[:, 0:128], [pt], [tk[2]])
                            copy("dve", PTp[s][nxt][:], ps2[:, 0:128], [pt2], [tk[3]])
                        else:
                            ps2, pt2 = psbank()
                            mm(ps2[:, 0:128], Pp[s][cur][:], PTp[s][cur][:], True, True, [tk[2], tk[3]], [pt2])
                            copy("dve", PTp[s][nxt][:], ps2[:, 0:128], [pt2], [tk[3]])
                        yield
                        ps, pt = psbank()
                        mm(ps[:, 0:128], PTp[s][nxt][:], Np[s][cur][:], True, True, [tk[3], tk[4]], [pt])
                        S.op("dve", lambda e: e.tensor_tensor(out=Np[s][nxt][:], in0=ps[:, 0:128], in1=Np[s][cur][:], op=ALU.add), [pt, tk[4]], [tk[4]])
                        cur = nxt
                        yield
                    assert cur == 0

                for n0 in range(0, NT, NSET):
                    gens = [precompute(n0 + s, s) for s in range(NSET)]
                    alive = list(gens)
                    while alive:
                        for g_ in list(alive):
                            try:
                                next(g_)
                            except StopIteration:
                                alive.remove(g_)
                    for s in range(NSET):
                        n = n0 + s
                        tsl = slice(n * 128, (n + 1) * 128)
                        tk = t_set[s]
                        sc, sn_ = ST[n % 2], ST[(n + 1) % 2]
                        tsc, tsn = t_ST[n % 2], t_ST[(n + 1) % 2]
                        Nf = Np[s][0]
                        ps, pt = psbank()
                        mm(ps[:, 0:64], AR[:, n, 0:128], sc[:], True, False, [t_AR, tsc], [pt])
                        mm(ps[:, 0:64], AB2[s][:, 0:128], tok[:, n, 0:64], False, True, [tk[1], t_tok[n]], [pt])
                        copy("act", XTs[:], ps[:, 0:64], [pt], [t_XT])
                        ps, pt = psbank()
                        mm(ps[:, 0:64], Nf[:], XTs[:], True, True, [tk[4], t_XT], [pt])
                        copy("act", UTs[:], ps[:, 0:64], [pt], [t_UT])
                        ps, pt = psbank()
                        mm(ps[0:64, 0:128], sc[:], AR[:, n, 128:256], True, False, [tsc, t_AR], [pt])
                        mm(ps[0:64, 0:128], UTs[:], AB1[s][:, 128:256], False, False, [t_UT, tk[0]], [pt])
                        mm(ps[0:64, 0:128], tok[:, n, 0:64], AB2[s][:, 128:256], False, True, [t_tok[n], tk[1]], [pt])
                        copy("dve", Lb[:, tsl], ps[0:64, 0:128], [pt], [t_L])
                        ps, pt = psbank()
                        mm(ps[0:64, 0:64], tok[:, n, 64:128], UTs[:], True, False, [t_tok[n], t_UT], [pt])
                        mm(ps[0:64, 0:64], tok[:, n, 128:192], tok[:, n, 0:64], False, True, [t_tok[n]], [pt])
                        S.op("dve", lambda e: e.scalar_tensor_tensor(out=sn_[:], in0=sc[:], scalar=WCb[:, n:n + 1], in1=ps[0:64, 0:64],
                                                                     op0=ALU.mult, op1=ALU.add), [tsc, t_WC, pt], [tsn])
                for tc in range(4):
                    csl = slice(tc * 512, (tc + 1) * 512)
                    ps, pt = psbank()
                    mm(ps[0:64, :], onesm[:], Lb[:, csl], True, True, [t_rc, t_L], [pt])
                    S.op("dve", lambda e: e.tensor_tensor(out=Ab[:, csl], in0=Lb[:, csl], in1=ps[0:64, :], op=ALU.subtract), [t_L, pt], [t_A])
                S.op("pool", lambda e: e.tensor_tensor(out=BT[:], in0=Ab[:], in1=Ab[:], op=ALU.mult), [t_A], [t_BT])
                for tc in range(4):
                    csl = slice(tc * 512, (tc + 1) * 512)
                    ps, pt = psbank()
                    mm(ps[0:64, :], onesm[:], BT[:, csl], True, True, [t_rc, t_BT], [pt])
                    S.op("dve", lambda e: e.tensor_scalar(out=KT[:, csl], in0=ps[0:64, :], scalar1=64e-5, scalar2=None, op0=ALU.add), [pt], [t_KT])
                S.op("act", lambda e: e.activation(out=KT[:], in_=KT[:], func=AF.Sqrt), [t_KT], [t_KT])
                S.op("dve", lambda e: e.reciprocal(out=KT[:], in_=KT[:]), [t_KT], [t_KT])
                S.op("dve", lambda e: e.tensor_tensor(out=Ab[:], in0=Ab[:], in1=KT[:], op=ALU.mult), [t_A, t_KT], [t_A])
                S.op("dve", lambda e: e.tensor_scalar(out=Ab[:], in0=Ab[:], scalar1=par[:, 5, h:h + 1], scalar2=par[:, 6, h:h + 1],
                                                      op0=ALU.mult, op1=ALU.add), [t_A, t_rc], [t_A])
                S.op("pool", lambda e: e.tensor_tensor(out=BT[:], in0=Rb[:], in1=Kb[:], op=ALU.mult), [t_R, t_K], [t_BT])
                for tc in range(4):
                    csl = slice(tc * 512, (tc + 1) * 512)
                    ps, pt = psbank()
                    mm(ps[0:64, :], Rk[:], BT[:, csl], True, True, [t_Rk, t_BT], [pt])
                    S.op("dve", lambda e: e.tensor_tensor(out=KT[:, csl], in0=ps[0:64, :], in1=Vb[:, csl], op=ALU.mult), [pt, t_V], [t_KT])
                S.op("dve", lambda e: e.tensor_tensor(out=Ab[:], in0=Ab[:], in1=KT[:], op=ALU.add), [t_A, t_KT], [t_A])
                for tc in range(4):
                    csl = slice(tc * 512, (tc + 1) * 512)
                    ps, pt = psbank()
                    mm(ps[0:64, :], g2f[:, hs], GL[:, csl], True, True, [t_rc, t_lora], [pt])
                    S.op("dve", lambda e: e.tensor_tensor(out=ybf[:, csl], in0=ps[0:64, :], in1=Ab[:, csl], op=ALU.mult), [pt, t_A], [t_ybf])
                S.dma(yrw_d[h * 64:(h + 1) * 64, :], ybf[:], reads=[t_ybf], writes=[t_yrw[h]])
                if "rwkv1" in debug and h == 0:
                    break
        S.barrier()
        if "rwkv" in debug or "rwkv1" in debug:
            o = dbg_out("yrw", [512, S_], BF16)
            ytmp = sb("ytmp", [128, 4, S_], BF16)
            tt_ = Tok()
            S.dma(ytmp[:], yrw_d.rearrange("(c p) t -> p c t", p=128), reads=t_yrw, writes=[tt_])
            S.dma(o.rearrange("(c p) t -> p c t", p=128), ytmp[:], reads=[tt_], writes=[dbg_tok])
        if stage <= 4:
            S.finish(final_toks)
            return nc, hc, DBG

        x2_d = nc.dram_tensor("x2_scr", [S_, D_], F32, kind="Internal").ap()
        t_x2d = [Tok("x2d%d" % i) for i in range(NT)]
        x2T_d = nc.dram_tensor("x2T_scr", [128, 8, S_], BF16, kind="Internal").ap()
        t_x2T = [Tok("x2T%d" % i) for i in range(NT)]
        x2v = x2_d.rearrange("(n p) d -> n p d", p=128)
        load_gb(I["ln_mix_g"], I["ln_mix_b"])
        with ExitStack() as sm_:
            wn_b = sb("wn_b", [128, 4, D_], BF16, sm_)
            wr_b = sb("wr_b", [128, 4, D_], BF16, sm_)
            wo_b = sb("wo_b", [128, 8, D_], BF16, sm_)
            ynT = sb("ynT", [128, 4, S_], BF16, sm_)
            yrT = sb("yrT", [128, 4, S_], BF16, sm_)
            t_mw = Tok("mw")
            t_yn = Tok("yn")
            S.dma(wn_b[:], I["w_o_nsa"].rearrange("(k p) d -> p k d", p=128), writes=[t_mw], q="pool")
            S.dma(wr_b[:], I["w_o_rwkv"].rearrange("(k p) d -> p k d", p=128), writes=[t_mw], q="pool")
            for k2 in range(2):
                S.dma(wo_b[:, k2 * 4:(k2 + 1) * 4, :], I["w_out"].rearrange("(k p) d -> p k d", p=128)[:, k2 * 4:(k2 + 1) * 4, :], writes=[t_mw], q="pool")
            S.dma(ynT[:], onT_d.rearrange("(c p) t -> p c t", p=128), reads=t_onT, writes=[t_yn])
            S.dma(yrT[:], yrw_d.rearrange("(c p) t -> p c t", p=128), reads=t_yrw, writes=[t_yn])
            gtb = [sb("gtb%d" % i, [128, 2048], F32, sm_) for i in range(2)]
            hb2 = [sb("hb2_%d" % i, [128, D_], F32, sm_) for i in range(2)]
            m1b = [sb("m1b%d" % i, [128, D_], F32, sm_) for i in range(2)]
            mbb = [sb("mbb%d" % i, [128, D_], BF16, sm_) for i in range(2)]
            mTb = [sb("mTb%d" % i, [128, 8, 128], BF16, sm_) for i in range(2)]
            rsb = [sb("rsb%d" % i, [128, D_], F32, sm_) for i in range(2)]
            xnb2 = [sb("xnb2_%d" % i, [128, D_], F32, sm_) for i in range(2)]
            x2b = [sb("x2b%d" % i, [128, D_], F32, sm_) for i in range(2)]
            x2h = [sb("x2h%d" % i, [128, D_], BF16, sm_) for i in range(2)]
            tg, th2, tm1, tmb, tmT, trs, txn, tx2, tx2h = [[Tok(), Tok()] for _ in range(9)]
            mgv2 = mg_d.rearrange("(n p) c -> n p c", p=128)
            for i in range(NT):
                p = i % 2
                tsl = slice(i * 128, (i + 1) * 128)
                S.dma(gtb[p][:], mgv2[i], reads=[t_mgd[i]], writes=[tg[p]])
                S.dma(hb2[p][:], hv[i], reads=[t_hd[i]], writes=[th2[p]])
                for half in range(2):
                    dsl = slice(half * 512, (half + 1) * 512)
                    for c in range(4):
                        mm(PS[half][:, :], ynT[:, c, tsl], wn_b[:, c, dsl], c == 0, c == 3, [t_yn, t_mw], [PT[half]])
                    for c in range(4):
                        mm(PS[2 + half][:, :], yrT[:, c, tsl], wr_b[:, c, dsl], c == 0, c == 3, [t_yn, t_mw], [PT[2 + half]])
                for half in range(2):
                    dsl = slice(half * 512, (half + 1) * 512)
                    S.op("dve", lambda e: e.tensor_tensor(out=m1b[p][:, dsl], in0=PS[half][:, :], in1=gtb[p][:, dsl], op=ALU.mult),
                         [PT[half], tg[p]], [tm1[p]])
                    S.op("dve", lambda e: e.tensor_tensor(out=rsb[p][:, dsl], in0=PS[2 + half][:, :], in1=gtb[p][:, 1024 + half * 512:1024 + (half + 1) * 512],
                                                          op=ALU.mult), [PT[2 + half], tg[p]], [trs[p]])
                S.op("pool", lambda e: e.tensor_tensor(out=mbb[p][:], in0=m1b[p][:], in1=rsb[p][:], op=ALU.add), [tm1[p], trs[p]], [tmb[p]])
                psb = PS[4 + p].bitcast(BF16)
                for dk in range(8):
                    S.op("pe", lambda e: e.transpose(out=psb[:, dk * 128:(dk + 1) * 128], in_=mbb[p][:, dk * 128:(dk + 1) * 128],
                                                     identity=ident_bf[:]), [tmb[p], t_const], [PT[4 + p]])
                copy("act", mTb[p][:], psb.rearrange("p (k t) -> p k t", k=8), [PT[4 + p]], [tmT[p]])
                for half in range(2):
                    dsl = slice(half * 512, (half + 1) * 512)
                    for c in range(8):
                        mm(PS[6 + half][:, :], mTb[p][:, c, :], wo_b[:, c, dsl], c == 0, c == 7, [tmT[p], t_mw], [PT[6 + half]])
                    S.op("dve", lambda e: e.scalar_tensor_tensor(out=rsb[p][:, dsl], in0=hb2[p][:, dsl], scalar=ALPHA, in1=PS[6 + half][:, :],
                                                                 op0=ALU.mult, op1=ALU.add), [th2[p], PT[6 + half], tmb[p]], [trs[p]])
                ln_tile(p, rsb[p][:], x2b[p][:], trs[p], tx2[p], xnb2[p][:], txn[p])
                S.dma(x2v[i], x2b[p][:], reads=[tx2[p]], writes=[t_x2d[i]])
                S.op("act", lambda e: e.copy(out=x2h[p][:], in_=x2b[p][:]), [tx2[p]], [tx2h[p]])
                psb = PS[4 + p].bitcast(BF16)
                for dk in range(8):
                    S.op("pe", lambda e: e.transpose(out=psb[:, dk * 128:(dk + 1) * 128], in_=x2h[p][:, dk * 128:(dk + 1) * 128],
                                                     identity=ident_bf[:]), [tx2h[p], t_const], [PT[4 + p]])
                copy("dve", mTb[p][:], psb.rearrange("p (k t) -> p k t", k=8), [PT[4 + p]], [tmT[p]])
                S.dma(x2T_d[:, :, tsl], mTb[p][:], reads=[tmT[p]], writes=[t_x2T[i]])
        S.barrier()
        if "x2" in debug:
            o = dbg_out("x2", [S_, D_])
            xtmp = sb("xtmp", [128, NT, D_], F32)
            tt2 = Tok()
            S.dma(xtmp[:], x2_d.rearrange("(n p) d -> p n d", p=128), reads=t_x2d, writes=[tt2])
            S.dma(o.rearrange("(n p) d -> p n d", p=128), xtmp[:], reads=[tt2], writes=[dbg_tok])
        if stage <= 5:
            S.finish(final_toks)
            return nc, hc, DBG

        load_gb(I["ln_ffn_g"], I["ln_ffn_b"])
        outv = out_d.rearrange("(n p) d -> n p d", p=128)
        t_out = [Tok("out%d" % i) for i in range(NT)]
        final_toks.extend(t_out)
        uTv = uTb_d.rearrange("c p k e -> p c (k e)")
        vbv = vb_d.rearrange("(c p) d -> p c d", p=128)
        NEGBIG = -1.0e30
        with ExitStack() as sp_:
            skT = sb("skT", [64, 16, 128], BF16, sp_)
            sk_scope = ExitStack()
            skn = sb("skn", [128, 16, 64], F32, sk_scope)
            t_pw = Tok("pw")
            S.dma(skn[:], I["peer_sub_keys"].rearrange("h q n d -> n (h q) d"), writes=[t_pw])
            for hp in range(16):
                ps, pt = psbank()
                S.op("pe", lambda e: e.transpose(out=ps[0:64, 0:128], in_=skn[:, hp, :], identity=ident_f[:]), [t_pw, t_const], [pt])
                copy(evac_eng(), skT[:, hp, :], ps[0:64, 0:128], [pt], [t_pw])
            sk_scope.close()
            S.barrier()
            X1 = sb("X1", [128, 16384], BF16, sp_)
            X2 = sb("X2", [128, 16384], BF16, sp_)
            Trm = sb("Trm", [128, 16384], BF16, sp_)
            Orm = sb("Orm", [128, 16384], BF16, sp_)
            t_Trm, t_Orm = Tok("Trm"), Tok("Orm")
            t_X1p = [Tok("X1_%d" % i) for i in range(8)]
            t_X2p = [Tok("X2_%d" % i) for i in range(8)]
            gt_d = nc.dram_tensor("gt_scr", [NT, 128, 128, 128], BF16, kind="Internal").ap()
            t_gt = [Tok("gt%d" % i) for i in range(NT)]
            wqv = X1[:, 0:8192].rearrange("p (k d) -> p k d", k=8)
            X1v = X1[:].rearrange("p (n h a) -> p n h a", h=8, a=16)
            X2v = X2[:].rearrange("p (n h a) -> p n h a", h=8, a=16)
            cand = X2[:].bitcast(F32)[:, 0:2048].rearrange("p (h c) -> p h c", h=8)
            X2g = X2[:].rearrange("p (n t) -> p n t", t=128)
            Trm3 = Trm[:].rearrange("p (n t) -> p n t", t=128)
            Orm3 = Orm[:].rearrange("p (n t) -> p n t", t=128)
            qb = Trm[:, 0:1024]
            qTi = Trm[0:64, 1024:3072].rearrange("p (a t) -> p a t", t=128)
            wk = Trm[:, 3072:3584].bitcast(F32)
            wk2 = Trm[:, 3584:4096].bitcast(F32)
            x2Tg = Trm[:, 8192:9216].rearrange("p (k t) -> p k t", k=8)
            SE = sb("SE", [128, 8192], BF16, sp_)
            sc = SE[:, 0:4096].bitcast(F32).rearrange("p (a n) -> p a n", n=128)
            Ee = SE[:, 4096:8192].bitcast(F32).rearrange("p (a n) -> p a n", n=128)
            SD = sb("SD", [128, 8192], BF16, sp_)
            x2Td = sb("x2Td", [128, 8, 256], BF16, sp_)
            gtc = [sb("gtc%d" % i, [128, 2, 128], BF16, sp_) for i in range(4)]
            x2t_t = sb("x2t_t", [128, D_], F32, sp_)
            rso_t = sb("rso_t", [128, D_], F32, sp_)
            xno_t = sb("xno_t", [128, D_], F32, sp_)
            x2t, rso, xno = x2t_t[:], rso_t[:], xno_t[:]
            t_fin = Tok("fin")
            s16 = sb("s16", [128, 16, 16], F32, sp_)
            e16 = sb("e16", [128, 16, 16], F32, sp_)
            tops = sb("tops", [128, 8, 24], F32, sp_)
            stt_ = sb("stt_", [128, 8, 8], F32, sp_)
            exps = sb("exps", [128, 8, 16], F32, sp_)
            E1s = sb("E1s", [128, 8, 16], F32, sp_)
            thr = sb("thr", [128, 8, 16], F32, sp_)
            uTc = [SD[:, i * 1024:(i + 1) * 1024].rearrange("p (k e) -> p k e", k=8) for i in range(4)]
            vcb = [SD[:, 4096 + i * 1024:4096 + (i + 1) * 1024] for i in range(4)]
            Hb = [sb("Hb%d" % i, [128, 256], BF16, sp_) for i in range(2)]
            Hg = [sb("Hg%d" % i, [128, 256], BF16, sp_) for i in range(2)]
            (t_s16, t_e16, t_tops, t_st, t_exps, t_E1s, t_thr) = [Tok() for _ in range(7)]
            t_x2Tg = t_qb = t_qTi = t_wk = t_wk2 = t_Trm
            t_uTc = [Tok() for _ in range(4)]
            t_vcb = [Tok() for _ in range(4)]
            t_gtc = [Tok() for _ in range(4)]
            t_x2Td = Tok("x2Td")
            t_scL = [Tok("sc")]
            t_EeL = [Tok("Ee")]
            t_Hb = [Tok(), Tok()]
            t_Hg = [Tok(), Tok()]
            def gphase(i):
                S.dma(x2Tg, x2T_d[:, :, i * 128:(i + 1) * 128], reads=[t_x2T[i]], writes=[t_x2Tg])
                S.dma(wqv, wq_d[:, :, :], reads=[t_wqd], writes=[*t_X1p])
                for half in range(2):
                    ps, pt = psbank(6, 8)
                    for dk in range(8):
                        mm(ps[:, :], x2Tg[:, dk, :], wqv[:, dk, half * 512:(half + 1) * 512], dk == 0, dk == 7, [t_x2Tg, *t_X1p], [pt])
                    copy("act", qb[:, half * 512:(half + 1) * 512], ps[:, :], [pt], [t_qb])
                for b in range(2):
                    ps, pt = psbank(6, 8)
                    psb = ps.bitcast(BF16)
                    for jj in range(8):
                        hp = b * 8 + jj
                        S.op("pe", lambda e: e.transpose(out=psb[0:64, jj * 128:(jj + 1) * 128], in_=qb[:, hp * 64:(hp + 1) * 64],
                                                         identity=ident_bf[:]), [t_qb, t_const], [pt])
                    copy("act", qTi[:, b * 8:(b + 1) * 8, :], psb[0:64, :].rearrange("p (a t) -> p a t", t=128), [pt], [t_qTi])
                for b in range(4):
                    ps, pt = psbank(6, 8)
                    for j in range(4):
                        hp = b * 4 + j
                        mm(ps[:, j * 128:(j + 1) * 128], qTi[:, hp, :], skT[:, hp, :], True, True, [t_qTi, t_pw], [pt])
                    copy("act", sc[:, b * 4:(b + 1) * 4, :], ps[:, :].rearrange("p (a n) -> p a n", n=128), [pt], [*t_scL])
                for hp in range(16):
                    S.op("dve", lambda e: e.max(out=s16[:, hp, 0:8], in_=sc[:, hp, :]), [*t_scL], [t_s16])
                    S.op("dve", lambda e: e.match_replace(out=wk[:, 0:128], in_to_replace=s16[:, hp, 0:8], in_values=sc[:, hp, :], imm_value=NEGBIG),
                         [*t_scL, t_s16], [t_wk])
                    S.op("dve", lambda e: e.max(out=s16[:, hp, 8:16], in_=wk[:, 0:128]), [t_wk], [t_s16])
                s16v = s16[:].rearrange("p (h q) a -> p h q a", q=2)
                S.op("dve", lambda e: e.tensor_tensor(out=cand.rearrange("p h (a b) -> p h a b", b=16),
                                                      in0=s16v[:, :, 0, :].unsqueeze(3).to_broadcast([128, 8, 16, 16]),
                                                      in1=s16v[:, :, 1, :].unsqueeze(2).to_broadcast([128, 8, 16, 16]), op=ALU.add), [t_s16], [*t_X2p])
                for h in range(8):
                    S.op("dve", lambda e: e.max(out=tops[:, h, 0:8], in_=cand[:, h, :]), [*t_X2p], [t_tops])
                    S.op("dve", lambda e: e.match_replace(out=wk[:], in_to_replace=tops[:, h, 0:8], in_values=cand[:, h, :], imm_value=NEGBIG),
                         [*t_X2p, t_tops], [t_wk])
                    S.op("dve", lambda e: e.max(out=tops[:, h, 8:16], in_=wk[:]), [t_wk], [t_tops])
                    S.op("dve", lambda e: e.match_replace(out=wk2[:], in_to_replace=tops[:, h, 8:16], in_values=wk[:], imm_value=NEGBIG),
                         [t_wk, t_tops], [t_wk2])
                    S.op("dve", lambda e: e.max(out=tops[:, h, 16:24], in_=wk2[:]), [t_wk2], [t_tops])
                S.op("dve", lambda e: e.tensor_tensor(out=stt_[:, :, 0:1], in0=tops[:, :, 15:16], in1=tops[:, :, 16:17], op=ALU.add), [t_tops], [t_st])
                S.op("dve", lambda e: e.tensor_scalar(out=stt_[:, :, 0:1], in0=stt_[:, :, 0:1], scalar1=0.5, scalar2=None, op0=ALU.mult), [t_st], [t_st])
                S.op("dve", lambda e: e.tensor_tensor(out=exps[:], in0=tops[:, :, 0:16], in1=tops[:, :, 0:1].to_broadcast([128, 8, 16]),
                                                      op=ALU.subtract), [t_tops], [t_exps])
                S.op("act", lambda e: e.activation(out=exps[:], in_=exps[:], func=AF.Exp), [t_exps], [t_exps])
                S.op("dve", lambda e: e.tensor_reduce(out=stt_[:, :, 1:2], in_=exps[:], axis=AX.X, op=ALU.add), [t_exps, t_st], [t_st])
                S.op("dve", lambda e: e.reciprocal(out=stt_[:, :, 2:3], in_=stt_[:, :, 1:2]), [t_st], [t_st])
                S.op("dve", lambda e: e.tensor_tensor(out=Ee[:], in0=sc[:], in1=s16[:, :, 0:1].to_broadcast([128, 16, 128]), op=ALU.subtract),
                     [*t_scL, t_s16], [*t_EeL])
                S.op("act", lambda e: e.activation(out=Ee[:], in_=Ee[:], func=AF.Exp), [*t_EeL], [*t_EeL])
                S.op("dve", lambda e: e.tensor_tensor(out=e16[:], in0=s16[:], in1=s16[:, :, 0:1].to_broadcast([128, 16, 16]), op=ALU.subtract),
                     [t_s16], [t_e16])
                S.op("act", lambda e: e.activation(out=e16[:], in_=e16[:], func=AF.Exp), [t_e16], [t_e16])
                e16v = e16[:].rearrange("p (h q) a -> p h q a", q=2)
                S.op("dve", lambda e: e.tensor_tensor(out=E1s[:], in0=e16v[:, :, 0, :], in1=stt_[:, :, 2:3].to_broadcast([128, 8, 16]), op=ALU.mult),
                     [t_e16, t_st], [t_E1s])
                S.op("dve", lambda e: e.tensor_tensor(out=thr[:], in0=stt_[:, :, 0:1].to_broadcast([128, 8, 16]), in1=s16v[:, :, 0, :], op=ALU.subtract),
                     [t_s16, t_st], [t_thr])
                scv = sc[:].rearrange("p (h q) n -> p h q n", q=2)
                Eev = Ee[:].rearrange("p (h q) n -> p h q n", q=2)
                sc2b = scv[:, :, 1, :].rearrange("p h n -> p n h").unsqueeze(3).to_broadcast([128, 128, 8, 16])
                sc1b = scv[:, :, 0, :].rearrange("p h n -> p n h").unsqueeze(3).to_broadcast([128, 128, 8, 16])
                E2b = Eev[:, :, 1, :].rearrange("p h n -> p n h").unsqueeze(3).to_broadcast([128, 128, 8, 16])
                thrb = thr[:].unsqueeze(1).to_broadcast([128, 128, 8, 16])
                E1sb = E1s[:].unsqueeze(1).to_broadcast([128, 128, 8, 16])
                s1b = s16v[:, :, 0, :].unsqueeze(1).to_broadcast([128, 128, 8, 16])
                for pc in range(8):
                    nsl = slice(pc * 16, (pc + 1) * 16)
                    csl = slice(pc * 2048, (pc + 1) * 2048)
                    S.op("dve", lambda e: e.tensor_tensor(out=X1v[:, nsl], in0=sc2b[:, nsl], in1=thrb[:, nsl], op=ALU.is_ge), [*t_scL, t_thr], [t_X1p[pc]])
                    S.op("pool", lambda e: e.tensor_tensor(out=X2v[:, nsl], in0=E2b[:, nsl], in1=E1sb[:, nsl], op=ALU.mult), [*t_EeL, t_E1s], [t_X2p[pc]])
                    S.op("pool", lambda e: e.tensor_tensor(out=X1[:, csl], in0=X1[:, csl], in1=X2[:, csl], op=ALU.mult), [t_X1p[pc], t_X2p[pc]], [t_X1p[pc]])
                    for j8 in range(2):
                        ps, pt = psbank(6, 8)
                        psb = ps.bitcast(BF16)
                        for jj in range(8):
                            n = pc * 16 + j8 * 8 + jj
                            S.op("pe", lambda e: e.transpose(out=psb[:, jj * 128:(jj + 1) * 128], in_=X1[:, n * 128:(n + 1) * 128], identity=ident_bf[:]),
                                 [t_X1p[pc], t_const], [pt])
                        copy("act", Trm3[:, pc * 16 + j8 * 8:pc * 16 + (j8 + 1) * 8, :], psb.rearrange("p (n t) -> p n t", t=128), [pt], [t_Trm])
                for pc in range(8):
                    nsl = slice(pc * 16, (pc + 1) * 16)
                    S.op("dve", lambda e: e.tensor_tensor(out=X1v[:, nsl], in0=sc1b[:, nsl], in1=s1b[:, nsl], op=ALU.is_equal), [*t_scL, t_s16], [t_X1p[pc]])
                    for j8 in range(2):
                        ps, pt = psbank(6, 8)
                        psb = ps.bitcast(BF16)
                        for jj in range(8):
                            n = pc * 16 + j8 * 8 + jj
                            S.op("pe", lambda e: e.transpose(out=psb[:, jj * 128:(jj + 1) * 128], in_=X1[:, n * 128:(n + 1) * 128], identity=ident_bf[:]),
                                 [t_X1p[pc], t_const], [pt])
                        copy("act", Orm3[:, pc * 16 + j8 * 8:pc * 16 + (j8 + 1) * 8, :], psb.rearrange("p (n t) -> p n t", t=128), [pt], [t_Orm])
                for t4 in range(32):
                    ps, pt = psbank(6, 8)
                    for tt in range(4):
                        t = t4 * 4 + tt
                        mm(ps[:, tt * 128:(tt + 1) * 128], Trm3[:, :, t], Orm3[:, :, t], True, True, [t_Trm, t_Orm], [pt])
                    copy("act", X2g[:, :, t4 * 4:(t4 + 1) * 4].rearrange("p n t -> p t n"), ps.rearrange("p (t n) -> p t n", n=128), [pt], [*t_X2p])
                S.dma(gt_d[i].rearrange("p n t -> p (n t)"), X2[:], reads=[*t_X2p], writes=[t_gt[i]])


            def d_load(g2, n):
                p = n % 4
                S.dma(uTc[p].rearrange("p k e -> p (k e)"), uTv[:, n, :], reads=[t_uTb[k_ * 2 + n // 64] for k_ in range(8)], writes=[t_uTc[p]])
                S.dma(vcb[p], vbv[:, n, :], reads=[t_vb[n // 8]], writes=[t_vcb[p]])
                S.dma(gtc[p][:], gt_d[2 * g2:2 * g2 + 2, :, n, :].rearrange("s p t -> p s t"), reads=[t_gt[2 * g2], t_gt[2 * g2 + 1]], writes=[t_gtc[p]])

            def d_hpre(n):
                p, pp = n % 4, n % 2
                for dk in range(8):
                    mm(PS[4 + pp][:, 0:256], uTc[p][:, dk, :], x2Td[:, dk, :], dk == 0, dk == 7, [t_uTc[p], t_x2Td], [PT[4 + pp]])

            def d_gate(n):
                p, pp = n % 4, n % 2
                S.op("act", lambda e: e.activation(out=Hb[pp][:], in_=PS[4 + pp][:, 0:256], func=AF.Gelu_apprx_tanh), [PT[4 + pp]], [t_Hb[pp]])
                S.op("dve", lambda e: e.tensor_tensor(out=Hg[pp][:].rearrange("p (s t) -> p s t", s=2), in0=Hb[pp][:].rearrange("p (s t) -> p s t", s=2),
                                                      in1=gtc[p][:], op=ALU.mult), [t_Hb[pp], t_gtc[p]], [t_Hg[pp]])

            def d_out(n):
                p, pp = n % 4, n % 2
                for sub in range(2):
                    for half in range(2):
                        mm(PS[sub * 2 + half][:, :], Hg[pp][:, sub * 128:(sub + 1) * 128], vcb[p][:, half * 512:(half + 1) * 512],
                           n == 0, n == 127, [t_Hg[pp], t_vcb[p]], [PT[sub * 2 + half]])

            def _nfree(ap):
                n = 1
                for s_ in ap.shape[1:]:
                    n *= int(s_)
                return n

            def est_us(o):
                if o[0] != "op" or o[1] not in ("dve", "act"):
                    return 0.0
                name, a, k = o[2]
                outap = k.get("out", a[0] if a else None)
                n = _nfree(outap) if outap is not None else 64
                if o[1] == "act":
                    return 0.3 + n / 1400.0 * (2.5 if n >= 512 and outap.dtype == BF16 and k.get("func") is None and _nfree(k.get("in_")) == 512 else 1.0)
                if name in ("max", "match_replace", "tensor_reduce"):
                    src_ = k.get("in_", k.get("in_values"))
                    return 0.35 + _nfree(src_) / 960.0
                if name == "reciprocal":
                    return 0.35 + 8 * n / 960.0
                if name in ("tensor_tensor", "scalar_tensor_tensor"):
                    fast = all(k[x].dtype == BF16 for x in ("in0", "in1")) and outap.dtype == BF16
                    return 0.35 + (1.0 if fast else 2.0) * n / 960.0
                return 0.35 + n / 960.0

            gphase(0)
            gphase(1)
            npair = 1 if "peer1" in debug else NT // 2
            for g2 in range(npair):
                ops = []
                if g2 + 1 < npair:
                    S.rec = []
                    gphase(2 * g2 + 2)
                    gphase(2 * g2 + 3)
                    ops = S.rec
                    S.rec = None
                clk = []
                c_ = 0.0
                for o in ops:
                    clk.append(c_)
                    c_ += est_us(o)
                t_chunk = max(2.4, c_ * 0.6 / 126.0)
                pos = 0
                S.dma(x2Td[:], x2T_d[:, :, g2 * 256:(g2 + 1) * 256], reads=[t_x2T[2 * g2], t_x2T[2 * g2 + 1]], writes=[t_x2Td])
                for n in range(128):
                    d_load(g2, n)
                    d_hpre(n)
                    d_gate(n)
                    if n >= 1:
                        d_out(n - 1)
                    e_ = pos
                    while e_ < len(ops) and clk[e_] <= (n + 1) * t_chunk:
                        e_ += 1
                    S.replay(ops[pos:e_])
                    pos = e_
                d_out(127)
                S.replay(ops[pos:])
                for sub in range(2):
                    i = 2 * g2 + sub
                    S.dma(x2t, x2v[i], reads=[t_x2d[i]], writes=[t_fin])
                    for half in range(2):
                        dsl = slice(half * 512, (half + 1) * 512)
                        S.op("dve", lambda e: e.scalar_tensor_tensor(out=rso[:, dsl], in0=x2t[:, dsl], scalar=ALPHA, in1=PS[sub * 2 + half][:, :],
                                                                     op0=ALU.mult, op1=ALU.add), [t_fin, PT[sub * 2 + half]], [t_fin])
                    ln_tile(i % 2, rso, rso, t_fin, t_fin, xno, t_fin)
                    S.dma(outv[i], rso, reads=[t_fin], writes=[t_out[i]])

        S.finish(final_toks)
    return nc, hc, DBG


_CACHE = {}


def _prep_inputs(inputs):
    shared = {}
    for name, shp in IN_SPECS:
        if name in ("x", "peer_uT"):
            continue
        a = np.asarray(inputs[name], dtype=np.float32)
        shared[name] = np.ascontiguousarray(a.reshape(shp))
    shared["peer_uT"] = np.ascontiguousarray(np.asarray(inputs["peer_u"], dtype=np.float32).reshape(16384, D_).T)
    return shared


def kernel(**inputs):
    if "nc" not in _CACHE:
        _CACHE["nc"] = build()
    nc, hc, _ = _CACHE["nc"]
    shared = _prep_inputs(inputs)
    for k, v in hc.items():
        shared["c_" + k] = v
    x = np.asarray(inputs["x"], dtype=np.float32)
    in_maps = []
    for b in range(8):
        m = dict(shared)
        m["x"] = np.ascontiguousarray(x[b])
        in_maps.append(m)
    res = run_bass_kernel_spmd(nc, in_maps, core_ids=list(range(8)))
    return np.stack([np.asarray(r["out"], dtype=np.float32) for r in res.results], axis=0)
```

```python
import math
import numpy as np
import ml_dtypes
from contextlib import ExitStack
import concourse.bass as bass
import concourse.mybir as mybir
from concourse.alu_op_type import AluOpType as ALU
from concourse.mybir import ActivationFunctionType as AF
from concourse.bass_utils import run_bass_kernel_spmd

F32 = mybir.dt.float32
BF16 = mybir.dt.bfloat16
AX = mybir.AxisListType

S_ = 2048
D_ = 1024
NT = 16
D_IN = 5144
BIGNEG = -240000.0
LN_EPS = 1e-5
ALPHA = 2.0 ** 0.25
ATT_SCALE = 0.125


class Tok:
    __slots__ = ("w", "r", "name", "excl")

    def __init__(self, name="", excl=False):
        self.w = None
        self.r = {}
        self.name = name
        self.excl = excl


class _RecEng:
    def __init__(self):
        self.call = None

    def __getattr__(self, name):
        def f(*a, **k):
            self.call = (name, a, k)
            return self
        return f


class Sched:
    NDMA = 40

    def __init__(self, nc, es):
        self.nc = nc
        self.E = {"pe": nc.tensor, "act": nc.scalar, "dve": nc.vector,
                  "pool": nc.gpsimd, "sp": nc.sync}
        self.sem = {k: es.enter_context(nc.semaphore("s_" + k)) for k in self.E}
        self.cnt = {k: 0 for k in self.E}
        self.dsem = [es.enter_context(nc.semaphore("d%d" % i)) for i in range(self.NDMA)]
        self.dval = [0] * self.NDMA
        self.dnext = 0
        self.seen = {k: {} for k in self.E}
        self.ninst = 0
        self.es = es
        self.swsem = []
        self.nobar = set()
        self.rec = None

    def _semof(self, key):
        if isinstance(key, tuple):
            if key[0] == "w":
                return self.swsem[key[1]]
            return self.dsem[key[1]]
        return self.sem[key]

    def _wait(self, eng, key, val):
        if eng == "pe" and key == "pe":
            return
        if self.seen[eng].get(key, 0) >= val:
            return
        self.E[eng].wait_ge(self._semof(key), val)
        self.seen[eng][key] = val
        self.ninst += 1

    def _deps(self, eng, reads, writes):
        deps = {}
        for t in reads:
            if t.w is not None:
                k, c = t.w
                deps[k] = max(deps.get(k, 0), c)
        for t in writes:
            if t.w is not None:
                k, c = t.w
                deps[k] = max(deps.get(k, 0), c)
            for k, c in t.r.items():
                deps[k] = max(deps.get(k, 0), c)
        for k, c in deps.items():
            self._wait(eng, k, c)

    def replay(self, ops):
        for o in ops:
            if o[0] == "op":
                _, eng, (name, a, k), reads, writes = o
                self.op(eng, lambda e: getattr(e, name)(*a, **k), reads, writes)
            else:
                _, out, in_, reads, writes, q, kw = o
                self.dma(out, in_, reads, writes, q, **kw)

    def op(self, eng, fn, reads=(), writes=()):
        if self.rec is not None:
            pr = _RecEng()
            fn(pr)
            self.rec.append(("op", eng, pr.call, list(reads), list(writes)))
            return None
        ex = [t for t in reads if t.excl]
        if ex:
            reads = [t for t in reads if not t.excl]
            writes = list(writes) + ex
        self._deps(eng, reads, writes)
        inst = fn(self.E[eng])
        self.cnt[eng] += 1
        c = self.cnt[eng]
        inst.then_inc(self.sem[eng], 1)
        for t in reads:
            t.r[eng] = c
        for t in writes:
            t.w = (eng, c)
            t.r = {}
        self.ninst += 1
        return inst

    def dma(self, out, in_, reads=(), writes=(), q="sp", nobar=False, **kw):
        if self.rec is not None:
            self.rec.append(("dma", out, in_, list(reads), list(writes), q, kw))
            return None
        if q == "pool":
            if nobar:
                self.nobar.add(len(self.swsem))
            sem = self.es.enter_context(self.nc.semaphore("w%d" % len(self.swsem)))
            self.swsem.append(sem)
            key = ("w", len(self.swsem) - 1)
            if len(self.swsem) >= 2:
                self._wait(q, ("w", len(self.swsem) - 2), 16)
            self._deps(q, reads, writes)
            inst = self.E[q].dma_start(out=out, in_=in_, **kw)
            inst.then_inc(sem, 16)
            for t in reads:
                t.r[key] = 16
            for t in writes:
                t.w = (key, 16)
                t.r = {}
            self.ninst += 1
            return inst
        i = self.dnext
        self.dnext = (self.dnext + 1) % self.NDMA
        key = ("d", i)
        if self.dval[i] > 0:
            self._wait(q, key, self.dval[i])
        self._deps(q, reads, writes)
        inst = self.E[q].dma_start(out=out, in_=in_, **kw)
        self.dval[i] += 16
        inst.then_inc(self.dsem[i], 16)
        v = self.dval[i]
        for t in reads:
            t.r[key] = v
        for t in writes:
            t.w = (key, v)
            t.r = {}
        self.ninst += 1
        return inst

    def barrier(self):
        for eng in self.E:
            for key in self.E:
                if self.cnt[key] > 0 and not (eng == key == "pe"):
                    self._wait(eng, key, self.cnt[key])
            for i in range(self.NDMA):
                if self.dval[i] > 0:
                    self._wait(eng, ("d", i), self.dval[i])
            for i in range(len(self.swsem)):
                if i not in self.nobar:
                    self._wait(eng, ("w", i), 16)

    def finish(self, toks):
        for t in toks:
            if t.w is not None:
                self._wait("sp", t.w[0], t.w[1])


def _t5_bucket_np(dist):
    d = np.maximum(dist, 0)
    large = 16 + (np.log(np.maximum(d, 1).astype(np.float32) / 16) / math.log(128 / 16) * 16).astype(np.int32)
    large = np.minimum(large, 31)
    return np.where(d < 16, d, large)


FOFF = 2048
FLEN = 4352


def host_consts():
    c = {}
    c["ident_bf"] = np.eye(128, dtype=np.float32).astype(ml_dtypes.bfloat16)
    c["ident_f"] = np.eye(128, dtype=np.float32)
    c["J128"] = np.ascontiguousarray(np.eye(128, dtype=np.float32)[::-1])
    j127 = np.zeros((128, 128), np.float32)
    j127[:127, :127] = np.eye(127, dtype=np.float32)[::-1]
    c["J127"] = j127
    x = np.arange(FLEN)
    dist = x - FOFF
    oh = np.zeros((33, FLEN), np.float32)
    b = _t5_bucket_np(dist)
    valid = dist >= 0
    oh[b[valid], x[valid]] = 1.0
    oh[32, ~valid] = 1.0
    c["OHD"] = oh
    c0 = np.arange(127) * 16
    s0 = np.arange(32) * 64
    ov = np.clip(np.minimum(c0[:, None] + 32, s0[None, :] + 64) - np.maximum(c0[:, None], s0[None, :]), 0, None).astype(np.float32) / 32
    ovp = np.zeros((128, 32), np.float32)
    ovp[:127] = ov
    c["overlap"] = ovp
    t = np.arange(S_)
    cur = t // 64
    j = np.arange(32)
    forced = (j[None, :] == 0) | (j[None, :] == cur[:, None]) | (j[None, :] == cur[:, None] - 1)
    future = j[None, :] > cur[:, None]
    cs = np.where(future, -1e30, np.where(forced, 1.0e4, 0.0)).astype(np.float32)
    c["Csel"] = np.ascontiguousarray(cs.reshape(NT, 128, 32).transpose(1, 0, 2))
    ex = np.zeros((32, NT, 128), np.float32)
    for kt in range(NT):
        for s in range(128):
            ex[2 * kt + s // 64, kt, s] = 1.0
    c["Expand"] = ex.astype(ml_dtypes.bfloat16)
    s_ = np.arange(128)[:, None]
    t_ = np.arange(128)[None, :]
    c["TriLT"] = np.where(t_ < s_, 0.0, BIGNEG).astype(np.float32).astype(ml_dtypes.bfloat16)
    i_ = np.arange(128)[:, None]
    su = (i_ < t_).astype(np.float32)
    ui = (i_ <= t_).astype(np.float32)
    c["MaskAB"] = np.concatenate([su, ui], axis=1)
    c["MaskSL"] = (i_ > t_).astype(np.float32)
    return c


CONST_DT = {"ident_bf": BF16, "ident_f": F32, "J128": F32, "J127": F32, "OHD": F32, "overlap": F32,
            "Csel": F32, "Expand": BF16, "TriLT": BF16, "MaskAB": F32, "MaskSL": F32}

IN_SPECS = [
    ("x", [S_, D_]), ("ln_in_g", [D_]), ("ln_in_b", [D_]), ("rel_bias", [32, 8]), ("w_in", [D_, D_IN]),
    ("token_mu", [1792]), ("cmp_pos", [2, 32, 64]), ("cmp_w1", [2, 2048, 256]), ("cmp_b1", [2, 256]),
    ("cmp_w2", [2, 256, 64]), ("cmp_b2", [2, 64]), ("rwkv_w0", [512]), ("rwkv_w2", [64, 512]),
    ("rwkv_a0", [512]), ("rwkv_a2", [64, 512]), ("rwkv_g2", [128, 512]), ("rwkv_k_k", [512]),
    ("rwkv_k_a", [512]), ("rwkv_r_k", [512]), ("rwkv_lnx_g", [512]), ("rwkv_lnx_b", [512]),
    ("w_o_nsa", [512, D_]), ("w_o_rwkv", [512, D_]), ("w_out", [D_, D_]), ("ln_mix_g", [D_]), ("ln_mix_b", [D_]),
    ("peer_w_query", [D_, D_]), ("peer_sub_keys", [8, 2, 128, 64]), ("peer_uT", [D_, 16384]), ("peer_v", [16384, D_]),
    ("ln_ffn_g", [D_]), ("ln_ffn_b", [D_]),
]


def build(stage=99, debug=()):
    nc = bass.Bass("TRN2", target_bir_lowering=False)
    I = {}
    for name, shp in IN_SPECS:
        I[name] = nc.dram_tensor(name, shp, F32, kind="ExternalInput").ap()
    hc = host_consts()
    C = {}
    for name, arr in hc.items():
        C[name] = nc.dram_tensor("c_" + name, list(arr.shape), CONST_DT[name], kind="ExternalInput").ap()
    out_d = nc.dram_tensor("out", [S_, D_], F32, kind="ExternalOutput").ap()
    DBG = {}

    def dbg_out(name, shape, dt=F32):
        DBG[name] = nc.dram_tensor("dbg_" + name, shape, dt, kind="ExternalOutput").ap()
        return DBG[name]

    h_d = nc.dram_tensor("h_scr", [S_, D_], F32, kind="Internal").ap()
    F_d = nc.dram_tensor("F_scr", [8, FLEN], F32, kind="Internal").ap()

    with ExitStack() as es, nc.allow_non_contiguous_dma(reason="small param loads"):
        S = Sched(nc, es)

        def sb(name, shape, dt, stack=es):
            return stack.enter_context(nc.sbuf_tensor(name, shape, dt))

        PSALL = es.enter_context(nc.psum_tensor("psall", [128, 8, 512], F32))
        PS = [PSALL[:, i, :] for i in range(8)]
        PT = [Tok("ps%d" % i, excl=True) for i in range(8)]
        psrr = {}

        def psbank(lo=0, hi=8):
            i = psrr.get((lo, hi), lo)
            psrr[(lo, hi)] = lo + (i + 1 - lo) % (hi - lo)
            return PS[i], PT[i]

        evrr = [0]

        def evac_eng():
            evrr[0] ^= 1
            return "act" if evrr[0] else "dve"

        def copy(eng, out, in_, reads, writes):
            if eng == "act":
                return S.op("act", lambda e: e.copy(out=out, in_=in_), reads, writes)
            return S.op(eng, lambda e: e.tensor_copy(out=out, in_=in_), reads, writes)

        def mm(out, lhsT, rhs, start, stop, reads, writes):
            return S.op("pe", lambda e: e.matmul(out, lhsT=lhsT, rhs=rhs, start=start, stop=stop), reads, writes)

        final_toks = []
        dbg_tok = Tok("dbg")
        final_toks.append(dbg_tok)

        ident_bf = sb("ident_bf", [128, 128], BF16)
        ident_f = sb("ident_f", [128, 128], F32)
        t_const = Tok("const")
        S.dma(ident_bf[:], C["ident_bf"][:, :], writes=[t_const])
        S.dma(ident_f[:], C["ident_f"][:, :], writes=[t_const])

        uTb_d = nc.dram_tensor("uTb_scr", [128, 128, 8, 128], BF16, kind="Internal").ap()
        vb_d = nc.dram_tensor("vb_scr", [16384, D_], BF16, kind="Internal").ap()
        t_uTb = [Tok("uTb%d" % i) for i in range(16)]
        t_vb = [Tok("vb%d" % i) for i in range(16)]
        wq_d = nc.dram_tensor("wq_scr", [128, 8, D_], BF16, kind="Internal").ap()
        t_wqd = Tok("wqd")
        gb_bc = sb("gb_bc", [128, 2, D_], F32)
        t_gb = Tok("gb")

        def load_gb(g_ap, b_ap):
            S.dma(gb_bc[:, 0, :], g_ap.unsqueeze(0).to_broadcast([128, D_]), writes=[t_gb])
            S.dma(gb_bc[:, 1, :], b_ap.unsqueeze(0).to_broadcast([128, D_]), writes=[t_gb])

        lnst = sb("lnst", [128, 2, 2, 6], F32)
        lnmv = sb("lnmv", [128, 2, 4], F32)
        t_ln = [Tok("ln0"), Tok("ln1")]

        def ln_tile(par, src, dst, t_src, t_dst, xn_buf, t_xn, alpha_src=None):
            st = lnst[:, par]
            mv = lnmv[:, par]
            tl = t_ln[par]
            S.op("dve", lambda e: e.bn_stats(out=st[:, 0, :], in_=src[:, 0:512]), [t_src], [tl])
            S.op("dve", lambda e: e.bn_stats(out=st[:, 1, :], in_=src[:, 512:1024]), [t_src], [tl])
            S.op("dve", lambda e: e.bn_aggr(out=mv[:, 0:2], in_=st.rearrange("p a b -> p (a b)")), [tl], [tl])
            S.op("dve", lambda e: e.tensor_scalar(out=mv[:, 2:3], in0=mv[:, 1:2], scalar1=LN_EPS, scalar2=None, op0=ALU.add), [tl], [tl])
            S.op("act", lambda e: e.activation(out=mv[:, 2:3], in_=mv[:, 2:3], func=AF.Sqrt), [tl], [tl])
            S.op("dve", lambda e: e.reciprocal(out=mv[:, 2:3], in_=mv[:, 2:3]), [tl], [tl])
            S.op("dve", lambda e: e.tensor_scalar(out=mv[:, 3:4], in0=mv[:, 0:1], scalar1=mv[:, 2:3], scalar2=-1.0,
                                                  op0=ALU.mult, op1=ALU.mult), [tl], [tl])
            S.op("act", lambda e: e.activation(out=xn_buf, in_=src, func=AF.Identity, bias=mv[:, 3:4], scale=mv[:, 2:3]),
                 [t_src, tl], [t_xn])
            S.op("dve", lambda e: e.tensor_tensor(out=xn_buf, in0=xn_buf, in1=gb_bc[:, 0, :], op=ALU.mult), [t_xn, t_gb], [t_xn])
            S.op("pool", lambda e: e.tensor_tensor(out=dst, in0=xn_buf, in1=gb_bc[:, 1, :], op=ALU.add), [t_xn, t_gb], [t_dst])

        onT_d = nc.dram_tensor("onT_scr", [512, S_], BF16, kind="Internal").ap()
        yrw_d = nc.dram_tensor("yrw_scr", [512, S_], BF16, kind="Internal").ap()
        t_onT = [Tok("onT%d" % i) for i in range(NT)]
        t_yrw = [Tok("yrw%d" % i) for i in range(8)]
        rw_d = nc.dram_tensor("rw_scr", [1792, S_], F32, kind="Internal").ap()
        mg_d = nc.dram_tensor("mg_scr", [S_, 2048], F32, kind="Internal").ap()
        t_rwd = [Tok("rwd%d" % i) for i in range(28)]
        t_mgd = [Tok("mgd%d" % i) for i in range(NT)]
        wv = I["w_in"].rearrange("(k p) c -> p k c", p=128)
        with ExitStack() as sn:
            o_nsa = sb("o_nsa", [128, NT, 512], F32, sn)
            t_on = [Tok("on%d" % i) for i in range(NT)]
            qT = sb("qT", [64, 8, S_], BF16, sn)
            t_q = [[Tok() for _ in range(4)] for _ in range(8)]
            kT = sb("kT", [64, 4, S_], BF16, sn)
            t_k = [[Tok() for _ in range(4)] for _ in range(4)]
            Vt = sb("Vt", [128, NT, 4, 65], BF16, sn)
            t_V = [Tok() for _ in range(NT)]
            gate = sb("gate", [128, NT, 24], F32, sn)
            kcT = sb("kcT", [64, 2, 128], BF16, sn)
            t_kc = [Tok(), Tok()]
            VC = sb("VC", [128, 2, 98], F32, sn)
            t_VC = [Tok(), Tok()]
            S.op("pool", lambda e: e.memset(Vt[:, :, :, 64:65], 1.0), [], t_V)
            S.op("pool", lambda e: e.memset(VC[:], 0.0), [], t_VC)
            S.op("pool", lambda e: e.memset(VC[:, :, 64:65], 1.0), [], t_VC)
            S.dma(VC[:, 0, 65:97], C["overlap"][:, :], writes=[t_VC[0]])
            S.dma(VC[:, 1, 65:97], C["overlap"][:, :], writes=[t_VC[1]])

            with ExitStack() as s1:
                hT = sb("hT", [128, 8, S_], BF16, s1)
                t_hT = [Tok("hT%d" % i) for i in range(4)]
                t_hd = [Tok("hd%d" % i) for i in range(NT)]
                load_gb(I["ln_in_g"], I["ln_in_b"])
                xv = I["x"].rearrange("(n p) d -> n p d", p=128)
                hv = h_d.rearrange("(n p) d -> n p d", p=128)
                with ExitStack() as sa:
                    xbuf = [sb("xa%d" % i, [128, D_], F32, sa) for i in range(2)]
                    xnb = [sb("xn%d" % i, [128, D_], F32, sa) for i in range(2)]
                    hfb = [sb("hf%d" % i, [128, D_], F32, sa) for i in range(2)]
                    hbb = [sb("hb%d" % i, [128, D_], BF16, sa) for i in range(2)]
                    t_x = [Tok(), Tok()]
                    t_xn = [Tok(), Tok()]
                    t_hf = [Tok(), Tok()]
                    t_hb = [Tok(), Tok()]
                    for i in range(NT):
                        p = i % 2
                        S.dma(xbuf[p][:], xv[i], writes=[t_x[p]])
                        ln_tile(p, xbuf[p][:], hfb[p][:], t_x[p], t_hf[p], xnb[p][:], t_xn[p])
                        S.dma(hv[i], hfb[p][:], reads=[t_hf[p]], writes=[t_hd[i]])
                        S.op("act", lambda e: e.copy(out=hbb[p][:], in_=hfb[p][:]), [t_hf[p]], [t_hb[p]])
                        ps, pt = psbank()
                        psb = ps.bitcast(BF16)
                        for dk in range(8):
                            S.op("pe", lambda e: e.transpose(out=psb[:, dk * 128:(dk + 1) * 128], in_=hbb[p][:, dk * 128:(dk + 1) * 128],
                                                             identity=ident_bf[:]), [t_hb[p], t_const], [pt])
                        copy("dve", hT[:, :, i * 128:(i + 1) * 128], psb.rearrange("p (k t) -> p k t", k=8), [pt], [t_hT[i // 4]])
                S.barrier()
                Wn = sb("Wn", [128, 8, 1304], BF16, s1)
                t_Wn = [Tok(), Tok(), Tok()]
                for ci, c0 in enumerate(range(0, 1304, 512)):
                    c1 = min(1304, c0 + 512)
                    S.dma(Wn[:, :, c0:c1], wv[:, :, c0:c1], writes=[t_Wn[ci]], q="pool")
                s1c = ExitStack()
                kvA = sb("kvA", [64, 4, S_], BF16, s1c)
                kvB = sb("kvB", [64, 4, S_], BF16, s1c)
                t_kv = [[Tok() for _ in range(4)] for _ in range(4)]
                pos_sb = sb("pos_sb", [32, 2, 64], F32, s1c)
                posT = sb("posT", [64, 2, 32], F32, s1c)
                t_pos = Tok()
                S.dma(pos_sb[:], I["cmp_pos"].rearrange("k j d -> j k d"), writes=[t_pos])
                for kv in range(2):
                    ps, pt = psbank()
                    S.op("pe", lambda e: e.transpose(out=ps[0:64, 0:32], in_=pos_sb[:, kv, :], identity=ident_f[0:32, 0:32]),
                         [t_pos, t_const], [pt])
                    copy("dve", posT[:, kv, :], ps[0:64, 0:32], [pt], [t_pos])

                def proj_cm(col0, M, tc, evac):
                    ps, pt = psbank()
                    for dk in range(8):
                        mm(ps[0:M, :], Wn[:, dk, col0:col0 + M], hT[:, dk, tc * 512:(tc + 1) * 512], dk == 0, dk == 7,
                           t_Wn + [t_hT[tc]], [pt])
                    evac(ps[0:M, :], pt)

                for tc in range(4):
                    tsl = slice(tc * 512, (tc + 1) * 512)
                    for h in range(8):
                        proj_cm(h * 64, 64, tc, lambda ps, pt: copy(evac_eng(), qT[:, h, tsl], ps, [pt], [t_q[h][tc]]))
                    for b, base in ((0, 768), (1, 1024)):
                        for g in range(2):
                            proj_cm(base + g * 64, 64, tc,
                                    lambda ps, pt: copy(evac_eng(), kT[:, b * 2 + g, tsl], ps, [pt], [t_k[b * 2 + g][tc]]))
                    for kv, base in ((0, 512), (1, 640)):
                        for g in range(2):
                            idx = kv * 2 + g

                            def ev(ps, pt):
                                for dst, j0 in ((kvA, 0), (kvB, 16)):
                                    S.op("dve", lambda e: e.tensor_tensor(
                                        out=dst[:, idx, tsl].rearrange("p (b j) -> p b j", j=16),
                                        in0=ps.rearrange("p (b j) -> p b j", j=16),
                                        in1=posT[:, kv, j0:j0 + 16].unsqueeze(1).to_broadcast([64, 32, 16]), op=ALU.add),
                                        [pt, t_pos], [t_kv[idx][tc]])
                            proj_cm(base + g * 64, 64, tc, ev)
                for i in range(NT):
                    ps, pt = psbank()
                    for ri, (c0, n) in enumerate(((896, 128), (1152, 128), (1280, 24))):
                        for dk in range(8):
                            mm(ps[:, ri * 128:ri * 128 + n], hT[:, dk, i * 128:(i + 1) * 128], Wn[:, dk, c0:c0 + n], dk == 0, dk == 7,
                               t_Wn + [t_hT[i // 4]], [pt])
                    copy("dve", Vt[:, i, :, 0:64], ps[:, 0:256].rearrange("p (a d) -> p a d", d=64), [pt], [t_V[i]])
                    S.op("act", lambda e: e.activation(out=gate[:, i, :], in_=ps[:, 256:280], func=AF.Sigmoid), [pt], [t_V[i]])

                w2b = sb("w2b", [128, 2, 2, 64], BF16, s1c)
                b1T = sb("b1T", [128, 2, 2], F32, s1c)
                b2k = sb("b2k", [64, 1], F32, s1c)
                b2v = sb("b2v", [128, 64], F32, s1c)
                t_cw = Tok()
                for kv in range(2):
                    S.dma(w2b[:, kv], I["cmp_w2"][kv].rearrange("(c p) d -> p c d", p=128), writes=[t_cw], q="pool")
                    S.dma(b1T[:, kv, :], I["cmp_b1"][kv].rearrange("(c p) -> p c", p=128), writes=[t_cw])
                S.dma(b2k[:], I["cmp_b2"][0].unsqueeze(1), writes=[t_cw])
                S.dma(b2v[:], I["cmp_b2"][1].unsqueeze(0).to_broadcast([128, 64]), writes=[t_cw])
                w1s = sb("w1s", [64, 32, 256], BF16, s1c)
                w1b = [w1s, w1s]
                t_w1s = Tok()
                t_w1 = [t_w1s, t_w1s]
                hid = [sb("hid%d" % i, [128, 2, 128], BF16, s1c) for i in range(2)]
                t_hid = [Tok(), Tok()]
                for kv in range(2):
                    S.dma(w1s[:], I["cmp_w1"][kv].rearrange("(j d) h -> d j h", d=64), writes=[t_w1s], q="pool")
                    for g in range(2):
                        idx = kv * 2 + g
                        hb_ = hid[idx % 2]
                        th = t_hid[idx % 2]
                        for hcx in range(2):
                            ps, pt = psbank()
                            for j in range(32):
                                src = kvA if j < 16 else kvB
                                off = j if j < 16 else j
                                rhs = src[:, idx, off:off + 16 * 126 + 1:16]
                                mm(ps[:, 0:127], w1b[kv][:, j, hcx * 128:(hcx + 1) * 128], rhs, j == 0, j == 31,
                                   [t_w1[kv]] + t_kv[idx], [pt])
                            S.op("act", lambda e: e.activation(out=hb_[:, hcx, 0:127], in_=ps[:, 0:127], func=AF.Gelu_apprx_tanh,
                                                               bias=b1T[:, kv, hcx:hcx + 1], scale=1.0), [pt, t_cw], [th])
                        ps, pt = psbank()
                        if kv == 0:
                            for hcx in range(2):
                                mm(ps[0:64, 0:127], w2b[:, 0, hcx, :], hb_[:, hcx, 0:127], hcx == 0, hcx == 1, [t_cw, th], [pt])
                            S.op("act", lambda e: e.activation(out=kcT[:, g, 0:127], in_=ps[0:64, 0:127], func=AF.Identity,
                                                               bias=b2k[:, 0:1], scale=1.0), [pt, t_cw], [t_kc[g]])
                        else:
                            for hcx in range(2):
                                mm(ps[0:127, 0:64], hb_[:, hcx, 0:127], w2b[:, 1, hcx, :], hcx == 0, hcx == 1, [t_cw, th], [pt])
                            S.op("dve", lambda e: e.tensor_tensor(out=VC[0:127, g, 0:64], in0=ps[0:127, 0:64], in1=b2v[0:127, :],
                                                                  op=ALU.add), [pt, t_cw], [t_VC[g]])
                s1c.close()
                S.barrier()
                stg = [sb("stg%d" % i, [128, S_], F32, s1) for i in range(2)]
                t_stg = [Tok(), Tok()]
                t_Wr = Tok()
                for half in range(2):
                    for q2 in range(2):
                        S.dma(Wn[:, :, q2 * 448:(q2 + 1) * 448], wv[:, :, 1304 + half * 896 + q2 * 448:1304 + half * 896 + (q2 + 1) * 448],
                              reads=[], writes=t_Wn + [t_Wr], q="pool")
                    for cc in range(14):
                        c = half * 14 + cc
                        p = c % 2
                        for tc in range(4):
                            ps, pt = psbank()
                            for dk in range(8):
                                mm(ps[0:64, :], Wn[:, dk, cc * 64:(cc + 1) * 64], hT[:, dk, tc * 512:(tc + 1) * 512], dk == 0, dk == 7,
                                   [t_Wr, t_hT[tc]], [pt])
                            copy(evac_eng(), stg[p][0:64, tc * 512:(tc + 1) * 512], ps[0:64, :], [pt], [t_stg[p]])
                        S.dma(rw_d[c * 64:(c + 1) * 64, :], stg[p][0:64, :], reads=[t_stg[p]], writes=[t_rwd[c]])
                mgv = mg_d.rearrange("(n p) c -> n p c", p=128)
                for half in range(2):
                    for q4 in range(2):
                        S.dma(Wn[:, :, q4 * 512:(q4 + 1) * 512], wv[:, :, 3096 + half * 1024 + q4 * 512:3096 + half * 1024 + (q4 + 1) * 512],
                              reads=[], writes=t_Wn + [t_Wr], q="pool")
                    for i in range(NT):
                        p = i % 2
                        for q4 in range(2):
                            ps, pt = psbank()
                            for dk in range(8):
                                mm(ps[:, :], hT[:, dk, i * 128:(i + 1) * 128], Wn[:, dk, q4 * 512:(q4 + 1) * 512], dk == 0, dk == 7,
                                   [t_Wr, t_hT[i // 4]], [pt])
                            S.op("act", lambda e: e.activation(out=stg[p][:, q4 * 512:(q4 + 1) * 512], in_=ps[:, :], func=AF.Sigmoid),
                                 [pt], [t_stg[p]])
                        S.dma(mgv[i][:, half * 1024:(half + 1) * 1024], stg[p][:, 0:1024], reads=[t_stg[p]], writes=[t_mgd[i]])
            S.barrier()
            if "kv2" in debug:
                o = dbg_out("kT", [64, 4, S_], BF16)
                S.dma(o[:, :, :], kT[:], reads=[t for l in t_k for t in l], writes=[dbg_tok])
                o = dbg_out("Vt", [128, NT, 4, 65], BF16)
                S.dma(o[:, :, :, :], Vt[:], reads=t_V, writes=[dbg_tok])
                o = dbg_out("gate", [128, NT, 24])
                S.dma(o[:, :, :], gate[:], reads=t_V, writes=[dbg_tok])
            if "kc" in debug:
                o = dbg_out("kcT", [64, 2, 128], BF16)
                S.dma(o[:, :, :], kcT[:], reads=t_kc, writes=[dbg_tok])
                o = dbg_out("VC", [128, 2, 98])
                S.dma(o[:, :, :], VC[:], reads=t_VC, writes=[dbg_tok])
                o = dbg_out("qT", [64, 8, S_], BF16)
                S.dma(o[:, :, :], qT[:], reads=[t for l in t_q for t in l], writes=[dbg_tok])

            Theta = sb("Theta", [128, 8, 3, 128], BF16, sn)
            t_Th = [Tok() for _ in range(8)]
            TriLT = sb("TriLT", [128, 128], BF16, sn)
            Expand = sb("Expand", [32, NT, 128], BF16, sn)
            J128 = sb("J128", [128, 128], F32, sn)
            J127 = sb("J127", [128, 128], F32, sn)
            Csel = sb("Csel", [128, NT, 32], F32, sn)
            t_c2 = Tok("const2")
            S.dma(TriLT[:], C["TriLT"][:, :], writes=[t_c2])
            S.dma(Expand[:], C["Expand"][:, :, :], writes=[t_c2])
            S.dma(J128[:], C["J128"][:, :], writes=[t_c2])
            S.dma(J127[:], C["J127"][:, :], writes=[t_c2])
            S.dma(Csel[:], C["Csel"][:, :, :], writes=[t_c2])
            with ExitStack() as s2:
                OHD = sb("OHD", [33, FLEN], F32, s2)
                relbx = sb("relbx", [33, 8], F32, s2)
                F_sb = sb("F_sb", [8, FLEN], F32, s2)
                Hk = sb("Hk", [128, 8, 384], F32, s2)
                t_f = Tok()
                t_Fsb = Tok()
                t_Fd = Tok()
                t_Hk = Tok()
                S.dma(OHD[:], C["OHD"][:, :], writes=[t_f])
                S.op("pool", lambda e: e.memset(relbx[:], BIGNEG / 8.0), [], [t_f])
                S.dma(relbx[0:32, :], I["rel_bias"][:, :], writes=[t_f])
                for c0 in range(0, FLEN, 512):
                    n = min(512, FLEN - c0)
                    ps, pt = psbank()
                    mm(ps[0:8, 0:n], relbx[:, :], OHD[:, c0:c0 + n], True, True, [t_f], [pt])
                    S.op("act", lambda e: e.mul(F_sb[:, c0:c0 + n], ps[0:8, 0:n], 8.0), [pt], [t_Fsb])
                S.dma(F_d[:, :], F_sb[:], reads=[t_Fsb], writes=[t_Fd])
                hk_src = bass.AP(tensor=F_d.tensor, offset=1921, ap=[[1, 128], [FLEN, 8], [1, 384]])
                S.dma(Hk[:], hk_src, reads=[t_Fd], writes=[t_Hk])
                for h in range(8):
                    ps, pt = psbank()
                    mm(ps[:, 0:384], J128[:], Hk[:, h, :], True, True, [t_c2, t_Hk], [pt])
                    copy(evac_eng(), Theta[:, h, :, :], ps[:, 0:384].rearrange("p (a t) -> p a t", t=128), [pt], [t_Th[h]])
            S.barrier()
            if "theta" in debug:
                o = dbg_out("Theta", [128, 8, 3, 128], BF16)
                S.dma(o[:, :, :, :], Theta[:], reads=t_Th, writes=[dbg_tok])

            for k_ in range(8):
                for hf in range(2):
                    S.dma(uTb_d[hf * 64:(hf + 1) * 64, :, k_, :].rearrange("c p e -> p c e"),
                          I["peer_uT"][k_ * 128:(k_ + 1) * 128, hf * 8192:(hf + 1) * 8192].rearrange("p (c e) -> p c e", e=128),
                          writes=[t_uTb[k_ * 2 + hf]], q="pool", nobar=True)
            for k_ in range(16):
                S.dma(vb_d[k_ * 1024:(k_ + 1) * 1024, :], I["peer_v"][k_ * 1024:(k_ + 1) * 1024, :], writes=[t_vb[k_]], q="pool", nobar=True)
            S.dma(wq_d[:, :, :], I["peer_w_query"].rearrange("(k p) d -> p k d", p=128), writes=[t_wqd], q="pool", nobar=True)
            with ExitStack() as s3:
                imp = sb("imp", [128, NT, 2, 32], F32, s3)
                t_imp = [[Tok() for _ in range(2)] for _ in range(NT)]
                MnegT = sb("MnegT", [32, 2, S_], BF16, s3)
                t_Mn = [[Tok() for _ in range(NT)] for _ in range(2)]
                sm = [sb("sm%d" % i, [128, 4], F32, s3) for i in range(4)]
                t_sm = [Tok() for _ in range(4)]
                PTs = [sb("PTs%d" % i, [128, S_], BF16, s3) for i in range(2)]
                t_PTs = [Tok(), Tok()]
                PTw = [sb("PTw%d" % i, [128, 640], BF16, s3) for i in range(2)]
                t_PTw = [Tok(), Tok()]
                s3a = ExitStack()
                H16_0 = sb("H16_0", [128, S_], F32, s3a)
                H16 = [H16_0, H16_0]
                t_H16_0 = Tok()
                t_H16 = [t_H16_0, t_H16_0]
                Phi = [sb("Phi%d" % i, [128, S_], BF16, s3a) for i in range(2)]
                t_Phi = [Tok(), Tok()]
                PTc_0 = sb("PTc_0", [128, S_], F32, s3a)
                PTc = [PTc_0, PTc_0]
                t_PTc_0 = Tok()
                t_PTc = [t_PTc_0, t_PTc_0]
                smrr = [0]

                def evac_attn(ps, pt, i, h, br, first):
                    k = smrr[0]
                    smrr[0] = (k + 1) % 4
                    s_, ts = sm[k], t_sm[k]
                    S.op("dve", lambda e: e.tensor_scalar(out=s_[:, 0:1], in0=ps[:, 64:65], scalar1=1e-30, scalar2=None, op0=ALU.max),
                         [pt], [ts])
                    S.op("dve", lambda e: e.reciprocal(out=s_[:, 1:2], in_=s_[:, 0:1]), [ts], [ts])
                    S.op("dve", lambda e: e.tensor_tensor(out=s_[:, 2:3], in0=s_[:, 1:2], in1=gate[:, i, h * 3 + br:h * 3 + br + 1],
                                                          op=ALU.mult), [ts, t_V[i]], [ts])
                    dst = o_nsa[:, i, h * 64:(h + 1) * 64]
                    if first:
                        S.op("dve", lambda e: e.tensor_scalar(out=dst, in0=ps[:, 0:64], scalar1=s_[:, 2:3], scalar2=None, op0=ALU.mult),
                             [pt, ts], [t_on[i]])
                    else:
                        S.op("dve", lambda e: e.scalar_tensor_tensor(out=dst, in0=ps[:, 0:64], scalar=s_[:, 2:3], in1=dst,
                                                                     op0=ALU.mult, op1=ALU.add), [pt, ts], [t_on[i]])
                    return s_, ts

                for h in range(8):
                    g = h // 4
                    p = h % 2
                    src = bass.AP(tensor=F_d.tensor, offset=h * FLEN + 1, ap=[[16, 127], [1, S_]])
                    S.dma(H16[p][0:127, :], src, reads=[t_Fd], writes=[t_H16[p]])
                    for tc in range(4):
                        ps, pt = psbank()
                        mm(ps[0:127, :], J127[0:127, 0:127], H16[p][0:127, tc * 512:(tc + 1) * 512], True, True, [t_c2, t_H16[p]], [pt])
                        copy(evac_eng(), Phi[p][0:127, tc * 512:(tc + 1) * 512], ps[0:127, :], [pt], [t_Phi[p]])
                    for tc in range(4):
                        ps, pt = psbank()
                        mm(ps[0:127, :], kcT[:, g, 0:127], qT[:, h, tc * 512:(tc + 1) * 512], True, False, [t_kc[g], t_q[h][tc]], [pt])
                        mm(ps[0:127, :], ident_bf[0:127, 0:127], Phi[p][0:127, tc * 512:(tc + 1) * 512], False, True, [t_const, t_Phi[p]], [pt])
                        S.op("act", lambda e: e.activation(out=PTc[p][0:127, tc * 512:(tc + 1) * 512], in_=ps[0:127, :], func=AF.Exp,
                                                           scale=ATT_SCALE), [pt], [t_PTc[p]])
                    for i in range(NT):
                        ps, pt = psbank()
                        mm(ps[:, 0:98], PTc[p][0:127, i * 128:(i + 1) * 128], VC[0:127, g, :], True, True, [t_PTc[p], t_VC[g]], [pt])
                        s_, ts = evac_attn(ps, pt, i, h, 0, True)
                        dsti = imp[:, i, g, :]
                        if h % 4 == 0:
                            S.op("dve", lambda e: e.tensor_scalar(out=dsti, in0=ps[:, 65:97], scalar1=s_[:, 1:2], scalar2=None, op0=ALU.mult),
                                 [pt, ts], [t_imp[i][g]])
                        else:
                            S.op("dve", lambda e: e.scalar_tensor_tensor(out=dsti, in0=ps[:, 65:97], scalar=s_[:, 1:2], in1=dsti,
                                                                         op0=ALU.mult, op1=ALU.add), [pt, ts], [t_imp[i][g]])
                if "cmp" in debug:
                    o = dbg_out("o_cmp", [128, NT, 512])
                    S.dma(o[:, :, :], o_nsa[:], reads=t_on, writes=[dbg_tok])
                    o = dbg_out("imp", [128, NT, 2, 32])
                    S.dma(o[:, :, :, :], imp[:], reads=[t for l in t_imp for t in l], writes=[dbg_tok])
                scb = [sb("scb%d" % i, [128, 32], F32, s3a) for i in range(2)]
                cmb = [sb("cmb%d" % i, [128, 32, 32], F32, s3a) for i in range(2)]
                rkb = [sb("rkb%d" % i, [128, 32], F32, s3a) for i in range(2)]
                t_sel = [Tok(), Tok()]
                for i in range(NT):
                    for g in range(2):
                        p = (i * 2 + g) % 2
                        ts = t_sel[p]
                        S.op("dve", lambda e: e.tensor_tensor(out=scb[p][:], in0=imp[:, i, g, :], in1=Csel[:, i, :], op=ALU.add),
                             [t_imp[i][g], t_c2], [ts])
                        S.op("dve", lambda e: e.tensor_tensor(out=cmb[p][:], in0=scb[p][:].unsqueeze(1).to_broadcast([128, 32, 32]),
                                                              in1=scb[p][:].unsqueeze(2).to_broadcast([128, 32, 32]), op=ALU.is_gt), [ts], [ts])
                        S.op("dve", lambda e: e.tensor_reduce(out=rkb[p][:], in_=cmb[p][:], axis=AX.X, op=ALU.add), [ts], [ts])
                        S.op("dve", lambda e: e.tensor_scalar(out=rkb[p][:], in0=rkb[p][:], scalar1=16.0, scalar2=BIGNEG, op0=ALU.is_ge,
                                                              op1=ALU.mult), [ts], [ts])
                        ps, pt = psbank()
                        S.op("pe", lambda e: e.transpose(out=ps[0:32, 0:128], in_=rkb[p][:], identity=ident_f[:]), [ts, t_const], [pt])
                        copy("act", MnegT[:, g, i * 128:(i + 1) * 128], ps[0:32, 0:128], [pt], [t_Mn[g][i]])
                if "cmp" in debug:
                    o = dbg_out("MnegT", [32, 2, S_], BF16)
                    S.dma(o[:, :, :], MnegT[:], reads=[t for l in t_Mn for t in l], writes=[dbg_tok])
                if stage <= 2:
                    s3a.close()
                    S.finish(final_toks)
                    return nc, hc, DBG
                s3a.close()
                S.barrier()
                cnt = 0
                for h in range(8):
                    g = h // 4
                    for i in range(NT):
                        p = cnt % 2
                        cnt += 1
                        qsl = slice(i * 128, (i + 1) * 128)
                        nk = i + 1
                        kts = list(range(max(0, i - 4), i + 1))
                        for kt in range(nk):
                            b = kt // 4
                            out = PS[b][:, (kt % 4) * 128:(kt % 4 + 1) * 128]
                            pt = PT[b]
                            dl = i - kt
                            mm(out, ident_bf[:], Theta[:, h, min(dl, 2), :], True, False, [t_const, t_Th[h]], [pt])
                            mm(out, kT[:, g, kt * 128:(kt + 1) * 128], qT[:, h, qsl], False, False, [t_k[g][kt // 4], t_q[h][i // 4]], [pt])
                            mm(out, Expand[:, kt, :], MnegT[:, g, qsl], False, True, [t_c2, t_Mn[g][i]], [pt])
                        for n_, kt in enumerate(kts):
                            b = 4 + n_ // 4
                            out = PS[b][:, (n_ % 4) * 128:(n_ % 4 + 1) * 128]
                            pt = PT[b]
                            dl = i - kt
                            mm(out, ident_bf[:], Theta[:, h, min(dl, 2), :], True, False, [t_const, t_Th[h]], [pt])
                            if dl == 4:
                                mm(out, ident_bf[:], TriLT[:], False, False, [t_const, t_c2], [pt])
                            mm(out, kT[:, 2 + g, kt * 128:(kt + 1) * 128], qT[:, h, qsl], False, True, [t_k[2 + g][kt // 4], t_q[h][i // 4]], [pt])
                        for b in range((nk + 3) // 4):
                            n = min(4, nk - b * 4) * 128
                            S.op("act", lambda e: e.activation(out=PTs[p][:, b * 512:b * 512 + n], in_=PS[b][:, 0:n], func=AF.Exp,
                                                               scale=ATT_SCALE), [PT[b]], [t_PTs[p]])
                        for b in range((len(kts) + 3) // 4):
                            n = min(4, len(kts) - b * 4) * 128
                            S.op("act", lambda e: e.activation(out=PTw[p][:, b * 512:b * 512 + n], in_=PS[4 + b][:, 0:n], func=AF.Exp,
                                                               scale=ATT_SCALE), [PT[4 + b]], [t_PTw[p]])
                        ps, pt = psbank(6, 8)
                        for kt in range(nk):
                            mm(ps[:, 0:65], PTs[p][:, kt * 128:(kt + 1) * 128], Vt[:, kt, g, :], kt == 0, kt == nk - 1,
                               [t_PTs[p], t_V[kt]], [pt])
                        evac_attn(ps, pt, i, h, 1, False)
                        ps, pt = psbank(6, 8)
                        for n_, kt in enumerate(kts):
                            mm(ps[:, 0:65], PTw[p][:, n_ * 128:(n_ + 1) * 128], Vt[:, kt, 2 + g, :], n_ == 0, n_ == len(kts) - 1,
                               [t_PTw[p], t_V[kt]], [pt])
                        evac_attn(ps, pt, i, h, 2, False)
                onv = onT_d.rearrange("(cc p) t -> p cc t", p=128)
                stT = [sb("stT%d" % i, [128, 4, 128], BF16, s3) for i in range(2)]
                t_stT = [Tok(), Tok()]
                for i in range(NT):
                    p = i % 2
                    ps, pt = psbank(6, 8)
                    for cc in range(4):
                        S.op("pe", lambda e: e.transpose(out=ps[:, cc * 128:(cc + 1) * 128], in_=o_nsa[:, i, cc * 128:(cc + 1) * 128],
                                                         identity=ident_f[:]), [t_on[i], t_const], [pt])
                    copy(evac_eng(), stT[p][:], ps.rearrange("p (c t) -> p c t", c=4), [pt], [t_stT[p]])
                    S.dma(onv[:, :, i * 128:(i + 1) * 128], stT[p][:], reads=[t_stT[p]], writes=[t_onT[i]])
            S.barrier()
            if "theta2" in debug:
                o = dbg_out("Theta2", [128, 8, 3, 128], BF16)
                S.dma(o[:, :, :, :], Theta[:], reads=t_Th, writes=[dbg_tok])
        S.barrier()
        if "nsa" in debug:
            o = dbg_out("o_nsa", [128, NT, 512])
            S.dma(o[:, :, :], o_nsa[:], reads=t_on, writes=[dbg_tok])
        if stage <= 3:
            S.finish(final_toks)
            return nc, hc, DBG

        T1 = S_ + 1
        with ExitStack() as sr:
            MaskAB = sb("MaskAB", [128, 256], F32, sr)
            MaskSL = sb("MaskSL", [128, 128], F32, sr)
            ones64 = sb("ones64", [64, 64], F32, sr)
            onesm = sb("onesm", [64, 64], F32, sr)
            muT = sb("muT", [64, 28], F32, sr)
            mu128 = sb("mu128", [128, 1], F32, sr)
            par = sb("par", [64, 7, 8], F32, sr)
            parn = sb("parn", [64, 8], F32, sr)
            w2f = sb("w2f", [64, 512], F32, sr)
            a2f = sb("a2f", [64, 512], F32, sr)
            g2f = sb("g2f", [128, 512], F32, sr)
            t_rc = Tok("rconst")
            S.dma(MaskAB[:], C["MaskAB"][:, :], writes=[t_rc])
            S.dma(MaskSL[:], C["MaskSL"][:, :], writes=[t_rc])
            S.dma(muT[:], I["token_mu"].rearrange("(c p) -> p c", p=64), writes=[t_rc])
            S.dma(mu128[:], I["token_mu"][1664:1792].unsqueeze(1), writes=[t_rc])
            for k_, nm in enumerate(["rwkv_w0", "rwkv_a0", "rwkv_k_k", "rwkv_k_a", "rwkv_r_k", "rwkv_lnx_g", "rwkv_lnx_b"]):
                S.dma(par[:, k_, :], I[nm].rearrange("(h p) -> p h", p=64), writes=[t_rc])
            S.dma(w2f[:], I["rwkv_w2"][:, :], writes=[t_rc])
            S.dma(a2f[:], I["rwkv_a2"][:, :], writes=[t_rc])
            S.dma(g2f[:], I["rwkv_g2"][:, :], writes=[t_rc])
            S.op("pool", lambda e: e.memset(ones64[:], 1.0), [], [t_rc])
            S.op("pool", lambda e: e.memset(onesm[:], 1.0 / 64.0), [], [t_rc])
            S.op("dve", lambda e: e.tensor_scalar(out=parn[:], in0=par[:, 0, :], scalar1=-1.0, scalar2=None, op0=ALU.mult), [t_rc], [t_rc])

            raw = sb("raw", [128, T1], F32, sr)
            Db = sb("Db", [128, S_], F32, sr)
            t_raw = Tok("raw")
            t_D = Tok("D")
            S.op("pool", lambda e: e.memset(raw[:, 0:1], 0.0), [], [t_raw])

            def shift(P, mu_ap, out_ap, t_out):
                S.op("dve", lambda e: e.tensor_tensor(out=Db[0:P, :], in0=raw[0:P, 0:S_], in1=raw[0:P, 1:T1], op=ALU.subtract),
                     [t_raw], [t_D])
                S.op("dve", lambda e: e.scalar_tensor_tensor(out=out_ap, in0=Db[0:P, :], scalar=mu_ap, in1=raw[0:P, 1:T1],
                                                             op0=ALU.mult, op1=ALU.add), [t_D, t_raw, t_rc], [t_out])

            TW = sb("TW", [64, S_], F32, sr)
            AL = sb("AL", [64, S_], F32, sr)
            GL = sb("GL", [128, S_], F32, sr)
            t_lora = Tok("lora")
            S.dma(raw[0:64, 1:T1], rw_d[1536:1600, :], reads=[t_rwd[24]], writes=[t_raw])
            shift(64, muT[:, 24:25], TW[:], t_lora)
            S.op("act", lambda e: e.activation(out=TW[:], in_=TW[:], func=AF.Tanh), [t_lora], [t_lora])
            S.dma(raw[0:64, 1:T1], rw_d[1600:1664, :], reads=[t_rwd[25]], writes=[t_raw])
            shift(64, muT[:, 25:26], AL[:], t_lora)
            S.dma(raw[:, 1:T1], rw_d[1664:1792, :], reads=[t_rwd[26], t_rwd[27]], writes=[t_raw])
            shift(128, mu128[:, 0:1], GL[:], t_lora)
            S.op("act", lambda e: e.activation(out=GL[:], in_=GL[:], func=AF.Sigmoid), [t_lora], [t_lora])

            Rb = sb("Rb", [64, S_], F32, sr)
            Kb = sb("Kb", [64, S_], F32, sr)
            Vb = sb("Vb", [64, S_], F32, sr)
            Ab = sb("Ab", [64, S_], F32, sr)
            Eb = sb("Eb", [64, S_], F32, sr)
            Lb = sb("Lb", [64, S_], F32, sr)
            AR = sb("AR", [64, NT, 256], F32, sr)
            BT = sb("BT", [64, S_], F32, sr)
            KT = sb("KT", [64, S_], F32, sr)
            tok = sb("tok", [128, NT, 192], F32, sr)
            WCb = sb("WCb", [64, NT], F32, sr)
            Rk = sb("Rk", [64, 64], F32, sr)
            ybf = sb("ybf", [64, S_], BF16, sr)
            ST = [sb("ST%d" % i, [64, 64], F32, sr) for i in range(2)]
            NSET = 4
            AB1 = [sb("AB1_%d" % i, [128, 256], F32, sr) for i in range(NSET)]
            AB2 = [sb("AB2_%d" % i, [128, 256], F32, sr) for i in range(NSET)]
            Pp = [[sb("Pp%d_%d" % (i, j), [128, 128], BF16, sr) for j in range(2)] for i in range(NSET)]
            PTp = [[sb("PTp%d_%d" % (i, j), [128, 128], BF16, sr) for j in range(2)] for i in range(NSET)]
            Np = [[sb("Np%d_%d" % (i, j), [128, 128], BF16, sr) for j in range(2)] for i in range(NSET)]
            XTs = sb("XTs", [128, 64], BF16, sr)
            UTs = sb("UTs", [128, 64], F32, sr)
            t_R, t_K, t_V, t_A, t_E, t_L, t_AR, t_BT, t_KT, t_WC, t_Rk, t_ybf, t_XT, t_UT = [Tok() for _ in range(14)]
            t_tok = [Tok() for _ in range(NT)]
            t_ST = [Tok(), Tok()]
            t_set = [[Tok() for _ in range(6)] for _ in range(NSET)]
            v3 = lambda ap: ap.rearrange("p (n t) -> p n t", t=128)
            onesb = ones64[:, 0:1].to_broadcast([64, S_])

            for h in range(8):
                hs = slice(h * 64, (h + 1) * 64)
                for (row0, ci, dst, td) in ((h * 64, h, Rb, t_R), (512 + h * 64, 8 + h, Kb, t_K), (1024 + h * 64, 16 + h, Vb, t_V)):
                    S.dma(raw[0:64, 1:T1], rw_d[row0:row0 + 64, :], reads=[t_rwd[ci]], writes=[t_raw])
                    shift(64, muT[:, ci:ci + 1], dst[:], td)
                for tc in range(4):
                    ps, pt = psbank()
                    mm(ps[0:64, :], w2f[:, hs], TW[:, tc * 512:(tc + 1) * 512], True, True, [t_rc, t_lora], [pt])
                    S.op("act", lambda e: e.activation(out=Db[0:64, tc * 512:(tc + 1) * 512], in_=ps[0:64, :], func=AF.Exp,
                                                       bias=parn[:, h:h + 1], scale=-1.0), [pt, t_rc], [t_D])
                S.op("act", lambda e: e.activation(out=Db[0:64, :], in_=Db[0:64, :], func=AF.Ln, bias=1.0, scale=1.0), [t_D], [t_D])
                S.op("act", lambda e: e.activation(out=Db[0:64, :], in_=Db[0:64, :], func=AF.Exp, bias=-0.5, scale=-1.0), [t_D], [t_D])
                S.op("dve", lambda e: e.tensor_tensor_scan(out=raw[0:64, 1:T1], data0=onesb, data1=Db[0:64, :], initial=0.0,
                                                           op0=ALU.mult, op1=ALU.subtract), [t_D, t_rc], [t_raw])
                S.op("dve", lambda e: e.tensor_tensor(out=v3(Lb[:]), in0=v3(raw[0:64, 1:T1]),
                                                      in1=raw[0:64, 0:S_:128].unsqueeze(2).to_broadcast([64, NT, 128]), op=ALU.subtract),
                     [t_raw], [t_L])
                for tc in range(4):
                    ps, pt = psbank()
                    mm(ps[0:64, :], a2f[:, hs], AL[:, tc * 512:(tc + 1) * 512], True, True, [t_rc, t_lora], [pt])
                    S.op("act", lambda e: e.activation(out=Ab[:, tc * 512:(tc + 1) * 512], in_=ps[0:64, :], func=AF.Sigmoid,
                                                       bias=par[:, 1, h:h + 1], scale=1.0), [pt, t_rc], [t_A])
                S.op("dve", lambda e: e.tensor_scalar(out=Eb[:], in0=Kb[:], scalar1=par[:, 2, h:h + 1], scalar2=None, op0=ALU.mult),
                     [t_K, t_rc], [t_E])
                S.op("dve", lambda e: e.tensor_tensor(out=BT[:], in0=Eb[:], in1=Eb[:], op=ALU.mult), [t_E], [t_BT])
                for tc in range(4):
                    ps, pt = psbank()
                    mm(ps[0:64, :], ones64[:], BT[:, tc * 512:(tc + 1) * 512], True, True, [t_rc, t_BT], [pt])
                    S.op("act", lambda e: e.activation(out=KT[:, tc * 512:(tc + 1) * 512], in_=ps[0:64, :], func=AF.Sqrt), [pt], [t_KT])
                S.op("dve", lambda e: e.tensor_scalar(out=KT[:], in0=KT[:], scalar1=1e-12, scalar2=None, op0=ALU.max), [t_KT], [t_KT])
                S.op("dve", lambda e: e.reciprocal(out=KT[:], in_=KT[:]), [t_KT], [t_KT])
                S.op("dve", lambda e: e.tensor_tensor(out=Eb[:], in0=Eb[:], in1=KT[:], op=ALU.mult), [t_E, t_KT], [t_E])
                S.op("dve", lambda e: e.tensor_scalar(out=BT[:], in0=Ab[:], scalar1=-1.0, scalar2=par[:, 3, h:h + 1], op0=ALU.add, op1=ALU.mult),
                     [t_A, t_rc], [t_BT])
                S.op("dve", lambda e: e.scalar_tensor_tensor(out=Kb[:], in0=BT[:], scalar=1.0, in1=Kb[:], op0=ALU.add, op1=ALU.mult),
                     [t_BT, t_K], [t_K])
                S.op("dve", lambda e: e.tensor_tensor(out=Db[0:64, :], in0=Db[0:64, :], in1=Lb[:], op=ALU.add), [t_D, t_L], [t_D])
                S.op("act", lambda e: e.activation(out=Db[0:64, :], in_=Db[0:64, :], func=AF.Exp), [t_D], [t_D])
                S.op("dve", lambda e: e.scalar_tensor_tensor(out=AR[:, :, 0:128], in0=v3(Eb[:]), scalar=-1.0, in1=v3(Db[0:64, :]),
                                                             op0=ALU.mult, op1=ALU.mult), [t_E, t_D], [t_AR])
                S.op("act", lambda e: e.activation(out=raw[0:64, 1:T1], in_=Lb[:], func=AF.Exp), [t_L], [t_raw])
                S.op("dve", lambda e: e.tensor_tensor(out=AR[:, :, 128:256], in0=v3(Rb[:]), in1=v3(raw[0:64, 1:T1]), op=ALU.mult),
                     [t_R, t_raw], [t_AR])
                S.op("dve", lambda e: e.tensor_copy(out=WCb[:], in_=raw[0:64, 128:T1:128]), [t_raw], [t_WC])
                S.op("act", lambda e: e.activation(out=Db[0:64, :], in_=Lb[:], func=AF.Exp, scale=-1.0), [t_L, t_AR], [t_D])
                S.op("dve", lambda e: e.tensor_tensor(out=BT[:], in0=Eb[:], in1=Ab[:], op=ALU.mult), [t_E, t_A], [t_BT])
                S.op("dve", lambda e: e.tensor_tensor(out=BT[:], in0=BT[:], in1=Db[0:64, :], op=ALU.mult), [t_BT, t_D], [t_BT])
                S.op("dve", lambda e: e.tensor_tensor(out=KT[:], in0=Kb[:], in1=Db[0:64, :], op=ALU.mult), [t_K, t_D], [t_KT])
                wcb = WCb[:].unsqueeze(2).to_broadcast([64, NT, 128])
                S.op("dve", lambda e: e.tensor_tensor(out=v3(raw[0:64, 1:T1]), in0=v3(BT[:]), in1=wcb, op=ALU.mult), [t_BT, t_WC], [t_raw])
                S.op("dve", lambda e: e.tensor_tensor(out=v3(Db[0:64, :]), in0=v3(KT[:]), in1=wcb, op=ALU.mult), [t_KT, t_WC], [t_D])
                for n in range(NT):
                    ps, pt = psbank()
                    tsl = slice(n * 128, (n + 1) * 128)
                    S.op("pe", lambda e: e.transpose(out=ps[:, 0:64], in_=Vb[:, tsl], identity=ident_f[0:64, 0:64]), [t_V, t_const], [pt])
                    S.op("pe", lambda e: e.transpose(out=ps[:, 64:128], in_=raw[0:64, 1 + n * 128:1 + (n + 1) * 128],
                                                     identity=ident_f[0:64, 0:64]), [t_raw, t_const], [pt])
                    S.op("pe", lambda e: e.transpose(out=ps[:, 128:192], in_=Db[0:64, tsl], identity=ident_f[0:64, 0:64]), [t_D, t_const], [pt])
                    copy(evac_eng(), tok[:, n, :], ps[:, 0:192], [pt], [t_tok[n]])
                S.op("dve", lambda e: e.tensor_copy(out=Rk[:], in_=par[:, 4, h:h + 1].to_broadcast([64, 64])), [t_rc], [t_Rk])
                S.op("pool", lambda e: e.memset(ST[0][:], 0.0), [], [t_ST[0]])

                def precompute(n, s):
                    tsl = slice(n * 128, (n + 1) * 128)
                    tk = t_set[s]
                    ps, pt = psbank()
                    mm(ps[:, 0:256], BT[:, tsl], AR[:, n, :], True, True, [t_BT, t_AR], [pt])
                    S.op("dve", lambda e: e.tensor_tensor(out=AB1[s][:], in0=ps[:, 0:256], in1=MaskAB[:], op=ALU.mult), [pt, t_rc], [tk[0]])
                    yield
                    ps, pt = psbank()
                    mm(ps[:, 0:256], KT[:, tsl], AR[:, n, :], True, True, [t_KT, t_AR], [pt])
                    S.op("dve", lambda e: e.tensor_tensor(out=AB2[s][:], in0=ps[:, 0:256], in1=MaskAB[:], op=ALU.mult), [pt, t_rc], [tk[1]])
                    yield
                    ps, pt = psbank()
                    mm(ps[:, 0:128], AR[:, n, 0:128], BT[:, tsl], True, True, [t_BT, t_AR], [pt])
                    S.op("dve", lambda e: e.tensor_tensor(out=PTp[s][0][:], in0=ps[:, 0:128], in1=MaskSL[:], op=ALU.mult), [pt, t_rc], [tk[3]])
                    yield
                    S.op("pool", lambda e: e.tensor_copy(out=Pp[s][0][:], in_=AB1[s][:, 0:128]), [tk[0]], [tk[2]])
                    S.op("pool", lambda e: e.tensor_tensor(out=Np[s][0][:], in0=AB1[s][:, 0:128], in1=ident_f[:], op=ALU.add), [tk[0], t_const], [tk[4]])
                    yield
                    cur = 0
                    for j in range(1, 7):
                        nxt = 1 - cur
                        if j < 6:
                            ps, pt = psbank()
                            mm(ps[:, 0:128], PTp[s][cur][:], Pp[s][cur][:], True, True, [tk[2], tk[3]], [pt])
                            ps2, pt2 = psbank()
                            mm(ps2[:, 0:128], Pp[s][cur][:], PTp[s][cur][:], True, True, [tk[2], tk[3]], [pt2])
                            copy("act", Pp[s][nxt][:], ps[:, 0:128], [pt], [tk[2]])
                            copy("dve", PTp[s][nxt][:], ps2[:, 0:128], [pt2], [tk[3]])
                        else:
                            ps2, pt2 = psbank()
                            mm(ps2[:, 0:128], Pp[s][cur][:], PTp[s][cur][:], True, True, [tk[2], tk[3]], [pt2])
                            copy("dve", PTp[s][nxt][:], ps2[:, 0:128], [pt2], [tk[3]])
                        yield
                        ps, pt = psbank()
                        mm(ps[:, 0:128], PTp[s][nxt][:], Np[s][cur][:], True, True, [tk[3], tk[4]], [pt])
                        S.op("dve", lambda e: e.tensor_tensor(out=Np[s][nxt][:], in0=ps[:, 0:128], in1=Np[s][cur][:], op=ALU.add), [pt, tk[4]], [tk[4]])
                        cur = nxt
                        yield
                    assert cur == 0

                for n0 in range(0, NT, NSET):
                    gens = [precompute(n0 + s, s) for s in range(NSET)]
                    alive = list(gens)
                    while alive:
                        for g_ in list(alive):
                            try:
                                next(g_)
                            except StopIteration:
                                alive.remove(g_)
                    for s in range(NSET):
                        n = n0 + s
                        tsl = slice(n * 128, (n + 1) * 128)
                        tk = t_set[s]
                        sc, sn_ = ST[n % 2], ST[(n + 1) % 2]
                        tsc, tsn = t_ST[n % 2], t_ST[(n + 1) % 2]
                        Nf = Np[s][0]
                        ps, pt = psbank()
                        mm(ps[:, 0:64], AR[:, n, 0:128], sc[:], True, False, [t_AR, tsc], [pt])
                        mm(ps[:, 0:64], AB2[s][:, 0:128], tok[:, n, 0:64], False, True, [tk[1], t_tok[n]], [pt])
                        copy("act", XTs[:], ps[:, 0:64], [pt], [t_XT])
                        ps, pt = psbank()
                        mm(ps[:, 0:64], Nf[:], XTs[:], True, True, [tk[4], t_XT], [pt])
                        copy("act", UTs[:], ps[:, 0:64], [pt], [t_UT])
                        ps, pt = psbank()
                        mm(ps[0:64, 0:128], sc[:], AR[:, n, 128:256], True, False, [tsc, t_AR], [pt])
                        mm(ps[0:64, 0:128], UTs[:], AB1[s][:, 128:256], False, False, [t_UT, tk[0]], [pt])
                        mm(ps[0:64, 0:128], tok[:, n, 0:64], AB2[s][:, 128:256], False, True, [t_tok[n], tk[1]], [pt])
                        copy("dve", Lb[:, tsl], ps[0:64, 0:128], [pt], [t_L])
                        ps, pt = psbank()
                        mm(ps[0:64, 0:64], tok[:, n, 64:128], UTs[:], True, False, [t_tok[n], t_UT], [pt])
                        mm(ps[0:64, 0:64], tok[:, n, 128:192], tok[:, n, 0:64], False, True, [t_tok[n]], [pt])
                        S.op("dve", lambda e: e.scalar_tensor_tensor(out=sn_[:], in0=sc[:], scalar=WCb[:, n:n + 1], in1=ps[0:64, 0:64],
                                                                     op0=ALU.mult, op1=ALU.add), [tsc, t_WC, pt], [tsn])
                for tc in range(4):
                    csl = slice(tc * 512, (tc + 1) * 512)
                    ps, pt = psbank()
                    mm(ps[0:64, :], onesm[:], Lb[:, csl], True, True, [t_rc, t_L], [pt])
                    S.op("dve", lambda e: e.tensor_tensor(out=Ab[:, csl], in0=Lb[:, csl], in1=ps[0:64, :], op=ALU.subtract), [t_L, pt], [t_A])
                S.op("pool", lambda e: e.tensor_tensor(out=BT[:], in0=Ab[:], in1=Ab[:], op=ALU.mult), [t_A], [t_BT])
                for tc in range(4):
                    csl = slice(tc * 512, (tc + 1) * 512)
                    ps, pt = psbank()
                    mm(ps[0:64, :], onesm[:], BT[:, csl], True, True, [t_rc, t_BT], [pt])
                    S.op("dve", lambda e: e.tensor_scalar(out=KT[:, csl], in0=ps[0:64, :], scalar1=64e-5, scalar2=None, op0=ALU.add), [pt], [t_KT])
                S.op("act", lambda e: e.activation(out=KT[:], in_=KT[:], func=AF.Sqrt), [t_KT], [t_KT])
                S.op("dve", lambda e: e.reciprocal(out=KT[:], in_=KT[:]), [t_KT], [t_KT])
                S.op("dve", lambda e: e.tensor_tensor(out=Ab[:], in0=Ab[:], in1=KT[:], op=ALU.mult), [t_A, t_KT], [t_A])
                S.op("dve", lambda e: e.tensor_scalar(out=Ab[:], in0=Ab[:], scalar1=par[:, 5, h:h + 1], scalar2=par[:, 6, h:h + 1],
                                                      op0=ALU.mult, op1=ALU.add), [t_A, t_rc], [t_A])
                S.op("pool", lambda e: e.tensor_tensor(out=BT[:], in0=Rb[:], in1=Kb[:], op=ALU.mult), [t_R, t_K], [t_BT])
                for tc in range(4):
                    csl = slice(tc * 512, (tc + 1) * 512)
                    ps, pt = psbank()
                    mm(ps[0:64, :], Rk[:], BT[:, csl], True, True, [t_Rk, t_BT], [pt])
                    S.op("dve", lambda e: e.tensor_tensor(out=KT[:, csl], in0=ps[0:64, :], in1=Vb[:, csl], op=ALU.mult), [pt, t_V], [t_KT])
                S.op("dve", lambda e: e.tensor_tensor(out=Ab[:], in0=Ab[:], in1=KT[:], op=ALU.add), [t_A, t_KT], [t_A])
                for tc in range(4):
                    csl = slice(tc * 512, (tc + 1) * 512)
                    ps, pt = psbank()
                    mm(ps[0:64, :], g2f[:, hs], GL[:, csl], True, True, [t_rc, t_lora], [pt])
                    S.op("dve", lambda e: e.tensor_tensor(out=ybf[:, csl], in0=ps[0:64, :], in1=Ab[:, csl], op=ALU.mult), [pt, t_A], [t_ybf])
                S.dma(yrw_d[h * 64:(h + 1) * 64, :], ybf[:], reads=[t_ybf], writes=[t_yrw[h]])
                if "rwkv1" in debug and h == 0:
                    break
        S.barrier()
        if "rwkv" in debug or "rwkv1" in debug:
            o = dbg_out("yrw", [512, S_], BF16)
            ytmp = sb("ytmp", [128, 4, S_], BF16)
            tt_ = Tok()
            S.dma(ytmp[:], yrw_d.rearrange("(c p) t -> p c t", p=128), reads=t_yrw, writes=[tt_])
            S.dma(o.rearrange("(c p) t -> p c t", p=128), ytmp[:], reads=[tt_], writes=[dbg_tok])
        if stage <= 4:
            S.finish(final_toks)
            return nc, hc, DBG

        x2_d = nc.dram_tensor("x2_scr", [S_, D_], F32, kind="Internal").ap()
        t_x2d = [Tok("x2d%d" % i) for i in range(NT)]
        x2T_d = nc.dram_tensor("x2T_scr", [128, 8, S_], BF16, kind="Internal").ap()
        t_x2T = [Tok("x2T%d" % i) for i in range(NT)]
        x2v = x2_d.rearrange("(n p) d -> n p d", p=128)
        load_gb(I["ln_mix_g"], I["ln_mix_b"])
        with ExitStack() as sm_:
            wn_b = sb("wn_b", [128, 4, D_], BF16, sm_)
            wr_b = sb("wr_b", [128, 4, D_], BF16, sm_)
            wo_b = sb("wo_b", [128, 8, D_], BF16, sm_)
            ynT = sb("ynT", [128, 4, S_], BF16, sm_)
            yrT = sb("yrT", [128, 4, S_], BF16, sm_)
            t_mw = Tok("mw")
            t_yn = Tok("yn")
            S.dma(wn_b[:], I["w_o_nsa"].rearrange("(k p) d -> p k d", p=128), writes=[t_mw], q="pool")
            S.dma(wr_b[:], I["w_o_rwkv"].rearrange("(k p) d -> p k d", p=128), writes=[t_mw], q="pool")
            for k2 in range(2):
                S.dma(wo_b[:, k2 * 4:(k2 + 1) * 4, :], I["w_out"].rearrange("(k p) d -> p k d", p=128)[:, k2 * 4:(k2 + 1) * 4, :], writes=[t_mw], q="pool")
            S.dma(ynT[:], onT_d.rearrange("(c p) t -> p c t", p=128), reads=t_onT, writes=[t_yn])
            S.dma(yrT[:], yrw_d.rearrange("(c p) t -> p c t", p=128), reads=t_yrw, writes=[t_yn])
            gtb = [sb("gtb%d" % i, [128, 2048], F32, sm_) for i in range(2)]
            hb2 = [sb("hb2_%d" % i, [128, D_], F32, sm_) for i in range(2)]
            m1b = [sb("m1b%d" % i, [128, D_], F32, sm_) for i in range(2)]
            mbb = [sb("mbb%d" % i, [128, D_], BF16, sm_) for i in range(2)]
            mTb = [sb("mTb%d" % i, [128, 8, 128], BF16, sm_) for i in range(2)]
            rsb = [sb("rsb%d" % i, [128, D_], F32, sm_) for i in range(2)]
            xnb2 = [sb("xnb2_%d" % i, [128, D_], F32, sm_) for i in range(2)]
            x2b = [sb("x2b%d" % i, [128, D_], F32, sm_) for i in range(2)]
            x2h = [sb("x2h%d" % i, [128, D_], BF16, sm_) for i in range(2)]
            tg, th2, tm1, tmb, tmT, trs, txn, tx2, tx2h = [[Tok(), Tok()] for _ in range(9)]
            mgv2 = mg_d.rearrange("(n p) c -> n p c", p=128)
            for i in range(NT):
                p = i % 2
                tsl = slice(i * 128, (i + 1) * 128)
                S.dma(gtb[p][:], mgv2[i], reads=[t_mgd[i]], writes=[tg[p]])
                S.dma(hb2[p][:], hv[i], reads=[t_hd[i]], writes=[th2[p]])
                for half in range(2):
                    dsl = slice(half * 512, (half + 1) * 512)
                    for c in range(4):
                        mm(PS[half][:, :], ynT[:, c, tsl], wn_b[:, c, dsl], c == 0, c == 3, [t_yn, t_mw], [PT[half]])
                    for c in range(4):
                        mm(PS[2 + half][:, :], yrT[:, c, tsl], wr_b[:, c, dsl], c == 0, c == 3, [t_yn, t_mw], [PT[2 + half]])
                for half in range(2):
                    dsl = slice(half * 512, (half + 1) * 512)
                    S.op("dve", lambda e: e.tensor_tensor(out=m1b[p][:, dsl], in0=PS[half][:, :], in1=gtb[p][:, dsl], op=ALU.mult),
                         [PT[half], tg[p]], [tm1[p]])
                    S.op("dve", lambda e: e.tensor_tensor(out=rsb[p][:, dsl], in0=PS[2 + half][:, :], in1=gtb[p][:, 1024 + half * 512:1024 + (half + 1) * 512],
                                                          op=ALU.mult), [PT[2 + half], tg[p]], [trs[p]])
                S.op("pool", lambda e: e.tensor_tensor(out=mbb[p][:], in0=m1b[p][:], in1=rsb[p][:], op=ALU.add), [tm1[p], trs[p]], [tmb[p]])
                psb = PS[4 + p].bitcast(BF16)
                for dk in range(8):
                    S.op("pe", lambda e: e.transpose(out=psb[:, dk * 128:(dk + 1) * 128], in_=mbb[p][:, dk * 128:(dk + 1) * 128],
                                                     identity=ident_bf[:]), [tmb[p], t_const], [PT[4 + p]])
                copy("act", mTb[p][:], psb.rearrange("p (k t) -> p k t", k=8), [PT[4 + p]], [tmT[p]])
                for half in range(2):
                    dsl = slice(half * 512, (half + 1) * 512)
                    for c in range(8):
                        mm(PS[6 + half][:, :], mTb[p][:, c, :], wo_b[:, c, dsl], c == 0, c == 7, [tmT[p], t_mw], [PT[6 + half]])
                    S.op("dve", lambda e: e.scalar_tensor_tensor(out=rsb[p][:, dsl], in0=hb2[p][:, dsl], scalar=ALPHA, in1=PS[6 + half][:, :],
                                                                 op0=ALU.mult, op1=ALU.add), [th2[p], PT[6 + half], tmb[p]], [trs[p]])
                ln_tile(p, rsb[p][:], x2b[p][:], trs[p], tx2[p], xnb2[p][:], txn[p])
                S.dma(x2v[i], x2b[p][:], reads=[tx2[p]], writes=[t_x2d[i]])
                S.op("act", lambda e: e.copy(out=x2h[p][:], in_=x2b[p][:]), [tx2[p]], [tx2h[p]])
                psb = PS[4 + p].bitcast(BF16)
                for dk in range(8):
                    S.op("pe", lambda e: e.transpose(out=psb[:, dk * 128:(dk + 1) * 128], in_=x2h[p][:, dk * 128:(dk + 1) * 128],
                                                     identity=ident_bf[:]), [tx2h[p], t_const], [PT[4 + p]])
                copy("dve", mTb[p][:], psb.rearrange("p (k t) -> p k t", k=8), [PT[4 + p]], [tmT[p]])
                S.dma(x2T_d[:, :, tsl], mTb[p][:], reads=[tmT[p]], writes=[t_x2T[i]])
        S.barrier()
        if "x2" in debug:
            o = dbg_out("x2", [S_, D_])
            xtmp = sb("xtmp", [128, NT, D_], F32)
            tt2 = Tok()
            S.dma(xtmp[:], x2_d.rearrange("(n p) d -> p n d", p=128), reads=t_x2d, writes=[tt2])
            S.dma(o.rearrange("(n p) d -> p n d", p=128), xtmp[:], reads=[tt2], writes=[dbg_tok])
        if stage <= 5:
            S.finish(final_toks)
            return nc, hc, DBG

        load_gb(I["ln_ffn_g"], I["ln_ffn_b"])
        outv = out_d.rearrange("(n p) d -> n p d", p=128)
        t_out = [Tok("out%d" % i) for i in range(NT)]
        final_toks.extend(t_out)
        uTv = uTb_d.rearrange("c p k e -> p c (k e)")
        vbv = vb_d.rearrange("(c p) d -> p c d", p=128)
        NEGBIG = -1.0e30
        with ExitStack() as sp_:
            skT = sb("skT", [64, 16, 128], BF16, sp_)
            sk_scope = ExitStack()
            skn = sb("skn", [128, 16, 64], F32, sk_scope)
            t_pw = Tok("pw")
            S.dma(skn[:], I["peer_sub_keys"].rearrange("h q n d -> n (h q) d"), writes=[t_pw])
            for hp in range(16):
                ps, pt = psbank()
                S.op("pe", lambda e: e.transpose(out=ps[0:64, 0:128], in_=skn[:, hp, :], identity=ident_f[:]), [t_pw, t_const], [pt])
                copy(evac_eng(), skT[:, hp, :], ps[0:64, 0:128], [pt], [t_pw])
            sk_scope.close()
            S.barrier()
            X1 = sb("X1", [128, 16384], BF16, sp_)
            X2 = sb("X2", [128, 16384], BF16, sp_)
            Trm = sb("Trm", [128, 16384], BF16, sp_)
            Orm = sb("Orm", [128, 16384], BF16, sp_)
            t_Trm, t_Orm = Tok("Trm"), Tok("Orm")
            t_X1p = [Tok("X1_%d" % i) for i in range(8)]
            t_X2p = [Tok("X2_%d" % i) for i in range(8)]
            gt_d = nc.dram_tensor("gt_scr", [NT, 128, 128, 128], BF16, kind="Internal").ap()
            t_gt = [Tok("gt%d" % i) for i in range(NT)]
            wqv = X1[:, 0:8192].rearrange("p (k d) -> p k d", k=8)
            X1v = X1[:].rearrange("p (n h a) -> p n h a", h=8, a=16)
            X2v = X2[:].rearrange("p (n h a) -> p n h a", h=8, a=16)
            cand = X2[:].bitcast(F32)[:, 0:2048].rearrange("p (h c) -> p h c", h=8)
            X2g = X2[:].rearrange("p (n t) -> p n t", t=128)
            Trm3 = Trm[:].rearrange("p (n t) -> p n t", t=128)
            Orm3 = Orm[:].rearrange("p (n t) -> p n t", t=128)
            qb = Trm[:, 0:1024]
            qTi = Trm[0:64, 1024:3072].rearrange("p (a t) -> p a t", t=128)
            wk = Trm[:, 3072:3584].bitcast(F32)
            wk2 = Trm[:, 3584:4096].bitcast(F32)
            x2Tg = Trm[:, 8192:9216].rearrange("p (k t) -> p k t", k=8)
            SE = sb("SE", [128, 8192], BF16, sp_)
            sc = SE[:, 0:4096].bitcast(F32).rearrange("p (a n) -> p a n", n=128)
            Ee = SE[:, 4096:8192].bitcast(F32).rearrange("p (a n) -> p a n", n=128)
            SD = sb("SD", [128, 8192], BF16, sp_)
            x2Td = sb("x2Td", [128, 8, 256], BF16, sp_)
            gtc = [sb("gtc%d" % i, [128, 2, 128], BF16, sp_) for i in range(4)]
            x2t_t = sb("x2t_t", [128, D_], F32, sp_)
            rso_t = sb("rso_t", [128, D_], F32, sp_)
            xno_t = sb("xno_t", [128, D_], F32, sp_)
            x2t, rso, xno = x2t_t[:], rso_t[:], xno_t[:]
            t_fin = Tok("fin")
            s16 = sb("s16", [128, 16, 16], F32, sp_)
            e16 = sb("e16", [128, 16, 16], F32, sp_)
            tops = sb("tops", [128, 8, 24], F32, sp_)
            stt_ = sb("stt_", [128, 8, 8], F32, sp_)
            exps = sb("exps", [128, 8, 16], F32, sp_)
            E1s = sb("E1s", [128, 8, 16], F32, sp_)
            thr = sb("thr", [128, 8, 16], F32, sp_)
            uTc = [SD[:, i * 1024:(i + 1) * 1024].rearrange("p (k e) -> p k e", k=8) for i in range(4)]
            vcb = [SD[:, 4096 + i * 1024:4096 + (i + 1) * 1024] for i in range(4)]
            Hb = [sb("Hb%d" % i, [128, 256], BF16, sp_) for i in range(2)]
            Hg = [sb("Hg%d" % i, [128, 256], BF16, sp_) for i in range(2)]
            (t_s16, t_e16, t_tops, t_st, t_exps, t_E1s, t_thr) = [Tok() for _ in range(7)]
            t_x2Tg = t_qb = t_qTi = t_wk = t_wk2 = t_Trm
            t_uTc = [Tok() for _ in range(4)]
            t_vcb = [Tok() for _ in range(4)]
            t_gtc = [Tok() for _ in range(4)]
            t_x2Td = Tok("x2Td")
            t_scL = [Tok("sc")]
            t_EeL = [Tok("Ee")]
            t_Hb = [Tok(), Tok()]
            t_Hg = [Tok(), Tok()]
            def gphase(i):
                S.dma(x2Tg, x2T_d[:, :, i * 128:(i + 1) * 128], reads=[t_x2T[i]], writes=[t_x2Tg])
                S.dma(wqv, wq_d[:, :, :], reads=[t_wqd], writes=[*t_X1p])
                for half in range(2):
                    ps, pt = psbank(6, 8)
                    for dk in range(8):
                        mm(ps[:, :], x2Tg[:, dk, :], wqv[:, dk, half * 512:(half + 1) * 512], dk == 0, dk == 7, [t_x2Tg, *t_X1p], [pt])
                    copy("act", qb[:, half * 512:(half + 1) * 512], ps[:, :], [pt], [t_qb])
                for b in range(2):
                    ps, pt = psbank(6, 8)
                    psb = ps.bitcast(BF16)
                    for jj in range(8):
                        hp = b * 8 + jj
                        S.op("pe", lambda e: e.transpose(out=psb[0:64, jj * 128:(jj + 1) * 128], in_=qb[:, hp * 64:(hp + 1) * 64],
                                                         identity=ident_bf[:]), [t_qb, t_const], [pt])
                    copy("act", qTi[:, b * 8:(b + 1) * 8, :], psb[0:64, :].rearrange("p (a t) -> p a t", t=128), [pt], [t_qTi])
                for b in range(4):
                    ps, pt = psbank(6, 8)
                    for j in range(4):
                        hp = b * 4 + j
                        mm(ps[:, j * 128:(j + 1) * 128], qTi[:, hp, :], skT[:, hp, :], True, True, [t_qTi, t_pw], [pt])
                    copy("act", sc[:, b * 4:(b + 1) * 4, :], ps[:, :].rearrange("p (a n) -> p a n", n=128), [pt], [*t_scL])
                for hp in range(16):
                    S.op("dve", lambda e: e.max(out=s16[:, hp, 0:8], in_=sc[:, hp, :]), [*t_scL], [t_s16])
                    S.op("dve", lambda e: e.match_replace(out=wk[:, 0:128], in_to_replace=s16[:, hp, 0:8], in_values=sc[:, hp, :], imm_value=NEGBIG),
                         [*t_scL, t_s16], [t_wk])
                    S.op("dve", lambda e: e.max(out=s16[:, hp, 8:16], in_=wk[:, 0:128]), [t_wk], [t_s16])
                s16v = s16[:].rearrange("p (h q) a -> p h q a", q=2)
                S.op("dve", lambda e: e.tensor_tensor(out=cand.rearrange("p h (a b) -> p h a b", b=16),
                                                      in0=s16v[:, :, 0, :].unsqueeze(3).to_broadcast([128, 8, 16, 16]),
                                                      in1=s16v[:, :, 1, :].unsqueeze(2).to_broadcast([128, 8, 16, 16]), op=ALU.add), [t_s16], [*t_X2p])
                for h in range(8):
                    S.op("dve", lambda e: e.max(out=tops[:, h, 0:8], in_=cand[:, h, :]), [*t_X2p], [t_tops])
                    S.op("dve", lambda e: e.match_replace(out=wk[:], in_to_replace=tops[:, h, 0:8], in_values=cand[:, h, :], imm_value=NEGBIG),
                         [*t_X2p, t_tops], [t_wk])
                    S.op("dve", lambda e: e.max(out=tops[:, h, 8:16], in_=wk[:]), [t_wk], [t_tops])
                    S.op("dve", lambda e: e.match_replace(out=wk2[:], in_to_replace=tops[:, h, 8:16], in_values=wk[:], imm_value=NEGBIG),
                         [t_wk, t_tops], [t_wk2])
                    S.op("dve", lambda e: e.max(out=tops[:, h, 16:24], in_=wk2[:]), [t_wk2], [t_tops])
                S.op("dve", lambda e: e.tensor_tensor(out=stt_[:, :, 0:1], in0=tops[:, :, 15:16], in1=tops[:, :, 16:17], op=ALU.add), [t_tops], [t_st])
                S.op("dve", lambda e: e.tensor_scalar(out=stt_[:, :, 0:1], in0=stt_[:, :, 0:1], scalar1=0.5, scalar2=None, op0=ALU.mult), [t_st], [t_st])
                S.op("dve", lambda e: e.tensor_tensor(out=exps[:], in0=tops[:, :, 0:16], in1=tops[:, :, 0:1].to_broadcast([128, 8, 16]),
                                                      op=ALU.subtract), [t_tops], [t_exps])
                S.op("act", lambda e: e.activation(out=exps[:], in_=exps[:], func=AF.Exp), [t_exps], [t_exps])
                S.op("dve", lambda e: e.tensor_reduce(out=stt_[:, :, 1:2], in_=exps[:], axis=AX.X, op=ALU.add), [t_exps, t_st], [t_st])
                S.op("dve", lambda e: e.reciprocal(out=stt_[:, :, 2:3], in_=stt_[:, :, 1:2]), [t_st], [t_st])
                S.op("dve", lambda e: e.tensor_tensor(out=Ee[:], in0=sc[:], in1=s16[:, :, 0:1].to_broadcast([128, 16, 128]), op=ALU.subtract),
                     [*t_scL, t_s16], [*t_EeL])
                S.op("act", lambda e: e.activation(out=Ee[:], in_=Ee[:], func=AF.Exp), [*t_EeL], [*t_EeL])
                S.op("dve", lambda e: e.tensor_tensor(out=e16[:], in0=s16[:], in1=s16[:, :, 0:1].to_broadcast([128, 16, 16]), op=ALU.subtract),
                     [t_s16], [t_e16])
                S.op("act", lambda e: e.activation(out=e16[:], in_=e16[:], func=AF.Exp), [t_e16], [t_e16])
                e16v = e16[:].rearrange("p (h q) a -> p h q a", q=2)
                S.op("dve", lambda e: e.tensor_tensor(out=E1s[:], in0=e16v[:, :, 0, :], in1=stt_[:, :, 2:3].to_broadcast([128, 8, 16]), op=ALU.mult),
                     [t_e16, t_st], [t_E1s])
                S.op("dve", lambda e: e.tensor_tensor(out=thr[:], in0=stt_[:, :, 0:1].to_broadcast([128, 8, 16]), in1=s16v[:, :, 0, :], op=ALU.subtract),
                     [t_s16, t_st], [t_thr])
                scv = sc[:].rearrange("p (h q) n -> p h q n", q=2)
                Eev = Ee[:].rearrange("p (h q) n -> p h q n", q=2)
                sc2b = scv[:, :, 1, :].rearrange("p h n -> p n h").unsqueeze(3).to_broadcast([128, 128, 8, 16])
                sc1b = scv[:, :, 0, :].rearrange("p h n -> p n h").unsqueeze(3).to_broadcast([128, 128, 8, 16])
                E2b = Eev[:, :, 1, :].rearrange("p h n -> p n h").unsqueeze(3).to_broadcast([128, 128, 8, 16])
                thrb = thr[:].unsqueeze(1).to_broadcast([128, 128, 8, 16])
                E1sb = E1s[:].unsqueeze(1).to_broadcast([128, 128, 8, 16])
                s1b = s16v[:, :, 0, :].unsqueeze(1).to_broadcast([128, 128, 8, 16])
                for pc in range(8):
                    nsl = slice(pc * 16, (pc + 1) * 16)
                    csl = slice(pc * 2048, (pc + 1) * 2048)
                    S.op("dve", lambda e: e.tensor_tensor(out=X1v[:, nsl], in0=sc2b[:, nsl], in1=thrb[:, nsl], op=ALU.is_ge), [*t_scL, t_thr], [t_X1p[pc]])
                    S.op("pool", lambda e: e.tensor_tensor(out=X2v[:, nsl], in0=E2b[:, nsl], in1=E1sb[:, nsl], op=ALU.mult), [*t_EeL, t_E1s], [t_X2p[pc]])
                    S.op("dve", lambda e: e.tensor_tensor(out=X1[:, csl], in0=X1[:, csl], in1=X2[:, csl], op=ALU.mult), [t_X1p[pc], t_X2p[pc]], [t_X1p[pc]])
                    for j8 in range(2):
                        ps, pt = psbank(6, 8)
                        psb = ps.bitcast(BF16)
                        for jj in range(8):
                            n = pc * 16 + j8 * 8 + jj
                            S.op("pe", lambda e: e.transpose(out=psb[:, jj * 128:(jj + 1) * 128], in_=X1[:, n * 128:(n + 1) * 128], identity=ident_bf[:]),
                                 [t_X1p[pc], t_const], [pt])
                        copy("act", Trm3[:, pc * 16 + j8 * 8:pc * 16 + (j8 + 1) * 8, :], psb.rearrange("p (n t) -> p n t", t=128), [pt], [t_Trm])
                for pc in range(8):
                    nsl = slice(pc * 16, (pc + 1) * 16)
                    S.op("dve", lambda e: e.tensor_tensor(out=X1v[:, nsl], in0=sc1b[:, nsl], in1=s1b[:, nsl], op=ALU.is_equal), [*t_scL, t_s16], [t_X1p[pc]])
                    for j8 in range(2):
                        ps, pt = psbank(6, 8)
                        psb = ps.bitcast(BF16)
                        for jj in range(8):
                            n = pc * 16 + j8 * 8 + jj
                            S.op("pe", lambda e: e.transpose(out=psb[:, jj * 128:(jj + 1) * 128], in_=X1[:, n * 128:(n + 1) * 128], identity=ident_bf[:]),
                                 [t_X1p[pc], t_const], [pt])
                        copy("act", Orm3[:, pc * 16 + j8 * 8:pc * 16 + (j8 + 1) * 8, :], psb.rearrange("p (n t) -> p n t", t=128), [pt], [t_Orm])
                for t4 in range(32):
                    ps, pt = psbank(6, 8)
                    for tt in range(4):
                        t = t4 * 4 + tt
                        mm(ps[:, tt * 128:(tt + 1) * 128], Trm3[:, :, t], Orm3[:, :, t], True, True, [t_Trm, t_Orm], [pt])
                    copy("act", X2g[:, :, t4 * 4:(t4 + 1) * 4].rearrange("p n t -> p t n"), ps.rearrange("p (t n) -> p t n", n=128), [pt], [*t_X2p])
                S.dma(gt_d[i].rearrange("p n t -> p (n t)"), X2[:], reads=[*t_X2p], writes=[t_gt[i]])


            def d_load(g2, n):
                p = n % 4
                S.dma(uTc[p].rearrange("p k e -> p (k e)"), uTv[:, n, :], reads=[t_uTb[k_ * 2 + n // 64] for k_ in range(8)], writes=[t_uTc[p]])
                S.dma(vcb[p], vbv[:, n, :], reads=[t_vb[n // 8]], writes=[t_vcb[p]])
                S.dma(gtc[p][:], gt_d[2 * g2:2 * g2 + 2, :, n, :].rearrange("s p t -> p s t"), reads=[t_gt[2 * g2], t_gt[2 * g2 + 1]], writes=[t_gtc[p]])

            def d_hpre(n):
                p, pp = n % 4, n % 2
                for dk in range(8):
                    mm(PS[4 + pp][:, 0:256], uTc[p][:, dk, :], x2Td[:, dk, :], dk == 0, dk == 7, [t_uTc[p], t_x2Td], [PT[4 + pp]])

            def d_gate(n):
                p, pp = n % 4, n % 2
                S.op("act", lambda e: e.activation(out=Hb[pp][:], in_=PS[4 + pp][:, 0:256], func=AF.Gelu_apprx_tanh), [PT[4 + pp]], [t_Hb[pp]])
                S.op("dve", lambda e: e.tensor_tensor(out=Hg[pp][:].rearrange("p (s t) -> p s t", s=2), in0=Hb[pp][:].rearrange("p (s t) -> p s t", s=2),
                                                      in1=gtc[p][:], op=ALU.mult), [t_Hb[pp], t_gtc[p]], [t_Hg[pp]])

            def d_out(n):
                p, pp = n % 4, n % 2
                for sub in range(2):
                    for half in range(2):
                        mm(PS[sub * 2 + half][:, :], Hg[pp][:, sub * 128:(sub + 1) * 128], vcb[p][:, half * 512:(half + 1) * 512],
                           n == 0, n == 127, [t_Hg[pp], t_vcb[p]], [PT[sub * 2 + half]])

            def _nfree(ap):
                n = 1
                for s_ in ap.shape[1:]:
                    n *= int(s_)
                return n

            def est_us(o):
                if o[0] != "op" or o[1] not in ("dve", "act"):
                    return 0.0
                name, a, k = o[2]
                outap = k.get("out", a[0] if a else None)
                n = _nfree(outap) if outap is not None else 64
                if o[1] == "act":
                    return 0.3 + n / 1400.0 * (2.5 if n >= 512 and outap.dtype == BF16 and k.get("func") is None and _nfree(k.get("in_")) == 512 else 1.0)
                if name in ("max", "match_replace", "tensor_reduce"):
                    src_ = k.get("in_", k.get("in_values"))
                    return 0.35 + _nfree(src_) / 960.0
                if name == "reciprocal":
                    return 0.35 + 8 * n / 960.0
                if name in ("tensor_tensor", "scalar_tensor_tensor"):
                    fast = all(k[x].dtype == BF16 for x in ("in0", "in1")) and outap.dtype == BF16
                    return 0.35 + (1.0 if fast else 2.0) * n / 960.0
                return 0.35 + n / 960.0

            gphase(0)
            gphase(1)
            npair = 1 if "peer1" in debug else NT // 2
            for g2 in range(npair):
                ops = []
                if g2 + 1 < npair:
                    S.rec = []
                    gphase(2 * g2 + 2)
                    gphase(2 * g2 + 3)
                    ops = S.rec
                    S.rec = None
                clk = []
                c_ = 0.0
                for o in ops:
                    clk.append(c_)
                    c_ += est_us(o)
                t_chunk = max(2.0, c_ * 0.45 / 126.0)
                pos = 0
                S.dma(x2Td[:], x2T_d[:, :, g2 * 256:(g2 + 1) * 256], reads=[t_x2T[2 * g2], t_x2T[2 * g2 + 1]], writes=[t_x2Td])
                for n in range(128):
                    d_load(g2, n)
                    d_hpre(n)
                    d_gate(n)
                    if n >= 1:
                        d_out(n - 1)
                    e_ = pos
                    while e_ < len(ops) and clk[e_] <= (n + 1) * t_chunk:
                        e_ += 1
                    S.replay(ops[pos:e_])
                    pos = e_
                d_out(127)
                S.replay(ops[pos:])
                for sub in range(2):
                    i = 2 * g2 + sub
                    S.dma(x2t, x2v[i], reads=[t_x2d[i]], writes=[t_fin])
                    for half in range(2):
                        dsl = slice(half * 512, (half + 1) * 512)
                        S.op("dve", lambda e: e.scalar_tensor_tensor(out=rso[:, dsl], in0=x2t[:, dsl], scalar=ALPHA, in1=PS[sub * 2 + half][:, :],
                                                                     op0=ALU.mult, op1=ALU.add), [t_fin, PT[sub * 2 + half]], [t_fin])
                    ln_tile(i % 2, rso, rso, t_fin, t_fin, xno, t_fin)
                    S.dma(outv[i], rso, reads=[t_fin], writes=[t_out[i]])

        S.finish(final_toks)
    return nc, hc, DBG


_CACHE = {}


def _prep_inputs(inputs):
    shared = {}
    for name, shp in IN_SPECS:
        if name in ("x", "peer_uT"):
            continue
        a = np.asarray(inputs[name], dtype=np.float32)
        shared[name] = np.ascontiguousarray(a.reshape(shp))
    shared["peer_uT"] = np.ascontiguousarray(np.asarray(inputs["peer_u"], dtype=np.float32).reshape(16384, D_).T)
    return shared


def kernel(**inputs):
    if "nc" not in _CACHE:
        _CACHE["nc"] = build()
    nc, hc, _ = _CACHE["nc"]
    shared = _prep_inputs(inputs)
    for k, v in hc.items():
        shared["c_" + k] = v
    x = np.asarray(inputs["x"], dtype=np.float32)
    in_maps = []
    for b in range(8):
        m = dict(shared)
        m["x"] = np.ascontiguousarray(x[b])
        in_maps.append(m)
    res = run_bass_kernel_spmd(nc, in_maps, core_ids=list(range(8)))
    return np.stack([np.asarray(r["out"], dtype=np.float32) for r in res.results], axis=0)
```

```python
import math
import numpy as np
import ml_dtypes
from contextlib import ExitStack
import concourse.bass as bass
import concourse.mybir as mybir
from concourse.alu_op_type import AluOpType as ALU
from concourse.mybir import ActivationFunctionType as AF
from concourse.bass_utils import run_bass_kernel_spmd

F32 = mybir.dt.float32
BF16 = mybir.dt.bfloat16
AX = mybir.AxisListType

S_ = 2048
D_ = 1024
NT = 16
D_IN = 5144
BIGNEG = -240000.0
LN_EPS = 1e-5
ALPHA = 2.0 ** 0.25
ATT_SCALE = 0.125


class Tok:
    __slots__ = ("w", "r", "name", "excl")

    def __init__(self, name="", excl=False):
        self.w = None
        self.r = {}
        self.name = name
        self.excl = excl


class _RecEng:
    def __init__(self):
        self.call = None

    def __getattr__(self, name):
        def f(*a, **k):
            self.call = (name, a, k)
            return self
        return f


class Sched:
    NDMA = 40

    def __init__(self, nc, es):
        self.nc = nc
        self.E = {"pe": nc.tensor, "act": nc.scalar, "dve": nc.vector,
                  "pool": nc.gpsimd, "sp": nc.sync}
        self.sem = {k: es.enter_context(nc.semaphore("s_" + k)) for k in self.E}
        self.cnt = {k: 0 for k in self.E}
        self.dsem = [es.enter_context(nc.semaphore("d%d" % i)) for i in range(self.NDMA)]
        self.dval = [0] * self.NDMA
        self.dnext = 0
        self.seen = {k: {} for k in self.E}
        self.ninst = 0
        self.es = es
        self.swsem = []
        self.nobar = set()
        self.rec = None

    def _semof(self, key):
        if isinstance(key, tuple):
            if key[0] == "w":
                return self.swsem[key[1]]
            return self.dsem[key[1]]
        return self.sem[key]

    def _wait(self, eng, key, val):
        if eng == "pe" and key == "pe":
            return
        if self.seen[eng].get(key, 0) >= val:
            return
        self.E[eng].wait_ge(self._semof(key), val)
        self.seen[eng][key] = val
        self.ninst += 1

    def _deps(self, eng, reads, writes):
        deps = {}
        for t in reads:
            if t.w is not None:
                k, c = t.w
                deps[k] = max(deps.get(k, 0), c)
        for t in writes:
            if t.w is not None:
                k, c = t.w
                deps[k] = max(deps.get(k, 0), c)
            for k, c in t.r.items():
                deps[k] = max(deps.get(k, 0), c)
        for k, c in deps.items():
            self._wait(eng, k, c)

    def replay(self, ops):
        for o in ops:
            if o[0] == "op":
                _, eng, (name, a, k), reads, writes = o
                self.op(eng, lambda e: getattr(e, name)(*a, **k), reads, writes)
            else:
                _, out, in_, reads, writes, q, kw = o
                self.dma(out, in_, reads, writes, q, **kw)

    def op(self, eng, fn, reads=(), writes=()):
        if self.rec is not None:
            pr = _RecEng()
            fn(pr)
            self.rec.append(("op", eng, pr.call, list(reads), list(writes)))
            return None
        ex = [t for t in reads if t.excl]
        if ex:
            reads = [t for t in reads if not t.excl]
            writes = list(writes) + ex
        self._deps(eng, reads, writes)
        inst = fn(self.E[eng])
        self.cnt[eng] += 1
        c = self.cnt[eng]
        inst.then_inc(self.sem[eng], 1)
        for t in reads:
            t.r[eng] = c
        for t in writes:
            t.w = (eng, c)
            t.r = {}
        self.ninst += 1
        return inst

    def dma(self, out, in_, reads=(), writes=(), q="sp", nobar=False, **kw):
        if self.rec is not None:
            self.rec.append(("dma", out, in_, list(reads), list(writes), q, kw))
            return None
        if q == "pool":
            if nobar:
                self.nobar.add(len(self.swsem))
            sem = self.es.enter_context(self.nc.semaphore("w%d" % len(self.swsem)))
            self.swsem.append(sem)
            key = ("w", len(self.swsem) - 1)
            if len(self.swsem) >= 2:
                self._wait(q, ("w", len(self.swsem) - 2), 16)
            self._deps(q, reads, writes)
            inst = self.E[q].dma_start(out=out, in_=in_, **kw)
            inst.then_inc(sem, 16)
            for t in reads:
                t.r[key] = 16
            for t in writes:
                t.w = (key, 16)
                t.r = {}
            self.ninst += 1
            return inst
        i = self.dnext
        self.dnext = (self.dnext + 1) % self.NDMA
        key = ("d", i)
        if self.dval[i] > 0:
            self._wait(q, key, self.dval[i])
        self._deps(q, reads, writes)
        inst = self.E[q].dma_start(out=out, in_=in_, **kw)
        self.dval[i] += 16
        inst.then_inc(self.dsem[i], 16)
        v = self.dval[i]
        for t in reads:
            t.r[key] = v
        for t in writes:
            t.w = (key, v)
            t.r = {}
        self.ninst += 1
        return inst

    def barrier(self):
        for eng in self.E:
            for key in self.E:
                if self.cnt[key] > 0 and not (eng == key == "pe"):
                    self._wait(eng, key, self.cnt[key])
            for i in range(self.NDMA):
                if self.dval[i] > 0:
                    self._wait(eng, ("d", i), self.dval[i])
            for i in range(len(self.swsem)):
                if i not in self.nobar:
                    self._wait(eng, ("w", i), 16)

    def finish(self, toks):
        for t in toks:
            if t.w is not None:
                self._wait("sp", t.w[0], t.w[1])


def _t5_bucket_np(dist):
    d = np.maximum(dist, 0)
    large = 16 + (np.log(np.maximum(d, 1).astype(np.float32) / 16) / math.log(128 / 16) * 16).astype(np.int32)
    large = np.minimum(large, 31)
    return np.where(d < 16, d, large)


FOFF = 2048
FLEN = 4352


def host_consts():
    c = {}
    c["ident_bf"] = np.eye(128, dtype=np.float32).astype(ml_dtypes.bfloat16)
    c["ident_f"] = np.eye(128, dtype=np.float32)
    c["J128"] = np.ascontiguousarray(np.eye(128, dtype=np.float32)[::-1])
    j127 = np.zeros((128, 128), np.float32)
    j127[:127, :127] = np.eye(127, dtype=np.float32)[::-1]
    c["J127"] = j127
    x = np.arange(FLEN)
    dist = x - FOFF
    oh = np.zeros((33, FLEN), np.float32)
    b = _t5_bucket_np(dist)
    valid = dist >= 0
    oh[b[valid], x[valid]] = 1.0
    oh[32, ~valid] = 1.0
    c["OHD"] = oh
    c0 = np.arange(127) * 16
    s0 = np.arange(32) * 64
    ov = np.clip(np.minimum(c0[:, None] + 32, s0[None, :] + 64) - np.maximum(c0[:, None], s0[None, :]), 0, None).astype(np.float32) / 32
    ovp = np.zeros((128, 32), np.float32)
    ovp[:127] = ov
    c["overlap"] = ovp
    t = np.arange(S_)
    cur = t // 64
    j = np.arange(32)
    forced = (j[None, :] == 0) | (j[None, :] == cur[:, None]) | (j[None, :] == cur[:, None] - 1)
    future = j[None, :] > cur[:, None]
    cs = np.where(future, -1e30, np.where(forced, 1.0e4, 0.0)).astype(np.float32)
    c["Csel"] = np.ascontiguousarray(cs.reshape(NT, 128, 32).transpose(1, 0, 2))
    ex = np.zeros((32, NT, 128), np.float32)
    for kt in range(NT):
        for s in range(128):
            ex[2 * kt + s // 64, kt, s] = 1.0
    c["Expand"] = ex.astype(ml_dtypes.bfloat16)
    s_ = np.arange(128)[:, None]
    t_ = np.arange(128)[None, :]
    c["TriLT"] = np.where(t_ < s_, 0.0, BIGNEG).astype(np.float32).astype(ml_dtypes.bfloat16)
    i_ = np.arange(128)[:, None]
    su = (i_ < t_).astype(np.float32)
    ui = (i_ <= t_).astype(np.float32)
    c["MaskAB"] = np.concatenate([su, ui], axis=1)
    c["MaskSL"] = (i_ > t_).astype(np.float32)
    return c


CONST_DT = {"ident_bf": BF16, "ident_f": F32, "J128": F32, "J127": F32, "OHD": F32, "overlap": F32,
            "Csel": F32, "Expand": BF16, "TriLT": BF16, "MaskAB": F32, "MaskSL": F32}

IN_SPECS = [
    ("x", [S_, D_]), ("ln_in_g", [D_]), ("ln_in_b", [D_]), ("rel_bias", [32, 8]), ("w_in", [D_, D_IN]),
    ("token_mu", [1792]), ("cmp_pos", [2, 32, 64]), ("cmp_w1", [2, 2048, 256]), ("cmp_b1", [2, 256]),
    ("cmp_w2", [2, 256, 64]), ("cmp_b2", [2, 64]), ("rwkv_w0", [512]), ("rwkv_w2", [64, 512]),
    ("rwkv_a0", [512]), ("rwkv_a2", [64, 512]), ("rwkv_g2", [128, 512]), ("rwkv_k_k", [512]),
    ("rwkv_k_a", [512]), ("rwkv_r_k", [512]), ("rwkv_lnx_g", [512]), ("rwkv_lnx_b", [512]),
    ("w_o_nsa", [512, D_]), ("w_o_rwkv", [512, D_]), ("w_out", [D_, D_]), ("ln_mix_g", [D_]), ("ln_mix_b", [D_]),
    ("peer_w_query", [D_, D_]), ("peer_sub_keys", [8, 2, 128, 64]), ("peer_uT", [D_, 16384]), ("peer_v", [16384, D_]),
    ("ln_ffn_g", [D_]), ("ln_ffn_b", [D_]),
]


def build(stage=99, debug=()):
    nc = bass.Bass("TRN2", target_bir_lowering=False)
    I = {}
    for name, shp in IN_SPECS:
        I[name] = nc.dram_tensor(name, shp, F32, kind="ExternalInput").ap()
    hc = host_consts()
    C = {}
    for name, arr in hc.items():
        C[name] = nc.dram_tensor("c_" + name, list(arr.shape), CONST_DT[name], kind="ExternalInput").ap()
    out_d = nc.dram_tensor("out", [S_, D_], F32, kind="ExternalOutput").ap()
    DBG = {}

    def dbg_out(name, shape, dt=F32):
        DBG[name] = nc.dram_tensor("dbg_" + name, shape, dt, kind="ExternalOutput").ap()
        return DBG[name]

    h_d = nc.dram_tensor("h_scr", [S_, D_], F32, kind="Internal").ap()
    F_d = nc.dram_tensor("F_scr", [8, FLEN], F32, kind="Internal").ap()

    with ExitStack() as es, nc.allow_non_contiguous_dma(reason="small param loads"):
        S = Sched(nc, es)

        def sb(name, shape, dt, stack=es):
            return stack.enter_context(nc.sbuf_tensor(name, shape, dt))

        PSALL = es.enter_context(nc.psum_tensor("psall", [128, 8, 512], F32))
        PS = [PSALL[:, i, :] for i in range(8)]
        PT = [Tok("ps%d" % i, excl=True) for i in range(8)]
        psrr = {}

        def psbank(lo=0, hi=8):
            i = psrr.get((lo, hi), lo)
            psrr[(lo, hi)] = lo + (i + 1 - lo) % (hi - lo)
            return PS[i], PT[i]

        evrr = [0]

        def evac_eng():
            evrr[0] ^= 1
            return "act" if evrr[0] else "dve"

        def copy(eng, out, in_, reads, writes):
            if eng == "act":
                return S.op("act", lambda e: e.copy(out=out, in_=in_), reads, writes)
            return S.op(eng, lambda e: e.tensor_copy(out=out, in_=in_), reads, writes)

        def mm(out, lhsT, rhs, start, stop, reads, writes):
            return S.op("pe", lambda e: e.matmul(out, lhsT=lhsT, rhs=rhs, start=start, stop=stop), reads, writes)

        final_toks = []
        dbg_tok = Tok("dbg")
        final_toks.append(dbg_tok)

        ident_bf = sb("ident_bf", [128, 128], BF16)
        ident_f = sb("ident_f", [128, 128], F32)
        t_const = Tok("const")
        S.dma(ident_bf[:], C["ident_bf"][:, :], writes=[t_const])
        S.dma(ident_f[:], C["ident_f"][:, :], writes=[t_const])

        uTb_d = nc.dram_tensor("uTb_scr", [128, 128, 8, 128], BF16, kind="Internal").ap()
        vb_d = nc.dram_tensor("vb_scr", [16384, D_], BF16, kind="Internal").ap()
        t_uTb = [Tok("uTb%d" % i) for i in range(16)]
        t_vb = [Tok("vb%d" % i) for i in range(16)]
        wq_d = nc.dram_tensor("wq_scr", [128, 8, D_], BF16, kind="Internal").ap()
        t_wqd = Tok("wqd")
        gb_bc = sb("gb_bc", [128, 2, D_], F32)
        t_gb = Tok("gb")

        def load_gb(g_ap, b_ap):
            S.dma(gb_bc[:, 0, :], g_ap.unsqueeze(0).to_broadcast([128, D_]), writes=[t_gb])
            S.dma(gb_bc[:, 1, :], b_ap.unsqueeze(0).to_broadcast([128, D_]), writes=[t_gb])

        lnst = sb("lnst", [128, 2, 2, 6], F32)
        lnmv = sb("lnmv", [128, 2, 4], F32)
        t_ln = [Tok("ln0"), Tok("ln1")]

        def ln_tile(par, src, dst, t_src, t_dst, xn_buf, t_xn, alpha_src=None):
            st = lnst[:, par]
            mv = lnmv[:, par]
            tl = t_ln[par]
            S.op("dve", lambda e: e.bn_stats(out=st[:, 0, :], in_=src[:, 0:512]), [t_src], [tl])
            S.op("dve", lambda e: e.bn_stats(out=st[:, 1, :], in_=src[:, 512:1024]), [t_src], [tl])
            S.op("dve", lambda e: e.bn_aggr(out=mv[:, 0:2], in_=st.rearrange("p a b -> p (a b)")), [tl], [tl])
            S.op("dve", lambda e: e.tensor_scalar(out=mv[:, 2:3], in0=mv[:, 1:2], scalar1=LN_EPS, scalar2=None, op0=ALU.add), [tl], [tl])
            S.op("act", lambda e: e.activation(out=mv[:, 2:3], in_=mv[:, 2:3], func=AF.Sqrt), [tl], [tl])
            S.op("dve", lambda e: e.reciprocal(out=mv[:, 2:3], in_=mv[:, 2:3]), [tl], [tl])
            S.op("dve", lambda e: e.tensor_scalar(out=mv[:, 3:4], in0=mv[:, 0:1], scalar1=mv[:, 2:3], scalar2=-1.0,
                                                  op0=ALU.mult, op1=ALU.mult), [tl], [tl])
            S.op("act", lambda e: e.activation(out=xn_buf, in_=src, func=AF.Identity, bias=mv[:, 3:4], scale=mv[:, 2:3]),
                 [t_src, tl], [t_xn])
            S.op("dve", lambda e: e.tensor_tensor(out=xn_buf, in0=xn_buf, in1=gb_bc[:, 0, :], op=ALU.mult), [t_xn, t_gb], [t_xn])
            S.op("pool", lambda e: e.tensor_tensor(out=dst, in0=xn_buf, in1=gb_bc[:, 1, :], op=ALU.add), [t_xn, t_gb], [t_dst])

        onT_d = nc.dram_tensor("onT_scr", [512, S_], BF16, kind="Internal").ap()
        yrw_d = nc.dram_tensor("yrw_scr", [512, S_], BF16, kind="Internal").ap()
        t_onT = [Tok("onT%d" % i) for i in range(NT)]
        t_yrw = [Tok("yrw%d" % i) for i in range(8)]
        rw_d = nc.dram_tensor("rw_scr", [1792, S_], F32, kind="Internal").ap()
        mg_d = nc.dram_tensor("mg_scr", [S_, 2048], F32, kind="Internal").ap()
        t_rwd = [Tok("rwd%d" % i) for i in range(28)]
        t_mgd = [Tok("mgd%d" % i) for i in range(NT)]
        wv = I["w_in"].rearrange("(k p) c -> p k c", p=128)
        with ExitStack() as sn:
            o_nsa = sb("o_nsa", [128, NT, 512], F32, sn)
            t_on = [Tok("on%d" % i) for i in range(NT)]
            qT = sb("qT", [64, 8, S_], BF16, sn)
            t_q = [[Tok() for _ in range(4)] for _ in range(8)]
            kT = sb("kT", [64, 4, S_], BF16, sn)
            t_k = [[Tok() for _ in range(4)] for _ in range(4)]
            Vt = sb("Vt", [128, NT, 4, 65], BF16, sn)
            t_V = [Tok() for _ in range(NT)]
            gate = sb("gate", [128, NT, 24], F32, sn)
            kcT = sb("kcT", [64, 2, 128], BF16, sn)
            t_kc = [Tok(), Tok()]
            VC = sb("VC", [128, 2, 98], F32, sn)
            t_VC = [Tok(), Tok()]
            S.op("pool", lambda e: e.memset(Vt[:, :, :, 64:65], 1.0), [], t_V)
            S.op("pool", lambda e: e.memset(VC[:], 0.0), [], t_VC)
            S.op("pool", lambda e: e.memset(VC[:, :, 64:65], 1.0), [], t_VC)
            S.dma(VC[:, 0, 65:97], C["overlap"][:, :], writes=[t_VC[0]])
            S.dma(VC[:, 1, 65:97], C["overlap"][:, :], writes=[t_VC[1]])

            with ExitStack() as s1:
                hT = sb("hT", [128, 8, S_], BF16, s1)
                t_hT = [Tok("hT%d" % i) for i in range(4)]
                t_hd = [Tok("hd%d" % i) for i in range(NT)]
                load_gb(I["ln_in_g"], I["ln_in_b"])
                xv = I["x"].rearrange("(n p) d -> n p d", p=128)
                hv = h_d.rearrange("(n p) d -> n p d", p=128)
                with ExitStack() as sa:
                    xbuf = [sb("xa%d" % i, [128, D_], F32, sa) for i in range(2)]
                    xnb = [sb("xn%d" % i, [128, D_], F32, sa) for i in range(2)]
                    hfb = [sb("hf%d" % i, [128, D_], F32, sa) for i in range(2)]
                    hbb = [sb("hb%d" % i, [128, D_], BF16, sa) for i in range(2)]
                    t_x = [Tok(), Tok()]
                    t_xn = [Tok(), Tok()]
                    t_hf = [Tok(), Tok()]
                    t_hb = [Tok(), Tok()]
                    for i in range(NT):
                        p = i % 2
                        S.dma(xbuf[p][:], xv[i], writes=[t_x[p]])
                        ln_tile(p, xbuf[p][:], hfb[p][:], t_x[p], t_hf[p], xnb[p][:], t_xn[p])
                        S.dma(hv[i], hfb[p][:], reads=[t_hf[p]], writes=[t_hd[i]])
                        S.op("act", lambda e: e.copy(out=hbb[p][:], in_=hfb[p][:]), [t_hf[p]], [t_hb[p]])
                        ps, pt = psbank()
                        psb = ps.bitcast(BF16)
                        for dk in range(8):
                            S.op("pe", lambda e: e.transpose(out=psb[:, dk * 128:(dk + 1) * 128], in_=hbb[p][:, dk * 128:(dk + 1) * 128],
                                                             identity=ident_bf[:]), [t_hb[p], t_const], [pt])
                        copy("dve", hT[:, :, i * 128:(i + 1) * 128], psb.rearrange("p (k t) -> p k t", k=8), [pt], [t_hT[i // 4]])
                S.barrier()
                Wn = sb("Wn", [128, 8, 1304], BF16, s1)
                t_Wn = [Tok(), Tok(), Tok()]
                for ci, c0 in enumerate(range(0, 1304, 512)):
                    c1 = min(1304, c0 + 512)
                    S.dma(Wn[:, :, c0:c1], wv[:, :, c0:c1], writes=[t_Wn[ci]], q="pool")
                s1c = ExitStack()
                kvA = sb("kvA", [64, 4, S_], BF16, s1c)
                kvB = sb("kvB", [64, 4, S_], BF16, s1c)
                t_kv = [[Tok() for _ in range(4)] for _ in range(4)]
                pos_sb = sb("pos_sb", [32, 2, 64], F32, s1c)
                posT = sb("posT", [64, 2, 32], F32, s1c)
                t_pos = Tok()
                S.dma(pos_sb[:], I["cmp_pos"].rearrange("k j d -> j k d"), writes=[t_pos])
                for kv in range(2):
                    ps, pt = psbank()
                    S.op("pe", lambda e: e.transpose(out=ps[0:64, 0:32], in_=pos_sb[:, kv, :], identity=ident_f[0:32, 0:32]),
                         [t_pos, t_const], [pt])
                    copy("dve", posT[:, kv, :], ps[0:64, 0:32], [pt], [t_pos])

                def proj_cm(col0, M, tc, evac):
                    ps, pt = psbank()
                    for dk in range(8):
                        mm(ps[0:M, :], Wn[:, dk, col0:col0 + M], hT[:, dk, tc * 512:(tc + 1) * 512], dk == 0, dk == 7,
                           t_Wn + [t_hT[tc]], [pt])
                    evac(ps[0:M, :], pt)

                for tc in range(4):
                    tsl = slice(tc * 512, (tc + 1) * 512)
                    for h in range(8):
                        proj_cm(h * 64, 64, tc, lambda ps, pt: copy(evac_eng(), qT[:, h, tsl], ps, [pt], [t_q[h][tc]]))
                    for b, base in ((0, 768), (1, 1024)):
                        for g in range(2):
                            proj_cm(base + g * 64, 64, tc,
                                    lambda ps, pt: copy(evac_eng(), kT[:, b * 2 + g, tsl], ps, [pt], [t_k[b * 2 + g][tc]]))
                    for kv, base in ((0, 512), (1, 640)):
                        for g in range(2):
                            idx = kv * 2 + g

                            def ev(ps, pt):
                                for dst, j0 in ((kvA, 0), (kvB, 16)):
                                    S.op("dve", lambda e: e.tensor_tensor(
                                        out=dst[:, idx, tsl].rearrange("p (b j) -> p b j", j=16),
                                        in0=ps.rearrange("p (b j) -> p b j", j=16),
                                        in1=posT[:, kv, j0:j0 + 16].unsqueeze(1).to_broadcast([64, 32, 16]), op=ALU.add),
                                        [pt, t_pos], [t_kv[idx][tc]])
                            proj_cm(base + g * 64, 64, tc, ev)
                for i in range(NT):
                    ps, pt = psbank()
                    for ri, (c0, n) in enumerate(((896, 128), (1152, 128), (1280, 24))):
                        for dk in range(8):
                            mm(ps[:, ri * 128:ri * 128 + n], hT[:, dk, i * 128:(i + 1) * 128], Wn[:, dk, c0:c0 + n], dk == 0, dk == 7,
                               t_Wn + [t_hT[i // 4]], [pt])
                    copy("dve", Vt[:, i, :, 0:64], ps[:, 0:256].rearrange("p (a d) -> p a d", d=64), [pt], [t_V[i]])
                    S.op("act", lambda e: e.activation(out=gate[:, i, :], in_=ps[:, 256:280], func=AF.Sigmoid), [pt], [t_V[i]])

                w2b = sb("w2b", [128, 2, 2, 64], BF16, s1c)
                b1T = sb("b1T", [128, 2, 2], F32, s1c)
                b2k = sb("b2k", [64, 1], F32, s1c)
                b2v = sb("b2v", [128, 64], F32, s1c)
                t_cw = Tok()
                for kv in range(2):
                    S.dma(w2b[:, kv], I["cmp_w2"][kv].rearrange("(c p) d -> p c d", p=128), writes=[t_cw], q="pool")
                    S.dma(b1T[:, kv, :], I["cmp_b1"][kv].rearrange("(c p) -> p c", p=128), writes=[t_cw])
                S.dma(b2k[:], I["cmp_b2"][0].unsqueeze(1), writes=[t_cw])
                S.dma(b2v[:], I["cmp_b2"][1].unsqueeze(0).to_broadcast([128, 64]), writes=[t_cw])
                w1s = sb("w1s", [64, 32, 256], BF16, s1c)
                w1b = [w1s, w1s]
                t_w1s = Tok()
                t_w1 = [t_w1s, t_w1s]
                hid = [sb("hid%d" % i, [128, 2, 128], BF16, s1c) for i in range(2)]
                t_hid = [Tok(), Tok()]
                for kv in range(2):
                    S.dma(w1s[:], I["cmp_w1"][kv].rearrange("(j d) h -> d j h", d=64), writes=[t_w1s], q="pool")
                    for g in range(2):
                        idx = kv * 2 + g
                        hb_ = hid[idx % 2]
                        th = t_hid[idx % 2]
                        for hcx in range(2):
                            ps, pt = psbank()
                            for j in range(32):
                                src = kvA if j < 16 else kvB
                                off = j if j < 16 else j
                                rhs = src[:, idx, off:off + 16 * 126 + 1:16]
                                mm(ps[:, 0:127], w1b[kv][:, j, hcx * 128:(hcx + 1) * 128], rhs, j == 0, j == 31,
                                   [t_w1[kv]] + t_kv[idx], [pt])
                            S.op("act", lambda e: e.activation(out=hb_[:, hcx, 0:127], in_=ps[:, 0:127], func=AF.Gelu_apprx_tanh,
                                                               bias=b1T[:, kv, hcx:hcx + 1], scale=1.0), [pt, t_cw], [th])
                        ps, pt = psbank()
                        if kv == 0:
                            for hcx in range(2):
                                mm(ps[0:64, 0:127], w2b[:, 0, hcx, :], hb_[:, hcx, 0:127], hcx == 0, hcx == 1, [t_cw, th], [pt])
                            S.op("act", lambda e: e.activation(out=kcT[:, g, 0:127], in_=ps[0:64, 0:127], func=AF.Identity,
                                                               bias=b2k[:, 0:1], scale=1.0), [pt, t_cw], [t_kc[g]])
                        else:
                            for hcx in range(2):
                                mm(ps[0:127, 0:64], hb_[:, hcx, 0:127], w2b[:, 1, hcx, :], hcx == 0, hcx == 1, [t_cw, th], [pt])
                            S.op("dve", lambda e: e.tensor_tensor(out=VC[0:127, g, 0:64], in0=ps[0:127, 0:64], in1=b2v[0:127, :],
                                                                  op=ALU.add), [pt, t_cw], [t_VC[g]])
                s1c.close()
                S.barrier()
                stg = [sb("stg%d" % i, [128, S_], F32, s1) for i in range(2)]
                t_stg = [Tok(), Tok()]
                t_Wr = Tok()
                for half in range(2):
                    for q2 in range(2):
                        S.dma(Wn[:, :, q2 * 448:(q2 + 1) * 448], wv[:, :, 1304 + half * 896 + q2 * 448:1304 + half * 896 + (q2 + 1) * 448],
                              reads=[], writes=t_Wn + [t_Wr], q="pool")
                    for cc in range(14):
                        c = half * 14 + cc
                        p = c % 2
                        for tc in range(4):
                            ps, pt = psbank()
                            for dk in range(8):
                                mm(ps[0:64, :], Wn[:, dk, cc * 64:(cc + 1) * 64], hT[:, dk, tc * 512:(tc + 1) * 512], dk == 0, dk == 7,
                                   [t_Wr, t_hT[tc]], [pt])
                            copy(evac_eng(), stg[p][0:64, tc * 512:(tc + 1) * 512], ps[0:64, :], [pt], [t_stg[p]])
                        S.dma(rw_d[c * 64:(c + 1) * 64, :], stg[p][0:64, :], reads=[t_stg[p]], writes=[t_rwd[c]])
                mgv = mg_d.rearrange("(n p) c -> n p c", p=128)
                for half in range(2):
                    for q4 in range(2):
                        S.dma(Wn[:, :, q4 * 512:(q4 + 1) * 512], wv[:, :, 3096 + half * 1024 + q4 * 512:3096 + half * 1024 + (q4 + 1) * 512],
                              reads=[], writes=t_Wn + [t_Wr], q="pool")
                    for i in range(NT):
                        p = i % 2
                        for q4 in range(2):
                            ps, pt = psbank()
                            for dk in range(8):
                                mm(ps[:, :], hT[:, dk, i * 128:(i + 1) * 128], Wn[:, dk, q4 * 512:(q4 + 1) * 512], dk == 0, dk == 7,
                                   [t_Wr, t_hT[i // 4]], [pt])
                            S.op("act", lambda e: e.activation(out=stg[p][:, q4 * 512:(q4 + 1) * 512], in_=ps[:, :], func=AF.Sigmoid),
                                 [pt], [t_stg[p]])
                        S.dma(mgv[i][:, half * 1024:(half + 1) * 1024], stg[p][:, 0:1024], reads=[t_stg[p]], writes=[t_mgd[i]])
            S.barrier()
            if "kv2" in debug:
                o = dbg_out("kT", [64, 4, S_], BF16)
                S.dma(o[:, :, :], kT[:], reads=[t for l in t_k for t in l], writes=[dbg_tok])
                o = dbg_out("Vt", [128, NT, 4, 65], BF16)
                S.dma(o[:, :, :, :], Vt[:], reads=t_V, writes=[dbg_tok])
                o = dbg_out("gate", [128, NT, 24])
                S.dma(o[:, :, :], gate[:], reads=t_V, writes=[dbg_tok])
            if "kc" in debug:
                o = dbg_out("kcT", [64, 2, 128], BF16)
                S.dma(o[:, :, :], kcT[:], reads=t_kc, writes=[dbg_tok])
                o = dbg_out("VC", [128, 2, 98])
                S.dma(o[:, :, :], VC[:], reads=t_VC, writes=[dbg_tok])
                o = dbg_out("qT", [64, 8, S_], BF16)
                S.dma(o[:, :, :], qT[:], reads=[t for l in t_q for t in l], writes=[dbg_tok])

            Theta = sb("Theta", [128, 8, 3, 128], BF16, sn)
            t_Th = [Tok() for _ in range(8)]
            TriLT = sb("TriLT", [128, 128], BF16, sn)
            Expand = sb("Expand", [32, NT, 128], BF16, sn)
            J128 = sb("J128", [128, 128], F32, sn)
            J127 = sb("J127", [128, 128], F32, sn)
            Csel = sb("Csel", [128, NT, 32], F32, sn)
            t_c2 = Tok("const2")
            S.dma(TriLT[:], C["TriLT"][:, :], writes=[t_c2])
            S.dma(Expand[:], C["Expand"][:, :, :], writes=[t_c2])
            S.dma(J128[:], C["J128"][:, :], writes=[t_c2])
            S.dma(J127[:], C["J127"][:, :], writes=[t_c2])
            S.dma(Csel[:], C["Csel"][:, :, :], writes=[t_c2])
            with ExitStack() as s2:
                OHD = sb("OHD", [33, FLEN], F32, s2)
                relbx = sb("relbx", [33, 8], F32, s2)
                F_sb = sb("F_sb", [8, FLEN], F32, s2)
                Hk = sb("Hk", [128, 8, 384], F32, s2)
                t_f = Tok()
                t_Fsb = Tok()
                t_Fd = Tok()
                t_Hk = Tok()
                S.dma(OHD[:], C["OHD"][:, :], writes=[t_f])
                S.op("pool", lambda e: e.memset(relbx[:], BIGNEG / 8.0), [], [t_f])
                S.dma(relbx[0:32, :], I["rel_bias"][:, :], writes=[t_f])
                for c0 in range(0, FLEN, 512):
                    n = min(512, FLEN - c0)
                    ps, pt = psbank()
                    mm(ps[0:8, 0:n], relbx[:, :], OHD[:, c0:c0 + n], True, True, [t_f], [pt])
                    S.op("act", lambda e: e.mul(F_sb[:, c0:c0 + n], ps[0:8, 0:n], 8.0), [pt], [t_Fsb])
                S.dma(F_d[:, :], F_sb[:], reads=[t_Fsb], writes=[t_Fd])
                hk_src = bass.AP(tensor=F_d.tensor, offset=1921, ap=[[1, 128], [FLEN, 8], [1, 384]])
                S.dma(Hk[:], hk_src, reads=[t_Fd], writes=[t_Hk])
                for h in range(8):
                    ps, pt = psbank()
                    mm(ps[:, 0:384], J128[:], Hk[:, h, :], True, True, [t_c2, t_Hk], [pt])
                    copy(evac_eng(), Theta[:, h, :, :], ps[:, 0:384].rearrange("p (a t) -> p a t", t=128), [pt], [t_Th[h]])
            S.barrier()
            if "theta" in debug:
                o = dbg_out("Theta", [128, 8, 3, 128], BF16)
                S.dma(o[:, :, :, :], Theta[:], reads=t_Th, writes=[dbg_tok])

            for k_ in range(8):
                for hf in range(2):
                    S.dma(uTb_d[hf * 64:(hf + 1) * 64, :, k_, :].rearrange("c p e -> p c e"),
                          I["peer_uT"][k_ * 128:(k_ + 1) * 128, hf * 8192:(hf + 1) * 8192].rearrange("p (c e) -> p c e", e=128),
                          writes=[t_uTb[k_ * 2 + hf]], q="pool", nobar=True)
            for k_ in range(16):
                S.dma(vb_d[k_ * 1024:(k_ + 1) * 1024, :], I["peer_v"][k_ * 1024:(k_ + 1) * 1024, :], writes=[t_vb[k_]], q="pool", nobar=True)
            S.dma(wq_d[:, :, :], I["peer_w_query"].rearrange("(k p) d -> p k d", p=128), writes=[t_wqd], q="pool", nobar=True)
            with ExitStack() as s3:
                imp = sb("imp", [128, NT, 2, 32], F32, s3)
                t_imp = [[Tok() for _ in range(2)] for _ in range(NT)]
                MnegT = sb("MnegT", [32, 2, S_], BF16, s3)
                t_Mn = [[Tok() for _ in range(NT)] for _ in range(2)]
                sm = [sb("sm%d" % i, [128, 4], F32, s3) for i in range(4)]
                t_sm = [Tok() for _ in range(4)]
                PTs = [sb("PTs%d" % i, [128, S_], BF16, s3) for i in range(2)]
                t_PTs = [Tok(), Tok()]
                PTw = [sb("PTw%d" % i, [128, 640], BF16, s3) for i in range(2)]
                t_PTw = [Tok(), Tok()]
                s3a = ExitStack()
                H16_0 = sb("H16_0", [128, S_], F32, s3a)
                H16 = [H16_0, H16_0]
                t_H16_0 = Tok()
                t_H16 = [t_H16_0, t_H16_0]
                Phi = [sb("Phi%d" % i, [128, S_], BF16, s3a) for i in range(2)]
                t_Phi = [Tok(), Tok()]
                PTc_0 = sb("PTc_0", [128, S_], F32, s3a)
                PTc = [PTc_0, PTc_0]
                t_PTc_0 = Tok()
                t_PTc = [t_PTc_0, t_PTc_0]
                smrr = [0]

                def evac_attn(ps, pt, i, h, br, first):
                    k = smrr[0]
                    smrr[0] = (k + 1) % 4
                    s_, ts = sm[k], t_sm[k]
                    S.op("dve", lambda e: e.tensor_scalar(out=s_[:, 0:1], in0=ps[:, 64:65], scalar1=1e-30, scalar2=None, op0=ALU.max),
                         [pt], [ts])
                    S.op("dve", lambda e: e.reciprocal(out=s_[:, 1:2], in_=s_[:, 0:1]), [ts], [ts])
                    S.op("dve", lambda e: e.tensor_tensor(out=s_[:, 2:3], in0=s_[:, 1:2], in1=gate[:, i, h * 3 + br:h * 3 + br + 1],
                                                          op=ALU.mult), [ts, t_V[i]], [ts])
                    dst = o_nsa[:, i, h * 64:(h + 1) * 64]
                    if first:
                        S.op("dve", lambda e: e.tensor_scalar(out=dst, in0=ps[:, 0:64], scalar1=s_[:, 2:3], scalar2=None, op0=ALU.mult),
                             [pt, ts], [t_on[i]])
                    else:
                        S.op("dve", lambda e: e.scalar_tensor_tensor(out=dst, in0=ps[:, 0:64], scalar=s_[:, 2:3], in1=dst,
                                                                     op0=ALU.mult, op1=ALU.add), [pt, ts], [t_on[i]])
                    return s_, ts

                for h in range(8):
                    g = h // 4
                    p = h % 2
                    src = bass.AP(tensor=F_d.tensor, offset=h * FLEN + 1, ap=[[16, 127], [1, S_]])
                    S.dma(H16[p][0:127, :], src, reads=[t_Fd], writes=[t_H16[p]])
                    for tc in range(4):
                        ps, pt = psbank()
                        mm(ps[0:127, :], J127[0:127, 0:127], H16[p][0:127, tc * 512:(tc + 1) * 512], True, True, [t_c2, t_H16[p]], [pt])
                        copy(evac_eng(), Phi[p][0:127, tc * 512:(tc + 1) * 512], ps[0:127, :], [pt], [t_Phi[p]])
                    for tc in range(4):
                        ps, pt = psbank()
                        mm(ps[0:127, :], kcT[:, g, 0:127], qT[:, h, tc * 512:(tc + 1) * 512], True, False, [t_kc[g], t_q[h][tc]], [pt])
                        mm(ps[0:127, :], ident_bf[0:127, 0:127], Phi[p][0:127, tc * 512:(tc + 1) * 512], False, True, [t_const, t_Phi[p]], [pt])
                        S.op("act", lambda e: e.activation(out=PTc[p][0:127, tc * 512:(tc + 1) * 512], in_=ps[0:127, :], func=AF.Exp,
                                                           scale=ATT_SCALE), [pt], [t_PTc[p]])
                    for i in range(NT):
                        ps, pt = psbank()
                        mm(ps[:, 0:98], PTc[p][0:127, i * 128:(i + 1) * 128], VC[0:127, g, :], True, True, [t_PTc[p], t_VC[g]], [pt])
                        s_, ts = evac_attn(ps, pt, i, h, 0, True)
                        dsti = imp[:, i, g, :]
                        if h % 4 == 0:
                            S.op("dve", lambda e: e.tensor_scalar(out=dsti, in0=ps[:, 65:97], scalar1=s_[:, 1:2], scalar2=None, op0=ALU.mult),
                                 [pt, ts], [t_imp[i][g]])
                        else:
                            S.op("dve", lambda e: e.scalar_tensor_tensor(out=dsti, in0=ps[:, 65:97], scalar=s_[:, 1:2], in1=dsti,
                                                                         op0=ALU.mult, op1=ALU.add), [pt, ts], [t_imp[i][g]])
                if "cmp" in debug:
                    o = dbg_out("o_cmp", [128, NT, 512])
                    S.dma(o[:, :, :], o_nsa[:], reads=t_on, writes=[dbg_tok])
                    o = dbg_out("imp", [128, NT, 2, 32])
                    S.dma(o[:, :, :, :], imp[:], reads=[t for l in t_imp for t in l], writes=[dbg_tok])
                scb = [sb("scb%d" % i, [128, 32], F32, s3a) for i in range(2)]
                cmb = [sb("cmb%d" % i, [128, 32, 32], F32, s3a) for i in range(2)]
                rkb = [sb("rkb%d" % i, [128, 32], F32, s3a) for i in range(2)]
                t_sel = [Tok(), Tok()]
                for i in range(NT):
                    for g in range(2):
                        p = (i * 2 + g) % 2
                        ts = t_sel[p]
                        S.op("dve", lambda e: e.tensor_tensor(out=scb[p][:], in0=imp[:, i, g, :], in1=Csel[:, i, :], op=ALU.add),
                             [t_imp[i][g], t_c2], [ts])
                        S.op("dve", lambda e: e.tensor_tensor(out=cmb[p][:], in0=scb[p][:].unsqueeze(1).to_broadcast([128, 32, 32]),
                                                              in1=scb[p][:].unsqueeze(2).to_broadcast([128, 32, 32]), op=ALU.is_gt), [ts], [ts])
                        S.op("dve", lambda e: e.tensor_reduce(out=rkb[p][:], in_=cmb[p][:], axis=AX.X, op=ALU.add), [ts], [ts])
                        S.op("dve", lambda e: e.tensor_scalar(out=rkb[p][:], in0=rkb[p][:], scalar1=16.0, scalar2=BIGNEG, op0=ALU.is_ge,
                                                              op1=ALU.mult), [ts], [ts])
                        ps, pt = psbank()
                        S.op("pe", lambda e: e.transpose(out=ps[0:32, 0:128], in_=rkb[p][:], identity=ident_f[:]), [ts, t_const], [pt])
                        copy("act", MnegT[:, g, i * 128:(i + 1) * 128], ps[0:32, 0:128], [pt], [t_Mn[g][i]])
                if "cmp" in debug:
                    o = dbg_out("MnegT", [32, 2, S_], BF16)
                    S.dma(o[:, :, :], MnegT[:], reads=[t for l in t_Mn for t in l], writes=[dbg_tok])
                if stage <= 2:
                    s3a.close()
                    S.finish(final_toks)
                    return nc, hc, DBG
                s3a.close()
                S.barrier()
                cnt = 0
                for h in range(8):
                    g = h // 4
                    for i in range(NT):
                        p = cnt % 2
                        cnt += 1
                        qsl = slice(i * 128, (i + 1) * 128)
                        nk = i + 1
                        kts = list(range(max(0, i - 4), i + 1))
                        for kt in range(nk):
                            b = kt // 4
                            out = PS[b][:, (kt % 4) * 128:(kt % 4 + 1) * 128]
                            pt = PT[b]
                            dl = i - kt
                            mm(out, ident_bf[:], Theta[:, h, min(dl, 2), :], True, False, [t_const, t_Th[h]], [pt])
                            mm(out, kT[:, g, kt * 128:(kt + 1) * 128], qT[:, h, qsl], False, False, [t_k[g][kt // 4], t_q[h][i // 4]], [pt])
                            mm(out, Expand[:, kt, :], MnegT[:, g, qsl], False, True, [t_c2, t_Mn[g][i]], [pt])
                        for n_, kt in enumerate(kts):
                            b = 4 + n_ // 4
                            out = PS[b][:, (n_ % 4) * 128:(n_ % 4 + 1) * 128]
                            pt = PT[b]
                            dl = i - kt
                            mm(out, ident_bf[:], Theta[:, h, min(dl, 2), :], True, False, [t_const, t_Th[h]], [pt])
                            if dl == 4:
                                mm(out, ident_bf[:], TriLT[:], False, False, [t_const, t_c2], [pt])
                            mm(out, kT[:, 2 + g, kt * 128:(kt + 1) * 128], qT[:, h, qsl], False, True, [t_k[2 + g][kt // 4], t_q[h][i // 4]], [pt])
                        for b in range((nk + 3) // 4):
                            n = min(4, nk - b * 4) * 128
                            S.op("act", lambda e: e.activation(out=PTs[p][:, b * 512:b * 512 + n], in_=PS[b][:, 0:n], func=AF.Exp,
                                                               scale=ATT_SCALE), [PT[b]], [t_PTs[p]])
                        for b in range((len(kts) + 3) // 4):
                            n = min(4, len(kts) - b * 4) * 128
                            S.op("act", lambda e: e.activation(out=PTw[p][:, b * 512:b * 512 + n], in_=PS[4 + b][:, 0:n], func=AF.Exp,
                                                               scale=ATT_SCALE), [PT[4 + b]], [t_PTw[p]])
                        ps, pt = psbank(6, 8)
                        for kt in range(nk):
                            mm(ps[:, 0:65], PTs[p][:, kt * 128:(kt + 1) * 128], Vt[:, kt, g, :], kt == 0, kt == nk - 1,
                               [t_PTs[p], t_V[kt]], [pt])
                        evac_attn(ps, pt, i, h, 1, False)
                        ps, pt = psbank(6, 8)
                        for n_, kt in enumerate(kts):
                            mm(ps[:, 0:65], PTw[p][:, n_ * 128:(n_ + 1) * 128], Vt[:, kt, 2 + g, :], n_ == 0, n_ == len(kts) - 1,
                               [t_PTw[p], t_V[kt]], [pt])
                        evac_attn(ps, pt, i, h, 2, False)
                onv = onT_d.rearrange("(cc p) t -> p cc t", p=128)
                stT = [sb("stT%d" % i, [128, 4, 128], BF16, s3) for i in range(2)]
                t_stT = [Tok(), Tok()]
                for i in range(NT):
                    p = i % 2
                    ps, pt = psbank(6, 8)
                    for cc in range(4):
                        S.op("pe", lambda e: e.transpose(out=ps[:, cc * 128:(cc + 1) * 128], in_=o_nsa[:, i, cc * 128:(cc + 1) * 128],
                                                         identity=ident_f[:]), [t_on[i], t_const], [pt])
                    copy(evac_eng(), stT[p][:], ps.rearrange("p (c t) -> p c t", c=4), [pt], [t_stT[p]])
                    S.dma(onv[:, :, i * 128:(i + 1) * 128], stT[p][:], reads=[t_stT[p]], writes=[t_onT[i]])
            S.barrier()
            if "theta2" in debug:
                o = dbg_out("Theta2", [128, 8, 3, 128], BF16)
                S.dma(o[:, :, :, :], Theta[:], reads=t_Th, writes=[dbg_tok])
        S.barrier()
        if "nsa" in debug:
            o = dbg_out("o_nsa", [128, NT, 512])
            S.dma(o[:, :, :], o_nsa[:], reads=t_on, writes=[dbg_tok])
        if stage <= 3:
            S.finish(final_toks)
            return nc, hc, DBG

        T1 = S_ + 1
        with ExitStack() as sr:
            MaskAB = sb("MaskAB", [128, 256], F32, sr)
            MaskSL = sb("MaskSL", [128, 128], F32, sr)
            ones64 = sb("ones64", [64, 64], F32, sr)
            onesm = sb("onesm", [64, 64], F32, sr)
            muT = sb("muT", [64, 28], F32, sr)
            mu128 = sb("mu128", [128, 1], F32, sr)
            par = sb("par", [64, 7, 8], F32, sr)
            parn = sb("parn", [64, 8], F32, sr)
            w2f = sb("w2f", [64, 512], F32, sr)
            a2f = sb("a2f", [64, 512], F32, sr)
            g2f = sb("g2f", [128, 512], F32, sr)
            t_rc = Tok("rconst")
            S.dma(MaskAB[:], C["MaskAB"][:, :], writes=[t_rc])
            S.dma(MaskSL[:], C["MaskSL"][:, :], writes=[t_rc])
            S.dma(muT[:], I["token_mu"].rearrange("(c p) -> p c", p=64), writes=[t_rc])
            S.dma(mu128[:], I["token_mu"][1664:1792].unsqueeze(1), writes=[t_rc])
            for k_, nm in enumerate(["rwkv_w0", "rwkv_a0", "rwkv_k_k", "rwkv_k_a", "rwkv_r_k", "rwkv_lnx_g", "rwkv_lnx_b"]):
                S.dma(par[:, k_, :], I[nm].rearrange("(h p) -> p h", p=64), writes=[t_rc])
            S.dma(w2f[:], I["rwkv_w2"][:, :], writes=[t_rc])
            S.dma(a2f[:], I["rwkv_a2"][:, :], writes=[t_rc])
            S.dma(g2f[:], I["rwkv_g2"][:, :], writes=[t_rc])
            S.op("pool", lambda e: e.memset(ones64[:], 1.0), [], [t_rc])
            S.op("pool", lambda e: e.memset(onesm[:], 1.0 / 64.0), [], [t_rc])
            S.op("dve", lambda e: e.tensor_scalar(out=parn[:], in0=par[:, 0, :], scalar1=-1.0, scalar2=None, op0=ALU.mult), [t_rc], [t_rc])

            raw = sb("raw", [128, T1], F32, sr)
            Db = sb("Db", [128, S_], F32, sr)
            t_raw = Tok("raw")
            t_D = Tok("D")
            S.op("pool", lambda e: e.memset(raw[:, 0:1], 0.0), [], [t_raw])

            def shift(P, mu_ap, out_ap, t_out):
                S.op("dve", lambda e: e.tensor_tensor(out=Db[0:P, :], in0=raw[0:P, 0:S_], in1=raw[0:P, 1:T1], op=ALU.subtract),
                     [t_raw], [t_D])
                S.op("dve", lambda e: e.scalar_tensor_tensor(out=out_ap, in0=Db[0:P, :], scalar=mu_ap, in1=raw[0:P, 1:T1],
                                                             op0=ALU.mult, op1=ALU.add), [t_D, t_raw, t_rc], [t_out])

            TW = sb("TW", [64, S_], F32, sr)
            AL = sb("AL", [64, S_], F32, sr)
            GL = sb("GL", [128, S_], F32, sr)
            t_lora = Tok("lora")
            S.dma(raw[0:64, 1:T1], rw_d[1536:1600, :], reads=[t_rwd[24]], writes=[t_raw])
            shift(64, muT[:, 24:25], TW[:], t_lora)
            S.op("act", lambda e: e.activation(out=TW[:], in_=TW[:], func=AF.Tanh), [t_lora], [t_lora])
            S.dma(raw[0:64, 1:T1], rw_d[1600:1664, :], reads=[t_rwd[25]], writes=[t_raw])
            shift(64, muT[:, 25:26], AL[:], t_lora)
            S.dma(raw[:, 1:T1], rw_d[1664:1792, :], reads=[t_rwd[26], t_rwd[27]], writes=[t_raw])
            shift(128, mu128[:, 0:1], GL[:], t_lora)
            S.op("act", lambda e: e.activation(out=GL[:], in_=GL[:], func=AF.Sigmoid), [t_lora], [t_lora])

            Rb = sb("Rb", [64, S_], F32, sr)
            Kb = sb("Kb", [64, S_], F32, sr)
            Vb = sb("Vb", [64, S_], F32, sr)
            Ab = sb("Ab", [64, S_], F32, sr)
            Eb = sb("Eb", [64, S_], F32, sr)
            Lb = sb("Lb", [64, S_], F32, sr)
            AR = sb("AR", [64, NT, 256], F32, sr)
            BT = sb("BT", [64, S_], F32, sr)
            KT = sb("KT", [64, S_], F32, sr)
            tok = sb("tok", [128, NT, 192], F32, sr)
            WCb = sb("WCb", [64, NT], F32, sr)
            Rk = sb("Rk", [64, 64], F32, sr)
            ybf = sb("ybf", [64, S_], BF16, sr)
            ST = [sb("ST%d" % i, [64, 64], F32, sr) for i in range(2)]
            NSET = 4
            AB1 = [sb("AB1_%d" % i, [128, 256], F32, sr) for i in range(NSET)]
            AB2 = [sb("AB2_%d" % i, [128, 256], F32, sr) for i in range(NSET)]
            Pp = [[sb("Pp%d_%d" % (i, j), [128, 128], BF16, sr) for j in range(2)] for i in range(NSET)]
            PTp = [[sb("PTp%d_%d" % (i, j), [128, 128], BF16, sr) for j in range(2)] for i in range(NSET)]
            Np = [[sb("Np%d_%d" % (i, j), [128, 128], BF16, sr) for j in range(2)] for i in range(NSET)]
            XTs = sb("XTs", [128, 64], BF16, sr)
            UTs = sb("UTs", [128, 64], F32, sr)
            t_R, t_K, t_V, t_A, t_E, t_L, t_AR, t_BT, t_KT, t_WC, t_Rk, t_ybf, t_XT, t_UT = [Tok() for _ in range(14)]
            t_tok = [Tok() for _ in range(NT)]
            t_ST = [Tok(), Tok()]
            t_set = [[Tok() for _ in range(6)] for _ in range(NSET)]
            v3 = lambda ap: ap.rearrange("p (n t) -> p n t", t=128)
            onesb = ones64[:, 0:1].to_broadcast([64, S_])

            for h in range(8):
                hs = slice(h * 64, (h + 1) * 64)
                for (row0, ci, dst, td) in ((h * 64, h, Rb, t_R), (512 + h * 64, 8 + h, Kb, t_K), (1024 + h * 64, 16 + h, Vb, t_V)):
                    S.dma(raw[0:64, 1:T1], rw_d[row0:row0 + 64, :], reads=[t_rwd[ci]], writes=[t_raw])
                    shift(64, muT[:, ci:ci + 1], dst[:], td)
                for tc in range(4):
                    ps, pt = psbank()
                    mm(ps[0:64, :], w2f[:, hs], TW[:, tc * 512:(tc + 1) * 512], True, True, [t_rc, t_lora], [pt])
                    S.op("act", lambda e: e.activation(out=Db[0:64, tc * 512:(tc + 1) * 512], in_=ps[0:64, :], func=AF.Exp,
                                                       bias=parn[:, h:h + 1], scale=-1.0), [pt, t_rc], [t_D])
                S.op("act", lambda e: e.activation(out=Db[0:64, :], in_=Db[0:64, :], func=AF.Ln, bias=1.0, scale=1.0), [t_D], [t_D])
                S.op("act", lambda e: e.activation(out=Db[0:64, :], in_=Db[0:64, :], func=AF.Exp, bias=-0.5, scale=-1.0), [t_D], [t_D])
                S.op("dve", lambda e: e.tensor_tensor_scan(out=raw[0:64, 1:T1], data0=onesb, data1=Db[0:64, :], initial=0.0,
                                                           op0=ALU.mult, op1=ALU.subtract), [t_D, t_rc], [t_raw])
                S.op("dve", lambda e: e.tensor_tensor(out=v3(Lb[:]), in0=v3(raw[0:64, 1:T1]),
                                                      in1=raw[0:64, 0:S_:128].unsqueeze(2).to_broadcast([64, NT, 128]), op=ALU.subtract),
                     [t_raw], [t_L])
                for tc in range(4):
                    ps, pt = psbank()
                    mm(ps[0:64, :], a2f[:, hs], AL[:, tc * 512:(tc + 1) * 512], True, True, [t_rc, t_lora], [pt])
                    S.op("act", lambda e: e.activation(out=Ab[:, tc * 512:(tc + 1) * 512], in_=ps[0:64, :], func=AF.Sigmoid,
                                                       bias=par[:, 1, h:h + 1], scale=1.0), [pt, t_rc], [t_A])
                S.op("dve", lambda e: e.tensor_scalar(out=Eb[:], in0=Kb[:], scalar1=par[:, 2, h:h + 1], scalar2=None, op0=ALU.mult),
                     [t_K, t_rc], [t_E])
                S.op("dve", lambda e: e.tensor_tensor(out=BT[:], in0=Eb[:], in1=Eb[:], op=ALU.mult), [t_E], [t_BT])
                for tc in range(4):
                    ps, pt = psbank()
                    mm(ps[0:64, :], ones64[:], BT[:, tc * 512:(tc + 1) * 512], True, True, [t_rc, t_BT], [pt])
                    S.op("act", lambda e: e.activation(out=KT[:, tc * 512:(tc + 1) * 512], in_=ps[0:64, :], func=AF.Sqrt), [pt], [t_KT])
                S.op("dve", lambda e: e.tensor_scalar(out=KT[:], in0=KT[:], scalar1=1e-12, scalar2=None, op0=ALU.max), [t_KT], [t_KT])
                S.op("dve", lambda e: e.reciprocal(out=KT[:], in_=KT[:]), [t_KT], [t_KT])
                S.op("dve", lambda e: e.tensor_tensor(out=Eb[:], in0=Eb[:], in1=KT[:], op=ALU.mult), [t_E, t_KT], [t_E])
                S.op("dve", lambda e: e.tensor_scalar(out=BT[:], in0=Ab[:], scalar1=-1.0, scalar2=par[:, 3, h:h + 1], op0=ALU.add, op1=ALU.mult),
                     [t_A, t_rc], [t_BT])
                S.op("dve", lambda e: e.scalar_tensor_tensor(out=Kb[:], in0=BT[:], scalar=1.0, in1=Kb[:], op0=ALU.add, op1=ALU.mult),
                     [t_BT, t_K], [t_K])
                S.op("dve", lambda e: e.tensor_tensor(out=Db[0:64, :], in0=Db[0:64, :], in1=Lb[:], op=ALU.add), [t_D, t_L], [t_D])
                S.op("act", lambda e: e.activation(out=Db[0:64, :], in_=Db[0:64, :], func=AF.Exp), [t_D], [t_D])
                S.op("dve", lambda e: e.scalar_tensor_tensor(out=AR[:, :, 0:128], in0=v3(Eb[:]), scalar=-1.0, in1=v3(Db[0:64, :]),
                                                             op0=ALU.mult, op1=ALU.mult), [t_E, t_D], [t_AR])
                S.op("act", lambda e: e.activation(out=raw[0:64, 1:T1], in_=Lb[:], func=AF.Exp), [t_L], [t_raw])
                S.op("dve", lambda e: e.tensor_tensor(out=AR[:, :, 128:256], in0=v3(Rb[:]), in1=v3(raw[0:64, 1:T1]), op=ALU.mult),
                     [t_R, t_raw], [t_AR])
                S.op("dve", lambda e: e.tensor_copy(out=WCb[:], in_=raw[0:64, 128:T1:128]), [t_raw], [t_WC])
                S.op("act", lambda e: e.activation(out=Db[0:64, :], in_=Lb[:], func=AF.Exp, scale=-1.0), [t_L, t_AR], [t_D])
                S.op("dve", lambda e: e.tensor_tensor(out=BT[:], in0=Eb[:], in1=Ab[:], op=ALU.mult), [t_E, t_A], [t_BT])
                S.op("dve", lambda e: e.tensor_tensor(out=BT[:], in0=BT[:], in1=Db[0:64, :], op=ALU.mult), [t_BT, t_D], [t_BT])
                S.op("dve", lambda e: e.tensor_tensor(out=KT[:], in0=Kb[:], in1=Db[0:64, :], op=ALU.mult), [t_K, t_D], [t_KT])
                wcb = WCb[:].unsqueeze(2).to_broadcast([64, NT, 128])
                S.op("dve", lambda e: e.tensor_tensor(out=v3(raw[0:64, 1:T1]), in0=v3(BT[:]), in1=wcb, op=ALU.mult), [t_BT, t_WC], [t_raw])
                S.op("dve", lambda e: e.tensor_tensor(out=v3(Db[0:64, :]), in0=v3(KT[:]), in1=wcb, op=ALU.mult), [t_KT, t_WC], [t_D])
                for n in range(NT):
                    ps, pt = psbank()
                    tsl = slice(n * 128, (n + 1) * 128)
                    S.op("pe", lambda e: e.transpose(out=ps[:, 0:64], in_=Vb[:, tsl], identity=ident_f[0:64, 0:64]), [t_V, t_const], [pt])
                    S.op("pe", lambda e: e.transpose(out=ps[:, 64:128], in_=raw[0:64, 1 + n * 128:1 + (n + 1) * 128],
                                                     identity=ident_f[0:64, 0:64]), [t_raw, t_const], [pt])
                    S.op("pe", lambda e: e.transpose(out=ps[:, 128:192], in_=Db[0:64, tsl], identity=ident_f[0:64, 0:64]), [t_D, t_const], [pt])
                    copy(evac_eng(), tok[:, n, :], ps[:, 0:192], [pt], [t_tok[n]])
                S.op("dve", lambda e: e.tensor_copy(out=Rk[:], in_=par[:, 4, h:h + 1].to_broadcast([64, 64])), [t_rc], [t_Rk])
                S.op("pool", lambda e: e.memset(ST[0][:], 0.0), [], [t_ST[0]])

                def precompute(n, s):
                    tsl = slice(n * 128, (n + 1) * 128)
                    tk = t_set[s]
                    ps, pt = psbank()
                    mm(ps[:, 0:256], BT[:, tsl], AR[:, n, :], True, True, [t_BT, t_AR], [pt])
                    S.op("dve", lambda e: e.tensor_tensor(out=AB1[s][:], in0=ps[:, 0:256], in1=MaskAB[:], op=ALU.mult), [pt, t_rc], [tk[0]])
                    yield
                    ps, pt = psbank()
                    mm(ps[:, 0:256], KT[:, tsl], AR[:, n, :], True, True, [t_KT, t_AR], [pt])
                    S.op("dve", lambda e: e.tensor_tensor(out=AB2[s][:], in0=ps[:, 0:256], in1=MaskAB[:], op=ALU.mult), [pt, t_rc], [tk[1]])
                    yield
                    ps, pt = psbank()
                    mm(ps[:, 0:128], AR[:, n, 0:128], BT[:, tsl], True, True, [t_BT, t_AR], [pt])
                    S.op("dve", lambda e: e.tensor_tensor(out=PTp[s][0][:], in0=ps[:, 0:128], in1=MaskSL[:], op=ALU.mult), [pt, t_rc], [tk[3]])
                    yield
                    S.op("pool", lambda e: e.tensor_copy(out=Pp[s][0][:], in_=AB1[s][:, 0:128]), [tk[0]], [tk[2]])
                    S.op("pool", lambda e: e.tensor_tensor(out=Np[s][0][:], in0=AB1[s][:, 0:128], in1=ident_f[:], op=ALU.add), [tk[0], t_const], [tk[4]])
                    yield
                    cur = 0
                    for j in range(1, 7):
                        nxt = 1 - cur
                        if j < 6:
                            ps, pt = psbank()
                            mm(ps[:, 0:128], PTp[s][cur][:], Pp[s][cur][:], True, True, [tk[2], tk[3]], [pt])
                            ps2, pt2 = psbank()
                            mm(ps2[:, 0:128], Pp[s][cur][:], PTp[s][cur][:], True, True, [tk[2], tk[3]], [pt2])
                            copy("act", Pp[s][nxt][:], ps[:, 0:128], [pt], [tk[2]])
                            copy("dve", PTp[s][nxt][:], ps2[:, 0:128], [pt2], [tk[3]])
                        else:
                            ps2, pt2 = psbank()
                            mm(ps2[:, 0:128], Pp[s][cur][:], PTp[s][cur][:], True, True, [tk[2], tk[3]], [pt2])
                            copy("dve", PTp[s][nxt][:], ps2[:, 0:128], [pt2], [tk[3]])
                        yield
                        ps, pt = psbank()
                        mm(ps[:, 0:128], PTp[s][nxt][:], Np[s][cur][:], True, True, [tk[3], tk[4]], [pt])
                        S.op("dve", lambda e: e.tensor_tensor(out=Np[s][nxt][:], in0=ps[:, 0:128], in1=Np[s][cur][:], op=ALU.add), [pt, tk[4]], [tk[4]])
                        cur = nxt
                        yield
                    assert cur == 0

                for n0 in range(0, NT, NSET):
                    gens = [precompute(n0 + s, s) for s in range(NSET)]
                    alive = list(gens)
                    while alive:
                        for g_ in list(alive):
                            try:
                                next(g_)
                            except StopIteration:
                                alive.remove(g_)
                    for s in range(NSET):
                        n = n0 + s
                        tsl = slice(n * 128, (n + 1) * 128)
                        tk = t_set[s]
                        sc, sn_ = ST[n % 2], ST[(n + 1) % 2]
                        tsc, tsn = t_ST[n % 2], t_ST[(n + 1) % 2]
                        Nf = Np[s][0]
                        ps, pt = psbank()
                        mm(ps[:, 0:64], AR[:, n, 0:128], sc[:], True, False, [t_AR, tsc], [pt])
                        mm(ps[:, 0:64], AB2[s][:, 0:128], tok[:, n, 0:64], False, True, [tk[1], t_tok[n]], [pt])
                        copy("act", XTs[:], ps[:, 0:64], [pt], [t_XT])
                        ps, pt = psbank()
                        mm(ps[:, 0:64], Nf[:], XTs[:], True, True, [tk[4], t_XT], [pt])
                        copy("act", UTs[:], ps[:, 0:64], [pt], [t_UT])
                        ps, pt = psbank()
                        mm(ps[0:64, 0:128], sc[:], AR[:, n, 128:256], True, False, [tsc, t_AR], [pt])
                        mm(ps[0:64, 0:128], UTs[:], AB1[s][:, 128:256], False, False, [t_UT, tk[0]], [pt])
                        mm(ps[0:64, 0:128], tok[:, n, 0:64], AB2[s][:, 128:256], False, True, [t_tok[n], tk[1]], [pt])
                        copy("dve", Lb[:, tsl], ps[0:64, 0:128], [pt], [t_L])
                        ps, pt = psbank()
                        mm(ps[0:64, 0:64], tok[:, n, 64:128], UTs[:], True, False, [t_tok[n], t_UT], [pt])
                        mm(ps[0:64, 0:64], tok[:, n, 128:192], tok[:, n, 0:64], False, True, [t_tok[n]], [pt])
                        S.op("dve", lambda e: e.scalar_tensor_tensor(out=sn_[:], in0=sc[:], scalar=WCb[:, n:n + 1], in1=ps[0:64, 0:64],
                                                                     op0=ALU.mult, op1=ALU.add), [tsc, t_WC, pt], [tsn])
                for tc in range(4):
                    csl = slice(tc * 512, (tc + 1) * 512)
                    ps, pt = psbank()
                    mm(ps[0:64, :], onesm[:], Lb[:, csl], True, True, [t_rc, t_L], [pt])
                    S.op("dve", lambda e: e.tensor_tensor(out=Ab[:, csl], in0=Lb[:, csl], in1=ps[0:64, :], op=ALU.subtract), [t_L, pt], [t_A])
                S.op("pool", lambda e: e.tensor_tensor(out=BT[:], in0=Ab[:], in1=Ab[:], op=ALU.mult), [t_A], [t_BT])
                for tc in range(4):
                    csl = slice(tc * 512, (tc + 1) * 512)
                    ps, pt = psbank()
                    mm(ps[0:64, :], onesm[:], BT[:, csl], True, True, [t_rc, t_BT], [pt])
                    S.op("dve", lambda e: e.tensor_scalar(out=KT[:, csl], in0=ps[0:64, :], scalar1=64e-5, scalar2=None, op0=ALU.add), [pt], [t_KT])
                S.op("act", lambda e: e.activation(out=KT[:], in_=KT[:], func=AF.Sqrt), [t_KT], [t_KT])
                S.op("dve", lambda e: e.reciprocal(out=KT[:], in_=KT[:]), [t_KT], [t_KT])
                S.op("dve", lambda e: e.tensor_tensor(out=Ab[:], in0=Ab[:], in1=KT[:], op=ALU.mult), [t_A, t_KT], [t_A])
                S.op("dve", lambda e: e.tensor_scalar(out=Ab[:], in0=Ab[:], scalar1=par[:, 5, h:h + 1], scalar2=par[:, 6, h:h + 1],
                                                      op0=ALU.mult, op1=ALU.add), [t_A, t_rc], [t_A])
                S.op("pool", lambda e: e.tensor_tensor(out=BT[:], in0=Rb[:], in1=Kb[:], op=ALU.mult), [t_R, t_K], [t_BT])
                for tc in range(4):
                    csl = slice(tc * 512, (tc + 1) * 512)
                    ps, pt = psbank()
                    mm(ps[0:64, :], Rk[:], BT[:, csl], True, True, [t_Rk, t_BT], [pt])
                    S.op("dve", lambda e: e.tensor_tensor(out=KT[:, csl], in0=ps[0:64, :], in1=Vb[:, csl], op=ALU.mult), [pt, t_V], [t_KT])
                S.op("dve", lambda e: e.tensor_tensor(out=Ab[:], in0=Ab[:], in1=KT[:], op=ALU.add), [t_A, t_KT], [t_A])
                for tc in range(4):
                    csl = slice(tc * 512, (tc + 1) * 512)
                    ps, pt = psbank()
                    mm(ps[0:64, :], g2f[:, hs], GL[:, csl], True, True, [t_rc, t_lora], [pt])
                    S.op("dve", lambda e: e.tensor_tensor(out=ybf[:, csl], in0=ps[0:64, :], in1=Ab[:, csl], op=ALU.mult), [pt, t_A], [t_ybf])
                S.dma(yrw_d[h * 64:(h + 1) * 64, :], ybf[:], reads=[t_ybf], writes=[t_yrw[h]])
                if "rwkv1" in debug and h == 0:
                    break
        S.barrier()
        if "rwkv" in debug or "rwkv1" in debug:
            o = dbg_out("yrw", [512, S_], BF16)
            ytmp = sb("ytmp", [128, 4, S_], BF16)
            tt_ = Tok()
            S.dma(ytmp[:], yrw_d.rearrange("(c p) t -> p c t", p=128), reads=t_yrw, writes=[tt_])
            S.dma(o.rearrange("(c p) t -> p c t", p=128), ytmp[:], reads=[tt_], writes=[dbg_tok])
        if stage <= 4:
            S.finish(final_toks)
            return nc, hc, DBG

        x2_d = nc.dram_tensor("x2_scr", [S_, D_], F32, kind="Internal").ap()
        t_x2d = [Tok("x2d%d" % i) for i in range(NT)]
        x2T_d = nc.dram_tensor("x2T_scr", [128, 8, S_], BF16, kind="Internal").ap()
        t_x2T = [Tok("x2T%d" % i) for i in range(NT)]
        x2v = x2_d.rearrange("(n p) d -> n p d", p=128)
        load_gb(I["ln_mix_g"], I["ln_mix_b"])
        with ExitStack() as sm_:
            wn_b = sb("wn_b", [128, 4, D_], BF16, sm_)
            wr_b = sb("wr_b", [128, 4, D_], BF16, sm_)
            wo_b = sb("wo_b", [128, 8, D_], BF16, sm_)
            ynT = sb("ynT", [128, 4, S_], BF16, sm_)
            yrT = sb("yrT", [128, 4, S_], BF16, sm_)
            t_mw = Tok("mw")
            t_yn = Tok("yn")
            S.dma(wn_b[:], I["w_o_nsa"].rearrange("(k p) d -> p k d", p=128), writes=[t_mw], q="pool")
            S.dma(wr_b[:], I["w_o_rwkv"].rearrange("(k p) d -> p k d", p=128), writes=[t_mw], q="pool")
            for k2 in range(2):
                S.dma(wo_b[:, k2 * 4:(k2 + 1) * 4, :], I["w_out"].rearrange("(k p) d -> p k d", p=128)[:, k2 * 4:(k2 + 1) * 4, :], writes=[t_mw], q="pool")
            S.dma(ynT[:], onT_d.rearrange("(c p) t -> p c t", p=128), reads=t_onT, writes=[t_yn])
            S.dma(yrT[:], yrw_d.rearrange("(c p) t -> p c t", p=128), reads=t_yrw, writes=[t_yn])
            gtb = [sb("gtb%d" % i, [128, 2048], F32, sm_) for i in range(2)]
            hb2 = [sb("hb2_%d" % i, [128, D_], F32, sm_) for i in range(2)]
            m1b = [sb("m1b%d" % i, [128, D_], F32, sm_) for i in range(2)]
            mbb = [sb("mbb%d" % i, [128, D_], BF16, sm_) for i in range(2)]
            mTb = [sb("mTb%d" % i, [128, 8, 128], BF16, sm_) for i in range(2)]
            rsb = [sb("rsb%d" % i, [128, D_], F32, sm_) for i in range(2)]
            xnb2 = [sb("xnb2_%d" % i, [128, D_], F32, sm_) for i in range(2)]
            x2b = [sb("x2b%d" % i, [128, D_], F32, sm_) for i in range(2)]
            x2h = [sb("x2h%d" % i, [128, D_], BF16, sm_) for i in range(2)]
            tg, th2, tm1, tmb, tmT, trs, txn, tx2, tx2h = [[Tok(), Tok()] for _ in range(9)]
            mgv2 = mg_d.rearrange("(n p) c -> n p c", p=128)
            for i in range(NT):
                p = i % 2
                tsl = slice(i * 128, (i + 1) * 128)
                S.dma(gtb[p][:], mgv2[i], reads=[t_mgd[i]], writes=[tg[p]])
                S.dma(hb2[p][:], hv[i], reads=[t_hd[i]], writes=[th2[p]])
                for half in range(2):
                    dsl = slice(half * 512, (half + 1) * 512)
                    for c in range(4):
                        mm(PS[half][:, :], ynT[:, c, tsl], wn_b[:, c, dsl], c == 0, c == 3, [t_yn, t_mw], [PT[half]])
                    for c in range(4):
                        mm(PS[2 + half][:, :], yrT[:, c, tsl], wr_b[:, c, dsl], c == 0, c == 3, [t_yn, t_mw], [PT[2 + half]])
                for half in range(2):
                    dsl = slice(half * 512, (half + 1) * 512)
                    S.op("dve", lambda e: e.tensor_tensor(out=m1b[p][:, dsl], in0=PS[half][:, :], in1=gtb[p][:, dsl], op=ALU.mult),
                         [PT[half], tg[p]], [tm1[p]])
                    S.op("dve", lambda e: e.tensor_tensor(out=rsb[p][:, dsl], in0=PS[2 + half][:, :], in1=gtb[p][:, 1024 + half * 512:1024 + (half + 1) * 512],
                                                          op=ALU.mult), [PT[2 + half], tg[p]], [trs[p]])
                S.op("pool", lambda e: e.tensor_tensor(out=mbb[p][:], in0=m1b[p][:], in1=rsb[p][:], op=ALU.add), [tm1[p], trs[p]], [tmb[p]])
                psb = PS[4 + p].bitcast(BF16)
                for dk in range(8):
                    S.op("pe", lambda e: e.transpose(out=psb[:, dk * 128:(dk + 1) * 128], in_=mbb[p][:, dk * 128:(dk + 1) * 128],
                                                     identity=ident_bf[:]), [tmb[p], t_const], [PT[4 + p]])
                copy("act", mTb[p][:], psb.rearrange("p (k t) -> p k t", k=8), [PT[4 + p]], [tmT[p]])
                for half in range(2):
                    dsl = slice(half * 512, (half + 1) * 512)
                    for c in range(8):
                        mm(PS[6 + half][:, :], mTb[p][:, c, :], wo_b[:, c, dsl], c == 0, c == 7, [tmT[p], t_mw], [PT[6 + half]])
                    S.op("dve", lambda e: e.scalar_tensor_tensor(out=rsb[p][:, dsl], in0=hb2[p][:, dsl], scalar=ALPHA, in1=PS[6 + half][:, :],
                                                                 op0=ALU.mult, op1=ALU.add), [th2[p], PT[6 + half], tmb[p]], [trs[p]])
                ln_tile(p, rsb[p][:], x2b[p][:], trs[p], tx2[p], xnb2[p][:], txn[p])
                S.dma(x2v[i], x2b[p][:], reads=[tx2[p]], writes=[t_x2d[i]])
                S.op("act", lambda e: e.copy(out=x2h[p][:], in_=x2b[p][:]), [tx2[p]], [tx2h[p]])
                psb = PS[4 + p].bitcast(BF16)
                for dk in range(8):
                    S.op("pe", lambda e: e.transpose(out=psb[:, dk * 128:(dk + 1) * 128], in_=x2h[p][:, dk * 128:(dk + 1) * 128],
                                                     identity=ident_bf[:]), [tx2h[p], t_const], [PT[4 + p]])
                copy("dve", mTb[p][:], psb.rearrange("p (k t) -> p k t", k=8), [PT[4 + p]], [tmT[p]])
                S.dma(x2T_d[:, :, tsl], mTb[p][:], reads=[tmT[p]], writes=[t_x2T[i]])
        S.barrier()
        if "x2" in debug:
            o = dbg_out("x2", [S_, D_])
            xtmp = sb("xtmp", [128, NT, D_], F32)
            tt2 = Tok()
            S.dma(xtmp[:], x2_d.rearrange("(n p) d -> p n d", p=128), reads=t_x2d, writes=[tt2])
            S.dma(o.rearrange("(n p) d -> p n d", p=128), xtmp[:], reads=[tt2], writes=[dbg_tok])
        if stage <= 5:
            S.finish(final_toks)
            return nc, hc, DBG

        load_gb(I["ln_ffn_g"], I["ln_ffn_b"])
        outv = out_d.rearrange("(n p) d -> n p d", p=128)
        t_out = [Tok("out%d" % i) for i in range(NT)]
        final_toks.extend(t_out)
        uTv = uTb_d.rearrange("c p k e -> p c (k e)")
        vbv = vb_d.rearrange("(c p) d -> p c d", p=128)
        NEGBIG = -1.0e30
        with ExitStack() as sp_:
            skT = sb("skT", [64, 16, 128], BF16, sp_)
            sk_scope = ExitStack()
            skn = sb("skn", [128, 16, 64], F32, sk_scope)
            t_pw = Tok("pw")
            S.dma(skn[:], I["peer_sub_keys"].rearrange("h q n d -> n (h q) d"), writes=[t_pw])
            for hp in range(16):
                ps, pt = psbank()
                S.op("pe", lambda e: e.transpose(out=ps[0:64, 0:128], in_=skn[:, hp, :], identity=ident_f[:]), [t_pw, t_const], [pt])
                copy(evac_eng(), skT[:, hp, :], ps[0:64, 0:128], [pt], [t_pw])
            sk_scope.close()
            S.barrier()
            X1 = sb("X1", [128, 16384], BF16, sp_)
            X2 = sb("X2", [128, 16384], BF16, sp_)
            Trm = sb("Trm", [128, 16384], BF16, sp_)
            Orm = sb("Orm", [128, 16384], BF16, sp_)
            t_Trm, t_Orm = Tok("Trm"), Tok("Orm")
            t_X1p = [Tok("X1_%d" % i) for i in range(8)]
            t_X2p = [Tok("X2_%d" % i) for i in range(8)]
            gt_d = nc.dram_tensor("gt_scr", [NT, 128, 128, 128], BF16, kind="Internal").ap()
            t_gt = [Tok("gt%d" % i) for i in range(NT)]
            wqv = X1[:, 0:8192].rearrange("p (k d) -> p k d", k=8)
            X1v = X1[:].rearrange("p (n h a) -> p n h a", h=8, a=16)
            X2v = X2[:].rearrange("p (n h a) -> p n h a", h=8, a=16)
            cand = X2[:].bitcast(F32)[:, 0:2048].rearrange("p (h c) -> p h c", h=8)
            X2g = X2[:].rearrange("p (n t) -> p n t", t=128)
            Trm3 = Trm[:].rearrange("p (n t) -> p n t", t=128)
            Orm3 = Orm[:].rearrange("p (n t) -> p n t", t=128)
            qb = Trm[:, 0:1024]
            qTi = Trm[0:64, 1024:3072].rearrange("p (a t) -> p a t", t=128)
            wk = Trm[:, 3072:3584].bitcast(F32)
            wk2 = Trm[:, 3584:4096].bitcast(F32)
            x2Tg = Trm[:, 8192:9216].rearrange("p (k t) -> p k t", k=8)
            SE = sb("SE", [128, 8192], BF16, sp_)
            sc = SE[:, 0:4096].bitcast(F32).rearrange("p (a n) -> p a n", n=128)
            Ee = SE[:, 4096:8192].bitcast(F32).rearrange("p (a n) -> p a n", n=128)
            SD = sb("SD", [128, 8192], BF16, sp_)
            x2Td = sb("x2Td", [128, 8, 256], BF16, sp_)
            gtc = [sb("gtc%d" % i, [128, 2, 128], BF16, sp_) for i in range(4)]
            x2t_t = sb("x2t_t", [128, D_], F32, sp_)
            rso_t = sb("rso_t", [128, D_], F32, sp_)
            xno_t = sb("xno_t", [128, D_], F32, sp_)
            x2t, rso, xno = x2t_t[:], rso_t[:], xno_t[:]
            t_fin = Tok("fin")
            s16 = sb("s16", [128, 16, 16], F32, sp_)
            e16 = sb("e16", [128, 16, 16], F32, sp_)
            tops = sb("tops", [128, 8, 24], F32, sp_)
            stt_ = sb("stt_", [128, 8, 8], F32, sp_)
            exps = sb("exps", [128, 8, 16], F32, sp_)
            E1s = sb("E1s", [128, 8, 16], F32, sp_)
            thr = sb("thr", [128, 8, 16], F32, sp_)
            uTc = [SD[:, i * 1024:(i + 1) * 1024].rearrange("p (k e) -> p k e", k=8) for i in range(4)]
            vcb = [SD[:, 4096 + i * 1024:4096 + (i + 1) * 1024] for i in range(4)]
            Hb = [sb("Hb%d" % i, [128, 256], BF16, sp_) for i in range(2)]
            Hg = [sb("Hg%d" % i, [128, 256], BF16, sp_) for i in range(2)]
            (t_s16, t_e16, t_tops, t_st, t_exps, t_E1s, t_thr) = [Tok() for _ in range(7)]
            t_x2Tg = t_qb = t_qTi = t_wk = t_wk2 = t_Trm
            t_uTc = [Tok() for _ in range(4)]
            t_vcb = [Tok() for _ in range(4)]
            t_gtc = [Tok() for _ in range(4)]
            t_x2Td = Tok("x2Td")
            t_scL = [Tok("sc")]
            t_EeL = [Tok("Ee")]
            t_Hb = [Tok(), Tok()]
            t_Hg = [Tok(), Tok()]
            def gphase(i):
                S.dma(x2Tg, x2T_d[:, :, i * 128:(i + 1) * 128], reads=[t_x2T[i]], writes=[t_x2Tg])
                S.dma(wqv, wq_d[:, :, :], reads=[t_wqd], writes=[*t_X1p])
                for half in range(2):
                    ps, pt = psbank(6, 8)
                    for dk in range(8):
                        mm(ps[:, :], x2Tg[:, dk, :], wqv[:, dk, half * 512:(half + 1) * 512], dk == 0, dk == 7, [t_x2Tg, *t_X1p], [pt])
                    copy("act", qb[:, half * 512:(half + 1) * 512], ps[:, :], [pt], [t_qb])
                for b in range(2):
                    ps, pt = psbank(6, 8)
                    psb = ps.bitcast(BF16)
                    for jj in range(8):
                        hp = b * 8 + jj
                        S.op("pe", lambda e: e.transpose(out=psb[0:64, jj * 128:(jj + 1) * 128], in_=qb[:, hp * 64:(hp + 1) * 64],
                                                         identity=ident_bf[:]), [t_qb, t_const], [pt])
                    copy("act", qTi[:, b * 8:(b + 1) * 8, :], psb[0:64, :].rearrange("p (a t) -> p a t", t=128), [pt], [t_qTi])
                for b in range(4):
                    ps, pt = psbank(6, 8)
                    for j in range(4):
                        hp = b * 4 + j
                        mm(ps[:, j * 128:(j + 1) * 128], qTi[:, hp, :], skT[:, hp, :], True, True, [t_qTi, t_pw], [pt])
                    copy("act", sc[:, b * 4:(b + 1) * 4, :], ps[:, :].rearrange("p (a n) -> p a n", n=128), [pt], [*t_scL])
                for hp in range(16):
                    S.op("dve", lambda e: e.max(out=s16[:, hp, 0:8], in_=sc[:, hp, :]), [*t_scL], [t_s16])
                    S.op("dve", lambda e: e.match_replace(out=wk[:, 0:128], in_to_replace=s16[:, hp, 0:8], in_values=sc[:, hp, :], imm_value=NEGBIG),
                         [*t_scL, t_s16], [t_wk])
                    S.op("dve", lambda e: e.max(out=s16[:, hp, 8:16], in_=wk[:, 0:128]), [t_wk], [t_s16])
                s16v = s16[:].rearrange("p (h q) a -> p h q a", q=2)
                S.op("dve", lambda e: e.tensor_tensor(out=cand.rearrange("p h (a b) -> p h a b", b=16),
                                                      in0=s16v[:, :, 0, :].unsqueeze(3).to_broadcast([128, 8, 16, 16]),
                                                      in1=s16v[:, :, 1, :].unsqueeze(2).to_broadcast([128, 8, 16, 16]), op=ALU.add), [t_s16], [*t_X2p])
                for h in range(8):
                    S.op("dve", lambda e: e.max(out=tops[:, h, 0:8], in_=cand[:, h, :]), [*t_X2p], [t_tops])
                    S.op("dve", lambda e: e.match_replace(out=wk[:], in_to_replace=tops[:, h, 0:8], in_values=cand[:, h, :], imm_value=NEGBIG),
                         [*t_X2p, t_tops], [t_wk])
                    S.op("dve", lambda e: e.max(out=tops[:, h, 8:16], in_=wk[:]), [t_wk], [t_tops])
                    S.op("dve", lambda e: e.match_replace(out=wk2[:], in_to_replace=tops[:, h, 8:16], in_values=wk[:], imm_value=NEGBIG),
                         [t_wk, t_tops], [t_wk2])
                    S.op("dve", lambda e: e.max(out=tops[:, h, 16:24], in_=wk2[:]), [t_wk2], [t_tops])
                S.op("dve", lambda e: e.tensor_tensor(out=stt_[:, :, 0:1], in0=tops[:, :, 15:16], in1=tops[:, :, 16:17], op=ALU.add), [t_tops], [t_st])
                S.op("dve", lambda e: e.tensor_scalar(out=stt_[:, :, 0:1], in0=stt_[:, :, 0:1], scalar1=0.5, scalar2=None, op0=ALU.mult), [t_st], [t_st])
                S.op("dve", lambda e: e.tensor_tensor(out=exps[:], in0=tops[:, :, 0:16], in1=tops[:, :, 0:1].to_broadcast([128, 8, 16]),
                                                      op=ALU.subtract), [t_tops], [t_exps])
                S.op("act", lambda e: e.activation(out=exps[:], in_=exps[:], func=AF.Exp), [t_exps], [t_exps])
                S.op("dve", lambda e: e.tensor_reduce(out=stt_[:, :, 1:2], in_=exps[:], axis=AX.X, op=ALU.add), [t_exps, t_st], [t_st])
                S.op("dve", lambda e: e.reciprocal(out=stt_[:, :, 2:3], in_=stt_[:, :, 1:2]), [t_st], [t_st])
                S.op("dve", lambda e: e.tensor_tensor(out=Ee[:], in0=sc[:], in1=s16[:, :, 0:1].to_broadcast([128, 16, 128]), op=ALU.subtract),
                     [*t_scL, t_s16], [*t_EeL])
                S.op("act", lambda e: e.activation(out=Ee[:], in_=Ee[:], func=AF.Exp), [*t_EeL], [*t_EeL])
                S.op("dve", lambda e: e.tensor_tensor(out=e16[:], in0=s16[:], in1=s16[:, :, 0:1].to_broadcast([128, 16, 16]), op=ALU.subtract),
                     [t_s16], [t_e16])
                S.op("act", lambda e: e.activation(out=e16[:], in_=e16[:], func=AF.Exp), [t_e16], [t_e16])
                e16v = e16[:].rearrange("p (h q) a -> p h q a", q=2)
                S.op("dve", lambda e: e.tensor_tensor(out=E1s[:], in0=e16v[:, :, 0, :], in1=stt_[:, :, 2:3].to_broadcast([128, 8, 16]), op=ALU.mult),
                     [t_e16, t_st], [t_E1s])
                S.op("dve", lambda e: e.tensor_tensor(out=thr[:], in0=stt_[:, :, 0:1].to_broadcast([128, 8, 16]), in1=s16v[:, :, 0, :], op=ALU.subtract),
                     [t_s16, t_st], [t_thr])
                scv = sc[:].rearrange("p (h q) n -> p h q n", q=2)
                Eev = Ee[:].rearrange("p (h q) n -> p h q n", q=2)
                sc2b = scv[:, :, 1, :].rearrange("p h n -> p n h").unsqueeze(3).to_broadcast([128, 128, 8, 16])
                sc1b = scv[:, :, 0, :].rearrange("p h n -> p n h").unsqueeze(3).to_broadcast([128, 128, 8, 16])
                E2b = Eev[:, :, 1, :].rearrange("p h n -> p n h").unsqueeze(3).to_broadcast([128, 128, 8, 16])
                thrb = thr[:].unsqueeze(1).to_broadcast([128, 128, 8, 16])
                E1sb = E1s[:].unsqueeze(1).to_broadcast([128, 128, 8, 16])
                s1b = s16v[:, :, 0, :].unsqueeze(1).to_broadcast([128, 128, 8, 16])
                for pc in range(8):
                    nsl = slice(pc * 16, (pc + 1) * 16)
                    csl = slice(pc * 2048, (pc + 1) * 2048)
                    S.op("dve", lambda e: e.tensor_tensor(out=X1v[:, nsl], in0=sc2b[:, nsl], in1=thrb[:, nsl], op=ALU.is_ge), [*t_scL, t_thr], [t_X1p[pc]])
                    S.op("pool", lambda e: e.tensor_tensor(out=X2v[:, nsl], in0=E2b[:, nsl], in1=E1sb[:, nsl], op=ALU.mult), [*t_EeL, t_E1s], [t_X2p[pc]])
                    S.op("dve", lambda e: e.tensor_tensor(out=X1[:, csl], in0=X1[:, csl], in1=X2[:, csl], op=ALU.mult), [t_X1p[pc], t_X2p[pc]], [t_X1p[pc]])
                    for j8 in range(2):
                        ps, pt = psbank(6, 8)
                        psb = ps.bitcast(BF16)
                        for jj in range(8):
                            n = pc * 16 + j8 * 8 + jj
                            S.op("pe", lambda e: e.transpose(out=psb[:, jj * 128:(jj + 1) * 128], in_=X1[:, n * 128:(n + 1) * 128], identity=ident_bf[:]),
                                 [t_X1p[pc], t_const], [pt])
                        copy("act", Trm3[:, pc * 16 + j8 * 8:pc * 16 + (j8 + 1) * 8, :], psb.rearrange("p (n t) -> p n t", t=128), [pt], [t_Trm])
                for pc in range(8):
                    nsl = slice(pc * 16, (pc + 1) * 16)
                    S.op("dve", lambda e: e.tensor_tensor(out=X1v[:, nsl], in0=sc1b[:, nsl], in1=s1b[:, nsl], op=ALU.is_equal), [*t_scL, t_s16], [t_X1p[pc]])
                    for j8 in range(2):
                        ps, pt = psbank(6, 8)
                        psb = ps.bitcast(BF16)
                        for jj in range(8):
                            n = pc * 16 + j8 * 8 + jj
                            S.op("pe", lambda e: e.transpose(out=psb[:, jj * 128:(jj + 1) * 128], in_=X1[:, n * 128:(n + 1) * 128], identity=ident_bf[:]),
                                 [t_X1p[pc], t_const], [pt])
                        copy("act", Orm3[:, pc * 16 + j8 * 8:pc * 16 + (j8 + 1) * 8, :], psb.rearrange("p (n t) -> p n t", t=128), [pt], [t_Orm])
                for t4 in range(32):
                    ps, pt = psbank(6, 8)
                    for tt in range(4):
                        t = t4 * 4 + tt
                        mm(ps[:, tt * 128:(tt + 1) * 128], Trm3[:, :, t], Orm3[:, :, t], True, True, [t_Trm, t_Orm], [pt])
                    copy("act", X2g[:, :, t4 * 4:(t4 + 1) * 4].rearrange("p n t -> p t n"), ps.rearrange("p (t n) -> p t n", n=128), [pt], [*t_X2p])
                S.dma(gt_d[i].rearrange("p n t -> p (n t)"), X2[:], reads=[*t_X2p], writes=[t_gt[i]])


            def d_load(g2, n):
                p = n % 4
                S.dma(uTc[p].rearrange("p k e -> p (k e)"), uTv[:, n, :], reads=[t_uTb[k_ * 2 + n // 64] for k_ in range(8)], writes=[t_uTc[p]])
                S.dma(vcb[p], vbv[:, n, :], reads=[t_vb[n // 8]], writes=[t_vcb[p]])
                S.dma(gtc[p][:], gt_d[2 * g2:2 * g2 + 2, :, n, :].rearrange("s p t -> p s t"), reads=[t_gt[2 * g2], t_gt[2 * g2 + 1]], writes=[t_gtc[p]])

            def d_hpre(n):
                p, pp = n % 4, n % 2
                for dk in range(8):
                    mm(PS[4 + pp][:, 0:256], uTc[p][:, dk, :], x2Td[:, dk, :], dk == 0, dk == 7, [t_uTc[p], t_x2Td], [PT[4 + pp]])

            def d_gate(n):
                p, pp = n % 4, n % 2
                S.op("act", lambda e: e.activation(out=Hb[pp][:], in_=PS[4 + pp][:, 0:256], func=AF.Gelu_apprx_tanh), [PT[4 + pp]], [t_Hb[pp]])
                S.op("dve", lambda e: e.tensor_tensor(out=Hg[pp][:].rearrange("p (s t) -> p s t", s=2), in0=Hb[pp][:].rearrange("p (s t) -> p s t", s=2),
                                                      in1=gtc[p][:], op=ALU.mult), [t_Hb[pp], t_gtc[p]], [t_Hg[pp]])

            def d_out(n):
                p, pp = n % 4, n % 2
                for sub in range(2):
                    for half in range(2):
                        mm(PS[sub * 2 + half][:, :], Hg[pp][:, sub * 128:(sub + 1) * 128], vcb[p][:, half * 512:(half + 1) * 512],
                           n == 0, n == 127, [t_Hg[pp], t_vcb[p]], [PT[sub * 2 + half]])

            def _nfree(ap):
                n = 1
                for s_ in ap.shape[1:]:
                    n *= int(s_)
                return n

            def est_us(o):
                if o[0] != "op" or o[1] not in ("dve", "act"):
                    return 0.0
                name, a, k = o[2]
                outap = k.get("out", a[0] if a else None)
                n = _nfree(outap) if outap is not None else 64
                if o[1] == "act":
                    return 0.3 + n / 1400.0 * (2.5 if n >= 512 and outap.dtype == BF16 and k.get("func") is None and _nfree(k.get("in_")) == 512 else 1.0)
                if name in ("max", "match_replace", "tensor_reduce"):
                    src_ = k.get("in_", k.get("in_values"))
                    return 0.35 + _nfree(src_) / 960.0
                if name == "reciprocal":
                    return 0.35 + 8 * n / 960.0
                if name in ("tensor_tensor", "scalar_tensor_tensor"):
                    fast = all(k[x].dtype == BF16 for x in ("in0", "in1")) and outap.dtype == BF16
                    return 0.35 + (1.0 if fast else 2.0) * n / 960.0
                return 0.35 + n / 960.0

            gphase(0)
            gphase(1)
            npair = 1 if "peer1" in debug else NT // 2
            for g2 in range(npair):
                ops = []
                if g2 + 1 < npair:
                    S.rec = []
                    gphase(2 * g2 + 2)
                    gphase(2 * g2 + 3)
                    ops = S.rec
                    S.rec = None
                clk = []
                c_ = 0.0
                for o in ops:
                    clk.append(c_)
                    c_ += est_us(o)
                t_chunk = max(2.4, c_ * 0.6 / 126.0)
                pos = 0
                S.dma(x2Td[:], x2T_d[:, :, g2 * 256:(g2 + 1) * 256], reads=[t_x2T[2 * g2], t_x2T[2 * g2 + 1]], writes=[t_x2Td])
                for n in range(128):
                    d_load(g2, n)
                    d_hpre(n)
                    d_gate(n)
                    if n >= 1:
                        d_out(n - 1)
                    e_ = pos
                    while e_ < len(ops) and clk[e_] <= (n + 1) * t_chunk:
                        e_ += 1
                    S.replay(ops[pos:e_])
                    pos = e_
                d_out(127)
                S.replay(ops[pos:])
                for sub in range(2):
                    i = 2 * g2 + sub
                    S.dma(x2t, x2v[i], reads=[t_x2d[i]], writes=[t_fin])
                    for half in range(2):
                        dsl = slice(half * 512, (half + 1) * 512)
                        S.op("dve", lambda e: e.scalar_tensor_tensor(out=rso[:, dsl], in0=x2t[:, dsl], scalar=ALPHA, in1=PS[sub * 2 + half][:, :],
                                                                     op0=ALU.mult, op1=ALU.add), [t_fin, PT[sub * 2 + half]], [t_fin])
                    ln_tile(i % 2, rso, rso, t_fin, t_fin, xno, t_fin)
                    S.dma(outv[i], rso, reads=[t_fin], writes=[t_out[i]])

        S.finish(final_toks)
    return nc, hc, DBG


_CACHE = {}


def _prep_inputs(inputs):
    shared = {}
    for name, shp in IN_SPECS:
        if name in ("x", "peer_uT"):
            continue
        a = np.asarray(inputs[name], dtype=np.float32)
        shared[name] = np.ascontiguousarray(a.reshape(shp))
    shared["peer_uT"] = np.ascontiguousarray(np.asarray(inputs["peer_u"], dtype=np.float32).reshape(16384, D_).T)
    return shared


def kernel(**inputs):
    if "nc" not in _CACHE:
        _CACHE["nc"] = build()
    nc, hc, _ = _CACHE["nc"]
    shared = _prep_inputs(inputs)
    for k, v in hc.items():
        shared["c_" + k] = v
    x = np.asarray(inputs["x"], dtype=np.float32)
    in_maps = []
    for b in range(8):
        m = dict(shared)
        m["x"] = np.ascontiguousarray(x[b])
        in_maps.append(m)
    res = run_bass_kernel_spmd(nc, in_maps, core_ids=list(range(8)))
    return np.stack([np.asarray(r["out"], dtype=np.float32) for r in res.results], axis=0)
```

```python
import math
import numpy as np
import ml_dtypes
from contextlib import ExitStack
import concourse.bass as bass
import concourse.mybir as mybir
from concourse.alu_op_type import AluOpType as ALU
from concourse.mybir import ActivationFunctionType as AF
from concourse.bass_utils import run_bass_kernel_spmd

F32 = mybir.dt.float32
BF16 = mybir.dt.bfloat16
AX = mybir.AxisListType

S_ = 2048
D_ = 1024
NT = 16
D_IN = 5144
BIGNEG = -240000.0
LN_EPS = 1e-5
ALPHA = 2.0 ** 0.25
ATT_SCALE = 0.125


class Tok:
    __slots__ = ("w", "r", "name", "excl")

    def __init__(self, name="", excl=False):
        self.w = None
        self.r = {}
        self.name = name
        self.excl = excl


class _RecEng:
    def __init__(self):
        self.call = None

    def __getattr__(self, name):
        def f(*a, **k):
            self.call = (name, a, k)
            return self
        return f


class Sched:
    NDMA = 40

    def __init__(self, nc, es):
        self.nc = nc
        self.E = {"pe": nc.tensor, "act": nc.scalar, "dve": nc.vector,
                  "pool": nc.gpsimd, "sp": nc.sync}
        self.sem = {k: es.enter_context(nc.semaphore("s_" + k)) for k in self.E}
        self.cnt = {k: 0 for k in self.E}
        self.dsem = [es.enter_context(nc.semaphore("d%d" % i)) for i in range(self.NDMA)]
        self.dval = [0] * self.NDMA
        self.dnext = 0
        self.seen = {k: {} for k in self.E}
        self.ninst = 0
        self.es = es
        self.swsem = []
        self.nobar = set()
        self.rec = None

    def _semof(self, key):
        if isinstance(key, tuple):
            if key[0] == "w":
                return self.swsem[key[1]]
            return self.dsem[key[1]]
        return self.sem[key]

    def _wait(self, eng, key, val):
        if eng == "pe" and key == "pe":
            return
        if self.seen[eng].get(key, 0) >= val:
            return
        self.E[eng].wait_ge(self._semof(key), val)
        self.seen[eng][key] = val
        self.ninst += 1

    def _deps(self, eng, reads, writes):
        deps = {}
        for t in reads:
            if t.w is not None:
                k, c = t.w
                deps[k] = max(deps.get(k, 0), c)
        for t in writes:
            if t.w is not None:
                k, c = t.w
                deps[k] = max(deps.get(k, 0), c)
            for k, c in t.r.items():
                deps[k] = max(deps.get(k, 0), c)
        for k, c in deps.items():
            self._wait(eng, k, c)

    def replay(self, ops):
        for o in ops:
            if o[0] == "op":
                _, eng, (name, a, k), reads, writes = o
                self.op(eng, lambda e: getattr(e, name)(*a, **k), reads, writes)
            else:
                _, out, in_, reads, writes, q, kw = o
                self.dma(out, in_, reads, writes, q, **kw)

    def op(self, eng, fn, reads=(), writes=()):
        if self.rec is not None:
            pr = _RecEng()
            fn(pr)
            self.rec.append(("op", eng, pr.call, list(reads), list(writes)))
            return None
        ex = [t for t in reads if t.excl]
        if ex:
            reads = [t for t in reads if not t.excl]
            writes = list(writes) + ex
        self._deps(eng, reads, writes)
        inst = fn(self.E[eng])
        self.cnt[eng] += 1
        c = self.cnt[eng]
        inst.then_inc(self.sem[eng], 1)
        for t in reads:
            t.r[eng] = c
        for t in writes:
            t.w = (eng, c)
            t.r = {}
        self.ninst += 1
        return inst

    def dma(self, out, in_, reads=(), writes=(), q="sp", nobar=False, **kw):
        if self.rec is not None:
            self.rec.append(("dma", out, in_, list(reads), list(writes), q, kw))
            return None
        if q == "pool":
            if nobar:
                self.nobar.add(len(self.swsem))
            sem = self.es.enter_context(self.nc.semaphore("w%d" % len(self.swsem)))
            self.swsem.append(sem)
            key = ("w", len(self.swsem) - 1)
            if len(self.swsem) >= 2:
                self._wait(q, ("w", len(self.swsem) - 2), 16)
            self._deps(q, reads, writes)
            inst = self.E[q].dma_start(out=out, in_=in_, **kw)
            inst.then_inc(sem, 16)
            for t in reads:
                t.r[key] = 16
            for t in writes:
                t.w = (key, 16)
                t.r = {}
            self.ninst += 1
            return inst
        i = self.dnext
        self.dnext = (self.dnext + 1) % self.NDMA
        key = ("d", i)
        if self.dval[i] > 0:
            self._wait(q, key, self.dval[i])
        self._deps(q, reads, writes)
        inst = self.E[q].dma_start(out=out, in_=in_, **kw)
        self.dval[i] += 16
        inst.then_inc(self.dsem[i], 16)
        v = self.dval[i]
        for t in reads:
            t.r[key] = v
        for t in writes:
            t.w = (key, v)
            t.r = {}
        self.ninst += 1
        return inst

    def barrier(self):
        for eng in self.E:
            for key in self.E:
                if self.cnt[key] > 0 and not (eng == key == "pe"):
                    self._wait(eng, key, self.cnt[key])
            for i in range(self.NDMA):
                if self.dval[i] > 0:
                    self._wait(eng, ("d", i), self.dval[i])
            for i in range(len(self.swsem)):
                if i not in self.nobar:
                    self._wait(eng, ("w", i), 16)

    def finish(self, toks):
        for t in toks:
            if t.w is not None:
                self._wait("sp", t.w[0], t.w[1])


def _t5_bucket_np(dist):
    d = np.maximum(dist, 0)
    large = 16 + (np.log(np.maximum(d, 1).astype(np.float32) / 16) / math.log(128 / 16) * 16).astype(np.int32)
    large = np.minimum(large, 31)
    return np.where(d < 16, d, large)


FOFF = 2048
FLEN = 4352


def host_consts():
    c = {}
    c["ident_bf"] = np.eye(128, dtype=np.float32).astype(ml_dtypes.bfloat16)
    c["ident_f"] = np.eye(128, dtype=np.float32)
    c["J128"] = np.ascontiguousarray(np.eye(128, dtype=np.float32)[::-1])
    j127 = np.zeros((128, 128), np.float32)
    j127[:127, :127] = np.eye(127, dtype=np.float32)[::-1]
    c["J127"] = j127
    x = np.arange(FLEN)
    dist = x - FOFF
    oh = np.zeros((33, FLEN), np.float32)
    b = _t5_bucket_np(dist)
    valid = dist >= 0
    oh[b[valid], x[valid]] = 1.0
    oh[32, ~valid] = 1.0
    c["OHD"] = oh
    c0 = np.arange(127) * 16
    s0 = np.arange(32) * 64
    ov = np.clip(np.minimum(c0[:, None] + 32, s0[None, :] + 64) - np.maximum(c0[:, None], s0[None, :]), 0, None).astype(np.float32) / 32
    ovp = np.zeros((128, 32), np.float32)
    ovp[:127] = ov
    c["overlap"] = ovp
    t = np.arange(S_)
    cur = t // 64
    j = np.arange(32)
    forced = (j[None, :] == 0) | (j[None, :] == cur[:, None]) | (j[None, :] == cur[:, None] - 1)
    future = j[None, :] > cur[:, None]
    cs = np.where(future, -1e30, np.where(forced, 1.0e4, 0.0)).astype(np.float32)
    c["Csel"] = np.ascontiguousarray(cs.reshape(NT, 128, 32).transpose(1, 0, 2))
    ex = np.zeros((32, NT, 128), np.float32)
    for kt in range(NT):
        for s in range(128):
            ex[2 * kt + s // 64, kt, s] = 1.0
    c["Expand"] = ex.astype(ml_dtypes.bfloat16)
    s_ = np.arange(128)[:, None]
    t_ = np.arange(128)[None, :]
    c["TriLT"] = np.where(t_ < s_, 0.0, BIGNEG).astype(np.float32).astype(ml_dtypes.bfloat16)
    i_ = np.arange(128)[:, None]
    su = (i_ < t_).astype(np.float32)
    ui = (i_ <= t_).astype(np.float32)
    c["MaskAB"] = np.concatenate([su, ui], axis=1)
    c["MaskSL"] = (i_ > t_).astype(np.float32)
    return c


CONST_DT = {"ident_bf": BF16, "ident_f": F32, "J128": F32, "J127": F32, "OHD": F32, "overlap": F32,
            "Csel": F32, "Expand": BF16, "TriLT": BF16, "MaskAB": F32, "MaskSL": F32}

IN_SPECS = [
    ("x", [S_, D_]), ("ln_in_g", [D_]), ("ln_in_b", [D_]), ("rel_bias", [32, 8]), ("w_in", [D_, D_IN]),
    ("token_mu", [1792]), ("cmp_pos", [2, 32, 64]), ("cmp_w1", [2, 2048, 256]), ("cmp_b1", [2, 256]),
    ("cmp_w2", [2, 256, 64]), ("cmp_b2", [2, 64]), ("rwkv_w0", [512]), ("rwkv_w2", [64, 512]),
    ("rwkv_a0", [512]), ("rwkv_a2", [64, 512]), ("rwkv_g2", [128, 512]), ("rwkv_k_k", [512]),
    ("rwkv_k_a", [512]), ("rwkv_r_k", [512]), ("rwkv_lnx_g", [512]), ("rwkv_lnx_b", [512]),
    ("w_o_nsa", [512, D_]), ("w_o_rwkv", [512, D_]), ("w_out", [D_, D_]), ("ln_mix_g", [D_]), ("ln_mix_b", [D_]),
    ("peer_w_query", [D_, D_]), ("peer_sub_keys", [8, 2, 128, 64]), ("peer_uT", [D_, 16384]), ("peer_v", [16384, D_]),
    ("ln_ffn_g", [D_]), ("ln_ffn_b", [D_]),
]


def build(stage=99, debug=()):
    nc = bass.Bass("TRN2", target_bir_lowering=False)
    I = {}
    for name, shp in IN_SPECS:
        I[name] = nc.dram_tensor(name, shp, F32, kind="ExternalInput").ap()
    hc = host_consts()
    C = {}
    for name, arr in hc.items():
        C[name] = nc.dram_tensor("c_" + name, list(arr.shape), CONST_DT[name], kind="ExternalInput").ap()
    out_d = nc.dram_tensor("out", [S_, D_], F32, kind="ExternalOutput").ap()
    DBG = {}

    def dbg_out(name, shape, dt=F32):
        DBG[name] = nc.dram_tensor("dbg_" + name, shape, dt, kind="ExternalOutput").ap()
        return DBG[name]

    h_d = nc.dram_tensor("h_scr", [S_, D_], F32, kind="Internal").ap()
    F_d = nc.dram_tensor("F_scr", [8, FLEN], F32, kind="Internal").ap()

    with ExitStack() as es, nc.allow_non_contiguous_dma(reason="small param loads"):
        S = Sched(nc, es)

        def sb(name, shape, dt, stack=es):
            return stack.enter_context(nc.sbuf_tensor(name, shape, dt))

        PSALL = es.enter_context(nc.psum_tensor("psall", [128, 8, 512], F32))
        PS = [PSALL[:, i, :] for i in range(8)]
        PT = [Tok("ps%d" % i, excl=True) for i in range(8)]
        psrr = {}

        def psbank(lo=0, hi=8):
            i = psrr.get((lo, hi), lo)
            psrr[(lo, hi)] = lo + (i + 1 - lo) % (hi - lo)
            return PS[i], PT[i]

        evrr = [0]

        def evac_eng():
            evrr[0] ^= 1
            return "act" if evrr[0] else "dve"

        def copy(eng, out, in_, reads, writes):
            if eng == "act":
                return S.op("act", lambda e: e.copy(out=out, in_=in_), reads, writes)
            return S.op(eng, lambda e: e.tensor_copy(out=out, in_=in_), reads, writes)

        def mm(out, lhsT, rhs, start, stop, reads, writes):
            return S.op("pe", lambda e: e.matmul(out, lhsT=lhsT, rhs=rhs, start=start, stop=stop), reads, writes)

        final_toks = []
        dbg_tok = Tok("dbg")
        final_toks.append(dbg_tok)

        ident_bf = sb("ident_bf", [128, 128], BF16)
        ident_f = sb("ident_f", [128, 128], F32)
        t_const = Tok("const")
        S.dma(ident_bf[:], C["ident_bf"][:, :], writes=[t_const])
        S.dma(ident_f[:], C["ident_f"][:, :], writes=[t_const])

        uTb_d = nc.dram_tensor("uTb_scr", [128, 128, 8, 128], BF16, kind="Internal").ap()
        vb_d = nc.dram_tensor("vb_scr", [16384, D_], BF16, kind="Internal").ap()
        t_uTb = [Tok("uTb%d" % i) for i in range(16)]
        t_vb = [Tok("vb%d" % i) for i in range(16)]
        wq_d = nc.dram_tensor("wq_scr", [128, 8, D_], BF16, kind="Internal").ap()
        t_wqd = Tok("wqd")
        gb_bc = sb("gb_bc", [128, 2, D_], F32)
        t_gb = Tok("gb")

        def load_gb(g_ap, b_ap):
            S.dma(gb_bc[:, 0, :], g_ap.unsqueeze(0).to_broadcast([128, D_]), writes=[t_gb])
            S.dma(gb_bc[:, 1, :], b_ap.unsqueeze(0).to_broadcast([128, D_]), writes=[t_gb])

        lnst = sb("lnst", [128, 2, 2, 6], F32)
        lnmv = sb("lnmv", [128, 2, 4], F32)
        t_ln = [Tok("ln0"), Tok("ln1")]

        def ln_tile(par, src, dst, t_src, t_dst, xn_buf, t_xn, alpha_src=None):
            st = lnst[:, par]
            mv = lnmv[:, par]
            tl = t_ln[par]
            S.op("dve", lambda e: e.bn_stats(out=st[:, 0, :], in_=src[:, 0:512]), [t_src], [tl])
            S.op("dve", lambda e: e.bn_stats(out=st[:, 1, :], in_=src[:, 512:1024]), [t_src], [tl])
            S.op("dve", lambda e: e.bn_aggr(out=mv[:, 0:2], in_=st.rearrange("p a b -> p (a b)")), [tl], [tl])
            S.op("dve", lambda e: e.tensor_scalar(out=mv[:, 2:3], in0=mv[:, 1:2], scalar1=LN_EPS, scalar2=None, op0=ALU.add), [tl], [tl])
            S.op("act", lambda e: e.activation(out=mv[:, 2:3], in_=mv[:, 2:3], func=AF.Sqrt), [tl], [tl])
            S.op("dve", lambda e: e.reciprocal(out=mv[:, 2:3], in_=mv[:, 2:3]), [tl], [tl])
            S.op("dve", lambda e: e.tensor_scalar(out=mv[:, 3:4], in0=mv[:, 0:1], scalar1=mv[:, 2:3], scalar2=-1.0,
                                                  op0=ALU.mult, op1=ALU.mult), [tl], [tl])
            S.op("act", lambda e: e.activation(out=xn_buf, in_=src, func=AF.Identity, bias=mv[:, 3:4], scale=mv[:, 2:3]),
                 [t_src, tl], [t_xn])
            S.op("dve", lambda e: e.tensor_tensor(out=xn_buf, in0=xn_buf, in1=gb_bc[:, 0, :], op=ALU.mult), [t_xn, t_gb], [t_xn])
            S.op("pool", lambda e: e.tensor_tensor(out=dst, in0=xn_buf, in1=gb_bc[:, 1, :], op=ALU.add), [t_xn, t_gb], [t_dst])

        onT_d = nc.dram_tensor("onT_scr", [512, S_], BF16, kind="Internal").ap()
        yrw_d = nc.dram_tensor("yrw_scr", [512, S_], BF16, kind="Internal").ap()
        t_onT = [Tok("onT%d" % i) for i in range(NT)]
        t_yrw = [Tok("yrw%d" % i) for i in range(8)]
        rw_d = nc.dram_tensor("rw_scr", [1792, S_], F32, kind="Internal").ap()
        mg_d = nc.dram_tensor("mg_scr", [S_, 2048], F32, kind="Internal").ap()
        t_rwd = [Tok("rwd%d" % i) for i in range(28)]
        t_mgd = [Tok("mgd%d" % i) for i in range(NT)]
        wv = I["w_in"].rearrange("(k p) c -> p k c", p=128)
        with ExitStack() as sn:
            o_nsa = sb("o_nsa", [128, NT, 512], F32, sn)
            t_on = [Tok("on%d" % i) for i in range(NT)]
            qT = sb("qT", [64, 8, S_], BF16, sn)
            t_q = [[Tok() for _ in range(4)] for _ in range(8)]
            kT = sb("kT", [64, 4, S_], BF16, sn)
            t_k = [[Tok() for _ in range(4)] for _ in range(4)]
            Vt = sb("Vt", [128, NT, 4, 65], BF16, sn)
            t_V = [Tok() for _ in range(NT)]
            gate = sb("gate", [128, NT, 24], F32, sn)
            kcT = sb("kcT", [64, 2, 128], BF16, sn)
            t_kc = [Tok(), Tok()]
            VC = sb("VC", [128, 2, 98], F32, sn)
            t_VC = [Tok(), Tok()]
            S.op("pool", lambda e: e.memset(Vt[:, :, :, 64:65], 1.0), [], t_V)
            S.op("pool", lambda e: e.memset(VC[:], 0.0), [], t_VC)
            S.op("pool", lambda e: e.memset(VC[:, :, 64:65], 1.0), [], t_VC)
            S.dma(VC[:, 0, 65:97], C["overlap"][:, :], writes=[t_VC[0]])
            S.dma(VC[:, 1, 65:97], C["overlap"][:, :], writes=[t_VC[1]])

            with ExitStack() as s1:
                hT = sb("hT", [128, 8, S_], BF16, s1)
                t_hT = [Tok("hT%d" % i) for i in range(4)]
                t_hd = [Tok("hd%d" % i) for i in range(NT)]
                load_gb(I["ln_in_g"], I["ln_in_b"])
                xv = I["x"].rearrange("(n p) d -> n p d", p=128)
                hv = h_d.rearrange("(n p) d -> n p d", p=128)
                with ExitStack() as sa:
                    xbuf = [sb("xa%d" % i, [128, D_], F32, sa) for i in range(2)]
                    xnb = [sb("xn%d" % i, [128, D_], F32, sa) for i in range(2)]
                    hfb = [sb("hf%d" % i, [128, D_], F32, sa) for i in range(2)]
                    hbb = [sb("hb%d" % i, [128, D_], BF16, sa) for i in range(2)]
                    t_x = [Tok(), Tok()]
                    t_xn = [Tok(), Tok()]
                    t_hf = [Tok(), Tok()]
                    t_hb = [Tok(), Tok()]
                    for i in range(NT):
                        p = i % 2
                        S.dma(xbuf[p][:], xv[i], writes=[t_x[p]])
                        ln_tile(p, xbuf[p][:], hfb[p][:], t_x[p], t_hf[p], xnb[p][:], t_xn[p])
                        S.dma(hv[i], hfb[p][:], reads=[t_hf[p]], writes=[t_hd[i]])
                        S.op("act", lambda e: e.copy(out=hbb[p][:], in_=hfb[p][:]), [t_hf[p]], [t_hb[p]])
                        ps, pt = psbank()
                        psb = ps.bitcast(BF16)
                        for dk in range(8):
                            S.op("pe", lambda e: e.transpose(out=psb[:, dk * 128:(dk + 1) * 128], in_=hbb[p][:, dk * 128:(dk + 1) * 128],
                                                             identity=ident_bf[:]), [t_hb[p], t_const], [pt])
                        copy("dve", hT[:, :, i * 128:(i + 1) * 128], psb.rearrange("p (k t) -> p k t", k=8), [pt], [t_hT[i // 4]])
                S.barrier()
                Wn = sb("Wn", [128, 8, 1304], BF16, s1)
                t_Wn = [Tok(), Tok(), Tok()]
                for ci, c0 in enumerate(range(0, 1304, 512)):
                    c1 = min(1304, c0 + 512)
                    S.dma(Wn[:, :, c0:c1], wv[:, :, c0:c1], writes=[t_Wn[ci]], q="pool")
                s1c = ExitStack()
                kvA = sb("kvA", [64, 4, S_], BF16, s1c)
                kvB = sb("kvB", [64, 4, S_], BF16, s1c)
                t_kv = [[Tok() for _ in range(4)] for _ in range(4)]
                pos_sb = sb("pos_sb", [32, 2, 64], F32, s1c)
                posT = sb("posT", [64, 2, 32], F32, s1c)
                t_pos = Tok()
                S.dma(pos_sb[:], I["cmp_pos"].rearrange("k j d -> j k d"), writes=[t_pos])
                for kv in range(2):
                    ps, pt = psbank()
                    S.op("pe", lambda e: e.transpose(out=ps[0:64, 0:32], in_=pos_sb[:, kv, :], identity=ident_f[0:32, 0:32]),
                         [t_pos, t_const], [pt])
                    copy("dve", posT[:, kv, :], ps[0:64, 0:32], [pt], [t_pos])

                def proj_cm(col0, M, tc, evac):
                    ps, pt = psbank()
                    for dk in range(8):
                        mm(ps[0:M, :], Wn[:, dk, col0:col0 + M], hT[:, dk, tc * 512:(tc + 1) * 512], dk == 0, dk == 7,
                           t_Wn + [t_hT[tc]], [pt])
                    evac(ps[0:M, :], pt)

                for tc in range(4):
                    tsl = slice(tc * 512, (tc + 1) * 512)
                    for h in range(8):
                        proj_cm(h * 64, 64, tc, lambda ps, pt: copy(evac_eng(), qT[:, h, tsl], ps, [pt], [t_q[h][tc]]))
                    for b, base in ((0, 768), (1, 1024)):
                        for g in range(2):
                            proj_cm(base + g * 64, 64, tc,
                                    lambda ps, pt: copy(evac_eng(), kT[:, b * 2 + g, tsl], ps, [pt], [t_k[b * 2 + g][tc]]))
                    for kv, base in ((0, 512), (1, 640)):
                        for g in range(2):
                            idx = kv * 2 + g

                            def ev(ps, pt):
                                for dst, j0 in ((kvA, 0), (kvB, 16)):
                                    S.op("dve", lambda e: e.tensor_tensor(
                                        out=dst[:, idx, tsl].rearrange("p (b j) -> p b j", j=16),
                                        in0=ps.rearrange("p (b j) -> p b j", j=16),
                                        in1=posT[:, kv, j0:j0 + 16].unsqueeze(1).to_broadcast([64, 32, 16]), op=ALU.add),
                                        [pt, t_pos], [t_kv[idx][tc]])
                            proj_cm(base + g * 64, 64, tc, ev)
                for i in range(NT):
                    ps, pt = psbank()
                    for ri, (c0, n) in enumerate(((896, 128), (1152, 128), (1280, 24))):
                        for dk in range(8):
                            mm(ps[:, ri * 128:ri * 128 + n], hT[:, dk, i * 128:(i + 1) * 128], Wn[:, dk, c0:c0 + n], dk == 0, dk == 7,
                               t_Wn + [t_hT[i // 4]], [pt])
                    copy("dve", Vt[:, i, :, 0:64], ps[:, 0:256].rearrange("p (a d) -> p a d", d=64), [pt], [t_V[i]])
                    S.op("act", lambda e: e.activation(out=gate[:, i, :], in_=ps[:, 256:280], func=AF.Sigmoid), [pt], [t_V[i]])

                w2b = sb("w2b", [128, 2, 2, 64], BF16, s1c)
                b1T = sb("b1T", [128, 2, 2], F32, s1c)
                b2k = sb("b2k", [64, 1], F32, s1c)
                b2v = sb("b2v", [128, 64], F32, s1c)
                t_cw = Tok()
                for kv in range(2):
                    S.dma(w2b[:, kv], I["cmp_w2"][kv].rearrange("(c p) d -> p c d", p=128), writes=[t_cw], q="pool")
                    S.dma(b1T[:, kv, :], I["cmp_b1"][kv].rearrange("(c p) -> p c", p=128), writes=[t_cw])
                S.dma(b2k[:], I["cmp_b2"][0].unsqueeze(1), writes=[t_cw])
                S.dma(b2v[:], I["cmp_b2"][1].unsqueeze(0).to_broadcast([128, 64]), writes=[t_cw])
                w1s = sb("w1s", [64, 32, 256], BF16, s1c)
                w1b = [w1s, w1s]
                t_w1s = Tok()
                t_w1 = [t_w1s, t_w1s]
                hid = [sb("hid%d" % i, [128, 2, 128], BF16, s1c) for i in range(2)]
                t_hid = [Tok(), Tok()]
                for kv in range(2):
                    S.dma(w1s[:], I["cmp_w1"][kv].rearrange("(j d) h -> d j h", d=64), writes=[t_w1s], q="pool")
                    for g in range(2):
                        idx = kv * 2 + g
                        hb_ = hid[idx % 2]
                        th = t_hid[idx % 2]
                        for hcx in range(2):
                            ps, pt = psbank()
                            for j in range(32):
                                src = kvA if j < 16 else kvB
                                off = j if j < 16 else j
                                rhs = src[:, idx, off:off + 16 * 126 + 1:16]
                                mm(ps[:, 0:127], w1b[kv][:, j, hcx * 128:(hcx + 1) * 128], rhs, j == 0, j == 31,
                                   [t_w1[kv]] + t_kv[idx], [pt])
                            S.op("act", lambda e: e.activation(out=hb_[:, hcx, 0:127], in_=ps[:, 0:127], func=AF.Gelu_apprx_tanh,
                                                               bias=b1T[:, kv, hcx:hcx + 1], scale=1.0), [pt, t_cw], [th])
                        ps, pt = psbank()
                        if kv == 0:
                            for hcx in range(2):
                                mm(ps[0:64, 0:127], w2b[:, 0, hcx, :], hb_[:, hcx, 0:127], hcx == 0, hcx == 1, [t_cw, th], [pt])
                            S.op("act", lambda e: e.activation(out=kcT[:, g, 0:127], in_=ps[0:64, 0:127], func=AF.Identity,
                                                               bias=b2k[:, 0:1], scale=1.0), [pt, t_cw], [t_kc[g]])
                        else:
                            for hcx in range(2):
                                mm(ps[0:127, 0:64], hb_[:, hcx, 0:127], w2b[:, 1, hcx, :], hcx == 0, hcx == 1, [t_cw, th], [pt])
                            S.op("dve", lambda e: e.tensor_tensor(out=VC[0:127, g, 0:64], in0=ps[0:127, 0:64], in1=b2v[0:127, :],
                                                                  op=ALU.add), [pt, t_cw], [t_VC[g]])
                s1c.close()
                S.barrier()
                stg = [sb("stg%d" % i, [128, S_], F32, s1) for i in range(2)]
                t_stg = [Tok(), Tok()]
                t_Wr = Tok()
                for half in range(2):
                    for q2 in range(2):
                        S.dma(Wn[:, :, q2 * 448:(q2 + 1) * 448], wv[:, :, 1304 + half * 896 + q2 * 448:1304 + half * 896 + (q2 + 1) * 448],
                              reads=[], writes=t_Wn + [t_Wr], q="pool")
                    for cc in range(14):
                        c = half * 14 + cc
                        p = c % 2
                        for tc in range(4):
                            ps, pt = psbank()
                            for dk in range(8):
                                mm(ps[0:64, :], Wn[:, dk, cc * 64:(cc + 1) * 64], hT[:, dk, tc * 512:(tc + 1) * 512], dk == 0, dk == 7,
                                   [t_Wr, t_hT[tc]], [pt])
                            copy(evac_eng(), stg[p][0:64, tc * 512:(tc + 1) * 512], ps[0:64, :], [pt], [t_stg[p]])
                        S.dma(rw_d[c * 64:(c + 1) * 64, :], stg[p][0:64, :], reads=[t_stg[p]], writes=[t_rwd[c]])
                mgv = mg_d.rearrange("(n p) c -> n p c", p=128)
                for half in range(2):
                    for q4 in range(2):
                        S.dma(Wn[:, :, q4 * 512:(q4 + 1) * 512], wv[:, :, 3096 + half * 1024 + q4 * 512:3096 + half * 1024 + (q4 + 1) * 512],
                              reads=[], writes=t_Wn + [t_Wr], q="pool")
                    for i in range(NT):
                        p = i % 2
                        for q4 in range(2):
                            ps, pt = psbank()
                            for dk in range(8):
                                mm(ps[:, :], hT[:, dk, i * 128:(i + 1) * 128], Wn[:, dk, q4 * 512:(q4 + 1) * 512], dk == 0, dk == 7,
                                   [t_Wr, t_hT[i // 4]], [pt])
                            S.op("act", lambda e: e.activation(out=stg[p][:, q4 * 512:(q4 + 1) * 512], in_=ps[:, :], func=AF.Sigmoid),
                                 [pt], [t_stg[p]])
                        S.dma(mgv[i][:, half * 1024:(half + 1) * 1024], stg[p][:, 0:1024], reads=[t_stg[p]], writes=[t_mgd[i]])
            S.barrier()
            if "kv2" in debug:
                o = dbg_out("kT", [64, 4, S_], BF16)
                S.dma(o[:, :, :], kT[:], reads=[t for l in t_k for t in l], writes=[dbg_tok])
                o = dbg_out("Vt", [128, NT, 4, 65], BF16)
                S.dma(o[:, :, :, :], Vt[:], reads=t_V, writes=[dbg_tok])
                o = dbg_out("gate", [128, NT, 24])
                S.dma(o[:, :, :], gate[:], reads=t_V, writes=[dbg_tok])
            if "kc" in debug:
                o = dbg_out("kcT", [64, 2, 128], BF16)
                S.dma(o[:, :, :], kcT[:], reads=t_kc, writes=[dbg_tok])
                o = dbg_out("VC", [128, 2, 98])
                S.dma(o[:, :, :], VC[:], reads=t_VC, writes=[dbg_tok])
                o = dbg_out("qT", [64, 8, S_], BF16)
                S.dma(o[:, :, :], qT[:], reads=[t for l in t_q for t in l], writes=[dbg_tok])

            Theta = sb("Theta", [128, 8, 3, 128], BF16, sn)
            t_Th = [Tok() for _ in range(8)]
            TriLT = sb("TriLT", [128, 128], BF16, sn)
            Expand = sb("Expand", [32, NT, 128], BF16, sn)
            J128 = sb("J128", [128, 128], F32, sn)
            J127 = sb("J127", [128, 128], F32, sn)
            Csel = sb("Csel", [128, NT, 32], F32, sn)
            t_c2 = Tok("const2")
            S.dma(TriLT[:], C["TriLT"][:, :], writes=[t_c2])
            S.dma(Expand[:], C["Expand"][:, :, :], writes=[t_c2])
            S.dma(J128[:], C["J128"][:, :], writes=[t_c2])
            S.dma(J127[:], C["J127"][:, :], writes=[t_c2])
            S.dma(Csel[:], C["Csel"][:, :, :], writes=[t_c2])
            with ExitStack() as s2:
                OHD = sb("OHD", [33, FLEN], F32, s2)
                relbx = sb("relbx", [33, 8], F32, s2)
                F_sb = sb("F_sb", [8, FLEN], F32, s2)
                Hk = sb("Hk", [128, 8, 384], F32, s2)
                t_f = Tok()
                t_Fsb = Tok()
                t_Fd = Tok()
                t_Hk = Tok()
                S.dma(OHD[:], C["OHD"][:, :], writes=[t_f])
                S.op("pool", lambda e: e.memset(relbx[:], BIGNEG / 8.0), [], [t_f])
                S.dma(relbx[0:32, :], I["rel_bias"][:, :], writes=[t_f])
                for c0 in range(0, FLEN, 512):
                    n = min(512, FLEN - c0)
                    ps, pt = psbank()
                    mm(ps[0:8, 0:n], relbx[:, :], OHD[:, c0:c0 + n], True, True, [t_f], [pt])
                    S.op("act", lambda e: e.mul(F_sb[:, c0:c0 + n], ps[0:8, 0:n], 8.0), [pt], [t_Fsb])
                S.dma(F_d[:, :], F_sb[:], reads=[t_Fsb], writes=[t_Fd])
                hk_src = bass.AP(tensor=F_d.tensor, offset=1921, ap=[[1, 128], [FLEN, 8], [1, 384]])
                S.dma(Hk[:], hk_src, reads=[t_Fd], writes=[t_Hk])
                for h in range(8):
                    ps, pt = psbank()
                    mm(ps[:, 0:384], J128[:], Hk[:, h, :], True, True, [t_c2, t_Hk], [pt])
                    copy(evac_eng(), Theta[:, h, :, :], ps[:, 0:384].rearrange("p (a t) -> p a t", t=128), [pt], [t_Th[h]])
            S.barrier()
            if "theta" in debug:
                o = dbg_out("Theta", [128, 8, 3, 128], BF16)
                S.dma(o[:, :, :, :], Theta[:], reads=t_Th, writes=[dbg_tok])

            for k_ in range(8):
                for hf in range(2):
                    S.dma(uTb_d[hf * 64:(hf + 1) * 64, :, k_, :].rearrange("c p e -> p c e"),
                          I["peer_uT"][k_ * 128:(k_ + 1) * 128, hf * 8192:(hf + 1) * 8192].rearrange("p (c e) -> p c e", e=128),
                          writes=[t_uTb[k_ * 2 + hf]], q="pool", nobar=True)
            for k_ in range(16):
                S.dma(vb_d[k_ * 1024:(k_ + 1) * 1024, :], I["peer_v"][k_ * 1024:(k_ + 1) * 1024, :], writes=[t_vb[k_]], q="pool", nobar=True)
            S.dma(wq_d[:, :, :], I["peer_w_query"].rearrange("(k p) d -> p k d", p=128), writes=[t_wqd], q="pool", nobar=True)
            with ExitStack() as s3:
                imp = sb("imp", [128, NT, 2, 32], F32, s3)
                t_imp = [[Tok() for _ in range(2)] for _ in range(NT)]
                MnegT = sb("MnegT", [32, 2, S_], BF16, s3)
                t_Mn = [[Tok() for _ in range(NT)] for _ in range(2)]
                sm = [sb("sm%d" % i, [128, 4], F32, s3) for i in range(4)]
                t_sm = [Tok() for _ in range(4)]
                PTs = [sb("PTs%d" % i, [128, S_], BF16, s3) for i in range(2)]
                t_PTs = [Tok(), Tok()]
                PTw = [sb("PTw%d" % i, [128, 640], BF16, s3) for i in range(2)]
                t_PTw = [Tok(), Tok()]
                s3a = ExitStack()
                H16_0 = sb("H16_0", [128, S_], F32, s3a)
                H16 = [H16_0, H16_0]
                t_H16_0 = Tok()
                t_H16 = [t_H16_0, t_H16_0]
                Phi = [sb("Phi%d" % i, [128, S_], BF16, s3a) for i in range(2)]
                t_Phi = [Tok(), Tok()]
                PTc_0 = sb("PTc_0", [128, S_], F32, s3a)
                PTc = [PTc_0, PTc_0]
                t_PTc_0 = Tok()
                t_PTc = [t_PTc_0, t_PTc_0]
                smrr = [0]

                def evac_attn(ps, pt, i, h, br, first):
                    k = smrr[0]
                    smrr[0] = (k + 1) % 4
                    s_, ts = sm[k], t_sm[k]
                    S.op("dve", lambda e: e.tensor_scalar(out=s_[:, 0:1], in0=ps[:, 64:65], scalar1=1e-30, scalar2=None, op0=ALU.max),
                         [pt], [ts])
                    S.op("dve", lambda e: e.reciprocal(out=s_[:, 1:2], in_=s_[:, 0:1]), [ts], [ts])
                    S.op("dve", lambda e: e.tensor_tensor(out=s_[:, 2:3], in0=s_[:, 1:2], in1=gate[:, i, h * 3 + br:h * 3 + br + 1],
                                                          op=ALU.mult), [ts, t_V[i]], [ts])
                    dst = o_nsa[:, i, h * 64:(h + 1) * 64]
                    if first:
                        S.op("dve", lambda e: e.tensor_scalar(out=dst, in0=ps[:, 0:64], scalar1=s_[:, 2:3], scalar2=None, op0=ALU.mult),
                             [pt, ts], [t_on[i]])
                    else:
                        S.op("dve", lambda e: e.scalar_tensor_tensor(out=dst, in0=ps[:, 0:64], scalar=s_[:, 2:3], in1=dst,
                                                                     op0=ALU.mult, op1=ALU.add), [pt, ts], [t_on[i]])
                    return s_, ts

                for h in range(8):
                    g = h // 4
                    p = h % 2
                    src = bass.AP(tensor=F_d.tensor, offset=h * FLEN + 1, ap=[[16, 127], [1, S_]])
                    S.dma(H16[p][0:127, :], src, reads=[t_Fd], writes=[t_H16[p]])
                    for tc in range(4):
                        ps, pt = psbank()
                        mm(ps[0:127, :], J127[0:127, 0:127], H16[p][0:127, tc * 512:(tc + 1) * 512], True, True, [t_c2, t_H16[p]], [pt])
                        copy(evac_eng(), Phi[p][0:127, tc * 512:(tc + 1) * 512], ps[0:127, :], [pt], [t_Phi[p]])
                    for tc in range(4):
                        ps, pt = psbank()
                        mm(ps[0:127, :], kcT[:, g, 0:127], qT[:, h, tc * 512:(tc + 1) * 512], True, False, [t_kc[g], t_q[h][tc]], [pt])
                        mm(ps[0:127, :], ident_bf[0:127, 0:127], Phi[p][0:127, tc * 512:(tc + 1) * 512], False, True, [t_const, t_Phi[p]], [pt])
                        S.op("act", lambda e: e.activation(out=PTc[p][0:127, tc * 512:(tc + 1) * 512], in_=ps[0:127, :], func=AF.Exp,
                                                           scale=ATT_SCALE), [pt], [t_PTc[p]])
                    for i in range(NT):
                        ps, pt = psbank()
                        mm(ps[:, 0:98], PTc[p][0:127, i * 128:(i + 1) * 128], VC[0:127, g, :], True, True, [t_PTc[p], t_VC[g]], [pt])
                        s_, ts = evac_attn(ps, pt, i, h, 0, True)
                        dsti = imp[:, i, g, :]
                        if h % 4 == 0:
                            S.op("dve", lambda e: e.tensor_scalar(out=dsti, in0=ps[:, 65:97], scalar1=s_[:, 1:2], scalar2=None, op0=ALU.mult),
                                 [pt, ts], [t_imp[i][g]])
                        else:
                            S.op("dve", lambda e: e.scalar_tensor_tensor(out=dsti, in0=ps[:, 65:97], scalar=s_[:, 1:2], in1=dsti,
                                                                         op0=ALU.mult, op1=ALU.add), [pt, ts], [t_imp[i][g]])
                if "cmp" in debug:
                    o = dbg_out("o_cmp", [128, NT, 512])
                    S.dma(o[:, :, :], o_nsa[:], reads=t_on, writes=[dbg_tok])
                    o = dbg_out("imp", [128, NT, 2, 32])
                    S.dma(o[:, :, :, :], imp[:], reads=[t for l in t_imp for t in l], writes=[dbg_tok])
                scb = [sb("scb%d" % i, [128, 32], F32, s3a) for i in range(2)]
                cmb = [sb("cmb%d" % i, [128, 32, 32], F32, s3a) for i in range(2)]
                rkb = [sb("rkb%d" % i, [128, 32], F32, s3a) for i in range(2)]
                t_sel = [Tok(), Tok()]
                for i in range(NT):
                    for g in range(2):
                        p = (i * 2 + g) % 2
                        ts = t_sel[p]
                        S.op("dve", lambda e: e.tensor_tensor(out=scb[p][:], in0=imp[:, i, g, :], in1=Csel[:, i, :], op=ALU.add),
                             [t_imp[i][g], t_c2], [ts])
                        S.op("dve", lambda e: e.tensor_tensor(out=cmb[p][:], in0=scb[p][:].unsqueeze(1).to_broadcast([128, 32, 32]),
                                                              in1=scb[p][:].unsqueeze(2).to_broadcast([128, 32, 32]), op=ALU.is_gt), [ts], [ts])
                        S.op("dve", lambda e: e.tensor_reduce(out=rkb[p][:], in_=cmb[p][:], axis=AX.X, op=ALU.add), [ts], [ts])
                        S.op("dve", lambda e: e.tensor_scalar(out=rkb[p][:], in0=rkb[p][:], scalar1=16.0, scalar2=BIGNEG, op0=ALU.is_ge,
                                                              op1=ALU.mult), [ts], [ts])
                        ps, pt = psbank()
                        S.op("pe", lambda e: e.transpose(out=ps[0:32, 0:128], in_=rkb[p][:], identity=ident_f[:]), [ts, t_const], [pt])
                        copy("act", MnegT[:, g, i * 128:(i + 1) * 128], ps[0:32, 0:128], [pt], [t_Mn[g][i]])
                if "cmp" in debug:
                    o = dbg_out("MnegT", [32, 2, S_], BF16)
                    S.dma(o[:, :, :], MnegT[:], reads=[t for l in t_Mn for t in l], writes=[dbg_tok])
                if stage <= 2:
                    s3a.close()
                    S.finish(final_toks)
                    return nc, hc, DBG
                s3a.close()
                S.barrier()
                cnt = 0
                for h in range(8):
                    g = h // 4
                    for i in range(NT):
                        p = cnt % 2
                        cnt += 1
                        qsl = slice(i * 128, (i + 1) * 128)
                        nk = i + 1
                        kts = list(range(max(0, i - 4), i + 1))
                        for kt in range(nk):
                            b = kt // 4
                            out = PS[b][:, (kt % 4) * 128:(kt % 4 + 1) * 128]
                            pt = PT[b]
                            dl = i - kt
                            mm(out, ident_bf[:], Theta[:, h, min(dl, 2), :], True, False, [t_const, t_Th[h]], [pt])
                            mm(out, kT[:, g, kt * 128:(kt + 1) * 128], qT[:, h, qsl], False, False, [t_k[g][kt // 4], t_q[h][i // 4]], [pt])
                            mm(out, Expand[:, kt, :], MnegT[:, g, qsl], False, True, [t_c2, t_Mn[g][i]], [pt])
                        for n_, kt in enumerate(kts):
                            b = 4 + n_ // 4
                            out = PS[b][:, (n_ % 4) * 128:(n_ % 4 + 1) * 128]
                            pt = PT[b]
                            dl = i - kt
                            mm(out, ident_bf[:], Theta[:, h, min(dl, 2), :], True, False, [t_const, t_Th[h]], [pt])
                            if dl == 4:
                                mm(out, ident_bf[:], TriLT[:], False, False, [t_const, t_c2], [pt])
                            mm(out, kT[:, 2 + g, kt * 128:(kt + 1) * 128], qT[:, h, qsl], False, True, [t_k[2 + g][kt // 4], t_q[h][i // 4]], [pt])
                        for b in range((nk + 3) // 4):
                            n = min(4, nk - b * 4) * 128
                            S.op("act", lambda e: e.activation(out=PTs[p][:, b * 512:b * 512 + n], in_=PS[b][:, 0:n], func=AF.Exp,
                                                               scale=ATT_SCALE), [PT[b]], [t_PTs[p]])
                        for b in range((len(kts) + 3) // 4):
                            n = min(4, len(kts) - b * 4) * 128
                            S.op("act", lambda e: e.activation(out=PTw[p][:, b * 512:b * 512 + n], in_=PS[4 + b][:, 0:n], func=AF.Exp,
                                                               scale=ATT_SCALE), [PT[4 + b]], [t_PTw[p]])
                        ps, pt = psbank(6, 8)
                        for kt in range(nk):
                            mm(ps[:, 0:65], PTs[p][:, kt * 128:(kt + 1) * 128], Vt[:, kt, g, :], kt == 0, kt == nk - 1,
                               [t_PTs[p], t_V[kt]], [pt])
                        evac_attn(ps, pt, i, h, 1, False)
                        ps, pt = psbank(6, 8)
                        for n_, kt in enumerate(kts):
                            mm(ps[:, 0:65], PTw[p][:, n_ * 128:(n_ + 1) * 128], Vt[:, kt, 2 + g, :], n_ == 0, n_ == len(kts) - 1,
                               [t_PTw[p], t_V[kt]], [pt])
                        evac_attn(ps, pt, i, h, 2, False)
                onv = onT_d.rearrange("(cc p) t -> p cc t", p=128)
                stT = [sb("stT%d" % i, [128, 4, 128], BF16, s3) for i in range(2)]
                t_stT = [Tok(), Tok()]
                for i in range(NT):
                    p = i % 2
                    ps, pt = psbank(6, 8)
                    for cc in range(4):
                        S.op("pe", lambda e: e.transpose(out=ps[:, cc * 128:(cc + 1) * 128], in_=o_nsa[:, i, cc * 128:(cc + 1) * 128],
                                                         identity=ident_f[:]), [t_on[i], t_const], [pt])
                    copy(evac_eng(), stT[p][:], ps.rearrange("p (c t) -> p c t", c=4), [pt], [t_stT[p]])
                    S.dma(onv[:, :, i * 128:(i + 1) * 128], stT[p][:], reads=[t_stT[p]], writes=[t_onT[i]])
            S.barrier()
            if "theta2" in debug:
                o = dbg_out("Theta2", [128, 8, 3, 128], BF16)
                S.dma(o[:, :, :, :], Theta[:], reads=t_Th, writes=[dbg_tok])
        S.barrier()
        if "nsa" in debug:
            o = dbg_out("o_nsa", [128, NT, 512])
            S.dma(o[:, :, :], o_nsa[:], reads=t_on, writes=[dbg_tok])
        if stage <= 3:
            S.finish(final_toks)
            return nc, hc, DBG

        T1 = S_ + 1
        with ExitStack() as sr:
            MaskAB = sb("MaskAB", [128, 256], F32, sr)
            MaskSL = sb("MaskSL", [128, 128], F32, sr)
            ones64 = sb("ones64", [64, 64], F32, sr)
            onesm = sb("onesm", [64, 64], F32, sr)
            muT = sb("muT", [64, 28], F32, sr)
            mu128 = sb("mu128", [128, 1], F32, sr)
            par = sb("par", [64, 7, 8], F32, sr)
            parn = sb("parn", [64, 8], F32, sr)
            w2f = sb("w2f", [64, 512], F32, sr)
            a2f = sb("a2f", [64, 512], F32, sr)
            g2f = sb("g2f", [128, 512], F32, sr)
            t_rc = Tok("rconst")
            S.dma(MaskAB[:], C["MaskAB"][:, :], writes=[t_rc])
            S.dma(MaskSL[:], C["MaskSL"][:, :], writes=[t_rc])
            S.dma(muT[:], I["token_mu"].rearrange("(c p) -> p c", p=64), writes=[t_rc])
            S.dma(mu128[:], I["token_mu"][1664:1792].unsqueeze(1), writes=[t_rc])
            for k_, nm in enumerate(["rwkv_w0", "rwkv_a0", "rwkv_k_k", "rwkv_k_a", "rwkv_r_k", "rwkv_lnx_g", "rwkv_lnx_b"]):
                S.dma(par[:, k_, :], I[nm].rearrange("(h p) -> p h", p=64), writes=[t_rc])
            S.dma(w2f[:], I["rwkv_w2"][:, :], writes=[t_rc])
            S.dma(a2f[:], I["rwkv_a2"][:, :], writes=[t_rc])
            S.dma(g2f[:], I["rwkv_g2"][:, :], writes=[t_rc])
            S.op("pool", lambda e: e.memset(ones64[:], 1.0), [], [t_rc])
            S.op("pool", lambda e: e.memset(onesm[:], 1.0 / 64.0), [], [t_rc])
            S.op("dve", lambda e: e.tensor_scalar(out=parn[:], in0=par[:, 0, :], scalar1=-1.0, scalar2=None, op0=ALU.mult), [t_rc], [t_rc])

            raw = sb("raw", [128, T1], F32, sr)
            Db = sb("Db", [128, S_], F32, sr)
            t_raw = Tok("raw")
            t_D = Tok("D")
            S.op("pool", lambda e: e.memset(raw[:, 0:1], 0.0), [], [t_raw])

            def shift(P, mu_ap, out_ap, t_out):
                S.op("dve", lambda e: e.tensor_tensor(out=Db[0:P, :], in0=raw[0:P, 0:S_], in1=raw[0:P, 1:T1], op=ALU.subtract),
                     [t_raw], [t_D])
                S.op("dve", lambda e: e.scalar_tensor_tensor(out=out_ap, in0=Db[0:P, :], scalar=mu_ap, in1=raw[0:P, 1:T1],
                                                             op0=ALU.mult, op1=ALU.add), [t_D, t_raw, t_rc], [t_out])

            TW = sb("TW", [64, S_], F32, sr)
            AL = sb("AL", [64, S_], F32, sr)
            GL = sb("GL", [128, S_], F32, sr)
            t_lora = Tok("lora")
            S.dma(raw[0:64, 1:T1], rw_d[1536:1600, :], reads=[t_rwd[24]], writes=[t_raw])
            shift(64, muT[:, 24:25], TW[:], t_lora)
            S.op("act", lambda e: e.activation(out=TW[:], in_=TW[:], func=AF.Tanh), [t_lora], [t_lora])
            S.dma(raw[0:64, 1:T1], rw_d[1600:1664, :], reads=[t_rwd[25]], writes=[t_raw])
            shift(64, muT[:, 25:26], AL[:], t_lora)
            S.dma(raw[:, 1:T1], rw_d[1664:1792, :], reads=[t_rwd[26], t_rwd[27]], writes=[t_raw])
            shift(128, mu128[:, 0:1], GL[:], t_lora)
            S.op("act", lambda e: e.activation(out=GL[:], in_=GL[:], func=AF.Sigmoid), [t_lora], [t_lora])

            Rb = sb("Rb", [64, S_], F32, sr)
            Kb = sb("Kb", [64, S_], F32, sr)
            Vb = sb("Vb", [64, S_], F32, sr)
            Ab = sb("Ab", [64, S_], F32, sr)
            Eb = sb("Eb", [64, S_], F32, sr)
            Lb = sb("Lb", [64, S_], F32, sr)
            AR = sb("AR", [64, NT, 256], BF16, sr)
            BT = sb("BT", [64, S_], F32, sr)
            KT = sb("KT", [64, S_], F32, sr)
            tok = sb("tok", [128, NT, 192], BF16, sr)
            BTb = sb("BTb", [64, S_], BF16, sr)
            KTb = sb("KTb", [64, S_], BF16, sr)
            STb = [sb("STb%d" % i, [64, 64], BF16, sr) for i in range(2)]
            t_BTb, t_KTb = Tok(), Tok()
            t_STb = [Tok(), Tok()]
            WCb = sb("WCb", [64, NT], F32, sr)
            Rk = sb("Rk", [64, 64], F32, sr)
            ybf = sb("ybf", [64, S_], BF16, sr)
            ST = [sb("ST%d" % i, [64, 64], F32, sr) for i in range(2)]
            NSET = 4
            AB1 = [sb("AB1_%d" % i, [128, 256], BF16, sr) for i in range(NSET)]
            AB2 = [sb("AB2_%d" % i, [128, 256], BF16, sr) for i in range(NSET)]
            Pp = [[sb("Pp%d_%d" % (i, j), [128, 128], BF16, sr) for j in range(2)] for i in range(NSET)]
            PTp = [[sb("PTp%d_%d" % (i, j), [128, 128], BF16, sr) for j in range(2)] for i in range(NSET)]
            Np = [[sb("Np%d_%d" % (i, j), [128, 128], BF16, sr) for j in range(2)] for i in range(NSET)]
            XTs = sb("XTs", [128, 64], BF16, sr)
            UTs = sb("UTs", [128, 64], BF16, sr)
            t_R, t_K, t_V, t_A, t_E, t_L, t_AR, t_BT, t_KT, t_WC, t_Rk, t_ybf, t_XT, t_UT = [Tok() for _ in range(14)]
            t_tok = [Tok() for _ in range(NT)]
            t_ST = [Tok(), Tok()]
            t_set = [[Tok() for _ in range(6)] for _ in range(NSET)]
            v3 = lambda ap: ap.rearrange("p (n t) -> p n t", t=128)
            onesb = ones64[:, 0:1].to_broadcast([64, S_])

            for h in range(8):
                hs = slice(h * 64, (h + 1) * 64)
                for (row0, ci, dst, td) in ((h * 64, h, Rb, t_R), (512 + h * 64, 8 + h, Kb, t_K), (1024 + h * 64, 16 + h, Vb, t_V)):
                    S.dma(raw[0:64, 1:T1], rw_d[row0:row0 + 64, :], reads=[t_rwd[ci]], writes=[t_raw])
                    shift(64, muT[:, ci:ci + 1], dst[:], td)
                for tc in range(4):
                    ps, pt = psbank()
                    mm(ps[0:64, :], w2f[:, hs], TW[:, tc * 512:(tc + 1) * 512], True, True, [t_rc, t_lora], [pt])
                    S.op("act", lambda e: e.activation(out=Db[0:64, tc * 512:(tc + 1) * 512], in_=ps[0:64, :], func=AF.Exp,
                                                       bias=parn[:, h:h + 1], scale=-1.0), [pt, t_rc], [t_D])
                S.op("act", lambda e: e.activation(out=Db[0:64, :], in_=Db[0:64, :], func=AF.Ln, bias=1.0, scale=1.0), [t_D], [t_D])
                S.op("act", lambda e: e.activation(out=Db[0:64, :], in_=Db[0:64, :], func=AF.Exp, bias=-0.5, scale=-1.0), [t_D], [t_D])
                S.op("dve", lambda e: e.tensor_tensor_scan(out=raw[0:64, 1:T1], data0=onesb, data1=Db[0:64, :], initial=0.0,
                                                           op0=ALU.mult, op1=ALU.subtract), [t_D, t_rc], [t_raw])
                S.op("dve", lambda e: e.tensor_tensor(out=v3(Lb[:]), in0=v3(raw[0:64, 1:T1]),
                                                      in1=raw[0:64, 0:S_:128].unsqueeze(2).to_broadcast([64, NT, 128]), op=ALU.subtract),
                     [t_raw], [t_L])
                for tc in range(4):
                    ps, pt = psbank()
                    mm(ps[0:64, :], a2f[:, hs], AL[:, tc * 512:(tc + 1) * 512], True, True, [t_rc, t_lora], [pt])
                    S.op("act", lambda e: e.activation(out=Ab[:, tc * 512:(tc + 1) * 512], in_=ps[0:64, :], func=AF.Sigmoid,
                                                       bias=par[:, 1, h:h + 1], scale=1.0), [pt, t_rc], [t_A])
                S.op("dve", lambda e: e.tensor_scalar(out=Eb[:], in0=Kb[:], scalar1=par[:, 2, h:h + 1], scalar2=None, op0=ALU.mult),
                     [t_K, t_rc], [t_E])
                S.op("dve", lambda e: e.tensor_tensor(out=BT[:], in0=Eb[:], in1=Eb[:], op=ALU.mult), [t_E], [t_BT])
                for tc in range(4):
                    ps, pt = psbank()
                    mm(ps[0:64, :], ones64[:], BT[:, tc * 512:(tc + 1) * 512], True, True, [t_rc, t_BT], [pt])
                    S.op("act", lambda e: e.activation(out=KT[:, tc * 512:(tc + 1) * 512], in_=ps[0:64, :], func=AF.Sqrt), [pt], [t_KT])
                S.op("dve", lambda e: e.tensor_scalar(out=KT[:], in0=KT[:], scalar1=1e-12, scalar2=None, op0=ALU.max), [t_KT], [t_KT])
                S.op("dve", lambda e: e.reciprocal(out=KT[:], in_=KT[:]), [t_KT], [t_KT])
                S.op("dve", lambda e: e.tensor_tensor(out=Eb[:], in0=Eb[:], in1=KT[:], op=ALU.mult), [t_E, t_KT], [t_E])
                S.op("dve", lambda e: e.tensor_scalar(out=BT[:], in0=Ab[:], scalar1=-1.0, scalar2=par[:, 3, h:h + 1], op0=ALU.add, op1=ALU.mult),
                     [t_A, t_rc], [t_BT])
                S.op("dve", lambda e: e.scalar_tensor_tensor(out=Kb[:], in0=BT[:], scalar=1.0, in1=Kb[:], op0=ALU.add, op1=ALU.mult),
                     [t_BT, t_K], [t_K])
                S.op("dve", lambda e: e.tensor_tensor(out=Db[0:64, :], in0=Db[0:64, :], in1=Lb[:], op=ALU.add), [t_D, t_L], [t_D])
                S.op("act", lambda e: e.activation(out=Db[0:64, :], in_=Db[0:64, :], func=AF.Exp), [t_D], [t_D])
                S.op("dve", lambda e: e.scalar_tensor_tensor(out=AR[:, :, 0:128], in0=v3(Eb[:]), scalar=-1.0, in1=v3(Db[0:64, :]),
                                                             op0=ALU.mult, op1=ALU.mult), [t_E, t_D], [t_AR])
                S.op("act", lambda e: e.activation(out=raw[0:64, 1:T1], in_=Lb[:], func=AF.Exp), [t_L], [t_raw])
                S.op("dve", lambda e: e.tensor_tensor(out=AR[:, :, 128:256], in0=v3(Rb[:]), in1=v3(raw[0:64, 1:T1]), op=ALU.mult),
                     [t_R, t_raw], [t_AR])
                S.op("dve", lambda e: e.tensor_copy(out=WCb[:], in_=raw[0:64, 128:T1:128]), [t_raw], [t_WC])
                S.op("act", lambda e: e.activation(out=Db[0:64, :], in_=Lb[:], func=AF.Exp, scale=-1.0), [t_L, t_AR], [t_D])
                S.op("dve", lambda e: e.tensor_tensor(out=BT[:], in0=Eb[:], in1=Ab[:], op=ALU.mult), [t_E, t_A], [t_BT])
                S.op("dve", lambda e: e.tensor_tensor(out=BT[:], in0=BT[:], in1=Db[0:64, :], op=ALU.mult), [t_BT, t_D], [t_BT])
                S.op("dve", lambda e: e.tensor_tensor(out=KT[:], in0=Kb[:], in1=Db[0:64, :], op=ALU.mult), [t_K, t_D], [t_KT])
                S.op("pool", lambda e: e.tensor_copy(out=BTb[:], in_=BT[:]), [t_BT], [t_BTb])
                S.op("pool", lambda e: e.tensor_copy(out=KTb[:], in_=KT[:]), [t_KT], [t_KTb])
                wcb = WCb[:].unsqueeze(2).to_broadcast([64, NT, 128])
                S.op("dve", lambda e: e.tensor_tensor(out=v3(raw[0:64, 1:T1]), in0=v3(BT[:]), in1=wcb, op=ALU.mult), [t_BT, t_WC], [t_raw])
                S.op("dve", lambda e: e.tensor_tensor(out=v3(Db[0:64, :]), in0=v3(KT[:]), in1=wcb, op=ALU.mult), [t_KT, t_WC], [t_D])
                for n in range(NT):
                    ps, pt = psbank()
                    tsl = slice(n * 128, (n + 1) * 128)
                    S.op("pe", lambda e: e.transpose(out=ps[:, 0:64], in_=Vb[:, tsl], identity=ident_f[0:64, 0:64]), [t_V, t_const], [pt])
                    S.op("pe", lambda e: e.transpose(out=ps[:, 64:128], in_=raw[0:64, 1 + n * 128:1 + (n + 1) * 128],
                                                     identity=ident_f[0:64, 0:64]), [t_raw, t_const], [pt])
                    S.op("pe", lambda e: e.transpose(out=ps[:, 128:192], in_=Db[0:64, tsl], identity=ident_f[0:64, 0:64]), [t_D, t_const], [pt])
                    copy(evac_eng(), tok[:, n, :], ps[:, 0:192], [pt], [t_tok[n]])
                S.op("dve", lambda e: e.tensor_copy(out=Rk[:], in_=par[:, 4, h:h + 1].to_broadcast([64, 64])), [t_rc], [t_Rk])
                S.op("pool", lambda e: e.memset(ST[0][:], 0.0), [], [t_ST[0]])
                S.op("pool", lambda e: e.memset(STb[0][:], 0.0), [], [t_STb[0]])

                def precompute(n, s):
                    tsl = slice(n * 128, (n + 1) * 128)
                    tk = t_set[s]
                    ps, pt = psbank()
                    mm(ps[:, 0:256], BTb[:, tsl], AR[:, n, :], True, True, [t_BTb, t_AR], [pt])
                    S.op("dve", lambda e: e.tensor_tensor(out=AB1[s][:], in0=ps[:, 0:256], in1=MaskAB[:], op=ALU.mult), [pt, t_rc], [tk[0]])
                    yield
                    ps, pt = psbank()
                    mm(ps[:, 0:256], KTb[:, tsl], AR[:, n, :], True, True, [t_KTb, t_AR], [pt])
                    S.op("dve", lambda e: e.tensor_tensor(out=AB2[s][:], in0=ps[:, 0:256], in1=MaskAB[:], op=ALU.mult), [pt, t_rc], [tk[1]])
                    yield
                    ps, pt = psbank()
                    mm(ps[:, 0:128], AR[:, n, 0:128], BTb[:, tsl], True, True, [t_BTb, t_AR], [pt])
                    S.op("dve", lambda e: e.tensor_tensor(out=PTp[s][0][:], in0=ps[:, 0:128], in1=MaskSL[:], op=ALU.mult), [pt, t_rc], [tk[3]])
                    yield
                    S.op("pool", lambda e: e.tensor_copy(out=Pp[s][0][:], in_=AB1[s][:, 0:128]), [tk[0]], [tk[2]])
                    S.op("pool", lambda e: e.tensor_tensor(out=Np[s][0][:], in0=AB1[s][:, 0:128], in1=ident_f[:], op=ALU.add), [tk[0], t_const], [tk[4]])
                    yield
                    cur = 0
                    for j in range(1, 7):
                        nxt = 1 - cur
                        if j < 6:
                            ps, pt = psbank()
                            mm(ps[:, 0:128], PTp[s][cur][:], Pp[s][cur][:], True, True, [tk[2], tk[3]], [pt])
                            ps2, pt2 = psbank()
                            mm(ps2[:, 0:128], Pp[s][cur][:], PTp[s][cur][:], True, True, [tk[2], tk[3]], [pt2])
                            copy("act", Pp[s][nxt][:], ps[:, 0:128], [pt], [tk[2]])
                            copy("dve", PTp[s][nxt][:], ps2[:, 0:128], [pt2], [tk[3]])
                        else:
                            ps2, pt2 = psbank()
                            mm(ps2[:, 0:128], Pp[s][cur][:], PTp[s][cur][:], True, True, [tk[2], tk[3]], [pt2])
                            copy("dve", PTp[s][nxt][:], ps2[:, 0:128], [pt2], [tk[3]])
                        yield
                        ps, pt = psbank()
                        mm(ps[:, 0:128], PTp[s][nxt][:], Np[s][cur][:], True, True, [tk[3], tk[4]], [pt])
                        S.op("dve", lambda e: e.tensor_tensor(out=Np[s][nxt][:], in0=ps[:, 0:128], in1=Np[s][cur][:], op=ALU.add), [pt, tk[4]], [tk[4]])
                        cur = nxt
                        yield
                    assert cur == 0

                for n0 in range(0, NT, NSET):
                    gens = [precompute(n0 + s, s) for s in range(NSET)]
                    alive = list(gens)
                    while alive:
                        for g_ in list(alive):
                            try:
                                next(g_)
                            except StopIteration:
                                alive.remove(g_)
                    for s in range(NSET):
                        n = n0 + s
                        tsl = slice(n * 128, (n + 1) * 128)
                        tk = t_set[s]
                        sc, sn_ = ST[n % 2], ST[(n + 1) % 2]
                        tsc, tsn = t_ST[n % 2], t_ST[(n + 1) % 2]
                        scb, snb = STb[n % 2], STb[(n + 1) % 2]
                        tscb, tsnb = t_STb[n % 2], t_STb[(n + 1) % 2]
                        Nf = Np[s][0]
                        ps, pt = psbank()
                        mm(ps[:, 0:64], AR[:, n, 0:128], scb[:], True, False, [t_AR, tscb], [pt])
                        mm(ps[:, 0:64], AB2[s][:, 0:128], tok[:, n, 0:64], False, True, [tk[1], t_tok[n]], [pt])
                        copy("act", XTs[:], ps[:, 0:64], [pt], [t_XT])
                        ps, pt = psbank()
                        mm(ps[:, 0:64], Nf[:], XTs[:], True, True, [tk[4], t_XT], [pt])
                        copy("act", UTs[:], ps[:, 0:64], [pt], [t_UT])
                        ps, pt = psbank()
                        mm(ps[0:64, 0:128], scb[:], AR[:, n, 128:256], True, False, [tscb, t_AR], [pt])
                        mm(ps[0:64, 0:128], UTs[:], AB1[s][:, 128:256], False, False, [t_UT, tk[0]], [pt])
                        mm(ps[0:64, 0:128], tok[:, n, 0:64], AB2[s][:, 128:256], False, True, [t_tok[n], tk[1]], [pt])
                        copy("dve", Lb[:, tsl], ps[0:64, 0:128], [pt], [t_L])
                        ps, pt = psbank()
                        mm(ps[0:64, 0:64], tok[:, n, 64:128], UTs[:], True, False, [t_tok[n], t_UT], [pt])
                        mm(ps[0:64, 0:64], tok[:, n, 128:192], tok[:, n, 0:64], False, True, [t_tok[n]], [pt])
                        S.op("dve", lambda e: e.scalar_tensor_tensor(out=sn_[:], in0=sc[:], scalar=WCb[:, n:n + 1], in1=ps[0:64, 0:64],
                                                                     op0=ALU.mult, op1=ALU.add), [tsc, t_WC, pt], [tsn])
                        copy("act", snb[:], sn_[:], [tsn], [tsnb])
                for tc in range(4):
                    csl = slice(tc * 512, (tc + 1) * 512)
                    ps, pt = psbank()
                    mm(ps[0:64, :], onesm[:], Lb[:, csl], True, True, [t_rc, t_L], [pt])
                    S.op("dve", lambda e: e.tensor_tensor(out=Ab[:, csl], in0=Lb[:, csl], in1=ps[0:64, :], op=ALU.subtract), [t_L, pt], [t_A])
                S.op("pool", lambda e: e.tensor_tensor(out=BT[:], in0=Ab[:], in1=Ab[:], op=ALU.mult), [t_A], [t_BT])
                for tc in range(4):
                    csl = slice(tc * 512, (tc + 1) * 512)
                    ps, pt = psbank()
                    mm(ps[0:64, :], onesm[:], BT[:, csl], True, True, [t_rc, t_BT], [pt])
                    S.op("dve", lambda e: e.tensor_scalar(out=KT[:, csl], in0=ps[0:64, :], scalar1=64e-5, scalar2=None, op0=ALU.add), [pt], [t_KT])
                S.op("act", lambda e: e.activation(out=KT[:], in_=KT[:], func=AF.Sqrt), [t_KT], [t_KT])
                S.op("dve", lambda e: e.reciprocal(out=KT[:], in_=KT[:]), [t_KT], [t_KT])
                S.op("dve", lambda e: e.tensor_tensor(out=Ab[:], in0=Ab[:], in1=KT[:], op=ALU.mult), [t_A, t_KT], [t_A])
                S.op("dve", lambda e: e.tensor_scalar(out=Ab[:], in0=Ab[:], scalar1=par[:, 5, h:h + 1], scalar2=par[:, 6, h:h + 1],
                                                      op0=ALU.mult, op1=ALU.add), [t_A, t_rc], [t_A])
                S.op("pool", lambda e: e.tensor_tensor(out=BT[:], in0=Rb[:], in1=Kb[:], op=ALU.mult), [t_R, t_K], [t_BT])
                for tc in range(4):
                    csl = slice(tc * 512, (tc + 1) * 512)
                    ps, pt = psbank()
                    mm(ps[0:64, :], Rk[:], BT[:, csl], True, True, [t_Rk, t_BT], [pt])
                    S.op("dve", lambda e: e.tensor_tensor(out=KT[:, csl], in0=ps[0:64, :], in1=Vb[:, csl], op=ALU.mult), [pt, t_V], [t_KT])
                S.op("dve", lambda e: e.tensor_tensor(out=Ab[:], in0=Ab[:], in1=KT[:], op=ALU.add), [t_A, t_KT], [t_A])
                for tc in range(4):
                    csl = slice(tc * 512, (tc + 1) * 512)
                    ps, pt = psbank()
                    mm(ps[0:64, :], g2f[:, hs], GL[:, csl], True, True, [t_rc, t_lora], [pt])
                    S.op("dve", lambda e: e.tensor_tensor(out=ybf[:, csl], in0=ps[0:64, :], in1=Ab[:, csl], op=ALU.mult), [pt, t_A], [t_ybf])
                S.dma(yrw_d[h * 64:(h + 1) * 64, :], ybf[:], reads=[t_ybf], writes=[t_yrw[h]])
                if "rwkv1" in debug and h == 0:
                    break
        S.barrier()
        if "rwkv" in debug or "rwkv1" in debug:
            o = dbg_out("yrw", [512, S_], BF16)
            ytmp = sb("ytmp", [128, 4, S_], BF16)
            tt_ = Tok()
            S.dma(ytmp[:], yrw_d.rearrange("(c p) t -> p c t", p=128), reads=t_yrw, writes=[tt_])
            S.dma(o.rearrange("(c p) t -> p c t", p=128), ytmp[:], reads=[tt_], writes=[dbg_tok])
        if stage <= 4:
            S.finish(final_toks)
            return nc, hc, DBG

        x2_d = nc.dram_tensor("x2_scr", [S_, D_], F32, kind="Internal").ap()
        t_x2d = [Tok("x2d%d" % i) for i in range(NT)]
        x2T_d = nc.dram_tensor("x2T_scr", [128, 8, S_], BF16, kind="Internal").ap()
        t_x2T = [Tok("x2T%d" % i) for i in range(NT)]
        x2v = x2_d.rearrange("(n p) d -> n p d", p=128)
        load_gb(I["ln_mix_g"], I["ln_mix_b"])
        with ExitStack() as sm_:
            wn_b = sb("wn_b", [128, 4, D_], BF16, sm_)
            wr_b = sb("wr_b", [128, 4, D_], BF16, sm_)
            wo_b = sb("wo_b", [128, 8, D_], BF16, sm_)
            ynT = sb("ynT", [128, 4, S_], BF16, sm_)
            yrT = sb("yrT", [128, 4, S_], BF16, sm_)
            t_mw = Tok("mw")
            t_yn = Tok("yn")
            S.dma(wn_b[:], I["w_o_nsa"].rearrange("(k p) d -> p k d", p=128), writes=[t_mw], q="pool")
            S.dma(wr_b[:], I["w_o_rwkv"].rearrange("(k p) d -> p k d", p=128), writes=[t_mw], q="pool")
            for k2 in range(2):
                S.dma(wo_b[:, k2 * 4:(k2 + 1) * 4, :], I["w_out"].rearrange("(k p) d -> p k d", p=128)[:, k2 * 4:(k2 + 1) * 4, :], writes=[t_mw], q="pool")
            S.dma(ynT[:], onT_d.rearrange("(c p) t -> p c t", p=128), reads=t_onT, writes=[t_yn])
            S.dma(yrT[:], yrw_d.rearrange("(c p) t -> p c t", p=128), reads=t_yrw, writes=[t_yn])
            gtb = [sb("gtb%d" % i, [128, 2048], F32, sm_) for i in range(2)]
            hb2 = [sb("hb2_%d" % i, [128, D_], F32, sm_) for i in range(2)]
            m1b = [sb("m1b%d" % i, [128, D_], F32, sm_) for i in range(2)]
            mbb = [sb("mbb%d" % i, [128, D_], BF16, sm_) for i in range(2)]
            mTb = [sb("mTb%d" % i, [128, 8, 128], BF16, sm_) for i in range(2)]
            rsb = [sb("rsb%d" % i, [128, D_], F32, sm_) for i in range(2)]
            xnb2 = [sb("xnb2_%d" % i, [128, D_], F32, sm_) for i in range(2)]
            x2b = [sb("x2b%d" % i, [128, D_], F32, sm_) for i in range(2)]
            x2h = [sb("x2h%d" % i, [128, D_], BF16, sm_) for i in range(2)]
            tg, th2, tm1, tmb, tmT, trs, txn, tx2, tx2h = [[Tok(), Tok()] for _ in range(9)]
            mgv2 = mg_d.rearrange("(n p) c -> n p c", p=128)
            for i in range(NT):
                p = i % 2
                tsl = slice(i * 128, (i + 1) * 128)
                S.dma(gtb[p][:], mgv2[i], reads=[t_mgd[i]], writes=[tg[p]])
                S.dma(hb2[p][:], hv[i], reads=[t_hd[i]], writes=[th2[p]])
                for half in range(2):
                    dsl = slice(half * 512, (half + 1) * 512)
                    for c in range(4):
                        mm(PS[half][:, :], ynT[:, c, tsl], wn_b[:, c, dsl], c == 0, c == 3, [t_yn, t_mw], [PT[half]])
                    for c in range(4):
                        mm(PS[2 + half][:, :], yrT[:, c, tsl], wr_b[:, c, dsl], c == 0, c == 3, [t_yn, t_mw], [PT[2 + half]])
                for half in range(2):
                    dsl = slice(half * 512, (half + 1) * 512)
                    S.op("dve", lambda e: e.tensor_tensor(out=m1b[p][:, dsl], in0=PS[half][:, :], in1=gtb[p][:, dsl], op=ALU.mult),
                         [PT[half], tg[p]], [tm1[p]])
                    S.op("dve", lambda e: e.tensor_tensor(out=rsb[p][:, dsl], in0=PS[2 + half][:, :], in1=gtb[p][:, 1024 + half * 512:1024 + (half + 1) * 512],
                                                          op=ALU.mult), [PT[2 + half], tg[p]], [trs[p]])
                S.op("pool", lambda e: e.tensor_tensor(out=mbb[p][:], in0=m1b[p][:], in1=rsb[p][:], op=ALU.add), [tm1[p], trs[p]], [tmb[p]])
                psb = PS[4 + p].bitcast(BF16)
                for dk in range(8):
                    S.op("pe", lambda e: e.transpose(out=psb[:, dk * 128:(dk + 1) * 128], in_=mbb[p][:, dk * 128:(dk + 1) * 128],
                                                     identity=ident_bf[:]), [tmb[p], t_const], [PT[4 + p]])
                copy("act", mTb[p][:], psb.rearrange("p (k t) -> p k t", k=8), [PT[4 + p]], [tmT[p]])
                for half in range(2):
                    dsl = slice(half * 512, (half + 1) * 512)
                    for c in range(8):
                        mm(PS[6 + half][:, :], mTb[p][:, c, :], wo_b[:, c, dsl], c == 0, c == 7, [tmT[p], t_mw], [PT[6 + half]])
                    S.op("dve", lambda e: e.scalar_tensor_tensor(out=rsb[p][:, dsl], in0=hb2[p][:, dsl], scalar=ALPHA, in1=PS[6 + half][:, :],
                                                                 op0=ALU.mult, op1=ALU.add), [th2[p], PT[6 + half], tmb[p]], [trs[p]])
                ln_tile(p, rsb[p][:], x2b[p][:], trs[p], tx2[p], xnb2[p][:], txn[p])
                S.dma(x2v[i], x2b[p][:], reads=[tx2[p]], writes=[t_x2d[i]])
                S.op("act", lambda e: e.copy(out=x2h[p][:], in_=x2b[p][:]), [tx2[p]], [tx2h[p]])
                psb = PS[4 + p].bitcast(BF16)
                for dk in range(8):
                    S.op("pe", lambda e: e.transpose(out=psb[:, dk * 128:(dk + 1) * 128], in_=x2h[p][:, dk * 128:(dk + 1) * 128],
                                                     identity=ident_bf[:]), [tx2h[p], t_const], [PT[4 + p]])
                copy("dve", mTb[p][:], psb.rearrange("p (k t) -> p k t", k=8), [PT[4 + p]], [tmT[p]])
                S.dma(x2T_d[:, :, tsl], mTb[p][:], reads=[tmT[p]], writes=[t_x2T[i]])
        S.barrier()
        if "x2" in debug:
            o = dbg_out("x2", [S_, D_])
            xtmp = sb("xtmp", [128, NT, D_], F32)
            tt2 = Tok()
            S.dma(xtmp[:], x2_d.rearrange("(n p) d -> p n d", p=128), reads=t_x2d, writes=[tt2])
            S.dma(o.rearrange("(n p) d -> p n d", p=128), xtmp[:], reads=[tt2], writes=[dbg_tok])
        if stage <= 5:
            S.finish(final_toks)
            return nc, hc, DBG

        load_gb(I["ln_ffn_g"], I["ln_ffn_b"])
        outv = out_d.rearrange("(n p) d -> n p d", p=128)
        t_out = [Tok("out%d" % i) for i in range(NT)]
        final_toks.extend(t_out)
        uTv = uTb_d.rearrange("c p k e -> p c (k e)")
        vbv = vb_d.rearrange("(c p) d -> p c d", p=128)
        NEGBIG = -1.0e30
        with ExitStack() as sp_:
            skT = sb("skT", [64, 16, 128], BF16, sp_)
            sk_scope = ExitStack()
            skn = sb("skn", [128, 16, 64], F32, sk_scope)
            t_pw = Tok("pw")
            S.dma(skn[:], I["peer_sub_keys"].rearrange("h q n d -> n (h q) d"), writes=[t_pw])
            for hp in range(16):
                ps, pt = psbank()
                S.op("pe", lambda e: e.transpose(out=ps[0:64, 0:128], in_=skn[:, hp, :], identity=ident_f[:]), [t_pw, t_const], [pt])
                copy(evac_eng(), skT[:, hp, :], ps[0:64, 0:128], [pt], [t_pw])
            sk_scope.close()
            S.barrier()
            X1 = sb("X1", [128, 16384], BF16, sp_)
            X2 = sb("X2", [128, 16384], BF16, sp_)
            Trm = sb("Trm", [128, 16384], BF16, sp_)
            Orm = sb("Orm", [128, 16384], BF16, sp_)
            t_Trm, t_Orm = Tok("Trm"), Tok("Orm")
            t_X1p = [Tok("X1_%d" % i) for i in range(8)]
            t_X2p = [Tok("X2_%d" % i) for i in range(8)]
            gt_d = nc.dram_tensor("gt_scr", [NT, 128, 128, 128], BF16, kind="Internal").ap()
            t_gt = [Tok("gt%d" % i) for i in range(NT)]
            wqv = X1[:, 0:8192].rearrange("p (k d) -> p k d", k=8)
            X1v = X1[:].rearrange("p (n h a) -> p n h a", h=8, a=16)
            X2v = X2[:].rearrange("p (n h a) -> p n h a", h=8, a=16)
            cand = X2[:].bitcast(F32)[:, 0:2048].rearrange("p (h c) -> p h c", h=8)
            X2g = X2[:].rearrange("p (n t) -> p n t", t=128)
            Trm3 = Trm[:].rearrange("p (n t) -> p n t", t=128)
            Orm3 = Orm[:].rearrange("p (n t) -> p n t", t=128)
            qb = Trm[:, 0:1024]
            qTi = Trm[0:64, 1024:3072].rearrange("p (a t) -> p a t", t=128)
            wk = Trm[:, 3072:3584].bitcast(F32)
            wk2 = Trm[:, 3584:4096].bitcast(F32)
            x2Tg = Trm[:, 8192:9216].rearrange("p (k t) -> p k t", k=8)
            SE = sb("SE", [128, 8192], BF16, sp_)
            sc = SE[:, 0:4096].bitcast(F32).rearrange("p (a n) -> p a n", n=128)
            Ee = SE[:, 4096:8192].bitcast(F32).rearrange("p (a n) -> p a n", n=128)
            SD = sb("SD", [128, 8192], BF16, sp_)
            x2Td = sb("x2Td", [128, 8, 256], BF16, sp_)
            gtc = [sb("gtc%d" % i, [128, 2, 128], BF16, sp_) for i in range(4)]
            x2t_t = sb("x2t_t", [128, D_], F32, sp_)
            rso_t = sb("rso_t", [128, D_], F32, sp_)
            xno_t = sb("xno_t", [128, D_], F32, sp_)
            x2t, rso, xno = x2t_t[:], rso_t[:], xno_t[:]
            t_fin = Tok("fin")
            s16 = sb("s16", [128, 16, 16], F32, sp_)
            e16 = sb("e16", [128, 16, 16], F32, sp_)
            tops = sb("tops", [128, 8, 24], F32, sp_)
            stt_ = sb("stt_", [128, 8, 8], F32, sp_)
            exps = sb("exps", [128, 8, 16], F32, sp_)
            E1s = sb("E1s", [128, 8, 16], F32, sp_)
            thr = sb("thr", [128, 8, 16], F32, sp_)
            uTc = [SD[:, i * 1024:(i + 1) * 1024].rearrange("p (k e) -> p k e", k=8) for i in range(4)]
            vcb = [SD[:, 4096 + i * 1024:4096 + (i + 1) * 1024] for i in range(4)]
            Hb = [sb("Hb%d" % i, [128, 256], BF16, sp_) for i in range(2)]
            Hg = [sb("Hg%d" % i, [128, 256], BF16, sp_) for i in range(2)]
            (t_s16, t_e16, t_tops, t_st, t_exps, t_E1s, t_thr) = [Tok() for _ in range(7)]
            t_x2Tg = t_qb = t_qTi = t_wk = t_wk2 = t_Trm
            t_uTc = [Tok() for _ in range(4)]
            t_vcb = [Tok() for _ in range(4)]
            t_gtc = [Tok() for _ in range(4)]
            t_x2Td = Tok("x2Td")
            t_scL = [Tok("sc")]
            t_EeL = [Tok("Ee")]
            t_Hb = [Tok(), Tok()]
            t_Hg = [Tok(), Tok()]
            def gphase(i):
                S.dma(x2Tg, x2T_d[:, :, i * 128:(i + 1) * 128], reads=[t_x2T[i]], writes=[t_x2Tg])
                S.dma(wqv, wq_d[:, :, :], reads=[t_wqd], writes=[*t_X1p])
                for half in range(2):
                    ps, pt = psbank(6, 8)
                    for dk in range(8):
                        mm(ps[:, :], x2Tg[:, dk, :], wqv[:, dk, half * 512:(half + 1) * 512], dk == 0, dk == 7, [t_x2Tg, *t_X1p], [pt])
                    copy("act", qb[:, half * 512:(half + 1) * 512], ps[:, :], [pt], [t_qb])
                for b in range(2):
                    ps, pt = psbank(6, 8)
                    psb = ps.bitcast(BF16)
                    for jj in range(8):
                        hp = b * 8 + jj
                        S.op("pe", lambda e: e.transpose(out=psb[0:64, jj * 128:(jj + 1) * 128], in_=qb[:, hp * 64:(hp + 1) * 64],
                                                         identity=ident_bf[:]), [t_qb, t_const], [pt])
                    copy("act", qTi[:, b * 8:(b + 1) * 8, :], psb[0:64, :].rearrange("p (a t) -> p a t", t=128), [pt], [t_qTi])
                for b in range(4):
                    ps, pt = psbank(6, 8)
                    for j in range(4):
                        hp = b * 4 + j
                        mm(ps[:, j * 128:(j + 1) * 128], qTi[:, hp, :], skT[:, hp, :], True, True, [t_qTi, t_pw], [pt])
                    copy("act", sc[:, b * 4:(b + 1) * 4, :], ps[:, :].rearrange("p (a n) -> p a n", n=128), [pt], [*t_scL])
                for hp in range(16):
                    S.op("dve", lambda e: e.max(out=s16[:, hp, 0:8], in_=sc[:, hp, :]), [*t_scL], [t_s16])
                    S.op("dve", lambda e: e.match_replace(out=wk[:, 0:128], in_to_replace=s16[:, hp, 0:8], in_values=sc[:, hp, :], imm_value=NEGBIG),
                         [*t_scL, t_s16], [t_wk])
                    S.op("dve", lambda e: e.max(out=s16[:, hp, 8:16], in_=wk[:, 0:128]), [t_wk], [t_s16])
                s16v = s16[:].rearrange("p (h q) a -> p h q a", q=2)
                S.op("dve", lambda e: e.tensor_tensor(out=cand.rearrange("p h (a b) -> p h a b", b=16),
                                                      in0=s16v[:, :, 0, :].unsqueeze(3).to_broadcast([128, 8, 16, 16]),
                                                      in1=s16v[:, :, 1, :].unsqueeze(2).to_broadcast([128, 8, 16, 16]), op=ALU.add), [t_s16], [*t_X2p])
                for h in range(8):
                    S.op("dve", lambda e: e.max(out=tops[:, h, 0:8], in_=cand[:, h, :]), [*t_X2p], [t_tops])
                    S.op("dve", lambda e: e.match_replace(out=wk[:], in_to_replace=tops[:, h, 0:8], in_values=cand[:, h, :], imm_value=NEGBIG),
                         [*t_X2p, t_tops], [t_wk])
                    S.op("dve", lambda e: e.max(out=tops[:, h, 8:16], in_=wk[:]), [t_wk], [t_tops])
                    S.op("dve", lambda e: e.match_replace(out=wk2[:], in_to_replace=tops[:, h, 8:16], in_values=wk[:], imm_value=NEGBIG),
                         [t_wk, t_tops], [t_wk2])
                    S.op("dve", lambda e: e.max(out=tops[:, h, 16:24], in_=wk2[:]), [t_wk2], [t_tops])
                S.op("dve", lambda e: e.tensor_tensor(out=stt_[:, :, 0:1], in0=tops[:, :, 15:16], in1=tops[:, :, 16:17], op=ALU.add), [t_tops], [t_st])
                S.op("dve", lambda e: e.tensor_scalar(out=stt_[:, :, 0:1], in0=stt_[:, :, 0:1], scalar1=0.5, scalar2=None, op0=ALU.mult), [t_st], [t_st])
                S.op("dve", lambda e: e.tensor_tensor(out=exps[:], in0=tops[:, :, 0:16], in1=tops[:, :, 0:1].to_broadcast([128, 8, 16]),
                                                      op=ALU.subtract), [t_tops], [t_exps])
                S.op("act", lambda e: e.activation(out=exps[:], in_=exps[:], func=AF.Exp), [t_exps], [t_exps])
                S.op("dve", lambda e: e.tensor_reduce(out=stt_[:, :, 1:2], in_=exps[:], axis=AX.X, op=ALU.add), [t_exps, t_st], [t_st])
                S.op("dve", lambda e: e.reciprocal(out=stt_[:, :, 2:3], in_=stt_[:, :, 1:2]), [t_st], [t_st])
                S.op("dve", lambda e: e.tensor_tensor(out=Ee[:], in0=sc[:], in1=s16[:, :, 0:1].to_broadcast([128, 16, 128]), op=ALU.subtract),
                     [*t_scL, t_s16], [*t_EeL])
                S.op("act", lambda e: e.activation(out=Ee[:], in_=Ee[:], func=AF.Exp), [*t_EeL], [*t_EeL])
                S.op("dve", lambda e: e.tensor_tensor(out=e16[:], in0=s16[:], in1=s16[:, :, 0:1].to_broadcast([128, 16, 16]), op=ALU.subtract),
                     [t_s16], [t_e16])
                S.op("act", lambda e: e.activation(out=e16[:], in_=e16[:], func=AF.Exp), [t_e16], [t_e16])
                e16v = e16[:].rearrange("p (h q) a -> p h q a", q=2)
                S.op("dve", lambda e: e.tensor_tensor(out=E1s[:], in0=e16v[:, :, 0, :], in1=stt_[:, :, 2:3].to_broadcast([128, 8, 16]), op=ALU.mult),
                     [t_e16, t_st], [t_E1s])
                S.op("dve", lambda e: e.tensor_tensor(out=thr[:], in0=stt_[:, :, 0:1].to_broadcast([128, 8, 16]), in1=s16v[:, :, 0, :], op=ALU.subtract),
                     [t_s16, t_st], [t_thr])
                scv = sc[:].rearrange("p (h q) n -> p h q n", q=2)
                Eev = Ee[:].rearrange("p (h q) n -> p h q n", q=2)
                sc2b = scv[:, :, 1, :].rearrange("p h n -> p n h").unsqueeze(3).to_broadcast([128, 128, 8, 16])
                sc1b = scv[:, :, 0, :].rearrange("p h n -> p n h").unsqueeze(3).to_broadcast([128, 128, 8, 16])
                E2b = Eev[:, :, 1, :].rearrange("p h n -> p n h").unsqueeze(3).to_broadcast([128, 128, 8, 16])
                thrb = thr[:].unsqueeze(1).to_broadcast([128, 128, 8, 16])
                E1sb = E1s[:].unsqueeze(1).to_broadcast([128, 128, 8, 16])
                s1b = s16v[:, :, 0, :].unsqueeze(1).to_broadcast([128, 128, 8, 16])
                for pc in range(8):
                    nsl = slice(pc * 16, (pc + 1) * 16)
                    csl = slice(pc * 2048, (pc + 1) * 2048)
                    S.op("dve", lambda e: e.tensor_tensor(out=X1v[:, nsl], in0=sc2b[:, nsl], in1=thrb[:, nsl], op=ALU.is_ge), [*t_scL, t_thr], [t_X1p[pc]])
                    S.op("pool", lambda e: e.tensor_tensor(out=X2v[:, nsl], in0=E2b[:, nsl], in1=E1sb[:, nsl], op=ALU.mult), [*t_EeL, t_E1s], [t_X2p[pc]])
                    S.op("dve", lambda e: e.tensor_tensor(out=X1[:, csl], in0=X1[:, csl], in1=X2[:, csl], op=ALU.mult), [t_X1p[pc], t_X2p[pc]], [t_X1p[pc]])
                    for j8 in range(2):
                        ps, pt = psbank(6, 8)
                        psb = ps.bitcast(BF16)
                        for jj in range(8):
                            n = pc * 16 + j8 * 8 + jj
                            S.op("pe", lambda e: e.transpose(out=psb[:, jj * 128:(jj + 1) * 128], in_=X1[:, n * 128:(n + 1) * 128], identity=ident_bf[:]),
                                 [t_X1p[pc], t_const], [pt])
                        copy("act", Trm3[:, pc * 16 + j8 * 8:pc * 16 + (j8 + 1) * 8, :], psb.rearrange("p (n t) -> p n t", t=128), [pt], [t_Trm])
                for pc in range(8):
                    nsl = slice(pc * 16, (pc + 1) * 16)
                    S.op("dve", lambda e: e.tensor_tensor(out=X1v[:, nsl], in0=sc1b[:, nsl], in1=s1b[:, nsl], op=ALU.is_equal), [*t_scL, t_s16], [t_X1p[pc]])
                    for j8 in range(2):
                        ps, pt = psbank(6, 8)
                        psb = ps.bitcast(BF16)
                        for jj in range(8):
                            n = pc * 16 + j8 * 8 + jj
                            S.op("pe", lambda e: e.transpose(out=psb[:, jj * 128:(jj + 1) * 128], in_=X1[:, n * 128:(n + 1) * 128], identity=ident_bf[:]),
                                 [t_X1p[pc], t_const], [pt])
                        copy("act", Orm3[:, pc * 16 + j8 * 8:pc * 16 + (j8 + 1) * 8, :], psb.rearrange("p (n t) -> p n t", t=128), [pt], [t_Orm])
                for t4 in range(32):
                    ps, pt = psbank(6, 8)
                    for tt in range(4):
                        t = t4 * 4 + tt
                        mm(ps[:, tt * 128:(tt + 1) * 128], Trm3[:, :, t], Orm3[:, :, t], True, True, [t_Trm, t_Orm], [pt])
                    copy("act", X2g[:, :, t4 * 4:(t4 + 1) * 4].rearrange("p n t -> p t n"), ps.rearrange("p (t n) -> p t n", n=128), [pt], [*t_X2p])
                S.dma(gt_d[i].rearrange("p n t -> p (n t)"), X2[:], reads=[*t_X2p], writes=[t_gt[i]])


            def d_load(g2, n):
                p = n % 4
                S.dma(uTc[p].rearrange("p k e -> p (k e)"), uTv[:, n, :], reads=[t_uTb[k_ * 2 + n // 64] for k_ in range(8)], writes=[t_uTc[p]])
                S.dma(vcb[p], vbv[:, n, :], reads=[t_vb[n // 8]], writes=[t_vcb[p]])
                S.dma(gtc[p][:], gt_d[2 * g2:2 * g2 + 2, :, n, :].rearrange("s p t -> p s t"), reads=[t_gt[2 * g2], t_gt[2 * g2 + 1]], writes=[t_gtc[p]])

            def d_hpre(n):
                p, pp = n % 4, n % 2
                for dk in range(8):
                    mm(PS[4 + pp][:, 0:256], uTc[p][:, dk, :], x2Td[:, dk, :], dk == 0, dk == 7, [t_uTc[p], t_x2Td], [PT[4 + pp]])

            def d_gate(n):
                p, pp = n % 4, n % 2
                S.op("act", lambda e: e.activation(out=Hb[pp][:], in_=PS[4 + pp][:, 0:256], func=AF.Gelu_apprx_tanh), [PT[4 + pp]], [t_Hb[pp]])
                S.op("dve", lambda e: e.tensor_tensor(out=Hg[pp][:].rearrange("p (s t) -> p s t", s=2), in0=Hb[pp][:].rearrange("p (s t) -> p s t", s=2),
                                                      in1=gtc[p][:], op=ALU.mult), [t_Hb[pp], t_gtc[p]], [t_Hg[pp]])

            def d_out(n):
                p, pp = n % 4, n % 2
                for sub in range(2):
                    for half in range(2):
                        mm(PS[sub * 2 + half][:, :], Hg[pp][:, sub * 128:(sub + 1) * 128], vcb[p][:, half * 512:(half + 1) * 512],
                           n == 0, n == 127, [t_Hg[pp], t_vcb[p]], [PT[sub * 2 + half]])

            def _nfree(ap):
                n = 1
                for s_ in ap.shape[1:]:
                    n *= int(s_)
                return n

            def est_us(o):
                if o[0] != "op" or o[1] not in ("dve", "act"):
                    return 0.0
                name, a, k = o[2]
                outap = k.get("out", a[0] if a else None)
                n = _nfree(outap) if outap is not None else 64
                if o[1] == "act":
                    return 0.3 + n / 1400.0 * (2.5 if n >= 512 and outap.dtype == BF16 and k.get("func") is None and _nfree(k.get("in_")) == 512 else 1.0)
                if name in ("max", "match_replace", "tensor_reduce"):
                    src_ = k.get("in_", k.get("in_values"))
                    return 0.35 + _nfree(src_) / 960.0
                if name == "reciprocal":
                    return 0.35 + 8 * n / 960.0
                if name in ("tensor_tensor", "scalar_tensor_tensor"):
                    fast = all(k[x].dtype == BF16 for x in ("in0", "in1")) and outap.dtype == BF16
                    return 0.35 + (1.0 if fast else 2.0) * n / 960.0
                return 0.35 + n / 960.0

            gphase(0)
            gphase(1)
            npair = 1 if "peer1" in debug else NT // 2
            for g2 in range(npair):
                ops = []
                if g2 + 1 < npair:
                    S.rec = []
                    gphase(2 * g2 + 2)
                    gphase(2 * g2 + 3)
                    ops = S.rec
                    S.rec = None
                clk = []
                c_ = 0.0
                for o in ops:
                    clk.append(c_)
                    c_ += est_us(o)
                t_chunk = max(2.4, c_ * 0.6 / 126.0)
                pos = 0
                S.dma(x2Td[:], x2T_d[:, :, g2 * 256:(g2 + 1) * 256], reads=[t_x2T[2 * g2], t_x2T[2 * g2 + 1]], writes=[t_x2Td])
                for n in range(128):
                    d_load(g2, n)
                    d_hpre(n)
                    d_gate(n)
                    if n >= 1:
                        d_out(n - 1)
                    e_ = pos
                    while e_ < len(ops) and clk[e_] <= (n + 1) * t_chunk:
                        e_ += 1
                    S.replay(ops[pos:e_])
                    pos = e_
                d_out(127)
                S.replay(ops[pos:])
                for sub in range(2):
                    i = 2 * g2 + sub
                    S.dma(x2t, x2v[i], reads=[t_x2d[i]], writes=[t_fin])
                    for half in range(2):
                        dsl = slice(half * 512, (half + 1) * 512)
                        S.op("dve", lambda e: e.scalar_tensor_tensor(out=rso[:, dsl], in0=x2t[:, dsl], scalar=ALPHA, in1=PS[sub * 2 + half][:, :],
                                                                     op0=ALU.mult, op1=ALU.add), [t_fin, PT[sub * 2 + half]], [t_fin])
                    ln_tile(i % 2, rso, rso, t_fin, t_fin, xno, t_fin)
                    S.dma(outv[i], rso, reads=[t_fin], writes=[t_out[i]])

        S.finish(final_toks)
    return nc, hc, DBG


_CACHE = {}


def _prep_inputs(inputs):
    shared = {}
    for name, shp in IN_SPECS:
        if name in ("x", "peer_uT"):
            continue
        a = np.asarray(inputs[name], dtype=np.float32)
        shared[name] = np.ascontiguousarray(a.reshape(shp))
    shared["peer_uT"] = np.ascontiguousarray(np.asarray(inputs["peer_u"], dtype=np.float32).reshape(16384, D_).T)
    return shared


def kernel(**inputs):
    if "nc" not in _CACHE:
        _CACHE["nc"] = build()
    nc, hc, _ = _CACHE["nc"]
    shared = _prep_inputs(inputs)
    for k, v in hc.items():
        shared["c_" + k] = v
    x = np.asarray(inputs["x"], dtype=np.float32)
    in_maps = []
    for b in range(8):
        m = dict(shared)
        m["x"] = np.ascontiguousarray(x[b])
        in_maps.append(m)
    res = run_bass_kernel_spmd(nc, in_maps, core_ids=list(range(8)))
    return np.stack([np.asarray(r["out"], dtype=np.float32) for r in res.results], axis=0)
```
